# Optimizing a Trainium2 kernel written in Bass

```python
import math, functools
import jax, jax.numpy as jnp
from jax import lax
import numpy as np

D_MODEL = 1024
BATCH = 2
SEQ = 8192
DEPTH = 2

GRID_W = 64
CTX_LEN = 256
N_DIR = 2
N_BRANCH = 4
BRANCH_W = D_MODEL // 2
RET_HEADS = 4
RET_DK = BRANCH_W // RET_HEADS
RET_CHUNK = 128
ROPE_BASE = 10000.0
LRU_W = BRANCH_W
LRU_BLOCKS = 4
LRU_BLOCK = LRU_W // LRU_BLOCKS
LRU_CONV = 4
LRU_C = 8.0
GDN_HEADS = 4
GDN_DK = BRANCH_W // GDN_HEADS
GDN_CHUNK = 64
GDN_CONV = 4
RWKV_N = 64
RWKV_HEADS = BRANCH_W // RWKV_N
RWKV_DECAY_RANK = 64
RWKV_A_RANK = 64
RWKV_GATE_RANK = 128
RWKV_IN = 3 * BRANCH_W + RWKV_GATE_RANK + N_DIR * RWKV_DECAY_RANK + N_DIR * RWKV_A_RANK
D_FF = ((8 * D_MODEL + 3 * 256 - 1) // (3 * 256)) * 256
IN_SPLITS = (("ret", 4 * BRANCH_W), ("lru", 2 * LRU_W), ("gdn_qkv", 3 * BRANCH_W),
             ("gdn_z", BRANCH_W), ("gdn_a", N_DIR * GDN_HEADS), ("gdn_b", N_DIR * GDN_HEADS),
             ("rwkv", RWKV_IN), ("gates", N_BRANCH * D_MODEL))
N_IN = sum(w for _, w in IN_SPLITS)
EPS = 1e-6
RWKV_LN_EPS = 64e-5

kernel_name = "hybrid_gated_parallel_diffusion_block"

F32 = jnp.float32


def rms_norm(x, g=None, eps=EPS):
    xf = x.astype(F32)
    y = xf * lax.rsqrt(jnp.mean(xf * xf, axis=-1, keepdims=True) + eps)
    if g is not None:
        y = y * g.astype(F32)
    return y.astype(x.dtype)


def l2_normalize(x, eps=EPS):
    xf = x.astype(F32)
    return (xf * lax.rsqrt(jnp.sum(xf * xf, axis=-1, keepdims=True) + eps)).astype(x.dtype)


def modulate(x, shift, scale):
    return x * (1.0 + scale) + shift


def centred_dwconv(x, w):
    k = w.shape[0]
    return lax.conv_general_dilated(
        x, w[:, None, :].astype(x.dtype), window_strides=(1,),
        padding=[(k // 2, k - 1 - k // 2)],
        dimension_numbers=("NWC", "WIO", "NWC"), feature_group_count=x.shape[-1])


def centred_shift(x):
    xp = jnp.pad(x, ((0, 0), (1, 1), (0, 0)))
    return 0.5 * (xp[:, :-2] + xp[:, 2:])


def axial_rope(rows, dim):
    t = jnp.arange(rows * GRID_W)
    row = (t // GRID_W).astype(F32)
    col = (t % GRID_W).astype(F32)
    quarter = dim // 4
    inv = ROPE_BASE ** (-jnp.arange(quarter, dtype=F32) / quarter)
    ang = jnp.concatenate([row[:, None] * inv, col[:, None] * inv], axis=-1)
    return jnp.cos(ang), jnp.sin(ang)


def apply_rope(x, cos, sin):
    half = x.shape[-1] // 2
    x1, x2 = x[..., :half], x[..., half:]
    c, s = cos[None, :, None, :], sin[None, :, None, :]
    return jnp.concatenate([x1 * c - x2 * s, x1 * s + x2 * c], axis=-1)


def _flip(arrays):
    return tuple(a[:, ::-1] for a in arrays)


def bidirectional(core_f, core_b, ctx_f, ctx_b, lat_f, lat_b, s0):
    yc_f, sc_f = core_f(ctx_f, s0)
    yl_f, _ = core_f(lat_f, sc_f)
    yc_b, sc_b = core_b(_flip(ctx_b), s0)
    yl_b, _ = core_b(_flip(lat_b), sc_b)
    return yc_f + yc_b[:, ::-1], yl_f + yl_b[:, ::-1]


def retention_core(inputs, s0, log_gamma):
    q, k, v = (a.astype(F32) for a in inputs)
    b, t, h, dk = q.shape
    dv = v.shape[-1]
    n = t // RET_CHUNK
    q = q.reshape(b, n, RET_CHUNK, h, dk)
    k = k.reshape(b, n, RET_CHUNK, h, dk)
    v = v.reshape(b, n, RET_CHUNK, h, dv)
    lg = log_gamma.astype(F32)
    pos = jnp.arange(RET_CHUNK, dtype=F32)
    rel = pos[:, None] - pos[None, :]
    lower = rel >= 0
    dmat = jnp.where(lower, jnp.exp(jnp.where(lower, rel, 0.0)[None] * lg[:, None, None]), 0.0)
    scores = jnp.einsum("bnihd,bnjhd->bnhij", q, k) * dmat
    intra = jnp.einsum("bnhij,bnjhe->bnihe", scores, v)
    q_dec = jnp.exp((pos + 1.0)[:, None] * lg)
    k_dec = jnp.exp((RET_CHUNK - 1.0 - pos)[:, None] * lg)
    kv = jnp.einsum("bnjhd,bnjhe->nbhde", k * k_dec[:, :, None], v)
    chunk_decay = jnp.exp(RET_CHUNK * lg)[:, None, None]

    def step(s, kv_n):
        return s * chunk_decay + kv_n, s

    s_final, s_prev = lax.scan(step, s0, kv)
    inter = jnp.einsum("bnihd,nbhde->bnihe", q * q_dec[:, :, None], s_prev)
    return (intra + inter).reshape(b, t, h, dv), s_final


def _lru_combine(left, right):
    a_l, b_l = left
    a_r, b_r = right
    return a_l * a_r, a_r * b_l + b_r


def rglru_core(inputs, h0):
    a, bx = (x.astype(F32) for x in inputs)
    a_cum, h = lax.associative_scan(_lru_combine, (a, bx), axis=1)
    h = h + a_cum * h0[:, None, :]
    return h, h[:, -1]


def gated_delta_core(inputs, s0):
    q, k, v, g, beta = (a.astype(F32) for a in inputs)
    b, t, h, dk = q.shape
    dv = v.shape[-1]
    c = GDN_CHUNK
    n = t // c

    def blocks(a):
        return jnp.moveaxis(a.reshape((b, n, c) + a.shape[2:]), 3, 2)

    q, k, v, g, beta = (blocks(a) for a in (q, k, v, g, beta))
    gc = jnp.cumsum(g, axis=-1)
    pos = jnp.arange(c)
    incl = pos[:, None] >= pos[None, :]
    strict = pos[:, None] > pos[None, :]
    diff = gc[..., :, None] - gc[..., None, :]
    decay = jnp.where(incl, jnp.exp(jnp.where(incl, diff, 0.0)), 0.0)
    kbeta = k * beta[..., None]
    a_low = jnp.where(strict, jnp.einsum("bnhid,bnhjd->bnhij", kbeta, k) * decay, 0.0)
    eye = jnp.eye(c, dtype=F32)
    t_inv = lax.linalg.triangular_solve(a_low + eye, jnp.broadcast_to(eye, a_low.shape),
                                        left_side=True, lower=True, unit_diagonal=True)
    u = t_inv @ (v * beta[..., None])
    w = t_inv @ (kbeta * jnp.exp(gc)[..., None])
    qk = jnp.einsum("bnhid,bnhjd->bnhij", q, k) * decay
    q_dec = q * jnp.exp(gc)[..., None]
    g_last = gc[..., -1]
    k_dec = k * jnp.exp(g_last[..., None] - gc)[..., None]
    xs = tuple(jnp.moveaxis(a, 1, 0) for a in (u, w, qk, q_dec, k_dec, g_last))

    def step(s, x):
        u_n, w_n, qk_n, qd_n, kd_n, gl_n = x
        v_new = u_n - jnp.einsum("bhcd,bhde->bhce", w_n, s)
        o = jnp.einsum("bhcd,bhde->bhce", qd_n, s) + jnp.einsum("bhij,bhje->bhie", qk_n, v_new)
        s = s * jnp.exp(gl_n)[..., None, None] + jnp.einsum("bhcd,bhce->bhde", kd_n, v_new)
        return s, o

    s_final, o = lax.scan(step, s0, xs)
    o = jnp.transpose(o, (1, 0, 3, 2, 4)).reshape(b, t, h, dv)
    return o, s_final


def rwkv7_core(inputs, s0):
    r, logw, k, v, kk, a = (jnp.moveaxis(x.astype(F32), 1, 0) for x in inputs)

    def step(s, x):
        r_t, lw_t, k_t, v_t, kk_t, a_t = x
        sa = jnp.einsum("bhvk,bhk->bhv", s, -kk_t)
        s = (s * jnp.exp(lw_t)[:, :, None, :] + sa[..., None] * (kk_t * a_t)[:, :, None, :]
             + v_t[..., None] * k_t[:, :, None, :])
        return s, jnp.einsum("bhvk,bhk->bhv", s, r_t)

    s_final, y = lax.scan(step, s0, (r, logw, k, v, kk, a))
    return jnp.moveaxis(y, 0, 1), s_final


def retention_branch(p_ctx, p_lat, decay_exp, cos, sin):
    def heads(p):
        q, k, v, g = jnp.split(p, 4, axis=-1)
        shp = p.shape[:2] + (RET_HEADS, RET_DK)
        return q.reshape(shp), k.reshape(shp) * RET_DK ** -0.5, v.reshape(shp), g

    qc, kc, vc, gc = heads(p_ctx)
    ql, kl, vl, gl = heads(p_lat)
    ql, kl = apply_rope(ql, cos, sin), apply_rope(kl, cos, sin)
    log_gamma = jnp.log1p(-jnp.exp2(-decay_exp.astype(F32)))
    core_f = functools.partial(retention_core, log_gamma=log_gamma[0])
    core_b = functools.partial(retention_core, log_gamma=log_gamma[1])
    s0 = jnp.zeros((p_ctx.shape[0], RET_HEADS, RET_DK, RET_DK), F32)
    yc, yl = bidirectional(core_f, core_b, (qc, kc, vc), (qc, kc, vc), (ql, kl, vl), (ql, kl, vl), s0)

    def out(y, g):
        return (rms_norm(y).reshape(g.shape) * jax.nn.silu(g.astype(F32))).astype(g.dtype)

    return out(yc, gc), out(yl, gl)


def rglru_branch(p_ctx, p_lat, conv_w, conv_b, gate_w, gate_b, lam):
    log_sig_lam = jax.nn.log_sigmoid(lam.astype(F32))

    def prep(p):
        xb, yb = jnp.split(p, 2, axis=-1)
        xc = centred_dwconv(xb, conv_w) + conv_b
        bsz, t = xc.shape[:2]
        blk = xc.reshape(bsz, t, LRU_BLOCKS, LRU_BLOCK)
        gates = jnp.einsum("btnc,dgncz->btdgnz", blk, gate_w).reshape(bsz, t, N_DIR, 2, LRU_W) + gate_b
        gates = jax.nn.sigmoid(gates.astype(F32))
        log_a = LRU_C * gates[:, :, :, 0] * log_sig_lam
        a = jnp.exp(log_a)
        bx = jnp.sqrt(-jnp.expm1(2.0 * log_a)) * gates[:, :, :, 1] * xc[:, :, None].astype(F32)
        return (a[:, :, 0], bx[:, :, 0]), (a[:, :, 1], bx[:, :, 1]), yb

    cf, cb, yc = prep(p_ctx)
    lf, lb, yl = prep(p_lat)
    h0 = jnp.zeros((p_ctx.shape[0], LRU_W), F32)
    hc, hl = bidirectional(rglru_core, rglru_core, cf, cb, lf, lb, h0)
    return ((hc * jax.nn.gelu(yc.astype(F32))).astype(p_ctx.dtype),
            (hl * jax.nn.gelu(yl.astype(F32))).astype(p_lat.dtype))


def gated_deltanet_branch(qkv_c, z_c, a_c, b_c, qkv_l, z_l, a_l, b_l, conv_w, a_log, dt_bias, norm_g):
    def prep(qkv, a, b):
        qkv = jax.nn.silu(centred_dwconv(qkv, conv_w))
        q, k, v = jnp.split(qkv, 3, axis=-1)
        shp = qkv.shape[:2] + (GDN_HEADS, GDN_DK)
        q = l2_normalize(q.reshape(shp)) * GDN_DK ** -0.5
        k = l2_normalize(k.reshape(shp))
        v = v.reshape(shp)
        dshp = qkv.shape[:2] + (N_DIR, GDN_HEADS)
        g = -jnp.exp(a_log.astype(F32)) * jax.nn.softplus(a.reshape(dshp).astype(F32) + dt_bias)
        beta = jax.nn.sigmoid(b.reshape(dshp).astype(F32))
        return (q, k, v, g[:, :, 0], beta[:, :, 0]), (q, k, v, g[:, :, 1], beta[:, :, 1])

    cf, cb = prep(qkv_c, a_c, b_c)
    lf, lb = prep(qkv_l, a_l, b_l)
    s0 = jnp.zeros((qkv_c.shape[0], GDN_HEADS, GDN_DK, GDN_DK), F32)
    yc, yl = bidirectional(gated_delta_core, gated_delta_core, cf, cb, lf, lb, s0)

    def out(y, z):
        zh = z.reshape(y.shape).astype(F32)
        return (rms_norm(y, norm_g) * jax.nn.silu(zh)).reshape(z.shape).astype(z.dtype)

    return out(yc, z_c), out(yl, z_l)


def head_layer_norm(y, g, b, eps):
    mu = jnp.mean(y, axis=-1, keepdims=True)
    var = jnp.mean(jnp.square(y - mu), axis=-1, keepdims=True)
    yn = (y - mu) * lax.rsqrt(var + eps)
    return yn.reshape(y.shape[:2] + (-1,)) * g.astype(F32) + b.astype(F32)


def rwkv7_branch(p_ctx, p_lat, mu, w0, w2, a0, a2, g2, k_k, k_a, r_k, ln_g, ln_b):
    bw = BRANCH_W
    cuts = [bw, 2 * bw, 3 * bw, 3 * bw + RWKV_GATE_RANK, 3 * bw + RWKV_GATE_RANK + N_DIR * RWKV_DECAY_RANK]

    def prep(p):
        p = p + (centred_shift(p) - p) * mu
        bsz, t = p.shape[:2]
        r, k, v, gd, wd, ad = jnp.split(p, cuts, axis=-1)
        wd = wd.reshape(bsz, t, N_DIR, RWKV_DECAY_RANK)
        ad = ad.reshape(bsz, t, N_DIR, RWKV_A_RANK)
        w_raw = -jax.nn.softplus(-(w0 + jnp.einsum("btdr,dre->btde", jnp.tanh(wd), w2)).astype(F32)) - 0.5
        logw = -jnp.exp(w_raw)
        a = jax.nn.sigmoid((a0 + jnp.einsum("btdr,dre->btde", ad, a2)).astype(F32))
        g = (jax.nn.sigmoid(gd) @ g2).astype(F32)
        hs = (bsz, t, RWKV_HEADS, RWKV_N)
        kk = l2_normalize((k * k_k).reshape(hs))
        kd = k[:, :, None].astype(F32) * (1.0 + (a - 1.0) * k_a)
        rh, vh = r.reshape(hs), v.reshape(hs)
        fwd = (rh, logw[:, :, 0].reshape(hs), kd[:, :, 0].reshape(hs), vh, kk, a[:, :, 0].reshape(hs))
        bwd = (rh, logw[:, :, 1].reshape(hs), kd[:, :, 1].reshape(hs), vh, kk, a[:, :, 1].reshape(hs))
        ksum = (kd[:, :, 0] + kd[:, :, 1]).reshape(hs)
        bonus = jnp.sum(rh * ksum * r_k, axis=-1, keepdims=True) * vh
        return fwd, bwd, g, bonus.reshape(bsz, t, bw)

    cf, cb, gc, bc = prep(p_ctx)
    lf, lb, gl, bl = prep(p_lat)
    s0 = jnp.zeros((p_ctx.shape[0], RWKV_HEADS, RWKV_N, RWKV_N), F32)
    yc, yl = bidirectional(rwkv7_core, rwkv7_core, cf, cb, lf, lb, s0)

    def out(y, g, bonus, dtype):
        return ((head_layer_norm(y, ln_g, ln_b, RWKV_LN_EPS) + bonus) * g).astype(dtype)

    return out(yc, gc, bc, p_ctx.dtype), out(yl, gl, bl, p_lat.dtype)


def split_projection(p):
    out = {}
    off = 0
    for name, width in IN_SPLITS:
        out[name] = p[..., off:off + width]
        off += width
    return out


def merge_branches(branches, gate_pre, gate_b, branch_w, out_w):
    bsz, t = gate_pre.shape[:2]
    gates = jax.nn.sigmoid((gate_pre + gate_b).astype(F32)).reshape(bsz, t, N_BRANCH, D_MODEL)
    ys = jnp.stack(branches, axis=2)
    proj = jnp.einsum("btkw,kwd->btkd", ys, branch_w)
    merged = jnp.sum(gates.astype(proj.dtype) * proj, axis=2)
    return merged @ out_w


def swiglu(u, w1, w3, w2):
    return (jax.nn.silu(u @ w1) * (u @ w3)) @ w2


def setup_inputs(seed: int = 0) -> dict:
    key = jax.random.key(seed)
    ks = iter(jax.random.split(key, 48))
    L = DEPTH

    def nrm(shape, scale):
        return jax.random.normal(next(ks), shape, F32) * scale

    def unif(shape, lo, hi):
        return jax.random.uniform(next(ks), shape, F32, lo, hi)

    x = nrm((BATCH, SEQ, D_MODEL), 1.0)
    c = nrm((BATCH, D_MODEL), 1.0)
    ctx = nrm((BATCH, CTX_LEN, D_MODEL), 1.0)
    c_ctx = nrm((D_MODEL,), 1.0)
    mod_w = nrm((L, D_MODEL, 6 * D_MODEL), D_MODEL ** -0.5)
    mod_b = nrm((L, 6 * D_MODEL), 0.02)
    norm1_g = 1.0 + nrm((L, D_MODEL), 0.02)
    norm2_g = 1.0 + nrm((L, D_MODEL), 0.02)
    in_w = nrm((L, D_MODEL, N_IN), D_MODEL ** -0.5)
    gate_b = nrm((L, N_BRANCH * D_MODEL), 0.02)
    ret_decay_exp = 5.0 + jnp.arange(RET_HEADS, dtype=F32) + nrm((L, N_DIR, RET_HEADS), 0.1)
    lru_conv_w = nrm((L, LRU_CONV, LRU_W), LRU_CONV ** -0.5)
    lru_conv_b = nrm((L, LRU_W), 0.02)
    lru_gate_w = nrm((L, N_DIR, 2, LRU_BLOCKS, LRU_BLOCK, LRU_BLOCK), LRU_BLOCK ** -0.5)
    lru_gate_b = nrm((L, N_DIR, 2, LRU_W), 0.02)
    s = unif((L, N_DIR, LRU_W), 0.9, 0.999) ** (1.0 / LRU_C)
    lru_lambda = jnp.log(s) - jnp.log1p(-s)
    gdn_conv_w = nrm((L, GDN_CONV, 3 * BRANCH_W), GDN_CONV ** -0.5)
    gdn_a_log = jnp.log(unif((L, N_DIR, GDN_HEADS), 1.0, 16.0))
    dt = jnp.exp(unif((L, N_DIR, GDN_HEADS), math.log(1e-3), math.log(1e-1)))
    gdn_dt_bias = dt + jnp.log(-jnp.expm1(-dt))
    gdn_norm_g = 1.0 + nrm((L, GDN_DK), 0.02)
    rwkv_mu = unif((L, RWKV_IN), 0.0, 1.0)
    ramp = jnp.linspace(0.0, 1.0, BRANCH_W, dtype=F32)
    rwkv_w0 = -6.0 + 5.0 * ramp ** 0.9 + nrm((L, N_DIR, BRANCH_W), 0.1)
    rwkv_w2 = nrm((L, N_DIR, RWKV_DECAY_RANK, BRANCH_W), 0.1 * RWKV_DECAY_RANK ** -0.5)
    rwkv_a0 = nrm((L, N_DIR, BRANCH_W), 0.1)
    rwkv_a2 = nrm((L, N_DIR, RWKV_A_RANK, BRANCH_W), 0.1 * RWKV_A_RANK ** -0.5)
    rwkv_g2 = nrm((L, RWKV_GATE_RANK, BRANCH_W), RWKV_GATE_RANK ** -0.5)
    rwkv_k_k = 0.85 + nrm((L, BRANCH_W), 0.02)
    rwkv_k_a = 1.0 + nrm((L, BRANCH_W), 0.02)
    rwkv_r_k = nrm((L, RWKV_HEADS, RWKV_N), 0.1)
    rwkv_ln_g = 1.0 + nrm((L, BRANCH_W), 0.02)
    rwkv_ln_b = nrm((L, BRANCH_W), 0.02)
    branch_w = nrm((L, N_BRANCH, BRANCH_W, D_MODEL), BRANCH_W ** -0.5)
    out_w = nrm((L, D_MODEL, D_MODEL), D_MODEL ** -0.5)
    ffn_w1 = nrm((L, D_MODEL, D_FF), D_MODEL ** -0.5)
    ffn_w3 = nrm((L, D_MODEL, D_FF), D_MODEL ** -0.5)
    ffn_w2 = nrm((L, D_FF, D_MODEL), D_FF ** -0.5)
    final_norm_g = 1.0 + nrm((D_MODEL,), 0.02)
    return {"x": x, "c": c, "ctx": ctx, "c_ctx": c_ctx, "mod_w": mod_w, "mod_b": mod_b,
            "norm1_g": norm1_g, "norm2_g": norm2_g, "in_w": in_w, "gate_b": gate_b,
            "ret_decay_exp": ret_decay_exp, "lru_conv_w": lru_conv_w, "lru_conv_b": lru_conv_b,
            "lru_gate_w": lru_gate_w, "lru_gate_b": lru_gate_b, "lru_lambda": lru_lambda,
            "gdn_conv_w": gdn_conv_w, "gdn_a_log": gdn_a_log, "gdn_dt_bias": gdn_dt_bias,
            "gdn_norm_g": gdn_norm_g, "rwkv_mu": rwkv_mu, "rwkv_w0": rwkv_w0, "rwkv_w2": rwkv_w2,
            "rwkv_a0": rwkv_a0, "rwkv_a2": rwkv_a2, "rwkv_g2": rwkv_g2, "rwkv_k_k": rwkv_k_k,
            "rwkv_k_a": rwkv_k_a, "rwkv_r_k": rwkv_r_k, "rwkv_ln_g": rwkv_ln_g, "rwkv_ln_b": rwkv_ln_b,
            "branch_w": branch_w, "out_w": out_w, "ffn_w1": ffn_w1, "ffn_w3": ffn_w3,
            "ffn_w2": ffn_w2, "final_norm_g": final_norm_g}


def reference(x, c, ctx, c_ctx, mod_w, mod_b, norm1_g, norm2_g, in_w, gate_b, ret_decay_exp,
              lru_conv_w, lru_conv_b, lru_gate_w, lru_gate_b, lru_lambda, gdn_conv_w, gdn_a_log,
              gdn_dt_bias, gdn_norm_g, rwkv_mu, rwkv_w0, rwkv_w2, rwkv_a0, rwkv_a2, rwkv_g2,
              rwkv_k_k, rwkv_k_a, rwkv_r_k, rwkv_ln_g, rwkv_ln_b, branch_w, out_w, ffn_w1, ffn_w3,
              ffn_w2, final_norm_g):
    rows = x.shape[1] // GRID_W
    cos, sin = axial_rope(rows, RET_DK)
    cond_lat = jax.nn.silu(c)[:, None, :]
    cond_ctx = jax.nn.silu(c_ctx)[None, None, :]
    h_lat, h_ctx = x, ctx
    for l in range(DEPTH):
        last = l == DEPTH - 1
        ml = jnp.split(cond_lat @ mod_w[l] + mod_b[l], 6, axis=-1)
        mc = jnp.split(cond_ctx @ mod_w[l] + mod_b[l], 6, axis=-1)
        pl = split_projection(modulate(rms_norm(h_lat, norm1_g[l]), ml[0], ml[1]) @ in_w[l])
        pc = split_projection(modulate(rms_norm(h_ctx, norm1_g[l]), mc[0], mc[1]) @ in_w[l])

        ret_c, ret_l = retention_branch(pc["ret"], pl["ret"], ret_decay_exp[l], cos, sin)
        lru_c, lru_l = rglru_branch(pc["lru"], pl["lru"], lru_conv_w[l], lru_conv_b[l],
                                    lru_gate_w[l], lru_gate_b[l], lru_lambda[l])
        gdn_c, gdn_l = gated_deltanet_branch(pc["gdn_qkv"], pc["gdn_z"], pc["gdn_a"], pc["gdn_b"],
                                             pl["gdn_qkv"], pl["gdn_z"], pl["gdn_a"], pl["gdn_b"],
                                             gdn_conv_w[l], gdn_a_log[l], gdn_dt_bias[l], gdn_norm_g[l])
        rwkv_c, rwkv_l = rwkv7_branch(pc["rwkv"], pl["rwkv"], rwkv_mu[l], rwkv_w0[l], rwkv_w2[l],
                                      rwkv_a0[l], rwkv_a2[l], rwkv_g2[l], rwkv_k_k[l], rwkv_k_a[l],
                                      rwkv_r_k[l], rwkv_ln_g[l], rwkv_ln_b[l])

        y_lat = merge_branches([ret_l, lru_l, gdn_l, rwkv_l], pl["gates"], gate_b[l], branch_w[l], out_w[l])
        h_lat = h_lat + ml[2] * y_lat
        h_lat = h_lat + ml[5] * swiglu(modulate(rms_norm(h_lat, norm2_g[l]), ml[3], ml[4]),
                                       ffn_w1[l], ffn_w3[l], ffn_w2[l])
        if not last:
            y_ctx = merge_branches([ret_c, lru_c, gdn_c, rwkv_c], pc["gates"], gate_b[l], branch_w[l], out_w[l])
            h_ctx = h_ctx + mc[2] * y_ctx
            h_ctx = h_ctx + mc[5] * swiglu(modulate(rms_norm(h_ctx, norm2_g[l]), mc[3], mc[4]),
                                           ffn_w1[l], ffn_w3[l], ffn_w2[l])
    return rms_norm(h_lat, final_norm_g)
```

```python
import contextlib
import numpy as np
import concourse.bass as bass
import concourse.mybir as mybir
from concourse.bass_utils import run_bass_kernel_spmd

F32 = mybir.dt.float32
BF16 = mybir.dt.bfloat16
AF = mybir.ActivationFunctionType
ALU = mybir.AluOpType
AX = mybir.AxisListType

ENG_NAMES = ("pe", "act", "dve", "pool", "sp")
EPOCH = 16000
N_DMA_SEMS = 12

D = 1024
NB = 2
SEQ = 8192
CTX = 256
TTOT = SEQ + CTX
DFF = 2816
N_IN = 11152
GATE_OFF = N_IN - 4096
EPS = 1e-6


class Prog:
    def __init__(self, nc, same_engine_sync=True):
        self.nc = nc
        self.es = contextlib.ExitStack()
        self.ops = {e: [] for e in ENG_NAMES}
        self.cnt = {e: 0 for e in ENG_NAMES}
        self.sem = {}
        self.known = {e: {} for e in ENG_NAMES}
        self.semobj = {}
        self.ep = {}
        self.finals = []
        for e in ENG_NAMES:
            if e != "sp":
                self._new_epoch(e)
        self.dma_sems = {}
        self.dma_k = {}
        for q in ("sp", "act", "pool"):
            self.dma_sems[q] = []
            for i in range(N_DMA_SEMS):
                nm = f"d_{q}_{i}"
                self.semobj[nm] = self.es.enter_context(nc.semaphore(nm))
                self.dma_sems[q].append(nm)
            self.dma_k[q] = 0
        self.lastw = {}
        self.readers = {}
        self.same_engine_sync = same_engine_sync
        self.n_wait = 0
        self.n_ins = 0

    def _new_epoch(self, e):
        if e in self.sem:
            self.finals.append((self.sem[e], self.cnt[e]))
        k = self.ep.get(e, -1) + 1
        self.ep[e] = k
        nm = f"s_{e}_{k}"
        self.semobj[nm] = self.es.enter_context(self.nc.semaphore(nm))
        self.sem[e] = nm
        self.cnt[e] = 0

    def sb(self, name, shape, dtype=F32):
        return self.es.enter_context(self.nc.sbuf_tensor("sb_" + name, list(shape), dtype))

    def ps(self, name, shape, dtype=F32):
        return self.es.enter_context(self.nc.psum_tensor("ps_" + name, list(shape), dtype))

    def _deps(self, eng, reads, writes):
        need = {}
        for t in reads:
            for ev in self.lastw.get(t, ()):
                need[ev[0]] = max(need.get(ev[0], 0), ev[1])
        for t in writes:
            for ev in self.lastw.get(t, ()):
                need[ev[0]] = max(need.get(ev[0], 0), ev[1])
            for ev in self.readers.get(t, ()):
                need[ev[0]] = max(need.get(ev[0], 0), ev[1])
        kn = self.known[eng]
        for s, v in need.items():
            if kn.get(s, 0) >= v:
                continue
            if s.startswith("s_" + eng + "_") and (eng == "pe" or not self.same_engine_sync):
                continue
            self.ops[eng].append(("wait", self.semobj[s], v))
            self.n_wait += 1
            kn[s] = v

    def _commit(self, evs, reads, writes):
        for t in writes:
            self.lastw[t] = list(evs)
            self.readers[t] = []
        for t in reads:
            if t in writes:
                continue
            self.readers.setdefault(t, []).extend(evs)

    def op(self, eng, meth, reads=(), writes=(), **kw):
        if self.cnt[eng] >= EPOCH:
            self._new_epoch(eng)
        self._deps(eng, reads, writes)
        self.cnt[eng] += 1
        s = self.sem[eng]
        self.ops[eng].append(("ins", meth, kw, self.semobj[s], 1))
        self._commit([(s, self.cnt[eng])], reads, writes)
        self.n_ins += 1

    def dma(self, q, out, in_, reads=(), writes=(), **kw):
        self.dma_group([(q, out, in_)], reads, writes, **kw)

    def dma_group(self, items, reads=(), writes=(), **kw):
        for q in dict.fromkeys(it[0] for it in items):
            self._deps(q, reads, writes)
        evs = []
        for (q, out, in_) in items:
            k = self.dma_k[q]
            self.dma_k[q] += 1
            s = self.dma_sems[q][k % N_DMA_SEMS]
            target = 16 * (k // N_DMA_SEMS + 1)
            if target > 16 and self.known[q].get(s, 0) < target - 16:
                self.ops[q].append(("wait", self.semobj[s], target - 16))
                self.known[q][s] = target - 16
            self.ops[q].append(("ins", "dma_start", dict(kw, out=out, in_=in_), self.semobj[s], 16))
            evs.append((s, target))
            self.n_ins += 1
        self._commit(evs, reads, writes)

    def finish_wait(self, eng, tokens):
        self._deps(eng, tokens, ())

    def barrier(self):
        evs = [(self.sem[x], self.cnt[x]) for x in ENG_NAMES if x != "sp" and self.cnt[x] > 0] + list(self.finals)
        for q in self.dma_sems:
            k = self.dma_k[q]
            for i, s in enumerate(self.dma_sems[q]):
                n = (k - i + N_DMA_SEMS - 1) // N_DMA_SEMS if k > i else 0
                if n > 0:
                    evs.append((s, 16 * n))
        for e in ENG_NAMES:
            kn = self.known[e]
            for (s, v) in evs:
                if kn.get(s, 0) >= v:
                    continue
                if s.startswith("s_" + e + "_"):
                    continue
                self.ops[e].append(("wait", self.semobj[s], v))
                kn[s] = v
        self.lastw.clear()
        self.readers.clear()

    def arena_init(self, nwords):
        self.arena = self.sb("arena", [128, nwords], F32)
        self.aoff = 0
        self.anw = nwords

    def carve(self, shape, dtype=F32):
        n = 1
        for d in shape[1:]:
            n *= d
        nw = n if dtype == F32 else (n + 1) // 2
        assert self.aoff + nw <= self.anw, ("arena overflow", self.aoff, nw, self.anw)
        v = self.arena[0:shape[0], self.aoff:self.aoff + nw]
        self.aoff += nw
        if dtype != F32:
            v = v.bitcast(dtype)[:, 0:n]
        if len(shape) == 3:
            v = v.rearrange("p (a b) -> p a b", a=shape[1])
        elif len(shape) == 4:
            v = v.rearrange("p (a b c) -> p a b c", a=shape[1], b=shape[2])
        return v

    def mark(self):
        return self.aoff

    def release(self, m):
        self.barrier()
        self.aoff = m

    def emit(self):
        nc = self.nc
        with nc.Block() as block:
            def mk(ename):
                lst = self.ops[ename]

                def body(e):
                    for it in lst:
                        if it[0] == "wait":
                            e.wait_ge(it[1], it[2])
                        else:
                            getattr(e, it[1])(**it[2]).then_inc(it[3], it[4])
                return body
            block.tensor(mk("pe"))
            block.scalar(mk("act"))
            block.vector(mk("dve"))
            block.gpsimd(mk("pool"))
            block.sync(mk("sp"))
        self.es.close()


class RR:
    def __init__(self, items):
        self.items = list(items)
        self.i = 0

    def next(self):
        x = self.items[self.i % len(self.items)]
        self.i += 1
        return x


def _common_consts(P):
    c = {}
    c["ones_bf"] = P.sb("ones_bf", [128, 128], BF16)
    P.op("pool", "memset", ap=c["ones_bf"][:], constant=1.0, writes=["ones_bf"])
    c["ones_f"] = P.sb("ones_f", [128, 128], F32)
    P.op("pool", "memset", ap=c["ones_f"][:], constant=1.0, writes=["ones_f"])
    c["ident"] = P.sb("ident", [128, 128], F32)
    P.op("pool", "memset", ap=c["ident"][:], constant=1.0, writes=["ident"])
    P.op("pool", "affine_select", out=c["ident"][:], in_=c["ident"][:], pattern=[[-1, 128]],
         compare_op=ALU.is_equal, fill=0.0, base=0, channel_multiplier=1, reads=["ident"], writes=["ident"])
    c["epsb"] = P.sb("epsb", [128, 1], F32)
    P.op("pool", "memset", ap=c["epsb"][:], constant=EPS, writes=["epsb"])
    return c


def _mod_vectors(P, cst, cvec, mod_w, mod_b, nchunks, wst, psum, modsb):
    craw = P.sb("craw", [128, 8, 2], F32)
    csil = P.sb("csil", [128, 8, 2], F32)
    mb = P.sb("modb", [128, 48], F32)
    P.dma_group([("sp", craw[:, :, s], cvec[s, :].rearrange("(k p) -> p k", p=128)) for s in range(2)], writes=["craw"],
                allow_slow_non_contiguous=True)
    P.dma("sp", mb[:, 0:nchunks], mod_b[0, 0:nchunks * 128].rearrange("(j p) -> p j", p=128), writes=["modb"],
          allow_slow_non_contiguous=True)
    P.op("act", "activation", out=csil[:], in_=craw[:], func=AF.Silu, reads=["craw"], writes=["csil"])
    ng = nchunks // 4
    for g in range(ng):
        st = wst[g % 2]
        tok = ("wst", g % 2)
        v = st[:, 0:4096].rearrange("p (k c) -> p k c", k=8)
        P.dma_group([("sp" if kc % 2 == 0 else "pool", v[:, kc, :], mod_w[kc * 128:(kc + 1) * 128, g * 512:(g + 1) * 512])
                     for kc in range(8)], writes=[tok])
        for jj in range(4):
            j = g * 4 + jj
            for kc in range(8):
                P.op("pe", "matmul", out=psum[:, j, :], lhsT=v[:, kc, jj * 128:(jj + 1) * 128], rhs=csil[:, kc, :],
                     start=(kc == 0), stop=(kc == 7), reads=[tok, "csil"], writes=["modps"])
    for s in range(2):
        P.op("dve", "tensor_tensor", out=modsb[:, 0:nchunks, s], in0=psum[:, 0:nchunks, s], in1=mb[:, 0:nchunks],
             op=ALU.add, reads=["modps", "modb"], writes=["modsb"])


def _rms_stats(P, cst, hT, nk, t0, tn, htok, sq, ssps, rstd, tagsfx):
    for kc in range(nk):
        P.op("act", "activation", out=sq[:, kc, 0:tn], in_=hT[:, kc, t0:t0 + tn], func=AF.Square,
             reads=[htok(kc)], writes=[("sq", kc)])
    for kc in range(nk):
        P.op("pe", "matmul", out=ssps[:, 0:tn], lhsT=cst["ones_bf"][:], rhs=sq[:, kc, 0:tn], start=(kc == 0),
             stop=(kc == nk - 1), reads=[("sq", kc), "ones_bf"], writes=["ssps"])
    P.op("act", "activation", out=rstd[:, 0:tn], in_=ssps[:, 0:tn], func=AF.Ln, scale=1.0 / (nk * 128),
         bias=cst["epsb"][:], reads=["ssps", "epsb"], writes=["rstd"])
    P.op("act", "activation", out=rstd[:, 0:tn], in_=rstd[:, 0:tn], func=AF.Exp, scale=-0.5,
         reads=["rstd"], writes=["rstd"])


def build_B(last: bool):
    NL = 2048
    NC_ = 0 if last else 64
    NT = NL + NC_
    nc = bass.Bass("TRN2", target_bir_lowering=False)
    hT_in = nc.dram_tensor("hT", [D, NT], F32, kind="ExternalInput").ap()
    ysT_in = nc.dram_tensor("ysT", [2048, NT], F32, kind="ExternalInput").ap()
    cvec = nc.dram_tensor("cvec", [2, D], F32, kind="ExternalInput").ap()
    mod_w = nc.dram_tensor("mod_w", [D, 6 * D], F32, kind="ExternalInput").ap()
    mod_b = nc.dram_tensor("mod_b", [1, 6 * D], F32, kind="ExternalInput").ap()
    n1g = nc.dram_tensor("n1g", [1, D], F32, kind="ExternalInput").ap()
    n2g = nc.dram_tensor("n2g", [1, D], F32, kind="ExternalInput").ap()
    fng = nc.dram_tensor("fng", [1, D], F32, kind="ExternalInput").ap()
    wg = nc.dram_tensor("wg", [D, 4096], F32, kind="ExternalInput").ap()
    gate_b = nc.dram_tensor("gate_b", [1, 4096], F32, kind="ExternalInput").ap()
    wbr = nc.dram_tensor("wbr", [4, 512, D], F32, kind="ExternalInput").ap()
    wo = nc.dram_tensor("wo", [D, D], F32, kind="ExternalInput").ap()
    w1 = nc.dram_tensor("w1", [D, DFF], F32, kind="ExternalInput").ap()
    w3 = nc.dram_tensor("w3", [D, DFF], F32, kind="ExternalInput").ap()
    w2 = nc.dram_tensor("w2", [DFF, D], F32, kind="ExternalInput").ap()
    if last:
        out = nc.dram_tensor("out", [NL, D], F32, kind="ExternalOutput").ap()
    else:
        out = nc.dram_tensor("out", [D, NT], F32, kind="ExternalOutput").ap()

    P = Prog(nc)
    cst = _common_consts(P)
    HMAX = 1088
    hT = P.sb("hT", [128, 8, HMAX], F32)
    xu = P.sb("xu", [128, 8, HMAX], BF16)
    A = P.sb("A", [128, 22, HMAX], BF16)
    mg = P.sb("mg", [128, 8, HMAX], BF16)
    wst = [P.sb(f"wst{i}", [128, 4096], F32) for i in range(2)]
    wbf = [P.sb(f"wbf{i}", [128, 4096], BF16) for i in range(2)]
    yst = [P.sb(f"yst{i}", [128, 512], F32) for i in range(2)]
    sq = P.sb("sq", [128, 8, 512], BF16)
    rstd = P.sb("rstd", [128, 512], F32)
    tmp = [P.sb(f"tmp{i}", [128, 512], F32) for i in range(3)]
    modsb = P.sb("modsb", [128, 48, 2], F32)
    gvec = P.sb("gvec", [128, 8, 3], F32)
    gbv = P.sb("gbv", [128, 32], F32)
    GS = P.sb("GS", [128, 8, 2, 4], F32)
    psA = [P.ps(f"psA{i}", [128, 512], F32) for i in range(2)]
    psB = [P.ps(f"psB{i}", [128, 512], F32) for i in range(2)]
    psS = P.ps("psS", [128, 512], F32)
    psM = P.ps("psM", [128, 48, 2], F32)
    psT = [P.ps(f"psT{i}", [128, 512], F32) for i in range(2)]

    P.dma_group([("sp", gvec[:, :, i], g_[0, :].rearrange("(k p) -> p k", p=128)) for i, g_ in enumerate((n1g, n2g, fng))],
                writes=["gvec"], allow_slow_non_contiguous=True)
    P.dma("sp", gbv[:], gate_b[0, :].rearrange("(j p) -> p j", p=128), writes=["gbv"], allow_slow_non_contiguous=True)
    _mod_vectors(P, cst, cvec, mod_w, mod_b, 48, wst, psM, modsb)
    for s in range(2):
        for (gi, sc_c, sh_c, col) in ((0, 1, 0, 0), (1, 4, 3, 2)):
            P.op("dve", "scalar_tensor_tensor", out=GS[:, :, s, col], in0=modsb[:, sc_c * 8:(sc_c + 1) * 8, s], scalar=1.0,
                 in1=gvec[:, :, gi], op0=ALU.add, op1=ALU.mult, reads=["modsb", "gvec"], writes=["GS"])
            P.op("dve", "tensor_copy", out=GS[:, :, s, col + 1], in_=modsb[:, sh_c * 8:(sh_c + 1) * 8, s],
                 reads=["modsb"], writes=["GS"])

    castq = RR(["dve", "pool"])
    dq = RR(["sp", "pool"])

    def cast(out, in_, reads, writes):
        e = castq.next()
        P.op(e, "tensor_copy", out=out, in_=in_, reads=reads, writes=writes)

    wctr = [0]

    def load_w(pieces, n):
        i = wctr[0] % 2
        wctr[0] += 1
        P.dma_group([(dq.next(), vf(wst[i]), src) for (vf, src) in pieces], writes=[("wst", i)])
        cast(wbf[i][:, 0:n], wst[i][:, 0:n], [("wst", i)], [("wbf", i)])
        return wbf[i], ("wbf", i)

    halves = [(0, 1024, 0), (1024, 1024, NC_)]
    for (l0, nl, ncx) in halves:
        ntok = nl + ncx
        blocks = [(o, 512, 0) for o in range(0, nl, 512)] + ([(nl, ncx, 1)] if ncx else [])
        def gcol(o):
            return l0 + o if o < nl else NL + (o - nl)
        for kc in range(8):
            for (o, n, s) in blocks:
                P.dma(dq.next(), hT[:, kc, o:o + n], hT_in[kc * 128:(kc + 1) * 128, gcol(o):gcol(o) + n],
                      writes=[("h", kc, o)])
        yi = 0
        for c16 in range(16):
            for (o, n, s) in blocks:
                st = yst[yi % 2]
                P.dma(dq.next(), st[:, 0:n], ysT_in[c16 * 128:(c16 + 1) * 128, gcol(o):gcol(o) + n], writes=[("yst", yi % 2)])
                cast(A[:, c16, o:o + n], st[:, 0:n], [("yst", yi % 2)], [("A", c16, o)])
                yi += 1
        for (o, n, s) in blocks:
            _rms_stats(P, cst, hT, 8, o, n, lambda kc: ("h", kc, o), sq, psS, rstd, "")
            for kc in range(8):
                t = tmp[kc % 2]
                P.op("dve", "tensor_tensor", out=t[:, 0:n], in0=hT[:, kc, o:o + n], in1=rstd[:, 0:n], op=ALU.mult,
                     reads=[("h", kc, o), "rstd"], writes=[("tmp", kc % 2)])
                P.op("pool", "tensor_scalar", out=xu[:, kc, o:o + n], in0=t[:, 0:n], scalar1=GS[:, kc, s, 0:1],
                     scalar2=GS[:, kc, s, 1:2], op0=ALU.mult, op1=ALU.add, reads=[("tmp", kc % 2), "GS"],
                     writes=[("xu", kc, o)])
        for j in range(8):
            gsrc = wg.rearrange("(kc p) (k j c) -> p k kc j c", p=128, k=4, j=8)
            pieces = [((lambda t, k=k: t[:, k * 1024:(k + 1) * 1024].rearrange("p (kc c) -> p kc c", kc=8)),
                       gsrc[:, k, :, j, :]) for k in range(4)]
            wgt, wgtok = load_w(pieces, 4096)
            wgv = wgt[:, 0:4096].rearrange("p (k kc c) -> p k kc c", k=4, kc=8)
            bsrc = wbr.rearrange("k (kc p) (j c) -> p k kc j c", p=128, j=8)
            pieces = [((lambda t, k=k: t[:, k * 512:(k + 1) * 512].rearrange("p (kc c) -> p kc c", kc=4)),
                       bsrc[:, k, :, j, :]) for k in range(4)]
            wbt, wbtok = load_w(pieces, 2048)
            wbv = wbt[:, 0:2048].rearrange("p (k kc c) -> p k kc c", k=4, kc=4)
            for (o, n, s) in blocks:
                for k in range(4):
                    pa = psA[k % 2]
                    pb = psB[k % 2]
                    for kc in range(8):
                        P.op("pe", "matmul", out=pa[:, 0:n], lhsT=wgv[:, k, kc, :], rhs=xu[:, kc, o:o + n], start=(kc == 0),
                             stop=(kc == 7), reads=[wgtok, ("xu", kc, o)], writes=[("psA", k % 2)])
                    for kc in range(4):
                        P.op("pe", "matmul", out=pb[:, 0:n], lhsT=wbv[:, k, kc, :], rhs=A[:, k * 4 + kc, o:o + n],
                             start=(kc == 0), stop=(kc == 3), reads=[wbtok, ("A", k * 4 + kc, o)], writes=[("psB", k % 2)])
                    sg = tmp[k % 2]
                    P.op("act", "activation", out=sg[:, 0:n], in_=pa[:, 0:n], func=AF.Sigmoid,
                         bias=gbv[:, k * 8 + j:k * 8 + j + 1], reads=[("psA", k % 2), "gbv"], writes=[("tmp", k % 2)])
                    if k == 0:
                        P.op("dve", "tensor_tensor", out=tmp[2][:, 0:n], in0=sg[:, 0:n], in1=pb[:, 0:n], op=ALU.mult,
                             reads=[("tmp", 0), ("psB", 0)], writes=[("tmp", 2)])
                    else:
                        P.op("dve", "tensor_tensor", out=sg[:, 0:n], in0=sg[:, 0:n], in1=pb[:, 0:n], op=ALU.mult,
                             reads=[("tmp", k % 2), ("psB", k % 2)], writes=[("tmp", k % 2)])
                        if k < 3:
                            P.op("pool", "tensor_tensor", out=tmp[2][:, 0:n], in0=tmp[2][:, 0:n], in1=sg[:, 0:n], op=ALU.add,
                                 reads=[("tmp", 2), ("tmp", k % 2)], writes=[("tmp", 2)])
                        else:
                            P.op("pool", "tensor_tensor", out=mg[:, j, o:o + n], in0=tmp[2][:, 0:n], in1=sg[:, 0:n], op=ALU.add,
                                 reads=[("tmp", 2), ("tmp", k % 2)], writes=[("mg", j, o)])
        for j in range(8):
            osrc = wo.rearrange("(kc p) (j c) -> p kc j c", p=128, j=8)
            wt, wtok = load_w([((lambda t: t[:, 0:1024].rearrange("p (kc c) -> p kc c", kc=8)), osrc[:, :, j, :])], 1024)
            wv = wt[:, 0:1024].rearrange("p (kc c) -> p kc c", kc=8)
            for bi, (o, n, s) in enumerate(blocks):
                pa = psA[bi % 2]
                for kc in range(8):
                    P.op("pe", "matmul", out=pa[:, 0:n], lhsT=wv[:, kc, :], rhs=mg[:, kc, o:o + n], start=(kc == 0),
                         stop=(kc == 7), reads=[wtok, ("mg", kc, o)], writes=[("psA", bi % 2)])
                P.op("dve", "scalar_tensor_tensor", out=hT[:, j, o:o + n], in0=pa[:, 0:n], scalar=modsb[:, 16 + j, s:s + 1],
                     in1=hT[:, j, o:o + n], op0=ALU.mult, op1=ALU.add, reads=[("psA", bi % 2), "modsb", ("h", j, o)],
                     writes=[("h", j, o)])
        for (o, n, s) in blocks:
            _rms_stats(P, cst, hT, 8, o, n, lambda kc: ("h", kc, o), sq, psS, rstd, "")
            for kc in range(8):
                t = tmp[kc % 2]
                P.op("dve", "tensor_tensor", out=t[:, 0:n], in0=hT[:, kc, o:o + n], in1=rstd[:, 0:n], op=ALU.mult,
                     reads=[("h", kc, o), "rstd"], writes=[("tmp", kc % 2)])
                P.op("pool", "tensor_scalar", out=xu[:, kc, o:o + n], in0=t[:, 0:n], scalar1=GS[:, kc, s, 2:3],
                     scalar2=GS[:, kc, s, 3:4], op0=ALU.mult, op1=ALU.add, reads=[("tmp", kc % 2), "GS"],
                     writes=[("xu", kc, o)])
        for c2 in range(22):
            s1 = w1.rearrange("(kc p) (j c) -> p kc j c", p=128, c=128)
            s3 = w3.rearrange("(kc p) (j c) -> p kc j c", p=128, c=128)
            wt, wtok = load_w([((lambda t: t[:, 0:1024].rearrange("p (kc c) -> p kc c", kc=8)), s1[:, :, c2, :]),
                               ((lambda t: t[:, 1024:2048].rearrange("p (kc c) -> p kc c", kc=8)), s3[:, :, c2, :])], 2048)
            wv = wt[:, 0:2048].rearrange("p (m kc c) -> p m kc c", m=2, kc=8)
            for bi, (o, n, s) in enumerate(blocks):
                pa = psA[bi % 2]
                pb = psB[bi % 2]
                for kc in range(8):
                    P.op("pe", "matmul", out=pa[:, 0:n], lhsT=wv[:, 0, kc, :], rhs=xu[:, kc, o:o + n], start=(kc == 0),
                         stop=(kc == 7), reads=[wtok, ("xu", kc, o)], writes=[("psA", bi % 2)])
                for kc in range(8):
                    P.op("pe", "matmul", out=pb[:, 0:n], lhsT=wv[:, 1, kc, :], rhs=xu[:, kc, o:o + n], start=(kc == 0),
                         stop=(kc == 7), reads=[wtok, ("xu", kc, o)], writes=[("psB", bi % 2)])
                t = tmp[bi % 2]
                P.op("act", "activation", out=t[:, 0:n], in_=pa[:, 0:n], func=AF.Silu, reads=[("psA", bi % 2)],
                     writes=[("tmp", bi % 2)])
                P.op("dve", "tensor_tensor", out=A[:, c2, o:o + n], in0=t[:, 0:n], in1=pb[:, 0:n], op=ALU.mult,
                     reads=[("tmp", bi % 2), ("psB", bi % 2)], writes=[("A", c2, o)])
        for j in range(8):
            s2 = w2.rearrange("(kc p) (j c) -> p kc j c", p=128, j=8)
            wt, wtok = load_w([((lambda t: t[:, 0:2816].rearrange("p (kc c) -> p kc c", kc=22)), s2[:, :, j, :])], 2816)
            wv = wt[:, 0:2816].rearrange("p (kc c) -> p kc c", kc=22)
            for bi, (o, n, s) in enumerate(blocks):
                pa = psA[bi % 2]
                for kc in range(22):
                    P.op("pe", "matmul", out=pa[:, 0:n], lhsT=wv[:, kc, :], rhs=A[:, kc, o:o + n], start=(kc == 0),
                         stop=(kc == 21), reads=[wtok, ("A", kc, o)], writes=[("psA", bi % 2)])
                P.op("dve", "scalar_tensor_tensor", out=hT[:, j, o:o + n], in0=pa[:, 0:n], scalar=modsb[:, 40 + j, s:s + 1],
                     in1=hT[:, j, o:o + n], op0=ALU.mult, op1=ALU.add, reads=[("psA", bi % 2), "modsb", ("h", j, o)],
                     writes=[("h", j, o)])
        if not last:
            for kc in range(8):
                for (o, n, s) in blocks:
                    P.dma(dq.next(), out[kc * 128:(kc + 1) * 128, gcol(o):gcol(o) + n], hT[:, kc, o:o + n],
                          reads=[("h", kc, o)], writes=[("out", kc, gcol(o))])
        else:
            ti = 0
            for (o, n, s) in blocks:
                _rms_stats(P, cst, hT, 8, o, n, lambda kc: ("h", kc, o), sq, psS, rstd, "")
                for kc in range(8):
                    P.op("dve", "scalar_tensor_tensor", out=hT[:, kc, o:o + n], in0=hT[:, kc, o:o + n], scalar=gvec[:, kc, 2:3],
                         in1=rstd[:, 0:n], op0=ALU.mult, op1=ALU.mult, reads=[("h", kc, o), "rstd", "gvec"], writes=[("h", kc, o)])
                for tt in range(n // 128):
                    for q4 in range(2):
                        pt = psT[ti % 2]
                        for kk in range(4):
                            kc = q4 * 4 + kk
                            P.op("pe", "transpose", out=pt[:, kk * 128:(kk + 1) * 128], in_=hT[:, kc, o + tt * 128:o + (tt + 1) * 128],
                                 identity=cst["ident"][:], reads=[("h", kc, o), "ident"], writes=[("psT", ti % 2)])
                        ot = tmp[ti % 2]
                        P.op("act" if ti % 2 else "dve", "activation" if ti % 2 else "tensor_copy", out=ot[:], in_=pt[:],
                             reads=[("psT", ti % 2)], writes=[("tmp", ti % 2)], **({"func": AF.Copy} if ti % 2 else {}))
                        r0 = l0 + o + tt * 128
                        P.dma(dq.next(), out[r0:r0 + 128, q4 * 512:(q4 + 1) * 512], ot[:], reads=[("tmp", ti % 2)],
                              writes=[("out", r0, q4)])
                        ti += 1
    outs = [k for k in P.lastw if isinstance(k, tuple) and k[0] == "out"]
    P.finish_wait("sp", outs)
    P.emit()
    return nc, P


def _c(a):
    return np.ascontiguousarray(a, dtype=np.float32)


def B_inmaps(l, last, h_lat, h_ctx, ysl, ysc, inp):
    maps = []
    shared = {
        "mod_w": _c(inp["mod_w"][l]), "mod_b": _c(inp["mod_b"][l][None]), "n1g": _c(inp["norm1_g"][l][None]),
        "n2g": _c(inp["norm2_g"][l][None]), "fng": _c(inp["final_norm_g"][None]),
        "wg": _c(inp["in_w"][l][:, GATE_OFF:]), "gate_b": _c(inp["gate_b"][l][None]), "wbr": _c(inp["branch_w"][l]),
        "wo": _c(inp["out_w"][l]), "w1": _c(inp["ffn_w1"][l]), "w3": _c(inp["ffn_w3"][l]), "w2": _c(inp["ffn_w2"][l]),
    }
    for core in range(8):
        b, j = core // 4, core % 4
        hl = h_lat[b, j * 2048:(j + 1) * 2048]
        yl = ysl[b, j * 2048:(j + 1) * 2048]
        if not last:
            hl = np.concatenate([hl, h_ctx[b, j * 64:(j + 1) * 64]], 0)
            yl = np.concatenate([yl, ysc[b, j * 64:(j + 1) * 64]], 0)
        m = dict(shared)
        m["hT"] = _c(hl.T)
        m["ysT"] = _c(yl.T)
        m["cvec"] = _c(np.stack([inp["c"][b], inp["c_ctx"]], 0))
        maps.append(m)
    return maps


def B_gather(last, outs):
    if last:
        return np.stack([np.concatenate(outs[0:4], 0), np.concatenate(outs[4:8], 0)], 0), None
    hl = np.stack([np.concatenate([o[:, :2048].T for o in outs[b * 4:(b + 1) * 4]], 0) for b in range(2)], 0)
    hc = np.stack([np.concatenate([o[:, 2048:].T for o in outs[b * 4:(b + 1) * 4]], 0) for b in range(2)], 0)
    return hl, hc


A_SLOTS = ["ret_q", "ret_qs", "ret_k", "ret_ks", "ret_v", "ret_g", "lru_x", "lru_y", "gdn_q", "gdn_k", "gdn_v", "gdn_z",
           "gdn_ab", "rw_r", "rw_k", "rw_v", "rw_gd", "rw_wd", "rw_ad"]
A_OUTS = [(n, n, 0, 128) for n in A_SLOTS if n not in ("gdn_ab", "rw_wd", "rw_ad")] + [
    ("gdn_af", "gdn_ab", 0, 1), ("gdn_abk", "gdn_ab", 1, 1), ("gdn_bf", "gdn_ab", 2, 1), ("gdn_bb", "gdn_ab", 3, 1),
    ("rw_wdf", "rw_wd", 0, 64), ("rw_wdb", "rw_wd", 64, 64), ("rw_adf", "rw_ad", 0, 64), ("rw_adb", "rw_ad", 64, 64)]
NSLOT = len(A_SLOTS)
A_BLOCKS = [(0, 256, 1)] + [(256 + i * 512, 512, 0) for i in range(16)]
GELU_C = 1.5957691216057308


def _a1_inproj(P, cst, PS, io, PT, outs_enabled):
    hT_in, wA, cvec, mod_w, mod_b, n1g = io["hT"], io["wA"], io["cvec"], io["mod_w"], io["mod_b"], io["n1g"]
    m0 = P.mark()
    wAb = P.carve([128, 8, NSLOT * 128], BF16)
    modsb = P.carve([128, 16, 2], F32)
    GS = P.carve([128, 8, 2, 2], F32)
    gv = P.carve([128, 8], F32)
    m1 = P.mark()
    wst = [P.carve([128, 4096], F32) for _ in range(2)]
    P.dma("sp", gv, n1g[0, :].rearrange("(k p) -> p k", p=128), writes=["gv"], allow_slow_non_contiguous=True)
    _mod_vectors(P, cst, cvec, mod_w, mod_b, 16, wst, PS[7][:, 0:96].rearrange("p (j s) -> p j s", s=2), modsb)
    for s in range(2):
        P.op("dve", "scalar_tensor_tensor", out=GS[:, :, s, 0], in0=modsb[:, 8:16, s], scalar=1.0, in1=gv, op0=ALU.add,
             op1=ALU.mult, reads=["modsb", "gv"], writes=["GS"])
        P.op("dve", "tensor_copy", out=GS[:, :, s, 1], in_=modsb[:, 0:8, s], reads=["modsb"], writes=["GS"])
    for kc in range(8):
        st = wst[kc % 2]
        P.dma("sp" if kc % 2 == 0 else "pool", st[:, 0:NSLOT * 128], wA[kc * 128:(kc + 1) * 128, :], writes=[("wst", kc % 2)])
        P.op("dve" if kc % 2 == 0 else "pool", "tensor_copy", out=wAb[:, kc, :], in_=st[:, 0:NSLOT * 128],
             reads=[("wst", kc % 2)], writes=[("wAb", kc)])
    P.release(m1)
    hblk = [P.carve([128, 8, 512], F32) for _ in range(2)]
    xu = [P.carve([128, 8, 512], BF16) for _ in range(2)]
    sq = P.carve([128, 8, 512], BF16)
    rstd = P.carve([128, 512], F32)
    tmp = [P.carve([128, 512], F32) for _ in range(2)]
    stg = [P.carve([128, 512], F32) for _ in range(4)]
    oi = 0
    for bi, (t0, n, s) in enumerate(A_BLOCKS):
        hb_, xb = hblk[bi % 2], xu[bi % 2]
        for kc in range(8):
            P.dma("sp" if kc % 2 == 0 else "pool", hb_[:, kc, 0:n], hT_in[kc * 128:(kc + 1) * 128, t0:t0 + n],
                  writes=[("hb", bi % 2, kc)])
        _rms_stats(P, cst, hb_, 8, 0, n, lambda kc: ("hb", bi % 2, kc), sq, PS[6], rstd, "")
        for kc in range(8):
            t = tmp[kc % 2]
            P.op("dve", "tensor_tensor", out=t[:, 0:n], in0=hb_[:, kc, 0:n], in1=rstd[:, 0:n], op=ALU.mult,
                 reads=[("hb", bi % 2, kc), "rstd"], writes=[("tmp", kc % 2)])
            P.op("pool", "tensor_scalar", out=xb[:, kc, 0:n], in0=t[:, 0:n], scalar1=GS[:, kc, s, 0:1], scalar2=GS[:, kc, s, 1:2],
                 op0=ALU.mult, op1=ALU.add, reads=[("tmp", kc % 2), "GS"], writes=[("xu", bi % 2, kc)])
        for (name, slot, c0, M) in A_OUTS:
            if name not in outs_enabled:
                continue
            col = A_SLOTS.index(slot) * 128 + c0
            ps = PS[oi % 4]
            for kc in range(8):
                P.op("pe", "matmul", out=ps[0:M, 0:n], lhsT=wAb[:, kc, col:col + M], rhs=xb[:, kc, 0:n], start=(kc == 0),
                     stop=(kc == 7), reads=[("wAb", kc), ("xu", bi % 2, kc)], writes=[("PS", oi % 4)])
            sg = stg[oi % 4]
            if oi % 2 == 0:
                P.op("act", "activation", out=sg[0:M, 0:n], in_=ps[0:M, 0:n], func=AF.Copy, reads=[("PS", oi % 4)],
                     writes=[("stg", oi % 4)])
            else:
                P.op("dve", "tensor_copy", out=sg[0:M, 0:n], in_=ps[0:M, 0:n], reads=[("PS", oi % 4)], writes=[("stg", oi % 4)])
            P.dma("sp" if oi % 2 == 0 else "pool", PT[name][:, t0:t0 + n], sg[0:M, 0:n], reads=[("stg", oi % 4)],
                  writes=[("PT", name, bi)])
            oi += 1
    ptw = {k: list(v) for k, v in P.lastw.items() if isinstance(k, tuple) and k[0] == "PT"}
    P.release(m0)


def _lru(P, cst, PS, io, PT, ysT):
    m0 = P.mark()
    cw = P.carve([128, 4], F32)
    cb = P.carve([128, 1], F32)
    gw = P.carve([128, 4, 128], F32)
    gb = P.carve([128, 4], F32)
    lam = P.carve([128, 2], F32)
    L8 = P.carve([128, 2], F32)
    L16 = P.carve([128, 2], F32)
    onec = P.carve([128, 1], F32)
    hbk = P.carve([128, TTOT], F32)
    P.op("pool", "memset", ap=onec, constant=1.0, writes=["onec"])
    P.dma("sp", cw, io["lru_cw"].rearrange("j c -> c j"), writes=["cw"], allow_slow_non_contiguous=True)
    P.dma("sp", cb, io["lru_cb"].rearrange("o c -> c o"), writes=["cb"], allow_slow_non_contiguous=True)
    P.dma("sp", gw, io["lru_gw"].rearrange("g c z -> c g z"), writes=["gw"])
    P.dma("sp", gb, io["lru_gb"].rearrange("g z -> z g"), writes=["gb"], allow_slow_non_contiguous=True)
    P.dma("sp", lam, io["lru_lam"].rearrange("d z -> z d"), writes=["lam"], allow_slow_non_contiguous=True)
    P.op("act", "activation", out=L8, in_=lam, func=AF.Exp, scale=-1.0, reads=["lam"], writes=["L8"])
    P.op("act", "activation", out=L8, in_=L8, func=AF.Ln, bias=onec, reads=["L8", "onec"], writes=["L8"])
    P.op("dve", "tensor_scalar", out=L16, in0=L8, scalar1=-16.0, scalar2=None, op0=ALU.mult, reads=["L8"], writes=["L16"])
    P.op("dve", "tensor_scalar", out=L8, in0=L8, scalar1=-8.0, scalar2=None, op0=ALU.mult, reads=["L8"], writes=["L8"])
    xh = [P.carve([128, 516], F32) for _ in range(2)]
    xc = [P.carve([128, 512], F32) for _ in range(2)]
    yb = [P.carve([128, 512], F32) for _ in range(2)]
    gr = P.carve([128, 512], F32)
    gi = P.carve([128, 512], F32)
    av = P.carve([128, 512], F32)
    bx = P.carve([128, 512], F32)
    hf = [P.carve([128, 512], F32) for _ in range(2)]
    ot = [P.carve([128, 512], F32) for _ in range(2)]
    it = 0
    for d in (1, 0):
        order = [A_BLOCKS[0]] + (A_BLOCKS[:0:-1] if d == 1 else A_BLOCKS[1:])
        prev = None
        for bi, (t0, n, s) in enumerate(order):
            k2 = it % 2
            it += 1
            x_, c_ = xh[k2], xc[k2]
            seg0, seg1 = (0, 256) if s else (256, TTOT)
            lo, hi = max(t0 - 2, seg0), min(t0 + n + 1, seg1)
            if lo > t0 - 2 or hi < t0 + n + 1:
                P.op("pool", "memset", ap=x_[:, 0:n + 3], constant=0.0, writes=[("xh", k2)])
            P.dma("sp", x_[:, lo - (t0 - 2):hi - (t0 - 2)], PT["lru_x"][:, lo:hi], writes=[("xh", k2)])
            P.op("dve", "tensor_scalar", out=c_[:, 0:n], in0=x_[:, 0:n], scalar1=cw[:, 0:1], scalar2=cb[:, 0:1], op0=ALU.mult,
                 op1=ALU.add, reads=[("xh", k2), "cw", "cb"], writes=[("xc", k2)])
            for j in range(1, 4):
                P.op("dve", "scalar_tensor_tensor", out=c_[:, 0:n], in0=x_[:, j:j + n], scalar=cw[:, j:j + 1], in1=c_[:, 0:n],
                     op0=ALU.mult, op1=ALU.add, reads=[("xh", k2), "cw", ("xc", k2)], writes=[("xc", k2)])
            for g, gt in ((0, gr), (1, gi)):
                ps = PS[g]
                P.op("pe", "matmul", out=ps[:, 0:n], lhsT=gw[:, d * 2 + g, :], rhs=c_[:, 0:n], start=True, stop=True,
                     reads=["gw", ("xc", k2)], writes=[("PS", g)])
                P.op("act", "activation", out=gt[:, 0:n], in_=ps[:, 0:n], func=AF.Sigmoid, bias=gb[:, d * 2 + g:d * 2 + g + 1],
                     reads=[("PS", g), "gb"], writes=[("g", g)])
            P.op("act", "activation", out=av[:, 0:n], in_=gr[:, 0:n], func=AF.Exp, scale=L8[:, d:d + 1], reads=[("g", 0), "L8"],
                 writes=["av"])
            P.op("act", "activation", out=gr[:, 0:n], in_=gr[:, 0:n], func=AF.Exp, scale=L16[:, d:d + 1], reads=[("g", 0), "L16"],
                 writes=[("g", 0)])
            P.op("dve", "tensor_scalar", out=gr[:, 0:n], in0=gr[:, 0:n], scalar1=-1.0, scalar2=1.0, op0=ALU.mult, op1=ALU.add,
                 reads=[("g", 0)], writes=[("g", 0)])
            P.op("act", "activation", out=gr[:, 0:n], in_=gr[:, 0:n], func=AF.Sqrt, reads=[("g", 0)], writes=[("g", 0)])
            P.op("pool", "tensor_tensor", out=gi[:, 0:n], in0=gi[:, 0:n], in1=c_[:, 0:n], op=ALU.mult, reads=[("g", 1), ("xc", k2)],
                 writes=[("g", 1)])
            P.op("dve", "tensor_tensor", out=bx[:, 0:n], in0=gi[:, 0:n], in1=gr[:, 0:n], op=ALU.mult, reads=[("g", 1), ("g", 0)],
                 writes=["bx"])
            if d == 1:
                init = 0.0 if prev is None else hbk[:, prev:prev + 1]
                P.op("dve", "tensor_tensor_scan", out=hbk[:, t0:t0 + n][:, ::-1], data0=av[:, 0:n][:, ::-1], data1=bx[:, 0:n][:, ::-1],
                     initial=init, op0=ALU.mult, op1=ALU.add, reads=["av", "bx", "hbk_c"], writes=[("hbk", t0), "hbk_c"])
                prev = t0
            else:
                h_ = hf[k2]
                init = 0.0 if prev is None else hf[1 - k2][:, prev - 1:prev]
                P.op("dve", "tensor_tensor_scan", out=h_[:, 0:n], data0=av[:, 0:n], data1=bx[:, 0:n], initial=init, op0=ALU.mult,
                     op1=ALU.add, reads=["av", "bx", ("hf", 1 - k2)], writes=[("hf", k2)])
                prev = n
                y_ = yb[k2]
                P.dma("pool", y_[:, 0:n], PT["lru_y"][:, t0:t0 + n], writes=[("yb", k2)])
                o_ = ot[k2]
                P.op("pool", "tensor_tensor", out=o_[:, 0:n], in0=y_[:, 0:n], in1=y_[:, 0:n], op=ALU.mult, reads=[("yb", k2)],
                     writes=[("ot", k2)])
                P.op("pool", "tensor_scalar", out=o_[:, 0:n], in0=o_[:, 0:n], scalar1=0.044715, scalar2=1.0, op0=ALU.mult,
                     op1=ALU.add, reads=[("ot", k2)], writes=[("ot", k2)])
                P.op("pool", "tensor_tensor", out=o_[:, 0:n], in0=o_[:, 0:n], in1=y_[:, 0:n], op=ALU.mult, reads=[("ot", k2), ("yb", k2)],
                     writes=[("ot", k2)])
                P.op("act", "activation", out=o_[:, 0:n], in_=o_[:, 0:n], func=AF.Sigmoid, scale=GELU_C, reads=[("ot", k2)],
                     writes=[("ot", k2)])
                P.op("pool", "tensor_tensor", out=o_[:, 0:n], in0=o_[:, 0:n], in1=y_[:, 0:n], op=ALU.mult, reads=[("ot", k2), ("yb", k2)],
                     writes=[("ot", k2)])
                P.op("dve", "tensor_tensor", out=y_[:, 0:n], in0=h_[:, 0:n], in1=hbk[:, t0:t0 + n], op=ALU.add,
                     reads=[("hf", k2), ("hbk", t0)], writes=[("yb", k2)])
                P.op("dve", "tensor_tensor", out=o_[:, 0:n], in0=o_[:, 0:n], in1=y_[:, 0:n], op=ALU.mult, reads=[("ot", k2), ("yb", k2)],
                     writes=[("ot", k2)])
                P.dma("sp", ysT[128:256, t0:t0 + n], o_[:, 0:n], reads=[("ot", k2)], writes=[("ysT", 1, t0)])
    P.release(m0)


def build_A(enabled=("inproj", "lru", "ret", "gdn", "rwkv")):
    nc = bass.Bass("TRN2", target_bir_lowering=False)
    io = {}

    def inp(name, shape):
        io[name] = nc.dram_tensor(name, list(shape), F32, kind="ExternalInput").ap()
    inp("hT", [D, TTOT]); inp("wA", [D, NSLOT * 128]); inp("cvec", [2, D]); inp("mod_w", [D, 6 * D]); inp("mod_b", [1, 6 * D])
    inp("n1g", [1, D])
    inp("lru_cw", [4, 128]); inp("lru_cb", [1, 128]); inp("lru_gw", [4, 128, 128]); inp("lru_gb", [4, 128]); inp("lru_lam", [2, 128])
    inp("cA", [128, CA_COLS]); inp("ropeC", [128, TTOT]); inp("ropeS", [128, TTOT]); inp("ret_de", [1, 2])
    inp("gdn_cw", [3, 4, 128]); inp("gdn_sc", [1, 4]); inp("gdn_ng", [1, 128])
    inp("rw_pc", [13, 128]); inp("rw_pc64", [4, 64]); inp("rw_w2", [2, 64, 128]); inp("rw_a2", [2, 64, 128]); inp("rw_g2", [128, 128])
    OB = {m: nc.dram_tensor("ob_" + m, [TTOT, 128], F32, kind="Internal").ap() for m in ("ret", "gdn", "rwkv")}
    ysT = nc.dram_tensor("ysT", [512, TTOT], F32, kind="ExternalOutput").ap()
    ptkind = "ExternalOutput" if "debug_pt" in enabled else "Internal"
    PT = {n: nc.dram_tensor("pt_" + n, [M, TTOT], F32, kind=ptkind).ap() for (n, _, _, M) in A_OUTS}
    P = Prog(nc)
    cst = _common_consts(P)
    PS = [P.ps(f"bank{i}", [128, 512], F32) for i in range(8)]
    P.arena_init(44000)
    need = set()
    if "lru" in enabled:
        need |= {"lru_x", "lru_y"}
    if "ret" in enabled:
        need |= {"ret_q", "ret_qs", "ret_k", "ret_ks", "ret_v", "ret_g"}
    if "gdn" in enabled:
        need |= {"gdn_q", "gdn_k", "gdn_v", "gdn_z", "gdn_af", "gdn_abk", "gdn_bf", "gdn_bb"}
    if "rwkv" in enabled:
        need |= {"rw_r", "rw_k", "rw_v", "rw_gd", "rw_wdf", "rw_wdb", "rw_adf", "rw_adb"}
    if "debug_pt" in enabled:
        need = set(n for (n, _, _, _) in A_OUTS)
    _a1_inproj(P, cst, PS, io, PT, need)
    if "lru" in enabled:
        _lru(P, cst, PS, io, PT, ysT)
    if "ret" in enabled:
        _ret(P, cst, PS, io, PT, ysT, OB["ret"])
    if "gdn" in enabled:
        _gdn(P, cst, PS, io, PT, ysT, OB["gdn"])
    if "rwkv" in enabled:
        _rwkv(P, cst, PS, io, PT, ysT, OB["rwkv"])
    P.barrier()
    P.emit()
    return nc, P


def A_weight_cols(h):
    def rng(a, n=128):
        return list(range(a, a + n))
    cols = {}
    cols["ret_q"] = rng(0 + h * 128)
    cols["ret_qs"] = rng(h * 128 + 64, 64) + rng(h * 128, 64)
    cols["ret_k"] = rng(512 + h * 128)
    cols["ret_ks"] = rng(512 + h * 128 + 64, 64) + rng(512 + h * 128, 64)
    cols["ret_v"] = rng(1024 + h * 128)
    cols["ret_g"] = rng(1536 + h * 128)
    cols["lru_x"] = rng(2048 + h * 128)
    cols["lru_y"] = rng(2560 + h * 128)
    cols["gdn_q"] = rng(3072 + h * 128)
    cols["gdn_k"] = rng(3584 + h * 128)
    cols["gdn_v"] = rng(4096 + h * 128)
    cols["gdn_z"] = rng(4608 + h * 128)
    cols["gdn_ab"] = [5120 + h, 5120 + 4 + h, 5128 + h, 5128 + 4 + h] + [-1] * 124
    R0 = 5136
    cols["rw_r"] = rng(R0 + h * 128)
    cols["rw_k"] = rng(R0 + 512 + h * 128)
    cols["rw_v"] = rng(R0 + 1024 + h * 128)
    cols["rw_gd"] = rng(R0 + 1536)
    cols["rw_wd"] = rng(R0 + 1664)
    cols["rw_ad"] = rng(R0 + 1792)
    idx = []
    for s in A_SLOTS:
        idx += cols[s]
    return np.array(idx)


def A_inmaps(l, h_lat, h_ctx, inp):
    maps = []
    in_w = np.concatenate([inp["in_w"][l], np.zeros((D, 1), np.float32)], 1)
    for core in range(8):
        b, h = core // 4, core % 4
        m = {}
        m["hT"] = _c(np.concatenate([h_ctx[b], h_lat[b]], 0).T)
        m["wA"] = _c(in_w[:, A_weight_cols(h)])
        m["cvec"] = _c(np.stack([inp["c"][b], inp["c_ctx"]], 0))
        m["mod_w"] = _c(inp["mod_w"][l]); m["mod_b"] = _c(inp["mod_b"][l][None]); m["n1g"] = _c(inp["norm1_g"][l][None])
        sl = slice(h * 128, (h + 1) * 128)
        m["lru_cw"] = _c(inp["lru_conv_w"][l][:, sl]); m["lru_cb"] = _c(inp["lru_conv_b"][l][None, sl])
        m["lru_gw"] = _c(inp["lru_gate_w"][l][:, :, h].reshape(4, 128, 128))
        m["lru_gb"] = _c(inp["lru_gate_b"][l][:, :, sl].reshape(4, 128)); m["lru_lam"] = _c(inp["lru_lambda"][l][:, sl])
        m["cA"] = make_cA(); m["ropeC"], m["ropeS"] = make_rope()
        m["ret_de"] = _c(inp["ret_decay_exp"][l][:, h][None])
        gcw = inp["gdn_conv_w"][l]
        m["gdn_cw"] = _c(np.stack([gcw[:, i * 512 + h * 128:i * 512 + (h + 1) * 128] for i in range(3)], 0))
        m["gdn_sc"] = _c(np.concatenate([inp["gdn_a_log"][l][:, h], inp["gdn_dt_bias"][l][:, h]])[None])
        m["gdn_ng"] = _c(inp["gdn_norm_g"][l][None])
        mu = inp["rwkv_mu"][l]
        m["rw_pc"] = _c(np.stack([mu[0:512][sl], mu[512:1024][sl], mu[1024:1536][sl], mu[1536:1664], inp["rwkv_w0"][l][0][sl],
                                  inp["rwkv_w0"][l][1][sl], inp["rwkv_a0"][l][0][sl], inp["rwkv_a0"][l][1][sl], inp["rwkv_k_k"][l][sl],
                                  inp["rwkv_k_a"][l][sl], inp["rwkv_r_k"][l].reshape(512)[sl], inp["rwkv_ln_g"][l][sl],
                                  inp["rwkv_ln_b"][l][sl]], 0))
        m["rw_pc64"] = _c(np.stack([mu[1664:1728], mu[1728:1792], mu[1792:1856], mu[1856:1920]], 0))
        m["rw_w2"] = _c(inp["rwkv_w2"][l][:, :, sl]); m["rw_a2"] = _c(inp["rwkv_a2"][l][:, :, sl]); m["rw_g2"] = _c(inp["rwkv_g2"][l][:, sl])
        maps.append(m)
    return maps


CH = 64


class Core:
    def __init__(self, P, cst, PS, NH, has_delta, identC):
        self.P, self.cst, self.PS, self.NH, self.hd, self.delta = P, cst, PS, NH, 128 // NH, has_delta
        self.identC = identC
        W = NH * CH
        self.sets = []
        for i in range(2):
            d = {}
            for nm in ("Pm", "Nm", "Pm2", "Nm2", "Xa", "Aak", "Aqb", "Aqk", "X", "U"):
                d[nm] = P.carve([CH, W], F32) if nm not in ("X", "U") else P.carve([CH, 128], F32)
            self.sets.append(d)
        self.k = 0

    def run(self, w, S, Stok, O, Otok):
        P, PS, NH, hd = self.P, self.PS, self.NH, self.hd
        k = self.k
        self.k += 1
        T = self.sets[k % 2]
        tk = lambda nm: ("core", nm, k % 2)
        W = NH * CH
        bA, bB, bC, bD, bE = PS[2], PS[3], PS[4], PS[5], PS[6]
        tA, tB, tC, tD, tE = (("bank", i) for i in (2, 3, 4, 5, 6))
        hsl = [slice(h * hd, (h + 1) * hd) for h in range(NH)]
        csl = [slice(h * CH, (h + 1) * CH) for h in range(NH)]

        def mm(out, otok, lhsT, ltok, rhs, rtok, start=True, stop=True):
            P.op("pe", "matmul", out=out, lhsT=lhsT, rhs=rhs, start=start, stop=stop, reads=[ltok, rtok], writes=[otok])

        def A(nm):
            return w[nm][0]

        def Tk(nm):
            return w[nm][1]

        def L(nm, h):
            return w[nm][0] if NH == 1 else w[nm + "m"][h][0]

        def Lt(nm, h):
            return w[nm][1] if NH == 1 else w[nm + "m"][h][1]

        def Rr(nm):
            return w[nm][0]
        r0, r2 = bA[0:CH, 0:W], bA[0:CH, 128:128 + W]
        r1, r3 = bB[0:CH, 0:W], bB[0:CH, 128:128 + W]
        xr, ur = bC[0:CH, 0:128], bC[0:CH, 128:256]
        orr = bD[0:CH, 0:128]
        sr = bE[:, 0:hd]
        if self.delta:
            for h in range(NH):
                mm(r0[:, csl[h]], tA, L("bT", h), Lt("bT", h), A("aT"), Tk("aT"))
            P.op("dve", "tensor_tensor", out=T["Pm"], in0=r0, in1=A("MexT"), op=ALU.mult, reads=[Tk("MexT")],
                 writes=[tk("Pm"), tA])
            for h in range(NH):
                P.op("pe", "transpose", out=r1[:, csl[h]], in_=T["Pm"][:, csl[h]], identity=self.cst["ident"][0:CH, 0:CH],
                     reads=[tk("Pm"), "ident"], writes=[tB])
            P.op("act", "activation", out=T["Nm"], in_=r1, func=AF.Copy, reads=[], writes=[tk("Nm"), tB])
            P.op("pool", "tensor_tensor", out=T["Xa"], in0=T["Pm"], in1=self.identC, op=ALU.add, reads=[tk("Pm"), "cAs"],
                 writes=[tk("Xa")])
            Pm, Nm, Pm2, Nm2 = "Pm", "Nm", "Pm2", "Nm2"
            for lvl in range(1, 6):
                if lvl < 5:
                    for h in range(NH):
                        mm(r0[:, csl[h]], tA, T[Nm][:, csl[h]], tk(Nm), T[Pm][:, csl[h]], tk(Pm))
                    P.op("act", "activation", out=T[Pm2], in_=r0, func=AF.Copy, reads=[], writes=[tk(Pm2), tA])
                for h in range(NH):
                    mm(r1[:, csl[h]], tB, T[Pm][:, csl[h]], tk(Pm), T[Nm][:, csl[h]], tk(Nm))
                P.op("dve", "tensor_copy", out=T[Nm2], in_=r1, reads=[], writes=[tk(Nm2), tB])
                for h in range(NH):
                    mm(r2[:, csl[h]], tA, T[Nm2][:, csl[h]], tk(Nm2), T["Xa"][:, csl[h]], tk("Xa"))
                P.op("dve", "tensor_tensor", out=T["Xa"], in0=r2, in1=T["Xa"], op=ALU.add, reads=[], writes=[tk("Xa"), tA])
                Pm, Pm2 = Pm2, Pm
                Nm, Nm2 = Nm2, Nm
            for h in range(NH):
                mm(r3[:, csl[h]], tB, L("kT", h), Lt("kT", h), A("aT"), Tk("aT"))
            P.op("dve", "tensor_tensor", out=T["Aak"], in0=r3, in1=A("MexT"), op=ALU.mult, reads=[Tk("MexT")],
                 writes=[tk("Aak"), tB])
            for h in range(NH):
                mm(r0[:, csl[h]], tA, L("bT", h), Lt("bT", h), A("qT"), Tk("qT"))
            P.op("dve", "tensor_tensor", out=T["Aqb"], in0=r0, in1=A("MinT"), op=ALU.mult, reads=[Tk("MinT")],
                 writes=[tk("Aqb"), tA])
        for h in range(NH):
            mm(r1[:, csl[h]], tB, L("kT", h), Lt("kT", h), A("qT"), Tk("qT"))
        P.op("dve", "tensor_tensor", out=T["Aqk"], in0=r1, in1=A("MinT"), op=ALU.mult, reads=[Tk("MinT")],
             writes=[tk("Aqk"), tB])
        if self.delta:
            for h in range(NH):
                mm(xr[:, hsl[h]], tC, L("aTs", h), Lt("aTs", h), S, Stok, True, False)
                mm(xr[:, hsl[h]], tC, T["Aak"][:, csl[h]], tk("Aak"), A("V")[:, hsl[h]], Tk("V"), False, True)
            P.op("act", "activation", out=T["X"], in_=xr, func=AF.Copy, reads=[], writes=[tk("X"), tC])
            for h in range(NH):
                mm(ur[:, hsl[h]], tC, T["Xa"][:, csl[h]], tk("Xa"), T["X"][:, hsl[h]], tk("X"))
            P.op("act", "activation", out=T["U"], in_=ur, func=AF.Copy, reads=[], writes=[tk("U"), tC])
        for h in range(NH):
            mm(orr[:, hsl[h]], tD, L("qTs", h), Lt("qTs", h), S, Stok, True, False)
            if self.delta:
                mm(orr[:, hsl[h]], tD, T["Aqb"][:, csl[h]], tk("Aqb"), T["U"][:, hsl[h]], tk("U"), False, False)
            mm(orr[:, hsl[h]], tD, T["Aqk"][:, csl[h]], tk("Aqk"), A("V")[:, hsl[h]], Tk("V"), False, True)
        P.op("act", "activation", out=O, in_=orr, func=AF.Copy, reads=[], writes=[Otok, tD])
        for h in range(NH):
            if self.delta:
                mm(sr[hsl[h], :], tE, A("Bst")[:, hsl[h]], Tk("Bst"), T["U"][:, hsl[h]], tk("U"), True, False)
            mm(sr[hsl[h], :], tE, A("Kst")[:, hsl[h]], Tk("Kst"), A("V")[:, hsl[h]], Tk("V"), not self.delta, True)
        P.op("dve", "scalar_tensor_tensor", out=S, in0=S, scalar=A("cs"), in1=sr, op0=ALU.mult, op1=ALU.add,
             reads=[Tk("cs")], writes=[Stok, tE])


def _dir_chunks(d):
    order = [A_BLOCKS[0]] + (A_BLOCKS[:0:-1] if d == 1 else A_BLOCKS[1:])
    res = []
    for (t0, n, s) in order:
        offs = list(range(0, n, CH))
        if d == 1:
            offs = offs[::-1]
        res.append((t0, n, s, offs))
    return res


CA = {}
_o = 0
for _nm, _n in (("RELF", 64), ("RELB", 64), ("MASKF", 64), ("MASKB", 64), ("MSTRF", 64), ("MSTRB", 64), ("POS1F", 512), ("POS1B", 512),
                ("KDF", 512), ("KDB", 512), ("RSTF", 512), ("RSTB", 512), ("NEGINF", 64), ("NEGINB", 64), ("NEGEXF", 64),
                ("NEGEXB", 64), ("IDC", 128), ("BLK", 128)):
    CA[_nm] = (_o, _n)
    _o += _n
CA_COLS = _o


def make_cA():
    c = np.zeros((128, CA_COLS), np.float32)
    s = np.arange(64)[:, None]
    t = np.arange(64)[None, :]
    def put(nm, a):
        o, n = CA[nm]
        c[:a.shape[0], o:o + n] = a
    put("RELF", np.where(s <= t, t - s, 0)); put("RELB", np.where(s >= t, s - t, 0))
    put("MASKF", (s <= t) * 1.0); put("MASKB", (s >= t) * 1.0)
    put("MSTRF", (s < t) * 1.0); put("MSTRB", (s > t) * 1.0)
    tt = np.arange(512)[None, :] % 64
    put("POS1F", np.broadcast_to(tt + 1.0, (128, 512))); put("POS1B", np.broadcast_to(64.0 - tt, (128, 512)))
    put("KDF", np.broadcast_to(63.0 - tt, (128, 512))); put("KDB", np.broadcast_to(tt * 1.0, (128, 512)))
    put("RSTF", np.broadcast_to((tt != 0) * 1.0, (128, 512))); put("RSTB", np.broadcast_to((tt != 63) * 1.0, (128, 512)))
    NEG = -30000.0
    put("NEGINF", np.where(s <= t, 0.0, NEG)); put("NEGINB", np.where(s >= t, 0.0, NEG))
    put("NEGEXF", np.where(s < t, 0.0, NEG)); put("NEGEXB", np.where(s > t, 0.0, NEG))
    put("IDC", np.concatenate([np.eye(64), np.eye(64)], 1))
    blk = np.zeros((128, 128)); blk[:64, :64] = 1; blk[64:, 64:] = 1
    put("BLK", blk)
    return c


def make_rope():
    tpos = np.arange(SEQ)
    row = (tpos // 64).astype(np.float32)
    col = (tpos % 64).astype(np.float32)
    inv = (10000.0 ** (-np.arange(32, dtype=np.float32) / 32)).astype(np.float32)
    ang = np.concatenate([row[:, None] * inv, col[:, None] * inv], -1)
    cos, sin = np.cos(ang).astype(np.float32), np.sin(ang).astype(np.float32)
    CC = np.ones((128, TTOT), np.float32)
    SS = np.zeros((128, TTOT), np.float32)
    CC[:64, CTX:] = cos.T; CC[64:, CTX:] = cos.T
    SS[:64, CTX:] = -sin.T; SS[64:, CTX:] = sin.T
    return CC, SS


def _ld_const(P, io, cAs, nm, rows=128):
    o, n = CA[nm]
    return cAs[0:rows, o:o + n]


def _ret(P, cst, PS, io, PT, ysT, OB):
    m0 = P.mark()
    cAs = P.carve([128, CA_COLS], F32)
    P.dma("sp", cAs, io["cA"], writes=["cAs"])
    lg = P.carve([128, 2], F32)
    onec = P.carve([128, 1], F32)
    P.op("pool", "memset", ap=onec, constant=1.0, writes=["onec"])
    P.dma("sp", lg, io["ret_de"].partition_broadcast(128), writes=["lg"])
    P.op("act", "activation", out=lg, in_=lg, func=AF.Exp, scale=-float(np.log(2.0)), reads=["lg"], writes=["lg"])
    P.op("act", "activation", out=lg, in_=lg, func=AF.Ln, scale=-1.0, bias=onec, reads=["lg", "onec"], writes=["lg"])
    MinT = [P.carve([CH, CH], F32) for _ in range(2)]
    POSQ = [P.carve([128, 512], F32) for _ in range(2)]
    KDEC = [P.carve([128, 512], F32) for _ in range(2)]
    csc = P.carve([128, 2], F32)
    P.op("act", "activation", out=csc, in_=lg, func=AF.Exp, scale=float(CH), reads=["lg"], writes=["csc"])
    for d in range(2):
        sfx = "FB"[d]
        P.op("act", "activation", out=MinT[d], in_=_ld_const(P, io, cAs, "REL" + sfx, 64), func=AF.Exp, scale=lg[0:CH, d:d + 1],
             reads=["cAs", "lg"], writes=[("MinT", d)])
        P.op("dve", "scalar_tensor_tensor", out=MinT[d], in0=MinT[d], scalar=128.0 ** -0.5, in1=_ld_const(P, io, cAs, "MASK" + sfx, 64),
             op0=ALU.mult, op1=ALU.mult, reads=[("MinT", d), "cAs"], writes=[("MinT", d)])
        P.op("act", "activation", out=POSQ[d], in_=_ld_const(P, io, cAs, "POS1" + sfx), func=AF.Exp, scale=lg[:, d:d + 1],
             reads=["cAs", "lg"], writes=[("POSQ", d)])
        P.op("act", "activation", out=KDEC[d], in_=_ld_const(P, io, cAs, "KD" + sfx), func=AF.Exp, scale=lg[:, d:d + 1],
             reads=["cAs", "lg"], writes=[("KDEC", d)])
        P.op("dve", "tensor_scalar", out=KDEC[d], in0=KDEC[d], scalar1=128.0 ** -0.5, scalar2=None, op0=ALU.mult,
             reads=[("KDEC", d)], writes=[("KDEC", d)])
    core = Core(P, cst, PS, 1, False, None)
    S = P.carve([128, 128], F32)
    ld = {nm: [P.carve([128, 512], F32) for _ in range(2)] for nm in ("q", "qs", "k", "ks", "v", "cc", "ss", "g")}
    qr = [P.carve([128, 512], F32) for _ in range(2)]
    kr = [P.carve([128, 512], F32) for _ in range(2)]
    qsd = [P.carve([128, 512], F32) for _ in range(2)]
    kdc = [P.carve([128, 512], F32) for _ in range(2)]
    outb = [P.carve([128, 512], F32) for _ in range(2)]
    Vt = [P.carve([CH, 128], F32) for _ in range(2)]
    Kst = [P.carve([CH, 128], F32) for _ in range(2)]
    Ot = [P.carve([CH, 128], F32) for _ in range(2)]
    Ob = [P.carve([CH, 128], F32) for _ in range(2)]
    ssq = [P.carve([CH, 2], F32) for _ in range(2)]
    junk = P.carve([CH, 128], F32)
    epsC = cst["epsb"]
    bi_ = 0
    ci = 0
    for d in (1, 0):
        P.op("pool", "memset", ap=S, constant=0.0, reads=[], writes=["S"])
        for (t0, n, s, offs) in _dir_chunks(d):
            b2 = bi_ % 2
            bi_ += 1
            names = ["q", "qs", "k", "ks", "v", "cc", "ss"] + (["g"] if d == 0 else [])
            for i, nm in enumerate(names):
                src = {"q": PT["ret_q"], "qs": PT["ret_qs"], "k": PT["ret_k"], "ks": PT["ret_ks"], "v": PT["ret_v"],
                       "g": PT["ret_g"], "cc": io["ropeC"], "ss": io["ropeS"]}[nm]
                P.dma("sp" if i % 2 == 0 else "pool", ld[nm][b2][:, 0:n], src[:, t0:t0 + n], writes=[("ld", nm, b2)])
            for (dst, a, b_, nm) in ((qr[b2], "q", "qs", "qr"), (kr[b2], "k", "ks", "kr")):
                P.op("dve", "tensor_tensor", out=dst[:, 0:n], in0=ld[a][b2][:, 0:n], in1=ld["cc"][b2][:, 0:n], op=ALU.mult,
                     reads=[("ld", a, b2), ("ld", "cc", b2)], writes=[(nm, b2)])
                P.op("pool", "tensor_tensor", out=ld[b_][b2][:, 0:n], in0=ld[b_][b2][:, 0:n], in1=ld["ss"][b2][:, 0:n], op=ALU.mult,
                     reads=[("ld", b_, b2), ("ld", "ss", b2)], writes=[("ld", b_, b2)])
                P.op("dve", "tensor_tensor", out=dst[:, 0:n], in0=dst[:, 0:n], in1=ld[b_][b2][:, 0:n], op=ALU.add,
                     reads=[(nm, b2), ("ld", b_, b2)], writes=[(nm, b2)])
            P.op("pool", "tensor_tensor", out=qsd[b2][:, 0:n], in0=qr[b2][:, 0:n], in1=POSQ[d][:, 0:n], op=ALU.mult,
                 reads=[("qr", b2), ("POSQ", d)], writes=[("qsd", b2)])
            P.op("pool", "tensor_tensor", out=kdc[b2][:, 0:n], in0=kr[b2][:, 0:n], in1=KDEC[d][:, 0:n], op=ALU.mult,
                 reads=[("kr", b2), ("KDEC", d)], writes=[("kdc", b2)])
            if d == 0:
                P.op("act", "activation", out=ld["g"][b2][:, 0:n], in_=ld["g"][b2][:, 0:n], func=AF.Silu, reads=[("ld", "g", b2)],
                     writes=[("ld", "g", b2)])
            for c0 in offs:
                c2 = ci % 2
                ci += 1
                cs_ = slice(c0, c0 + CH)
                tp = PS[7][0:CH, c2 * 256:c2 * 256 + 128]
                P.op("pe", "transpose", out=tp, in_=ld["v"][b2][:, cs_], identity=cst["ident"][:], reads=[("ld", "v", b2), "ident"],
                     writes=[("bank", 7)])
                P.op("act", "activation", out=Vt[c2], in_=tp, func=AF.Copy, reads=[], writes=[("Vt", c2), ("bank", 7)])
                tp2 = PS[7][0:CH, c2 * 256 + 128:c2 * 256 + 256]
                P.op("pe", "transpose", out=tp2, in_=kdc[b2][:, cs_], identity=cst["ident"][:], reads=[("kdc", b2), "ident"],
                     writes=[("bank", 7)])
                P.op("dve", "tensor_copy", out=Kst[c2], in_=tp2, reads=[], writes=[("Kst", c2), ("bank", 7)])
                w = {"kT": (kr[b2][:, cs_], ("kr", b2)), "qT": (qr[b2][:, cs_], ("qr", b2)), "qTs": (qsd[b2][:, cs_], ("qsd", b2)),
                     "MinT": (MinT[d], ("MinT", d)), "Kst": (Kst[c2], ("Kst", c2)), "V": (Vt[c2], ("Vt", c2)),
                     "cs": (csc[:, d:d + 1], "csc")}
                core.run(w, S, "S", Ot[c2], ("Ot", c2))
                tg = t0 + c0
                if d == 1:
                    P.dma("sp", OB[tg:tg + CH, :], Ot[c2], reads=[("Ot", c2)], writes=[("OB", tg)])
                else:
                    P.dma("sp", Ob[c2], OB[tg:tg + CH, :], reads=[("OB", tg)], writes=[("Ob", c2)])
                    P.op("dve", "tensor_tensor", out=Ot[c2], in0=Ot[c2], in1=Ob[c2], op=ALU.add, reads=[("Ot", c2), ("Ob", c2)],
                         writes=[("Ot", c2)])
                    P.op("act", "activation", out=junk, in_=Ot[c2], func=AF.Square, accum_out=ssq[c2][:, 0:1], reads=[("Ot", c2)],
                         writes=["junk", ("ssq", c2)])
                    P.op("act", "activation", out=ssq[c2][:, 1:2], in_=ssq[c2][:, 0:1], func=AF.Ln, scale=1.0 / 128, bias=epsC[0:CH, :],
                         reads=[("ssq", c2), "epsb"], writes=[("ssq", c2)])
                    P.op("act", "activation", out=ssq[c2][:, 1:2], in_=ssq[c2][:, 1:2], func=AF.Exp, scale=-0.5, reads=[("ssq", c2)],
                         writes=[("ssq", c2)])
                    P.op("dve", "tensor_scalar", out=Ot[c2], in0=Ot[c2], scalar1=ssq[c2][:, 1:2], scalar2=None, op0=ALU.mult,
                         reads=[("Ot", c2), ("ssq", c2)], writes=[("Ot", c2)])
                    tp3 = PS[1][:, c2 * 256:c2 * 256 + CH]
                    P.op("pe", "transpose", out=tp3, in_=Ot[c2], identity=cst["ident"][0:CH, 0:CH], reads=[("Ot", c2), "ident"],
                         writes=[("bank", 1)])
                    P.op("dve", "tensor_tensor", out=outb[b2][:, cs_], in0=tp3, in1=ld["g"][b2][:, cs_], op=ALU.mult,
                         reads=[("ld", "g", b2)], writes=[("outb", b2), ("bank", 1)])
            if d == 0:
                P.dma("pool", ysT[0:128, t0:t0 + n], outb[b2][:, 0:n], reads=[("outb", b2)], writes=[("ysT", 0, t0)])
    P.release(m0)


def _gdn(P, cst, PS, io, PT, ysT, OB):
    m0 = P.mark()
    cAs = P.carve([128, CA_COLS], F32)
    P.dma("sp", cAs, io["cA"], writes=["cAs"])
    cw = P.carve([128, 3, 4], F32)
    sc = P.carve([1, 4], F32)
    nga = P.carve([1, 2], F32)
    ng = P.carve([128, 1], F32)
    one1 = P.carve([1, 128], F32)
    P.op("pool", "memset", ap=one1, constant=1.0, writes=["one1"])
    P.dma("sp", cw, io["gdn_cw"].rearrange("m j c -> c m j"), writes=["cw"], allow_slow_non_contiguous=True)
    P.dma("sp", sc, io["gdn_sc"], writes=["sc"])
    P.dma("sp", ng, io["gdn_ng"].rearrange("o c -> c o"), writes=["ng"], allow_slow_non_contiguous=True)
    P.op("act", "activation", out=nga, in_=sc[:, 0:2], func=AF.Exp, reads=["sc"], writes=["nga"])
    P.op("dve", "tensor_scalar", out=nga, in0=nga, scalar1=-1.0, scalar2=None, op0=ALU.mult, reads=["nga"], writes=["nga"])
    core = Core(P, cst, PS, 1, True, _ld_const(P, io, cAs, "IDC", 64)[:, 0:CH])
    S = P.carve([128, 128], F32)
    xh = {nm: [P.carve([128, 516], F32) for _ in range(2)] for nm in "qkv"}
    cv = {nm: [P.carve([128, 512], F32) for _ in range(2)] for nm in "qkv"}
    zt = [P.carve([128, 512], F32) for _ in range(2)]
    sqt = P.carve([128, 512], F32)
    rs = P.carve([128, 512], F32)
    rows = {nm: [P.carve([1, 512], F32) for _ in range(2)] for nm in ("a", "b", "g", "gc", "gcx", "ngc", "r", "ein", "eex", "cb", "dend")}
    bt = {nm: [P.carve([128, 512], F32) for _ in range(2)] for nm in ("bT", "kT", "aTs", "qTs", "KsT", "BsT", "EinS")}
    outb = [P.carve([128, 512], F32) for _ in range(2)]
    Mx = [P.carve([CH, CH], F32) for _ in range(2)]
    Mi = [P.carve([CH, CH], F32) for _ in range(2)]
    Vt = [P.carve([CH, 128], F32) for _ in range(2)]
    Kst = [P.carve([CH, 128], F32) for _ in range(2)]
    Bst = [P.carve([CH, 128], F32) for _ in range(2)]
    Ot = [P.carve([CH, 128], F32) for _ in range(2)]
    Ob = [P.carve([CH, 128], F32) for _ in range(2)]
    ssq = [P.carve([CH, 2], F32) for _ in range(2)]
    junk = P.carve([CH, 128], F32)
    epsC = cst["epsb"]
    b0, b1, b7 = ("bank", 0), ("bank", 1), ("bank", 7)
    bi_ = 0
    ci = 0
    for d in (1, 0):
        sfx = "FB"[d]
        osfx = "BF"[d]
        P.op("pool", "memset", ap=S, constant=0.0, reads=[], writes=["S"])
        for (t0, n, s, offs) in _dir_chunks(d):
            b2 = bi_ % 2
            bi_ += 1
            seg0, seg1 = (0, 256) if s else (256, TTOT)
            lo, hi = max(t0 - 2, seg0), min(t0 + n + 1, seg1)
            for mi_, nm in enumerate("qkv"):
                x_ = xh[nm][b2]
                if lo > t0 - 2 or hi < t0 + n + 1:
                    P.op("pool", "memset", ap=x_[:, 0:n + 3], constant=0.0, writes=[("xh", nm, b2)])
                P.dma("sp" if mi_ != 1 else "pool", x_[:, lo - (t0 - 2):hi - (t0 - 2)], PT["gdn_" + nm][:, lo:hi], writes=[("xh", nm, b2)])
                c_ = cv[nm][b2]
                eng = "dve" if mi_ != 2 else "pool"
                P.op("dve", "tensor_scalar", out=c_[:, 0:n], in0=x_[:, 0:n], scalar1=cw[:, mi_, 0:1], scalar2=None, op0=ALU.mult,
                     reads=[("xh", nm, b2), "cw"], writes=[("cv", nm, b2)])
                for j in range(1, 4):
                    P.op("dve", "scalar_tensor_tensor", out=c_[:, 0:n], in0=x_[:, j:j + n], scalar=cw[:, mi_, j:j + 1], in1=c_[:, 0:n],
                         op0=ALU.mult, op1=ALU.add, reads=[("xh", nm, b2), "cw", ("cv", nm, b2)], writes=[("cv", nm, b2)])
                P.op("act", "activation", out=c_[:, 0:n], in_=c_[:, 0:n], func=AF.Silu, reads=[("cv", nm, b2)], writes=[("cv", nm, b2)])
            for nm, scl in (("q", 128.0 ** -0.5), ("k", 1.0)):
                c_ = cv[nm][b2]
                P.op("act", "activation", out=sqt[:, 0:n], in_=c_[:, 0:n], func=AF.Square, reads=[("cv", nm, b2)], writes=["sqt"])
                P.op("pe", "matmul", out=PS[1][:, 0:n], lhsT=cst["ones_f"][:], rhs=sqt[:, 0:n], start=True, stop=True,
                     reads=["sqt", "ones_f"], writes=[b1])
                P.op("act", "activation", out=rs[:, 0:n], in_=PS[1][:, 0:n], func=AF.Ln, bias=epsC[:], reads=["epsb"], writes=["rs", b1])
                P.op("act", "activation", out=rs[:, 0:n], in_=rs[:, 0:n], func=AF.Exp, scale=-0.5, reads=["rs"], writes=["rs"])
                P.op("dve", "scalar_tensor_tensor", out=c_[:, 0:n], in0=c_[:, 0:n], scalar=scl, in1=rs[:, 0:n], op0=ALU.mult, op1=ALU.mult,
                     reads=[("cv", nm, b2), "rs"], writes=[("cv", nm, b2)])
            R = {nm: rows[nm][b2] for nm in rows}
            rt = lambda nm: ("row", nm, b2)
            P.dma("pool", R["a"][:, 0:n], PT["gdn_af" if d == 0 else "gdn_abk"][:, t0:t0 + n], writes=[rt("a")])
            P.dma("pool", R["b"][:, 0:n], PT["gdn_bf" if d == 0 else "gdn_bb"][:, t0:t0 + n], writes=[rt("b")])
            P.op("act", "activation", out=R["g"][:, 0:n], in_=R["a"][:, 0:n], func=AF.Exp, bias=sc[:, 2 + d:3 + d], reads=[rt("a"), "sc"],
                 writes=[rt("g")])
            P.op("act", "activation", out=R["g"][:, 0:n], in_=R["g"][:, 0:n], func=AF.Ln, bias=one1[:, 0:1], reads=[rt("g"), "one1"],
                 writes=[rt("g")])
            P.op("dve", "tensor_scalar", out=R["g"][:, 0:n], in0=R["g"][:, 0:n], scalar1=nga[:, d:d + 1], scalar2=None, op0=ALU.mult,
                 reads=[rt("g"), "nga"], writes=[rt("g")])
            P.op("act", "activation", out=R["b"][:, 0:n], in_=R["b"][:, 0:n], func=AF.Sigmoid, reads=[rt("b")], writes=[rt("b")])
            rstm = _ld_const(P, io, cAs, "RST" + sfx, 1)
            rsto = _ld_const(P, io, cAs, "RST" + osfx, 1)
            rv = (lambda ap: ap[:, 0:n][:, ::-1]) if d == 1 else (lambda ap: ap[:, 0:n])
            rvo = (lambda ap: ap[:, 0:n][:, ::-1]) if d == 0 else (lambda ap: ap[:, 0:n])
            P.op("dve", "tensor_tensor_scan", out=rv(R["gc"]), data0=rv(rstm), data1=rv(R["g"]), initial=0.0, op0=ALU.mult, op1=ALU.add,
                 reads=[rt("g"), "cAs"], writes=[rt("gc")])
            P.op("dve", "tensor_tensor_scan", out=rvo(R["r"]), data0=rvo(rsto), data1=rvo(R["g"]), initial=0.0, op0=ALU.mult, op1=ALU.add,
                 reads=[rt("g"), "cAs"], writes=[rt("r")])
            P.op("dve", "tensor_tensor", out=R["gcx"][:, 0:n], in0=R["gc"][:, 0:n], in1=R["g"][:, 0:n], op=ALU.subtract,
                 reads=[rt("gc"), rt("g")], writes=[rt("gcx")])
            P.op("dve", "tensor_scalar", out=R["ngc"][:, 0:n], in0=R["gc"][:, 0:n], scalar1=-1.0, scalar2=None, op0=ALU.mult,
                 reads=[rt("gc")], writes=[rt("ngc")])
            P.op("dve", "tensor_tensor", out=R["r"][:, 0:n], in0=R["r"][:, 0:n], in1=R["g"][:, 0:n], op=ALU.subtract,
                 reads=[rt("r"), rt("g")], writes=[rt("r")])
            P.op("act", "activation", out=R["dend"][:, 0:n], in_=R["r"][:, 0:n], func=AF.Exp, reads=[rt("r")], writes=[rt("dend")])
            P.op("act", "activation", out=R["ein"][:, 0:n], in_=R["gc"][:, 0:n], func=AF.Exp, reads=[rt("gc")], writes=[rt("ein")])
            P.op("act", "activation", out=R["eex"][:, 0:n], in_=R["gcx"][:, 0:n], func=AF.Exp, reads=[rt("gcx")], writes=[rt("eex")])
            P.op("act", "activation", out=R["cb"][:, 0:n], in_=R["g"][:, 0:n], func=AF.Exp, reads=[rt("g")], writes=[rt("cb")])
            P.op("dve", "scalar_tensor_tensor", out=R["cb"][:, 0:n], in0=R["cb"][:, 0:n], scalar=-1.0, in1=R["b"][:, 0:n], op0=ALU.mult,
                 op1=ALU.mult, reads=[rt("cb"), rt("b")], writes=[rt("cb")])
            B = {nm: bt[nm][b2] for nm in bt}
            btk = lambda nm: ("bt", nm, b2)
            kn, qn = cv["k"][b2], cv["q"][b2]

            def bcast_mul(row, rtoks, dst, dtok, src, stok, bank, bk):
                P.op("pe", "matmul", out=bank[:, 0:n], lhsT=one1[0:1, :], rhs=row[:, 0:n], start=True, stop=True,
                     reads=rtoks + ["one1"], writes=[bk])
                if src is None:
                    P.op("act", "activation", out=dst[:, 0:n], in_=bank[:, 0:n], func=AF.Copy, reads=[], writes=[dtok, bk])
                else:
                    P.op("dve", "tensor_tensor", out=dst[:, 0:n], in0=src[:, 0:n], in1=bank[:, 0:n], op=ALU.mult, reads=[stok],
                         writes=[dtok, bk])
            bcast_mul(R["ein"], [rt("ein")], B["EinS"], btk("EinS"), None, None, PS[1], b1)
            P.op("pool", "tensor_tensor", out=B["qTs"][:, 0:n], in0=qn[:, 0:n], in1=B["EinS"][:, 0:n], op=ALU.mult,
                 reads=[("cv", "q", b2), btk("EinS")], writes=[btk("qTs")])
            bcast_mul(R["eex"], [rt("eex")], B["aTs"], btk("aTs"), kn, ("cv", "k", b2), PS[1], b1)
            bcast_mul(R["cb"], [rt("cb")], B["bT"], btk("bT"), kn, ("cv", "k", b2), PS[1], b1)
            bcast_mul(R["b"], [rt("b")], B["kT"], btk("kT"), kn, ("cv", "k", b2), PS[1], b1)
            bcast_mul(R["dend"], [rt("dend")], B["KsT"], btk("KsT"), B["kT"], btk("kT"), PS[1], b1)
            bcast_mul(R["dend"], [rt("dend")], B["BsT"], btk("BsT"), B["bT"], btk("bT"), PS[1], b1)
            if d == 0:
                P.dma("pool", zt[b2][:, 0:n], PT["gdn_z"][:, t0:t0 + n], writes=[("zt", b2)])
                P.op("act", "activation", out=zt[b2][:, 0:n], in_=zt[b2][:, 0:n], func=AF.Silu, reads=[("zt", b2)], writes=[("zt", b2)])
            for c0 in offs:
                c2 = ci % 2
                ci += 1
                cs_ = slice(c0, c0 + CH)
                for (row, negnm, Mt, mnm, col) in ((R["gcx"], "NEGEX" + sfx, Mx[c2], "Mx", 0), (R["gc"], "NEGIN" + sfx, Mi[c2], "Mi", 64)):
                    reg = PS[0][0:CH, col:col + CH]
                    P.op("pe", "matmul", out=reg, lhsT=one1[0:1, 0:CH], rhs=row[:, cs_], start=True, stop=False,
                         reads=["one1", rt("gcx"), rt("gc")], writes=[b0])
                    P.op("pe", "matmul", out=reg, lhsT=R["ngc"][:, cs_], rhs=one1[0:1, 0:CH], start=False, stop=False,
                         reads=["one1", rt("ngc")], writes=[b0])
                    P.op("pe", "matmul", out=reg, lhsT=cst["ident"][0:CH, 0:CH], rhs=_ld_const(P, io, cAs, negnm, 64), start=False,
                         stop=True, reads=["ident", "cAs"], writes=[b0])
                    P.op("act", "activation", out=Mt, in_=reg, func=AF.Exp, reads=[], writes=[(mnm, c2), b0])
                for (src, stok, dst, dnm, col, eng) in ((cv["v"][b2], ("cv", "v", b2), Vt[c2], "Vt", 0, "act"),
                                                        (B["KsT"], btk("KsT"), Kst[c2], "Kst", 128, "dve"),
                                                        (B["BsT"], btk("BsT"), Bst[c2], "Bst", 256, "act")):
                    tp = PS[7][0:CH, col:col + 128]
                    P.op("pe", "transpose", out=tp, in_=src[:, cs_], identity=cst["ident"][:], reads=[stok, "ident"], writes=[b7])
                    if eng == "act":
                        P.op("act", "activation", out=dst, in_=tp, func=AF.Copy, reads=[], writes=[(dnm, c2), b7])
                    else:
                        P.op("dve", "tensor_copy", out=dst, in_=tp, reads=[], writes=[(dnm, c2), b7])
                last = c0 + CH - 1 if d == 0 else c0
                w = {"aT": (kn[:, cs_], ("cv", "k", b2)), "qT": (qn[:, cs_], ("cv", "q", b2)), "bT": (B["bT"][:, cs_], btk("bT")),
                     "kT": (B["kT"][:, cs_], btk("kT")), "aTs": (B["aTs"][:, cs_], btk("aTs")), "qTs": (B["qTs"][:, cs_], btk("qTs")),
                     "MexT": (Mx[c2], ("Mx", c2)), "MinT": (Mi[c2], ("Mi", c2)), "Bst": (Bst[c2], ("Bst", c2)),
                     "Kst": (Kst[c2], ("Kst", c2)), "V": (Vt[c2], ("Vt", c2)), "cs": (B["EinS"][:, last:last + 1], btk("EinS"))}
                core.run(w, S, "S", Ot[c2], ("Ot", c2))
                tg = t0 + c0
                if d == 1:
                    P.dma("sp", OB[tg:tg + CH, :], Ot[c2], reads=[("Ot", c2)], writes=[("OB", tg)])
                else:
                    P.dma("sp", Ob[c2], OB[tg:tg + CH, :], reads=[("OB", tg)], writes=[("Ob", c2)])
                    P.op("dve", "tensor_tensor", out=Ot[c2], in0=Ot[c2], in1=Ob[c2], op=ALU.add, reads=[("Ot", c2), ("Ob", c2)],
                         writes=[("Ot", c2)])
                    P.op("act", "activation", out=junk, in_=Ot[c2], func=AF.Square, accum_out=ssq[c2][:, 0:1], reads=[("Ot", c2)],
                         writes=["junk", ("ssq", c2)])
                    P.op("act", "activation", out=ssq[c2][:, 1:2], in_=ssq[c2][:, 0:1], func=AF.Ln, scale=1.0 / 128, bias=epsC[0:CH, :],
                         reads=[("ssq", c2), "epsb"], writes=[("ssq", c2)])
                    P.op("act", "activation", out=ssq[c2][:, 1:2], in_=ssq[c2][:, 1:2], func=AF.Exp, scale=-0.5, reads=[("ssq", c2)],
                         writes=[("ssq", c2)])
                    P.op("dve", "tensor_scalar", out=Ot[c2], in0=Ot[c2], scalar1=ssq[c2][:, 1:2], scalar2=None, op0=ALU.mult,
                         reads=[("Ot", c2), ("ssq", c2)], writes=[("Ot", c2)])
                    tp3 = PS[7][:, 384:384 + CH]
                    P.op("pe", "transpose", out=tp3, in_=Ot[c2], identity=cst["ident"][0:CH, 0:CH], reads=[("Ot", c2), "ident"],
                         writes=[b7])
                    P.op("dve", "scalar_tensor_tensor", out=outb[b2][:, cs_], in0=tp3, scalar=ng[:, 0:1], in1=zt[b2][:, cs_],
                         op0=ALU.mult, op1=ALU.mult, reads=[("zt", b2), "ng"], writes=[("outb", b2), b7])
            if d == 0:
                P.dma("pool", ysT[256:384, t0:t0 + n], outb[b2][:, 0:n], reads=[("outb", b2)], writes=[("ysT", 2, t0)])
    P.release(m0)


NEG_EM05 = -float(np.exp(-0.5))
RWKV_LN_EPS = 64e-5


def _rwkv(P, cst, PS, io, PT, ysT, OB):
    m0 = P.mark()
    cAs = P.carve([128, CA_COLS], F32)
    P.dma("sp", cAs, io["cA"], writes=["cAs"])
    pc = P.carve([128, 16], F32)
    pc64 = P.carve([64, 4], F32)
    P.dma("sp", pc[:, 0:13], io["rw_pc"].rearrange("j c -> c j"), writes=["pc"], allow_slow_non_contiguous=True)
    P.dma("sp", pc64, io["rw_pc64"].rearrange("j c -> c j"), writes=["pc64"], allow_slow_non_contiguous=True)
    om = P.carve([128, 4], F32); hm = P.carve([128, 4], F32); om64 = P.carve([64, 4], F32); hm64 = P.carve([64, 4], F32)
    omka = P.carve([128, 1], F32)
    epsl = P.carve([128, 1], F32)
    P.op("pool", "memset", ap=epsl, constant=RWKV_LN_EPS, writes=["epsl"])
    for (o_, h_, src, tk_) in ((om, hm, pc[:, 0:4], "pc"), (om64, hm64, pc64, "pc64")):
        P.op("dve", "tensor_scalar", out=o_, in0=src, scalar1=-1.0, scalar2=1.0, op0=ALU.mult, op1=ALU.add, reads=[tk_], writes=["omhm"])
        P.op("dve", "tensor_scalar", out=h_, in0=src, scalar1=0.5, scalar2=None, op0=ALU.mult, reads=[tk_], writes=["omhm2"])
    P.op("dve", "tensor_scalar", out=omka, in0=pc[:, 9:10], scalar1=-1.0, scalar2=1.0, op0=ALU.mult, op1=ALU.add, reads=["pc"],
         writes=["omka"])
    w2 = P.carve([64, 2, 128], F32); a2 = P.carve([64, 2, 128], F32); g2 = P.carve([128, 128], F32)
    P.dma("sp", w2, io["rw_w2"].rearrange("d r c -> r d c"), writes=["w2"])
    P.dma("sp", a2, io["rw_a2"].rearrange("d r c -> r d c"), writes=["a2"])
    P.dma("sp", g2, io["rw_g2"], writes=["g2"])
    BLK = _ld_const(P, io, cAs, "BLK")
    MS = [P.carve([CH, 2, CH], F32) for _ in range(2)]
    MI = [P.carve([CH, 2, CH], F32) for _ in range(2)]
    for d in range(2):
        for hh in range(2):
            P.op("dve", "tensor_copy", out=MS[d][:, hh, :], in_=_ld_const(P, io, cAs, "MSTR" + "FB"[d], 64), reads=["cAs"], writes=[("MS", d)])
            P.op("dve", "tensor_copy", out=MI[d][:, hh, :], in_=_ld_const(P, io, cAs, "MASK" + "FB"[d], 64), reads=["cAs"], writes=[("MI", d)])
    core = Core(P, cst, PS, 2, True, _ld_const(P, io, cAs, "IDC", 64))
    S = P.carve([128, 64], F32)
    raw = {nm: [P.carve([128, 516], F32) for _ in range(2)] for nm in ("r", "k", "v", "gd", "wd", "ad", "adb")}
    X = {nm: P.carve([128, 512], F32) for nm in ("k", "gd", "wd", "ad", "adb", "s")}
    Xr = [P.carve([128, 512], F32) for _ in range(2)]
    Xv = [P.carve([128, 512], F32) for _ in range(2)]
    logw = P.carve([128, 512], F32); asig = P.carve([128, 512], F32); kk = P.carve([128, 512], F32); kd = P.carve([128, 512], F32)
    t1 = P.carve([128, 512], F32); t2 = P.carve([128, 512], F32)
    Einv = P.carve([128, 512], F32); Eex = P.carve([128, 512], F32)
    DB = {nm: [P.carve([128, 512], F32) for _ in range(2)] for nm in ("E", "qT", "aT", "KsT", "BsT", "qm0", "qm1", "am0", "am1", "bm0",
                                                                      "bm1", "km0", "km1")}
    SB1 = {nm: P.carve([128, 512], F32) for nm in ("kT", "bT")}
    gt = P.carve([128, 512], F32); bonus = P.carve([128, 512], F32); YT = P.carve([128, 512], F32)
    outb = [P.carve([128, 512], F32) for _ in range(2)]
    Vt = [P.carve([CH, 128], F32) for _ in range(2)]
    Kst = [P.carve([CH, 128], F32) for _ in range(2)]
    Bst = [P.carve([CH, 128], F32) for _ in range(2)]
    Ot = [P.carve([CH, 128], F32) for _ in range(2)]
    Ob = [P.carve([CH, 128], F32) for _ in range(2)]
    b0, b1, b7 = ("bank", 0), ("bank", 1), ("bank", 7)
    bi_ = 0
    ci = 0
    for d in (1, 0):
        sfx = "FB"[d]
        P.op("pool", "memset", ap=S, constant=0.0, reads=[], writes=["S"])
        for (t0, n, s, offs) in _dir_chunks(d):
            b2 = bi_ % 2
            bi_ += 1
            seg0, seg1 = (0, 256) if s else (256, TTOT)
            lo, hi = max(t0 - 1, seg0), min(t0 + n + 1, seg1)
            srcs = [("r", PT["rw_r"], 128, 0), ("k", PT["rw_k"], 128, 1), ("v", PT["rw_v"], 128, 2), ("gd", PT["rw_gd"], 128, 3),
                    ("wd", PT["rw_wdf" if d == 0 else "rw_wdb"], 64, d), ("ad", PT["rw_adf" if d == 0 else "rw_adb"], 64, 2 + d)]
            if d == 0:
                srcs.append(("adb", PT["rw_adb"], 64, 3))
            else:
                srcs = [x for x in srcs if x[0] != "gd"]
            for i, (nm, src, rows_, mc) in enumerate(srcs):
                x_ = raw[nm][b2]
                if lo > t0 - 1 or hi < t0 + n + 1:
                    P.op("pool", "memset", ap=x_[0:rows_, 0:n + 2], constant=0.0, writes=[("raw", nm, b2)])
                P.dma("sp" if i % 2 == 0 else "pool", x_[0:rows_, lo - (t0 - 1):hi - (t0 - 1)], src[:, lo:hi], writes=[("raw", nm, b2)])
                dst = Xr[b2] if nm == "r" else Xv[b2] if nm == "v" else X[nm]
                dtok = ("Xr", b2) if nm == "r" else ("Xv", b2) if nm == "v" else ("X", nm)
                omc, hmc = (om[:, mc:mc + 1], hm[:, mc:mc + 1]) if rows_ == 128 else (om64[:, mc:mc + 1], hm64[:, mc:mc + 1])
                P.op("pool", "tensor_tensor", out=X["s"][0:rows_, 0:n], in0=x_[0:rows_, 0:n], in1=x_[0:rows_, 2:n + 2], op=ALU.add,
                     reads=[("raw", nm, b2)], writes=[("X", "s")])
                P.op("dve", "tensor_scalar", out=dst[0:rows_, 0:n], in0=x_[0:rows_, 1:n + 1], scalar1=omc, scalar2=None, op0=ALU.mult,
                     reads=[("raw", nm, b2), "omhm"], writes=[dtok])
                P.op("dve", "scalar_tensor_tensor", out=dst[0:rows_, 0:n], in0=X["s"][0:rows_, 0:n], scalar=hmc, in1=dst[0:rows_, 0:n],
                     op0=ALU.mult, op1=ALU.add, reads=[("X", "s"), "omhm2", dtok], writes=[dtok])
            xr_, xv_ = Xr[b2], Xv[b2]
            P.op("act", "activation", out=X["wd"][0:64, 0:n], in_=X["wd"][0:64, 0:n], func=AF.Tanh, reads=[("X", "wd")], writes=[("X", "wd")])
            P.op("pe", "matmul", out=PS[0][:, 0:n], lhsT=w2[:, d, :], rhs=X["wd"][0:64, 0:n], start=True, stop=True, reads=["w2", ("X", "wd")],
                 writes=[b0])
            P.op("act", "activation", out=logw[:, 0:n], in_=PS[0][:, 0:n], func=AF.Sigmoid, bias=pc[:, 4 + d:5 + d], reads=["pc"],
                 writes=["logw", b0])
            P.op("dve", "tensor_scalar", out=logw[:, 0:n], in0=logw[:, 0:n], scalar1=NEG_EM05, scalar2=None, op0=ALU.mult, reads=["logw"],
                 writes=["logw"])
            P.op("pe", "matmul", out=PS[1][:, 0:n], lhsT=a2[:, d, :], rhs=X["ad"][0:64, 0:n], start=True, stop=True, reads=["a2", ("X", "ad")],
                 writes=[b1])
            P.op("act", "activation", out=asig[:, 0:n], in_=PS[1][:, 0:n], func=AF.Sigmoid, bias=pc[:, 6 + d:7 + d], reads=["pc"],
                 writes=["asig", b1])
            P.op("dve", "tensor_scalar", out=kk[:, 0:n], in0=X["k"][:, 0:n], scalar1=pc[:, 8:9], scalar2=None, op0=ALU.mult,
                 reads=[("X", "k"), "pc"], writes=["kk"])
            P.op("act", "activation", out=t1[:, 0:n], in_=kk[:, 0:n], func=AF.Square, reads=["kk"], writes=["t1"])
            P.op("pe", "matmul", out=PS[0][:, 0:n], lhsT=BLK, rhs=t1[:, 0:n], start=True, stop=True, reads=["cAs", "t1"], writes=[b0])
            P.op("act", "activation", out=t1[:, 0:n], in_=PS[0][:, 0:n], func=AF.Ln, bias=cst["epsb"][:], reads=["epsb"], writes=["t1", b0])
            P.op("act", "activation", out=t1[:, 0:n], in_=t1[:, 0:n], func=AF.Exp, scale=-0.5, reads=["t1"], writes=["t1"])
            P.op("dve", "tensor_tensor", out=kk[:, 0:n], in0=kk[:, 0:n], in1=t1[:, 0:n], op=ALU.mult, reads=["kk", "t1"], writes=["kk"])
            P.op("dve", "tensor_scalar", out=kd[:, 0:n], in0=asig[:, 0:n], scalar1=pc[:, 9:10], scalar2=omka[:, 0:1], op0=ALU.mult,
                 op1=ALU.add, reads=["asig", "pc", "omka"], writes=["kd"])
            P.op("dve", "tensor_tensor", out=kd[:, 0:n], in0=kd[:, 0:n], in1=X["k"][:, 0:n], op=ALU.mult, reads=["kd", ("X", "k")],
                 writes=["kd"])
            if d == 0:
                P.op("act", "activation", out=X["gd"][:, 0:n], in_=X["gd"][:, 0:n], func=AF.Sigmoid, reads=[("X", "gd")], writes=[("X", "gd")])
                P.op("pe", "matmul", out=PS[1][:, 0:n], lhsT=g2, rhs=X["gd"][:, 0:n], start=True, stop=True, reads=["g2", ("X", "gd")],
                     writes=[b1])
                P.op("act", "activation", out=gt[:, 0:n], in_=PS[1][:, 0:n], func=AF.Copy, reads=[], writes=["gt", b1])
                P.op("pe", "matmul", out=PS[0][:, 0:n], lhsT=a2[:, 1, :], rhs=X["adb"][0:64, 0:n], start=True, stop=True,
                     reads=["a2", ("X", "adb")], writes=[b0])
                P.op("act", "activation", out=t2[:, 0:n], in_=PS[0][:, 0:n], func=AF.Sigmoid, bias=pc[:, 7:8], reads=["pc"], writes=["t2", b0])
                P.op("dve", "tensor_scalar", out=t2[:, 0:n], in0=t2[:, 0:n], scalar1=pc[:, 9:10], scalar2=omka[:, 0:1], op0=ALU.mult,
                     op1=ALU.add, reads=["t2", "pc", "omka"], writes=["t2"])
                P.op("dve", "tensor_tensor", out=t2[:, 0:n], in0=t2[:, 0:n], in1=X["k"][:, 0:n], op=ALU.mult, reads=["t2", ("X", "k")],
                     writes=["t2"])
                P.op("dve", "tensor_tensor", out=t2[:, 0:n], in0=t2[:, 0:n], in1=kd[:, 0:n], op=ALU.add, reads=["t2", "kd"], writes=["t2"])
                P.op("dve", "scalar_tensor_tensor", out=t2[:, 0:n], in0=t2[:, 0:n], scalar=pc[:, 10:11], in1=xr_[:, 0:n], op0=ALU.mult,
                     op1=ALU.mult, reads=["t2", "pc", ("Xr", b2)], writes=["t2"])
                P.op("pe", "matmul", out=PS[1][:, 0:n], lhsT=BLK, rhs=t2[:, 0:n], start=True, stop=True, reads=["cAs", "t2"], writes=[b1])
                P.op("dve", "tensor_tensor", out=bonus[:, 0:n], in0=PS[1][:, 0:n], in1=xv_[:, 0:n], op=ALU.mult, reads=[("Xv", b2)],
                     writes=["bonus", b1])
            E = DB["E"][b2]
            rst = _ld_const(P, io, cAs, "RST" + sfx)
            rv = (lambda ap: ap[:, 0:n][:, ::-1]) if d == 1 else (lambda ap: ap[:, 0:n])
            P.op("dve", "tensor_tensor_scan", out=rv(E), data0=rv(rst), data1=rv(logw), initial=0.0, op0=ALU.mult, op1=ALU.add,
                 reads=["logw", "cAs"], writes=[("E", b2)])
            P.op("act", "activation", out=Einv[:, 0:n], in_=E[:, 0:n], func=AF.Exp, scale=-1.0, reads=[("E", b2)], writes=["Einv"])
            P.op("dve", "tensor_tensor", out=Eex[:, 0:n], in0=E[:, 0:n], in1=logw[:, 0:n], op=ALU.subtract, reads=[("E", b2), "logw"],
                 writes=["Eex"])
            P.op("act", "activation", out=Eex[:, 0:n], in_=Eex[:, 0:n], func=AF.Exp, reads=["Eex"], writes=["Eex"])
            P.op("act", "activation", out=E[:, 0:n], in_=E[:, 0:n], func=AF.Exp, reads=[("E", b2)], writes=[("E", b2)])
            T = {nm: DB[nm][b2] for nm in DB}
            T.update(SB1)
            dt = lambda nm: (nm, b2) if nm in DB else (nm, 0)
            P.op("pool", "tensor_tensor", out=T["qT"][:, 0:n], in0=xr_[:, 0:n], in1=E[:, 0:n], op=ALU.mult, reads=[("Xr", b2), ("E", b2)],
                 writes=[dt("qT")])
            P.op("pool", "tensor_tensor", out=T["aT"][:, 0:n], in0=kk[:, 0:n], in1=Eex[:, 0:n], op=ALU.mult, reads=["kk", "Eex"],
                 writes=[dt("aT")])
            P.op("dve", "tensor_tensor", out=T["kT"][:, 0:n], in0=kd[:, 0:n], in1=Einv[:, 0:n], op=ALU.mult, reads=["kd", "Einv"],
                 writes=[dt("kT")])
            P.op("pool", "tensor_tensor", out=t1[:, 0:n], in0=kk[:, 0:n], in1=asig[:, 0:n], op=ALU.mult, reads=["kk", "asig"], writes=["t1"])
            P.op("dve", "scalar_tensor_tensor", out=T["bT"][:, 0:n], in0=t1[:, 0:n], scalar=-1.0, in1=Einv[:, 0:n], op0=ALU.mult,
                 op1=ALU.mult, reads=["t1", "Einv"], writes=[dt("bT")])
            for (src_, pre, eng) in (("qT", "qm", "pool"), ("aT", "am", "dve"), ("bT", "bm", "pool"), ("kT", "km", "dve")):
                for hh in range(2):
                    P.op(eng, "tensor_scalar", out=T[pre + str(hh)][:, 0:n], in0=T[src_][:, 0:n], scalar1=BLK[:, hh * 64:hh * 64 + 1],
                         scalar2=None, op0=ALU.mult, reads=[dt(src_), "cAs"], writes=[dt(pre + str(hh))])
            for c0 in offs:
                last = c0 + CH - 1 if d == 0 else c0
                cs_ = slice(c0, c0 + CH)
                P.op("dve", "tensor_scalar", out=T["KsT"][:, cs_], in0=T["kT"][:, cs_], scalar1=E[:, last:last + 1], scalar2=None,
                     op0=ALU.mult, reads=[dt("kT"), ("E", b2)], writes=[("KsT", b2, c0)])
                P.op("pool", "tensor_scalar", out=T["BsT"][:, cs_], in0=T["bT"][:, cs_], scalar1=E[:, last:last + 1], scalar2=None,
                     op0=ALU.mult, reads=[dt("bT"), ("E", b2)], writes=[("BsT", b2, c0)])
            for c0 in offs:
                c2 = ci % 2
                ci += 1
                cs_ = slice(c0, c0 + CH)
                last = c0 + CH - 1 if d == 0 else c0
                for (src, stok, dst, dnm, col, eng) in ((xv_, ("Xv", b2), Vt[c2], "Vt", 0, "act"),
                                                        (T["KsT"], ("KsT", b2, c0), Kst[c2], "Kst", 128, "dve"),
                                                        (T["BsT"], ("BsT", b2, c0), Bst[c2], "Bst", 256, "act")):
                    tp = PS[7][0:CH, col:col + 128]
                    P.op("pe", "transpose", out=tp, in_=src[:, cs_], identity=cst["ident"][:], reads=[stok, "ident"], writes=[b7])
                    if eng == "act":
                        P.op("act", "activation", out=dst, in_=tp, func=AF.Copy, reads=[], writes=[(dnm, c2), b7])
                    else:
                        P.op("dve", "tensor_copy", out=dst, in_=tp, reads=[], writes=[(dnm, c2), b7])
                hm_ = lambda pre: [(T[pre + str(hh)][:, cs_], dt(pre + str(hh))) for hh in range(2)]
                w = {"aT": (T["aT"][:, cs_], dt("aT")), "qT": (T["qT"][:, cs_], dt("qT")), "bTm": hm_("bm"), "kTm": hm_("km"),
                     "aTsm": hm_("am"), "qTsm": hm_("qm"),
                     "MexT": (MS[d].rearrange("p a b -> p (a b)"), ("MS", d)), "MinT": (MI[d].rearrange("p a b -> p (a b)"), ("MI", d)),
                     "Bst": (Bst[c2], ("Bst", c2)), "Kst": (Kst[c2], ("Kst", c2)), "V": (Vt[c2], ("Vt", c2)),
                     "cs": (E[:, last:last + 1], ("E", b2))}
                core.run(w, S, "S", Ot[c2], ("Ot", c2))
                tg = t0 + c0
                if d == 1:
                    P.dma("sp", OB[tg:tg + CH, :], Ot[c2], reads=[("Ot", c2)], writes=[("OB", tg)])
                else:
                    P.dma("sp", Ob[c2], OB[tg:tg + CH, :], reads=[("OB", tg)], writes=[("Ob", c2)])
                    P.op("dve", "tensor_tensor", out=Ot[c2], in0=Ot[c2], in1=Ob[c2], op=ALU.add, reads=[("Ot", c2), ("Ob", c2)],
                         writes=[("Ot", c2)])
                    tp3 = PS[7][:, 384:384 + CH]
                    P.op("pe", "transpose", out=tp3, in_=Ot[c2], identity=cst["ident"][0:CH, 0:CH], reads=[("Ot", c2), "ident"],
                         writes=[b7])
                    P.op("act", "activation", out=YT[:, cs_], in_=tp3, func=AF.Copy, reads=[], writes=[("YT", c0), b7])
            if d == 0:
                ytoks = [("YT", c0) for c0 in offs]
                o_ = outb[b2]
                P.op("pe", "matmul", out=PS[0][:, 0:n], lhsT=BLK, rhs=YT[:, 0:n], start=True, stop=True, reads=["cAs"] + ytoks, writes=[b0])
                P.op("dve", "scalar_tensor_tensor", out=t1[:, 0:n], in0=PS[0][:, 0:n], scalar=-1.0 / 64, in1=YT[:, 0:n], op0=ALU.mult,
                     op1=ALU.add, reads=ytoks, writes=["t1", b0])
                P.op("act", "activation", out=t2[:, 0:n], in_=t1[:, 0:n], func=AF.Square, reads=["t1"], writes=["t2"])
                P.op("pe", "matmul", out=PS[1][:, 0:n], lhsT=BLK, rhs=t2[:, 0:n], start=True, stop=True, reads=["cAs", "t2"], writes=[b1])
                P.op("act", "activation", out=t2[:, 0:n], in_=PS[1][:, 0:n], func=AF.Ln, scale=1.0 / 64, bias=epsl[:], reads=["epsl"],
                     writes=["t2", b1])
                P.op("act", "activation", out=t2[:, 0:n], in_=t2[:, 0:n], func=AF.Exp, scale=-0.5, reads=["t2"], writes=["t2"])
                P.op("dve", "tensor_tensor", out=t1[:, 0:n], in0=t1[:, 0:n], in1=t2[:, 0:n], op=ALU.mult, reads=["t1", "t2"], writes=["t1"])
                P.op("dve", "tensor_scalar", out=t1[:, 0:n], in0=t1[:, 0:n], scalar1=pc[:, 11:12], scalar2=pc[:, 12:13], op0=ALU.mult,
                     op1=ALU.add, reads=["t1", "pc"], writes=["t1"])
                P.op("pool", "tensor_tensor", out=t1[:, 0:n], in0=t1[:, 0:n], in1=bonus[:, 0:n], op=ALU.add, reads=["t1", "bonus"],
                     writes=["t1"])
                P.op("dve", "tensor_tensor", out=o_[:, 0:n], in0=t1[:, 0:n], in1=gt[:, 0:n], op=ALU.mult, reads=["t1", "gt"],
                     writes=[("outb", b2)])
                P.dma("pool", ysT[384:512, t0:t0 + n], o_[:, 0:n], reads=[("outb", b2)], writes=[("ysT", 3, t0)])
    P.release(m0)


def A_gather(outs):
    ys = np.zeros((2, TTOT, 4, 512), np.float32)
    for core in range(8):
        b, h = core // 4, core % 4
        o = outs[core].reshape(4, 128, TTOT)
        ys[b, :, :, h * 128:(h + 1) * 128] = np.transpose(o, (2, 0, 1))
    ys = ys.reshape(2, TTOT, 2048)
    return ys[:, CTX:], ys[:, :CTX]


def kernel(**inputs):
    inp = {k: np.asarray(v) for k, v in inputs.items()}
    h_lat, h_ctx = inp["x"].astype(np.float32), inp["ctx"].astype(np.float32)
    cores = list(range(8))
    for l in range(2):
        last = l == 1
        ncA, _ = build_A()
        resA = run_bass_kernel_spmd(ncA, A_inmaps(l, h_lat, h_ctx, inp), core_ids=cores)
        ysl, ysc = A_gather([r["ysT"] for r in resA.results])
        ncB, _ = build_B(last)
        resB = run_bass_kernel_spmd(ncB, B_inmaps(l, last, h_lat, h_ctx, ysl, ysc, inp), core_ids=cores)
        h_lat, h_ctx = B_gather(last, [r["out"] for r in resB.results])
    return np.ascontiguousarray(h_lat, dtype=np.float32)
```

```python
import contextlib
import numpy as np
import concourse.bass as bass
import concourse.mybir as mybir
from concourse.bass_utils import run_bass_kernel_spmd

F32 = mybir.dt.float32
BF16 = mybir.dt.bfloat16
AF = mybir.ActivationFunctionType
ALU = mybir.AluOpType
AX = mybir.AxisListType

ENG_NAMES = ("pe", "act", "dve", "pool", "sp")
EPOCH = 16000
N_DMA_SEMS = 12

D = 1024
NB = 2
SEQ = 8192
CTX = 256
TTOT = SEQ + CTX
DFF = 2816
N_IN = 11152
GATE_OFF = N_IN - 4096
EPS = 1e-6


class Prog:
    def __init__(self, nc, same_engine_sync=True):
        self.nc = nc
        self.es = contextlib.ExitStack()
        self.ops = {e: [] for e in ENG_NAMES}
        self.cnt = {e: 0 for e in ENG_NAMES}
        self.sem = {}
        self.known = {e: {} for e in ENG_NAMES}
        self.semobj = {}
        self.ep = {}
        self.finals = []
        for e in ENG_NAMES:
            if e != "sp":
                self._new_epoch(e)
        self.dma_sems = {}
        self.dma_k = {}
        for q in ("sp", "act", "pool"):
            self.dma_sems[q] = []
            for i in range(N_DMA_SEMS):
                nm = f"d_{q}_{i}"
                self.semobj[nm] = self.es.enter_context(nc.semaphore(nm))
                self.dma_sems[q].append(nm)
            self.dma_k[q] = 0
        self.lastw = {}
        self.readers = {}
        self.env = {}
        self.same_engine_sync = same_engine_sync
        self.n_wait = 0
        self.n_ins = 0

    def _new_epoch(self, e):
        if e in self.sem:
            self.finals.append((self.sem[e], self.cnt[e]))
        k = self.ep.get(e, -1) + 1
        self.ep[e] = k
        nm = f"s_{e}_{k}"
        self.semobj[nm] = self.es.enter_context(self.nc.semaphore(nm))
        self.sem[e] = nm
        self.cnt[e] = 0

    def sb(self, name, shape, dtype=F32):
        return self.es.enter_context(self.nc.sbuf_tensor("sb_" + name, list(shape), dtype))

    def ps(self, name, shape, dtype=F32):
        return self.es.enter_context(self.nc.psum_tensor("ps_" + name, list(shape), dtype))

    def _deps(self, eng, reads, writes):
        need = {}
        for t in reads:
            for ev in self.lastw.get(t, ()):
                need[ev[0]] = max(need.get(ev[0], 0), ev[1])
        for t in writes:
            for ev in self.lastw.get(t, ()):
                need[ev[0]] = max(need.get(ev[0], 0), ev[1])
            for ev in self.readers.get(t, ()):
                need[ev[0]] = max(need.get(ev[0], 0), ev[1])
        kn = self.known[eng]
        for s, v in need.items():
            if kn.get(s, 0) >= v:
                continue
            if s.startswith("s_" + eng + "_") and (eng == "pe" or not self.same_engine_sync):
                continue
            self.ops[eng].append(("wait", self.semobj[s], v))
            self.n_wait += 1
            kn[s] = v

    def _commit(self, evs, reads, writes):
        for t in writes:
            self.lastw[t] = list(evs)
            self.readers[t] = []
        for t in reads:
            if t in writes:
                continue
            self.readers.setdefault(t, []).extend(evs)

    def op(self, eng, meth, reads=(), writes=(), **kw):
        if self.cnt[eng] >= EPOCH:
            self._new_epoch(eng)
        self._deps(eng, reads, writes)
        self.cnt[eng] += 1
        s = self.sem[eng]
        self.ops[eng].append(("ins", meth, kw, self.semobj[s], 1))
        self._commit([(s, self.cnt[eng])], reads, writes)
        self.n_ins += 1

    def dma(self, q, out, in_, reads=(), writes=(), **kw):
        self.dma_group([(q, out, in_)], reads, writes, **kw)

    def dma_group(self, items, reads=(), writes=(), **kw):
        for q in dict.fromkeys(it[0] for it in items):
            self._deps(q, reads, writes)
        evs = []
        for (q, out, in_) in items:
            k = self.dma_k[q]
            self.dma_k[q] += 1
            s = self.dma_sems[q][k % N_DMA_SEMS]
            target = 16 * (k // N_DMA_SEMS + 1)
            if target > 16 and self.known[q].get(s, 0) < target - 16:
                self.ops[q].append(("wait", self.semobj[s], target - 16))
                self.known[q][s] = target - 16
            self.ops[q].append(("ins", "dma_start", dict(kw, out=out, in_=in_), self.semobj[s], 16))
            evs.append((s, target))
            self.n_ins += 1
        self._commit(evs, reads, writes)

    def coll(self, kind, in_ap, out_ap, groups, reads=(), writes=()):
        q = "pool"
        self._deps(q, reads, writes)
        k = self.dma_k[q]
        self.dma_k[q] += 1
        s = self.dma_sems[q][k % N_DMA_SEMS]
        target = 16 * (k // N_DMA_SEMS + 1)
        if target > 16 and self.known[q].get(s, 0) < target - 16:
            self.ops[q].append(("wait", self.semobj[s], target - 16))
            self.known[q][s] = target - 16
        self.ops[q].append(("ins", "collective_compute", dict(kind=kind, op=ALU.bypass, replica_groups=groups, ins=[in_ap],
                                                               outs=[out_ap]), self.semobj[s], 16))
        self._commit([(s, target)], reads, writes)
        self.n_ins += 1

    def raw(self, eng, fn):
        self.ops[eng].append(("raw", fn))

    def finish_wait(self, eng, tokens):
        self._deps(eng, tokens, ())

    def barrier(self):
        evs = [(self.sem[x], self.cnt[x]) for x in ENG_NAMES if x != "sp" and self.cnt[x] > 0] + list(self.finals)
        for q in self.dma_sems:
            k = self.dma_k[q]
            for i, s in enumerate(self.dma_sems[q]):
                n = (k - i + N_DMA_SEMS - 1) // N_DMA_SEMS if k > i else 0
                if n > 0:
                    evs.append((s, 16 * n))
        for e in ENG_NAMES:
            kn = self.known[e]
            for (s, v) in evs:
                if kn.get(s, 0) >= v:
                    continue
                if s.startswith("s_" + e + "_"):
                    continue
                self.ops[e].append(("wait", self.semobj[s], v))
                kn[s] = v
        self.lastw.clear()
        self.readers.clear()

    def arena_init(self, nwords):
        self.arena = self.sb("arena", [128, nwords], F32)
        self.aoff = 0
        self.anw = nwords

    def carve(self, shape, dtype=F32):
        n = 1
        for d in shape[1:]:
            n *= d
        nw = n if dtype == F32 else (n + 1) // 2
        assert self.aoff + nw <= self.anw, ("arena overflow", self.aoff, nw, self.anw)
        v = self.arena[0:shape[0], self.aoff:self.aoff + nw]
        self.aoff += nw
        if dtype != F32:
            v = v.bitcast(dtype)[:, 0:n]
        if len(shape) == 3:
            v = v.rearrange("p (a b) -> p a b", a=shape[1])
        elif len(shape) == 4:
            v = v.rearrange("p (a b c) -> p a b c", a=shape[1], b=shape[2])
        return v

    def mark(self):
        return self.aoff

    def release(self, m):
        self.barrier()
        self.aoff = m

    def emit(self):
        nc = self.nc
        with nc.Block() as block:
            def mk(ename):
                lst = self.ops[ename]

                def body(e):
                    for it in lst:
                        if it[0] == "wait":
                            e.wait_ge(it[1], it[2])
                        elif it[0] == "raw":
                            it[1](e, self.env)
                        else:
                            kw = {k: (v(self.env) if callable(v) else v) for k, v in it[2].items()}
                            getattr(e, it[1])(**kw).then_inc(it[3], it[4])
                return body
            block.tensor(mk("pe"))
            block.scalar(mk("act"))
            block.vector(mk("dve"))
            block.gpsimd(mk("pool"))
            block.sync(mk("sp"))
        self.es.close()


class RR:
    def __init__(self, items):
        self.items = list(items)
        self.i = 0

    def next(self):
        x = self.items[self.i % len(self.items)]
        self.i += 1
        return x


def _common_consts(P):
    c = {}
    c["ones_bf"] = P.sb("ones_bf", [128, 128], BF16)
    P.op("pool", "memset", ap=c["ones_bf"][:], constant=1.0, writes=["ones_bf"])
    c["ones_f"] = P.sb("ones_f", [128, 128], F32)
    P.op("pool", "memset", ap=c["ones_f"][:], constant=1.0, writes=["ones_f"])
    c["ident"] = P.sb("ident", [128, 128], F32)
    P.op("pool", "memset", ap=c["ident"][:], constant=1.0, writes=["ident"])
    P.op("pool", "affine_select", out=c["ident"][:], in_=c["ident"][:], pattern=[[-1, 128]],
         compare_op=ALU.is_equal, fill=0.0, base=0, channel_multiplier=1, reads=["ident"], writes=["ident"])
    c["epsb"] = P.sb("epsb", [128, 1], F32)
    P.op("pool", "memset", ap=c["epsb"][:], constant=EPS, writes=["epsb"])
    return c


def _mod_vectors(P, cst, cvec, mod_w, mod_b, nchunks, wst, psum, modsb):
    craw = P.sb("craw", [128, 8, 2], F32)
    csil = P.sb("csil", [128, 8, 2], F32)
    mb = P.sb("modb", [128, 48], F32)
    P.dma_group([("sp", craw[:, :, s], cvec[s, :].rearrange("(k p) -> p k", p=128)) for s in range(2)], writes=["craw"],
                allow_slow_non_contiguous=True)
    P.dma("sp", mb[:, 0:nchunks], mod_b[0, 0:nchunks * 128].rearrange("(j p) -> p j", p=128), writes=["modb"],
          allow_slow_non_contiguous=True)
    P.op("act", "activation", out=csil[:], in_=craw[:], func=AF.Silu, reads=["craw"], writes=["csil"])
    ng = nchunks // 4
    for g in range(ng):
        st = wst[g % 2]
        tok = ("wst", g % 2)
        v = st[:, 0:4096].rearrange("p (k c) -> p k c", k=8)
        P.dma_group([("sp" if kc % 2 == 0 else "pool", v[:, kc, :], mod_w[kc * 128:(kc + 1) * 128, g * 512:(g + 1) * 512])
                     for kc in range(8)], writes=[tok])
        for jj in range(4):
            j = g * 4 + jj
            for kc in range(8):
                P.op("pe", "matmul", out=psum[:, j, :], lhsT=v[:, kc, jj * 128:(jj + 1) * 128], rhs=csil[:, kc, :],
                     start=(kc == 0), stop=(kc == 7), reads=[tok, "csil"], writes=["modps"])
    for s in range(2):
        P.op("dve", "tensor_tensor", out=modsb[:, 0:nchunks, s], in0=psum[:, 0:nchunks, s], in1=mb[:, 0:nchunks],
             op=ALU.add, reads=["modps", "modb"], writes=["modsb"])


def _rms_stats(P, cst, hT, nk, t0, tn, htok, sq, ssps, rstd, tagsfx):
    for kc in range(nk):
        P.op("act", "activation", out=sq[:, kc, 0:tn], in_=hT[:, kc, t0:t0 + tn], func=AF.Square,
             reads=[htok(kc)], writes=[("sq", kc)])
    for kc in range(nk):
        P.op("pe", "matmul", out=ssps[:, 0:tn], lhsT=cst["ones_bf"][:], rhs=sq[:, kc, 0:tn], start=(kc == 0),
             stop=(kc == nk - 1), reads=[("sq", kc), "ones_bf"], writes=["ssps"])
    P.op("act", "activation", out=rstd[:, 0:tn], in_=ssps[:, 0:tn], func=AF.Ln, scale=1.0 / (nk * 128),
         bias=cst["epsb"][:], reads=["ssps", "epsb"], writes=["rstd"])
    P.op("act", "activation", out=rstd[:, 0:tn], in_=rstd[:, 0:tn], func=AF.Exp, scale=-0.5,
         reads=["rstd"], writes=["rstd"])


def build_B(last: bool):
    NL = 2048
    NC_ = 0 if last else 64
    NT = NL + NC_
    nc = bass.Bass("TRN2", target_bir_lowering=False)
    hT_in = nc.dram_tensor("hT", [D, NT], F32, kind="ExternalInput").ap()
    ysT_in = nc.dram_tensor("ysT", [2048, NT], F32, kind="ExternalInput").ap()
    cvec = nc.dram_tensor("cvec", [2, D], F32, kind="ExternalInput").ap()
    mod_w = nc.dram_tensor("mod_w", [D, 6 * D], F32, kind="ExternalInput").ap()
    mod_b = nc.dram_tensor("mod_b", [1, 6 * D], F32, kind="ExternalInput").ap()
    n1g = nc.dram_tensor("n1g", [1, D], F32, kind="ExternalInput").ap()
    n2g = nc.dram_tensor("n2g", [1, D], F32, kind="ExternalInput").ap()
    fng = nc.dram_tensor("fng", [1, D], F32, kind="ExternalInput").ap()
    wg = nc.dram_tensor("wg", [D, 4096], F32, kind="ExternalInput").ap()
    gate_b = nc.dram_tensor("gate_b", [1, 4096], F32, kind="ExternalInput").ap()
    wbr = nc.dram_tensor("wbr", [4, 512, D], F32, kind="ExternalInput").ap()
    wo = nc.dram_tensor("wo", [D, D], F32, kind="ExternalInput").ap()
    w1 = nc.dram_tensor("w1", [D, DFF], F32, kind="ExternalInput").ap()
    w3 = nc.dram_tensor("w3", [D, DFF], F32, kind="ExternalInput").ap()
    w2 = nc.dram_tensor("w2", [DFF, D], F32, kind="ExternalInput").ap()
    if last:
        out = nc.dram_tensor("out", [NL, D], F32, kind="ExternalOutput").ap()
    else:
        out = nc.dram_tensor("out", [D, NT], F32, kind="ExternalOutput").ap()

    P = Prog(nc)
    cst = _common_consts(P)
    HMAX = 1088
    hT = P.sb("hT", [128, 8, HMAX], F32)
    xu = P.sb("xu", [128, 8, HMAX], BF16)
    A = P.sb("A", [128, 22, HMAX], BF16)
    mg = P.sb("mg", [128, 8, HMAX], BF16)
    wst = [P.sb(f"wst{i}", [128, 4096], F32) for i in range(2)]
    wbf = [P.sb(f"wbf{i}", [128, 4096], BF16) for i in range(2)]
    yst = [P.sb(f"yst{i}", [128, 512], F32) for i in range(2)]
    sq = P.sb("sq", [128, 8, 512], BF16)
    rstd = P.sb("rstd", [128, 512], F32)
    tmp = [P.sb(f"tmp{i}", [128, 512], F32) for i in range(3)]
    modsb = P.sb("modsb", [128, 48, 2], F32)
    gvec = P.sb("gvec", [128, 8, 3], F32)
    gbv = P.sb("gbv", [128, 32], F32)
    GS = P.sb("GS", [128, 8, 2, 4], F32)
    psA = [P.ps(f"psA{i}", [128, 512], F32) for i in range(2)]
    psB = [P.ps(f"psB{i}", [128, 512], F32) for i in range(2)]
    psS = P.ps("psS", [128, 512], F32)
    psM = P.ps("psM", [128, 48, 2], F32)
    psT = [P.ps(f"psT{i}", [128, 512], F32) for i in range(2)]

    P.dma_group([("sp", gvec[:, :, i], g_[0, :].rearrange("(k p) -> p k", p=128)) for i, g_ in enumerate((n1g, n2g, fng))],
                writes=["gvec"], allow_slow_non_contiguous=True)
    P.dma("sp", gbv[:], gate_b[0, :].rearrange("(j p) -> p j", p=128), writes=["gbv"], allow_slow_non_contiguous=True)
    _mod_vectors(P, cst, cvec, mod_w, mod_b, 48, wst, psM, modsb)
    for s in range(2):
        for (gi, sc_c, sh_c, col) in ((0, 1, 0, 0), (1, 4, 3, 2)):
            P.op("dve", "scalar_tensor_tensor", out=GS[:, :, s, col], in0=modsb[:, sc_c * 8:(sc_c + 1) * 8, s], scalar=1.0,
                 in1=gvec[:, :, gi], op0=ALU.add, op1=ALU.mult, reads=["modsb", "gvec"], writes=["GS"])
            P.op("dve", "tensor_copy", out=GS[:, :, s, col + 1], in_=modsb[:, sh_c * 8:(sh_c + 1) * 8, s],
                 reads=["modsb"], writes=["GS"])

    castq = RR(["dve", "pool"])
    dq = RR(["sp", "pool"])

    def cast(out, in_, reads, writes):
        e = castq.next()
        P.op(e, "tensor_copy", out=out, in_=in_, reads=reads, writes=writes)

    wctr = [0]

    def load_w(pieces, n):
        i = wctr[0] % 2
        wctr[0] += 1
        P.dma_group([(dq.next(), vf(wst[i]), src) for (vf, src) in pieces], writes=[("wst", i)])
        cast(wbf[i][:, 0:n], wst[i][:, 0:n], [("wst", i)], [("wbf", i)])
        return wbf[i], ("wbf", i)

    halves = [(0, 1024, 0), (1024, 1024, NC_)]
    for (l0, nl, ncx) in halves:
        ntok = nl + ncx
        blocks = [(o, 512, 0) for o in range(0, nl, 512)] + ([(nl, ncx, 1)] if ncx else [])
        def gcol(o):
            return l0 + o if o < nl else NL + (o - nl)
        for kc in range(8):
            for (o, n, s) in blocks:
                P.dma(dq.next(), hT[:, kc, o:o + n], hT_in[kc * 128:(kc + 1) * 128, gcol(o):gcol(o) + n],
                      writes=[("h", kc, o)])
        yi = 0
        for c16 in range(16):
            for (o, n, s) in blocks:
                st = yst[yi % 2]
                P.dma(dq.next(), st[:, 0:n], ysT_in[c16 * 128:(c16 + 1) * 128, gcol(o):gcol(o) + n], writes=[("yst", yi % 2)])
                cast(A[:, c16, o:o + n], st[:, 0:n], [("yst", yi % 2)], [("A", c16, o)])
                yi += 1
        for (o, n, s) in blocks:
            _rms_stats(P, cst, hT, 8, o, n, lambda kc: ("h", kc, o), sq, psS, rstd, "")
            for kc in range(8):
                t = tmp[kc % 2]
                P.op("dve", "tensor_tensor", out=t[:, 0:n], in0=hT[:, kc, o:o + n], in1=rstd[:, 0:n], op=ALU.mult,
                     reads=[("h", kc, o), "rstd"], writes=[("tmp", kc % 2)])
                P.op("pool", "tensor_scalar", out=xu[:, kc, o:o + n], in0=t[:, 0:n], scalar1=GS[:, kc, s, 0:1],
                     scalar2=GS[:, kc, s, 1:2], op0=ALU.mult, op1=ALU.add, reads=[("tmp", kc % 2), "GS"],
                     writes=[("xu", kc, o)])
        for j in range(8):
            gsrc = wg.rearrange("(kc p) (k j c) -> p k kc j c", p=128, k=4, j=8)
            pieces = [((lambda t, k=k: t[:, k * 1024:(k + 1) * 1024].rearrange("p (kc c) -> p kc c", kc=8)),
                       gsrc[:, k, :, j, :]) for k in range(4)]
            wgt, wgtok = load_w(pieces, 4096)
            wgv = wgt[:, 0:4096].rearrange("p (k kc c) -> p k kc c", k=4, kc=8)
            bsrc = wbr.rearrange("k (kc p) (j c) -> p k kc j c", p=128, j=8)
            pieces = [((lambda t, k=k: t[:, k * 512:(k + 1) * 512].rearrange("p (kc c) -> p kc c", kc=4)),
                       bsrc[:, k, :, j, :]) for k in range(4)]
            wbt, wbtok = load_w(pieces, 2048)
            wbv = wbt[:, 0:2048].rearrange("p (k kc c) -> p k kc c", k=4, kc=4)
            for (o, n, s) in blocks:
                for k in range(4):
                    pa = psA[k % 2]
                    pb = psB[k % 2]
                    for kc in range(8):
                        P.op("pe", "matmul", out=pa[:, 0:n], lhsT=wgv[:, k, kc, :], rhs=xu[:, kc, o:o + n], start=(kc == 0),
                             stop=(kc == 7), reads=[wgtok, ("xu", kc, o)], writes=[("psA", k % 2)])
                    for kc in range(4):
                        P.op("pe", "matmul", out=pb[:, 0:n], lhsT=wbv[:, k, kc, :], rhs=A[:, k * 4 + kc, o:o + n],
                             start=(kc == 0), stop=(kc == 3), reads=[wbtok, ("A", k * 4 + kc, o)], writes=[("psB", k % 2)])
                    sg = tmp[k % 2]
                    P.op("act", "activation", out=sg[:, 0:n], in_=pa[:, 0:n], func=AF.Sigmoid,
                         bias=gbv[:, k * 8 + j:k * 8 + j + 1], reads=[("psA", k % 2), "gbv"], writes=[("tmp", k % 2)])
                    if k == 0:
                        P.op("dve", "tensor_tensor", out=tmp[2][:, 0:n], in0=sg[:, 0:n], in1=pb[:, 0:n], op=ALU.mult,
                             reads=[("tmp", 0), ("psB", 0)], writes=[("tmp", 2)])
                    else:
                        P.op("dve", "tensor_tensor", out=sg[:, 0:n], in0=sg[:, 0:n], in1=pb[:, 0:n], op=ALU.mult,
                             reads=[("tmp", k % 2), ("psB", k % 2)], writes=[("tmp", k % 2)])
                        if k < 3:
                            P.op("pool", "tensor_tensor", out=tmp[2][:, 0:n], in0=tmp[2][:, 0:n], in1=sg[:, 0:n], op=ALU.add,
                                 reads=[("tmp", 2), ("tmp", k % 2)], writes=[("tmp", 2)])
                        else:
                            P.op("pool", "tensor_tensor", out=mg[:, j, o:o + n], in0=tmp[2][:, 0:n], in1=sg[:, 0:n], op=ALU.add,
                                 reads=[("tmp", 2), ("tmp", k % 2)], writes=[("mg", j, o)])
        for j in range(8):
            osrc = wo.rearrange("(kc p) (j c) -> p kc j c", p=128, j=8)
            wt, wtok = load_w([((lambda t: t[:, 0:1024].rearrange("p (kc c) -> p kc c", kc=8)), osrc[:, :, j, :])], 1024)
            wv = wt[:, 0:1024].rearrange("p (kc c) -> p kc c", kc=8)
            for bi, (o, n, s) in enumerate(blocks):
                pa = psA[bi % 2]
                for kc in range(8):
                    P.op("pe", "matmul", out=pa[:, 0:n], lhsT=wv[:, kc, :], rhs=mg[:, kc, o:o + n], start=(kc == 0),
                         stop=(kc == 7), reads=[wtok, ("mg", kc, o)], writes=[("psA", bi % 2)])
                P.op("dve", "scalar_tensor_tensor", out=hT[:, j, o:o + n], in0=pa[:, 0:n], scalar=modsb[:, 16 + j, s:s + 1],
                     in1=hT[:, j, o:o + n], op0=ALU.mult, op1=ALU.add, reads=[("psA", bi % 2), "modsb", ("h", j, o)],
                     writes=[("h", j, o)])
        for (o, n, s) in blocks:
            _rms_stats(P, cst, hT, 8, o, n, lambda kc: ("h", kc, o), sq, psS, rstd, "")
            for kc in range(8):
                t = tmp[kc % 2]
                P.op("dve", "tensor_tensor", out=t[:, 0:n], in0=hT[:, kc, o:o + n], in1=rstd[:, 0:n], op=ALU.mult,
                     reads=[("h", kc, o), "rstd"], writes=[("tmp", kc % 2)])
                P.op("pool", "tensor_scalar", out=xu[:, kc, o:o + n], in0=t[:, 0:n], scalar1=GS[:, kc, s, 2:3],
                     scalar2=GS[:, kc, s, 3:4], op0=ALU.mult, op1=ALU.add, reads=[("tmp", kc % 2), "GS"],
                     writes=[("xu", kc, o)])
        for c2 in range(22):
            s1 = w1.rearrange("(kc p) (j c) -> p kc j c", p=128, c=128)
            s3 = w3.rearrange("(kc p) (j c) -> p kc j c", p=128, c=128)
            wt, wtok = load_w([((lambda t: t[:, 0:1024].rearrange("p (kc c) -> p kc c", kc=8)), s1[:, :, c2, :]),
                               ((lambda t: t[:, 1024:2048].rearrange("p (kc c) -> p kc c", kc=8)), s3[:, :, c2, :])], 2048)
            wv = wt[:, 0:2048].rearrange("p (m kc c) -> p m kc c", m=2, kc=8)
            for bi, (o, n, s) in enumerate(blocks):
                pa = psA[bi % 2]
                pb = psB[bi % 2]
                for kc in range(8):
                    P.op("pe", "matmul", out=pa[:, 0:n], lhsT=wv[:, 0, kc, :], rhs=xu[:, kc, o:o + n], start=(kc == 0),
                         stop=(kc == 7), reads=[wtok, ("xu", kc, o)], writes=[("psA", bi % 2)])
                for kc in range(8):
                    P.op("pe", "matmul", out=pb[:, 0:n], lhsT=wv[:, 1, kc, :], rhs=xu[:, kc, o:o + n], start=(kc == 0),
                         stop=(kc == 7), reads=[wtok, ("xu", kc, o)], writes=[("psB", bi % 2)])
                t = tmp[bi % 2]
                P.op("act", "activation", out=t[:, 0:n], in_=pa[:, 0:n], func=AF.Silu, reads=[("psA", bi % 2)],
                     writes=[("tmp", bi % 2)])
                P.op("dve", "tensor_tensor", out=A[:, c2, o:o + n], in0=t[:, 0:n], in1=pb[:, 0:n], op=ALU.mult,
                     reads=[("tmp", bi % 2), ("psB", bi % 2)], writes=[("A", c2, o)])
        for j in range(8):
            s2 = w2.rearrange("(kc p) (j c) -> p kc j c", p=128, j=8)
            wt, wtok = load_w([((lambda t: t[:, 0:2816].rearrange("p (kc c) -> p kc c", kc=22)), s2[:, :, j, :])], 2816)
            wv = wt[:, 0:2816].rearrange("p (kc c) -> p kc c", kc=22)
            for bi, (o, n, s) in enumerate(blocks):
                pa = psA[bi % 2]
                for kc in range(22):
                    P.op("pe", "matmul", out=pa[:, 0:n], lhsT=wv[:, kc, :], rhs=A[:, kc, o:o + n], start=(kc == 0),
                         stop=(kc == 21), reads=[wtok, ("A", kc, o)], writes=[("psA", bi % 2)])
                P.op("dve", "scalar_tensor_tensor", out=hT[:, j, o:o + n], in0=pa[:, 0:n], scalar=modsb[:, 40 + j, s:s + 1],
                     in1=hT[:, j, o:o + n], op0=ALU.mult, op1=ALU.add, reads=[("psA", bi % 2), "modsb", ("h", j, o)],
                     writes=[("h", j, o)])
        if not last:
            for kc in range(8):
                for (o, n, s) in blocks:
                    P.dma(dq.next(), out[kc * 128:(kc + 1) * 128, gcol(o):gcol(o) + n], hT[:, kc, o:o + n],
                          reads=[("h", kc, o)], writes=[("out", kc, gcol(o))])
        else:
            ti = 0
            for (o, n, s) in blocks:
                _rms_stats(P, cst, hT, 8, o, n, lambda kc: ("h", kc, o), sq, psS, rstd, "")
                for kc in range(8):
                    P.op("dve", "scalar_tensor_tensor", out=hT[:, kc, o:o + n], in0=hT[:, kc, o:o + n], scalar=gvec[:, kc, 2:3],
                         in1=rstd[:, 0:n], op0=ALU.mult, op1=ALU.mult, reads=[("h", kc, o), "rstd", "gvec"], writes=[("h", kc, o)])
                for tt in range(n // 128):
                    for q4 in range(2):
                        pt = psT[ti % 2]
                        for kk in range(4):
                            kc = q4 * 4 + kk
                            P.op("pe", "transpose", out=pt[:, kk * 128:(kk + 1) * 128], in_=hT[:, kc, o + tt * 128:o + (tt + 1) * 128],
                                 identity=cst["ident"][:], reads=[("h", kc, o), "ident"], writes=[("psT", ti % 2)])
                        ot = tmp[ti % 2]
                        P.op("act" if ti % 2 else "dve", "activation" if ti % 2 else "tensor_copy", out=ot[:], in_=pt[:],
                             reads=[("psT", ti % 2)], writes=[("tmp", ti % 2)], **({"func": AF.Copy} if ti % 2 else {}))
                        r0 = l0 + o + tt * 128
                        P.dma(dq.next(), out[r0:r0 + 128, q4 * 512:(q4 + 1) * 512], ot[:], reads=[("tmp", ti % 2)],
                              writes=[("out", r0, q4)])
                        ti += 1
    outs = [k for k in P.lastw if isinstance(k, tuple) and k[0] == "out"]
    P.finish_wait("sp", outs)
    P.emit()
    return nc, P


def _c(a):
    return np.ascontiguousarray(a, dtype=np.float32)


def B_inmaps(l, last, h_lat, h_ctx, ysl, ysc, inp):
    maps = []
    shared = {
        "mod_w": _c(inp["mod_w"][l]), "mod_b": _c(inp["mod_b"][l][None]), "n1g": _c(inp["norm1_g"][l][None]),
        "n2g": _c(inp["norm2_g"][l][None]), "fng": _c(inp["final_norm_g"][None]),
        "wg": _c(inp["in_w"][l][:, GATE_OFF:]), "gate_b": _c(inp["gate_b"][l][None]), "wbr": _c(inp["branch_w"][l]),
        "wo": _c(inp["out_w"][l]), "w1": _c(inp["ffn_w1"][l]), "w3": _c(inp["ffn_w3"][l]), "w2": _c(inp["ffn_w2"][l]),
    }
    for core in range(8):
        b, j = core // 4, core % 4
        hl = h_lat[b, j * 2048:(j + 1) * 2048]
        yl = ysl[b, j * 2048:(j + 1) * 2048]
        if not last:
            hl = np.concatenate([hl, h_ctx[b, j * 64:(j + 1) * 64]], 0)
            yl = np.concatenate([yl, ysc[b, j * 64:(j + 1) * 64]], 0)
        m = dict(shared)
        m["hT"] = _c(hl.T)
        m["ysT"] = _c(yl.T)
        m["cvec"] = _c(np.stack([inp["c"][b], inp["c_ctx"]], 0))
        maps.append(m)
    return maps


def B_gather(last, outs):
    if last:
        return np.stack([np.concatenate(outs[0:4], 0), np.concatenate(outs[4:8], 0)], 0), None
    hl = np.stack([np.concatenate([o[:, :2048].T for o in outs[b * 4:(b + 1) * 4]], 0) for b in range(2)], 0)
    hc = np.stack([np.concatenate([o[:, 2048:].T for o in outs[b * 4:(b + 1) * 4]], 0) for b in range(2)], 0)
    return hl, hc


A_SLOTS = ["ret_q", "ret_qs", "ret_k", "ret_ks", "ret_v", "ret_g", "lru_x", "lru_y", "gdn_q", "gdn_k", "gdn_v", "gdn_z",
           "gdn_ab", "rw_r", "rw_k", "rw_v", "rw_gd", "rw_wd", "rw_ad"]
A_OUTS = [(n, n, 0, 128) for n in A_SLOTS if n not in ("gdn_ab", "rw_wd", "rw_ad")] + [
    ("gdn_af", "gdn_ab", 0, 1), ("gdn_abk", "gdn_ab", 1, 1), ("gdn_bf", "gdn_ab", 2, 1), ("gdn_bb", "gdn_ab", 3, 1),
    ("rw_wdf", "rw_wd", 0, 64), ("rw_wdb", "rw_wd", 64, 64), ("rw_adf", "rw_ad", 0, 64), ("rw_adb", "rw_ad", 64, 64)]
NSLOT = len(A_SLOTS)
A_BLOCKS = [(0, 256, 1)] + [(256 + i * 512, 512, 0) for i in range(16)]
GELU_C = 1.5957691216057308


def _a1_inproj(P, cst, PS, io, PT, outs_enabled):
    hT_in, wA, cvec, mod_w, mod_b, n1g = io["hT"], io["wA"], io["cvec"], io["mod_w"], io["mod_b"], io["n1g"]
    m0 = P.mark()
    wAb = P.carve([128, 8, NSLOT * 128], BF16)
    modsb = P.carve([128, 16, 2], F32)
    GS = P.carve([128, 8, 2, 2], F32)
    gv = P.carve([128, 8], F32)
    m1 = P.mark()
    wst = [P.carve([128, 4096], F32) for _ in range(2)]
    P.dma("sp", gv, n1g[0, :].rearrange("(k p) -> p k", p=128), writes=["gv"], allow_slow_non_contiguous=True)
    _mod_vectors(P, cst, cvec, mod_w, mod_b, 16, wst, PS[7][:, 0:96].rearrange("p (j s) -> p j s", s=2), modsb)
    for s in range(2):
        P.op("dve", "scalar_tensor_tensor", out=GS[:, :, s, 0], in0=modsb[:, 8:16, s], scalar=1.0, in1=gv, op0=ALU.add,
             op1=ALU.mult, reads=["modsb", "gv"], writes=["GS"])
        P.op("dve", "tensor_copy", out=GS[:, :, s, 1], in_=modsb[:, 0:8, s], reads=["modsb"], writes=["GS"])
    for kc in range(8):
        st = wst[kc % 2]
        P.dma("sp" if kc % 2 == 0 else "pool", st[:, 0:NSLOT * 128], wA[kc * 128:(kc + 1) * 128, :], writes=[("wst", kc % 2)])
        P.op("dve" if kc % 2 == 0 else "pool", "tensor_copy", out=wAb[:, kc, :], in_=st[:, 0:NSLOT * 128],
             reads=[("wst", kc % 2)], writes=[("wAb", kc)])
    P.release(m1)
    hblk = [P.carve([128, 8, 512], F32) for _ in range(2)]
    xu = [P.carve([128, 8, 512], BF16) for _ in range(2)]
    sq = P.carve([128, 8, 512], BF16)
    rstd = P.carve([128, 512], F32)
    tmp = [P.carve([128, 512], F32) for _ in range(2)]
    stg = [P.carve([128, 512], F32) for _ in range(4)]
    oi = 0
    for bi, (t0, n, s) in enumerate(A_BLOCKS):
        hb_, xb = hblk[bi % 2], xu[bi % 2]
        for kc in range(8):
            P.dma("sp" if kc % 2 == 0 else "pool", hb_[:, kc, 0:n], hT_in[kc * 128:(kc + 1) * 128, t0:t0 + n],
                  writes=[("hb", bi % 2, kc)])
        _rms_stats(P, cst, hb_, 8, 0, n, lambda kc: ("hb", bi % 2, kc), sq, PS[6], rstd, "")
        for kc in range(8):
            t = tmp[kc % 2]
            P.op("dve", "tensor_tensor", out=t[:, 0:n], in0=hb_[:, kc, 0:n], in1=rstd[:, 0:n], op=ALU.mult,
                 reads=[("hb", bi % 2, kc), "rstd"], writes=[("tmp", kc % 2)])
            P.op("pool", "tensor_scalar", out=xb[:, kc, 0:n], in0=t[:, 0:n], scalar1=GS[:, kc, s, 0:1], scalar2=GS[:, kc, s, 1:2],
                 op0=ALU.mult, op1=ALU.add, reads=[("tmp", kc % 2), "GS"], writes=[("xu", bi % 2, kc)])
        for (name, slot, c0, M) in A_OUTS:
            if name not in outs_enabled:
                continue
            col = A_SLOTS.index(slot) * 128 + c0
            ps = PS[oi % 4]
            for kc in range(8):
                P.op("pe", "matmul", out=ps[0:M, 0:n], lhsT=wAb[:, kc, col:col + M], rhs=xb[:, kc, 0:n], start=(kc == 0),
                     stop=(kc == 7), reads=[("wAb", kc), ("xu", bi % 2, kc)], writes=[("PS", oi % 4)])
            sg = stg[oi % 4]
            if oi % 2 == 0:
                P.op("act", "activation", out=sg[0:M, 0:n], in_=ps[0:M, 0:n], func=AF.Copy, reads=[("PS", oi % 4)],
                     writes=[("stg", oi % 4)])
            else:
                P.op("dve", "tensor_copy", out=sg[0:M, 0:n], in_=ps[0:M, 0:n], reads=[("PS", oi % 4)], writes=[("stg", oi % 4)])
            P.dma("sp" if oi % 2 == 0 else "pool", PT[name][:, t0:t0 + n], sg[0:M, 0:n], reads=[("stg", oi % 4)],
                  writes=[("PT", name, bi)])
            oi += 1
    ptw = {k: list(v) for k, v in P.lastw.items() if isinstance(k, tuple) and k[0] == "PT"}
    P.release(m0)


def _lru(P, cst, PS, io, PT, ysT):
    m0 = P.mark()
    cw = P.carve([128, 4], F32)
    cb = P.carve([128, 1], F32)
    gw = P.carve([128, 4, 128], F32)
    gb = P.carve([128, 4], F32)
    lam = P.carve([128, 2], F32)
    L8 = P.carve([128, 2], F32)
    L16 = P.carve([128, 2], F32)
    onec = P.carve([128, 1], F32)
    hbk = P.carve([128, TTOT], F32)
    P.op("pool", "memset", ap=onec, constant=1.0, writes=["onec"])
    P.dma("sp", cw, io["lru_cw"].rearrange("j c -> c j"), writes=["cw"], allow_slow_non_contiguous=True)
    P.dma("sp", cb, io["lru_cb"].rearrange("o c -> c o"), writes=["cb"], allow_slow_non_contiguous=True)
    P.dma("sp", gw, io["lru_gw"].rearrange("g c z -> c g z"), writes=["gw"])
    P.dma("sp", gb, io["lru_gb"].rearrange("g z -> z g"), writes=["gb"], allow_slow_non_contiguous=True)
    P.dma("sp", lam, io["lru_lam"].rearrange("d z -> z d"), writes=["lam"], allow_slow_non_contiguous=True)
    P.op("act", "activation", out=L8, in_=lam, func=AF.Exp, scale=-1.0, reads=["lam"], writes=["L8"])
    P.op("act", "activation", out=L8, in_=L8, func=AF.Ln, bias=onec, reads=["L8", "onec"], writes=["L8"])
    P.op("dve", "tensor_scalar", out=L16, in0=L8, scalar1=-16.0, scalar2=None, op0=ALU.mult, reads=["L8"], writes=["L16"])
    P.op("dve", "tensor_scalar", out=L8, in0=L8, scalar1=-8.0, scalar2=None, op0=ALU.mult, reads=["L8"], writes=["L8"])
    xh = [P.carve([128, 516], F32) for _ in range(2)]
    xc = [P.carve([128, 512], F32) for _ in range(2)]
    yb = [P.carve([128, 512], F32) for _ in range(2)]
    gr = P.carve([128, 512], F32)
    gi = P.carve([128, 512], F32)
    av = P.carve([128, 512], F32)
    bx = P.carve([128, 512], F32)
    hf = [P.carve([128, 512], F32) for _ in range(2)]
    ot = [P.carve([128, 512], F32) for _ in range(2)]
    it = 0
    for d in (1, 0):
        order = [A_BLOCKS[0]] + (A_BLOCKS[:0:-1] if d == 1 else A_BLOCKS[1:])
        prev = None
        for bi, (t0, n, s) in enumerate(order):
            k2 = it % 2
            it += 1
            x_, c_ = xh[k2], xc[k2]
            seg0, seg1 = (0, 256) if s else (256, TTOT)
            lo, hi = max(t0 - 2, seg0), min(t0 + n + 1, seg1)
            if lo > t0 - 2 or hi < t0 + n + 1:
                P.op("pool", "memset", ap=x_[:, 0:n + 3], constant=0.0, writes=[("xh", k2)])
            P.dma("sp", x_[:, lo - (t0 - 2):hi - (t0 - 2)], PT["lru_x"][:, lo:hi], writes=[("xh", k2)])
            P.op("dve", "tensor_scalar", out=c_[:, 0:n], in0=x_[:, 0:n], scalar1=cw[:, 0:1], scalar2=cb[:, 0:1], op0=ALU.mult,
                 op1=ALU.add, reads=[("xh", k2), "cw", "cb"], writes=[("xc", k2)])
            for j in range(1, 4):
                P.op("dve", "scalar_tensor_tensor", out=c_[:, 0:n], in0=x_[:, j:j + n], scalar=cw[:, j:j + 1], in1=c_[:, 0:n],
                     op0=ALU.mult, op1=ALU.add, reads=[("xh", k2), "cw", ("xc", k2)], writes=[("xc", k2)])
            for g, gt in ((0, gr), (1, gi)):
                ps = PS[g]
                P.op("pe", "matmul", out=ps[:, 0:n], lhsT=gw[:, d * 2 + g, :], rhs=c_[:, 0:n], start=True, stop=True,
                     reads=["gw", ("xc", k2)], writes=[("PS", g)])
                P.op("act", "activation", out=gt[:, 0:n], in_=ps[:, 0:n], func=AF.Sigmoid, bias=gb[:, d * 2 + g:d * 2 + g + 1],
                     reads=[("PS", g), "gb"], writes=[("g", g)])
            P.op("act", "activation", out=av[:, 0:n], in_=gr[:, 0:n], func=AF.Exp, scale=L8[:, d:d + 1], reads=[("g", 0), "L8"],
                 writes=["av"])
            P.op("act", "activation", out=gr[:, 0:n], in_=gr[:, 0:n], func=AF.Exp, scale=L16[:, d:d + 1], reads=[("g", 0), "L16"],
                 writes=[("g", 0)])
            P.op("dve", "tensor_scalar", out=gr[:, 0:n], in0=gr[:, 0:n], scalar1=-1.0, scalar2=1.0, op0=ALU.mult, op1=ALU.add,
                 reads=[("g", 0)], writes=[("g", 0)])
            P.op("act", "activation", out=gr[:, 0:n], in_=gr[:, 0:n], func=AF.Sqrt, reads=[("g", 0)], writes=[("g", 0)])
            P.op("pool", "tensor_tensor", out=gi[:, 0:n], in0=gi[:, 0:n], in1=c_[:, 0:n], op=ALU.mult, reads=[("g", 1), ("xc", k2)],
                 writes=[("g", 1)])
            P.op("dve", "tensor_tensor", out=bx[:, 0:n], in0=gi[:, 0:n], in1=gr[:, 0:n], op=ALU.mult, reads=[("g", 1), ("g", 0)],
                 writes=["bx"])
            if d == 1:
                init = 0.0 if prev is None else hbk[:, prev:prev + 1]
                P.op("dve", "tensor_tensor_scan", out=hbk[:, t0:t0 + n][:, ::-1], data0=av[:, 0:n][:, ::-1], data1=bx[:, 0:n][:, ::-1],
                     initial=init, op0=ALU.mult, op1=ALU.add, reads=["av", "bx", "hbk_c"], writes=[("hbk", t0), "hbk_c"])
                prev = t0
            else:
                h_ = hf[k2]
                init = 0.0 if prev is None else hf[1 - k2][:, prev - 1:prev]
                P.op("dve", "tensor_tensor_scan", out=h_[:, 0:n], data0=av[:, 0:n], data1=bx[:, 0:n], initial=init, op0=ALU.mult,
                     op1=ALU.add, reads=["av", "bx", ("hf", 1 - k2)], writes=[("hf", k2)])
                prev = n
                y_ = yb[k2]
                P.dma("pool", y_[:, 0:n], PT["lru_y"][:, t0:t0 + n], writes=[("yb", k2)])
                o_ = ot[k2]
                P.op("pool", "tensor_tensor", out=o_[:, 0:n], in0=y_[:, 0:n], in1=y_[:, 0:n], op=ALU.mult, reads=[("yb", k2)],
                     writes=[("ot", k2)])
                P.op("pool", "tensor_scalar", out=o_[:, 0:n], in0=o_[:, 0:n], scalar1=0.044715, scalar2=1.0, op0=ALU.mult,
                     op1=ALU.add, reads=[("ot", k2)], writes=[("ot", k2)])
                P.op("pool", "tensor_tensor", out=o_[:, 0:n], in0=o_[:, 0:n], in1=y_[:, 0:n], op=ALU.mult, reads=[("ot", k2), ("yb", k2)],
                     writes=[("ot", k2)])
                P.op("act", "activation", out=o_[:, 0:n], in_=o_[:, 0:n], func=AF.Sigmoid, scale=GELU_C, reads=[("ot", k2)],
                     writes=[("ot", k2)])
                P.op("pool", "tensor_tensor", out=o_[:, 0:n], in0=o_[:, 0:n], in1=y_[:, 0:n], op=ALU.mult, reads=[("ot", k2), ("yb", k2)],
                     writes=[("ot", k2)])
                P.op("dve", "tensor_tensor", out=y_[:, 0:n], in0=h_[:, 0:n], in1=hbk[:, t0:t0 + n], op=ALU.add,
                     reads=[("hf", k2), ("hbk", t0)], writes=[("yb", k2)])
                P.op("dve", "tensor_tensor", out=o_[:, 0:n], in0=o_[:, 0:n], in1=y_[:, 0:n], op=ALU.mult, reads=[("ot", k2), ("yb", k2)],
                     writes=[("ot", k2)])
                P.dma("sp", ysT[128:256, t0:t0 + n], o_[:, 0:n], reads=[("ot", k2)], writes=[("ysT", 1, t0)])
    P.release(m0)


def build_A(enabled=("inproj", "lru", "ret", "gdn", "rwkv")):
    nc = bass.Bass("TRN2", target_bir_lowering=False)
    io = {}

    def inp(name, shape):
        io[name] = nc.dram_tensor(name, list(shape), F32, kind="ExternalInput").ap()
    inp("hT", [D, TTOT]); inp("wA", [D, NSLOT * 128]); inp("cvec", [2, D]); inp("mod_w", [D, 6 * D]); inp("mod_b", [1, 6 * D])
    inp("n1g", [1, D])
    inp("lru_cw", [4, 128]); inp("lru_cb", [1, 128]); inp("lru_gw", [4, 128, 128]); inp("lru_gb", [4, 128]); inp("lru_lam", [2, 128])
    inp("cA", [128, CA_COLS]); inp("ropeC", [128, TTOT]); inp("ropeS", [128, TTOT]); inp("ret_de", [1, 2])
    inp("gdn_cw", [3, 4, 128]); inp("gdn_sc", [1, 4]); inp("gdn_ng", [1, 128])
    inp("rw_pc", [13, 128]); inp("rw_pc64", [4, 64]); inp("rw_w2", [2, 64, 128]); inp("rw_a2", [2, 64, 128]); inp("rw_g2", [128, 128])
    OB = {m: nc.dram_tensor("ob_" + m, [TTOT, 128], F32, kind="Internal").ap() for m in ("ret", "gdn", "rwkv")}
    ysT = nc.dram_tensor("ysT", [512, TTOT], F32, kind="ExternalOutput").ap()
    ptkind = "ExternalOutput" if "debug_pt" in enabled else "Internal"
    PT = {n: nc.dram_tensor("pt_" + n, [M, TTOT], F32, kind=ptkind).ap() for (n, _, _, M) in A_OUTS}
    P = Prog(nc)
    cst = _common_consts(P)
    PS = [P.ps(f"bank{i}", [128, 512], F32) for i in range(8)]
    P.arena_init(48000)
    need = set()
    if "lru" in enabled:
        need |= {"lru_x", "lru_y"}
    if "ret" in enabled:
        need |= {"ret_q", "ret_qs", "ret_k", "ret_ks", "ret_v", "ret_g"}
    if "gdn" in enabled:
        need |= {"gdn_q", "gdn_k", "gdn_v", "gdn_z", "gdn_af", "gdn_abk", "gdn_bf", "gdn_bb"}
    if "rwkv" in enabled:
        need |= {"rw_r", "rw_k", "rw_v", "rw_gd", "rw_wdf", "rw_wdb", "rw_adf", "rw_adb"}
    if "debug_pt" in enabled:
        need = set(n for (n, _, _, _) in A_OUTS)
    _a1_inproj(P, cst, PS, io, PT, need)
    if "lru" in enabled:
        _lru(P, cst, PS, io, PT, ysT)
    if "ret" in enabled:
        _ret(P, cst, PS, io, PT, ysT, OB["ret"])
    if "gdn" in enabled:
        _gdn(P, cst, PS, io, PT, ysT, OB["gdn"])
    if "rwkv" in enabled:
        _rwkv(P, cst, PS, io, PT, ysT, OB["rwkv"])
    P.barrier()
    P.emit()
    return nc, P


def A_weight_cols(h):
    def rng(a, n=128):
        return list(range(a, a + n))
    cols = {}
    cols["ret_q"] = rng(0 + h * 128)
    cols["ret_qs"] = rng(h * 128 + 64, 64) + rng(h * 128, 64)
    cols["ret_k"] = rng(512 + h * 128)
    cols["ret_ks"] = rng(512 + h * 128 + 64, 64) + rng(512 + h * 128, 64)
    cols["ret_v"] = rng(1024 + h * 128)
    cols["ret_g"] = rng(1536 + h * 128)
    cols["lru_x"] = rng(2048 + h * 128)
    cols["lru_y"] = rng(2560 + h * 128)
    cols["gdn_q"] = rng(3072 + h * 128)
    cols["gdn_k"] = rng(3584 + h * 128)
    cols["gdn_v"] = rng(4096 + h * 128)
    cols["gdn_z"] = rng(4608 + h * 128)
    cols["gdn_ab"] = [5120 + h, 5120 + 4 + h, 5128 + h, 5128 + 4 + h] + [-1] * 124
    R0 = 5136
    cols["rw_r"] = rng(R0 + h * 128)
    cols["rw_k"] = rng(R0 + 512 + h * 128)
    cols["rw_v"] = rng(R0 + 1024 + h * 128)
    cols["rw_gd"] = rng(R0 + 1536)
    cols["rw_wd"] = rng(R0 + 1664)
    cols["rw_ad"] = rng(R0 + 1792)
    idx = []
    for s in A_SLOTS:
        idx += cols[s]
    return np.array(idx)


def A_inmaps(l, h_lat, h_ctx, inp):
    maps = []
    in_w = np.concatenate([inp["in_w"][l], np.zeros((D, 1), np.float32)], 1)
    for core in range(8):
        b, h = core // 4, core % 4
        m = {}
        m["hT"] = _c(np.concatenate([h_ctx[b], h_lat[b]], 0).T)
        m["wA"] = _c(in_w[:, A_weight_cols(h)])
        m["cvec"] = _c(np.stack([inp["c"][b], inp["c_ctx"]], 0))
        m["mod_w"] = _c(inp["mod_w"][l]); m["mod_b"] = _c(inp["mod_b"][l][None]); m["n1g"] = _c(inp["norm1_g"][l][None])
        sl = slice(h * 128, (h + 1) * 128)
        m["lru_cw"] = _c(inp["lru_conv_w"][l][:, sl]); m["lru_cb"] = _c(inp["lru_conv_b"][l][None, sl])
        m["lru_gw"] = _c(inp["lru_gate_w"][l][:, :, h].reshape(4, 128, 128))
        m["lru_gb"] = _c(inp["lru_gate_b"][l][:, :, sl].reshape(4, 128)); m["lru_lam"] = _c(inp["lru_lambda"][l][:, sl])
        m["cA"] = make_cA(); m["ropeC"], m["ropeS"] = make_rope()
        m["ret_de"] = _c(inp["ret_decay_exp"][l][:, h][None])
        gcw = inp["gdn_conv_w"][l]
        m["gdn_cw"] = _c(np.stack([gcw[:, i * 512 + h * 128:i * 512 + (h + 1) * 128] for i in range(3)], 0))
        m["gdn_sc"] = _c(np.concatenate([inp["gdn_a_log"][l][:, h], inp["gdn_dt_bias"][l][:, h]])[None])
        m["gdn_ng"] = _c(inp["gdn_norm_g"][l][None])
        mu = inp["rwkv_mu"][l]
        m["rw_pc"] = _c(np.stack([mu[0:512][sl], mu[512:1024][sl], mu[1024:1536][sl], mu[1536:1664], inp["rwkv_w0"][l][0][sl],
                                  inp["rwkv_w0"][l][1][sl], inp["rwkv_a0"][l][0][sl], inp["rwkv_a0"][l][1][sl], inp["rwkv_k_k"][l][sl],
                                  inp["rwkv_k_a"][l][sl], inp["rwkv_r_k"][l].reshape(512)[sl], inp["rwkv_ln_g"][l][sl],
                                  inp["rwkv_ln_b"][l][sl]], 0))
        m["rw_pc64"] = _c(np.stack([mu[1664:1728], mu[1728:1792], mu[1792:1856], mu[1856:1920]], 0))
        m["rw_w2"] = _c(inp["rwkv_w2"][l][:, :, sl]); m["rw_a2"] = _c(inp["rwkv_a2"][l][:, :, sl]); m["rw_g2"] = _c(inp["rwkv_g2"][l][:, sl])
        maps.append(m)
    return maps


CH = 64


class Core:
    def __init__(self, P, cst, PS, NH, has_delta, identC):
        self.P, self.cst, self.PS, self.NH, self.hd, self.delta = P, cst, PS, NH, 128 // NH, has_delta
        self.identC = identC
        self.W = NH * CH
        self.G = 512 // self.W
        self.sets = []
        for i in range(2):
            d = {}
            for nm in (("Pm", "Nm", "Pm2", "Nm2", "Xa", "Aak", "Aqb", "Aqk") if has_delta else ("Aqk",)):
                d[nm] = P.carve([CH, 512], F32)
            self.sets.append(d)
        self.xu = [{nm: P.carve([CH, 128], F32) for nm in ("X", "U")} for _ in range(2)]
        self.kb = 0
        self.kc = 0

    def _mm(self, out, otok, lhsT, ltok, rhs, rtok, start=True, stop=True):
        self.P.op("pe", "matmul", out=out, lhsT=lhsT, rhs=rhs, start=start, stop=stop, reads=[ltok, rtok], writes=[otok])

    def _L(self, w, nm, h):
        return (w[nm][0], w[nm][1]) if self.NH == 1 else w[nm + "m"][h]

    def prep_gen(self, kb, ws, MexT, MexTok, MinT, MinTok):
        P, PS, NH, W = self.P, self.PS, self.NH, self.W
        T = self.sets[kb % 2]
        tk = lambda nm: ("core", nm, kb % 2)
        G = len(ws)
        GW = G * W
        MexTok = list(MexTok) if isinstance(MexTok, list) else [MexTok]
        MinTok = list(MinTok) if isinstance(MinTok, list) else [MinTok]
        bA, bB = PS[2], PS[3]
        tA, tB = ("bank", 2), ("bank", 3)
        rA, rB = bA[0:CH, 0:GW], bB[0:CH, 0:GW]
        cols = [(g, h, slice(g * W + h * CH, g * W + (h + 1) * CH)) for g in range(G) for h in range(NH)]
        mm = self._mm
        if self.delta:
            for (g, h, c) in cols:
                l, lt = self._L(ws[g], "bT", h)
                mm(rA[:, c], tA, l, lt, ws[g]["aT"][0], ws[g]["aT"][1])
            P.op("dve", "tensor_tensor", out=T["Pm"][:, 0:GW], in0=rA, in1=MexT, op=ALU.mult, reads=MexTok, writes=[tk("Pm"), tA])
            yield
            for (g, h, c) in cols:
                P.op("pe", "transpose", out=rB[:, c], in_=T["Pm"][:, c], identity=self.cst["ident"][0:CH, 0:CH],
                     reads=[tk("Pm"), "ident"], writes=[tB])
            P.op("act", "activation", out=T["Nm"][:, 0:GW], in_=rB, func=AF.Copy, reads=[], writes=[tk("Nm"), tB])
            P.op("pool", "tensor_tensor", out=T["Xa"][:, 0:GW], in0=T["Pm"][:, 0:GW], in1=self.identC[:, 0:GW], op=ALU.add,
                 reads=[tk("Pm"), "cAs"], writes=[tk("Xa")])
            yield
            Pm, Nm, Pm2, Nm2 = "Pm", "Nm", "Pm2", "Nm2"
            for lvl in range(1, 6):
                if lvl < 5:
                    for (g, h, c) in cols:
                        mm(rA[:, c], tA, T[Nm][:, c], tk(Nm), T[Pm][:, c], tk(Pm))
                    P.op("act", "activation", out=T[Pm2][:, 0:GW], in_=rA, func=AF.Copy, reads=[], writes=[tk(Pm2), tA])
                for (g, h, c) in cols:
                    mm(rB[:, c], tB, T[Pm][:, c], tk(Pm), T[Nm][:, c], tk(Nm))
                P.op("dve", "tensor_copy", out=T[Nm2][:, 0:GW], in_=rB, reads=[], writes=[tk(Nm2), tB])
                yield
                for (g, h, c) in cols:
                    mm(rA[:, c], tA, T[Nm2][:, c], tk(Nm2), T["Xa"][:, c], tk("Xa"))
                P.op("dve", "tensor_tensor", out=T["Xa"][:, 0:GW], in0=rA, in1=T["Xa"][:, 0:GW], op=ALU.add, reads=[],
                     writes=[tk("Xa"), tA])
                yield
                Pm, Pm2 = Pm2, Pm
                Nm, Nm2 = Nm2, Nm
            for (g, h, c) in cols:
                l, lt = self._L(ws[g], "kT", h)
                mm(rB[:, c], tB, l, lt, ws[g]["aT"][0], ws[g]["aT"][1])
            P.op("dve", "tensor_tensor", out=T["Aak"][:, 0:GW], in0=rB, in1=MexT, op=ALU.mult, reads=MexTok, writes=[tk("Aak"), tB])
            yield
            for (g, h, c) in cols:
                l, lt = self._L(ws[g], "bT", h)
                mm(rA[:, c], tA, l, lt, ws[g]["qT"][0], ws[g]["qT"][1])
            P.op("dve", "tensor_tensor", out=T["Aqb"][:, 0:GW], in0=rA, in1=MinT, op=ALU.mult, reads=MinTok, writes=[tk("Aqb"), tA])
            yield
        for (g, h, c) in cols:
            l, lt = self._L(ws[g], "kT", h)
            mm(rB[:, c], tB, l, lt, ws[g]["qT"][0], ws[g]["qT"][1])
        P.op("dve", "tensor_tensor", out=T["Aqk"][:, 0:GW], in0=rB, in1=MinT, op=ALU.mult, reads=MinTok, writes=[tk("Aqk"), tB])
        yield

    def recur_gen(self, kb, g, w, S, Stok, O, Otok):
        P, PS, NH, hd, W = self.P, self.PS, self.NH, self.hd, self.W
        T = self.sets[kb % 2]
        tk = lambda nm: ("core", nm, kb % 2)
        kc = self.kc
        self.kc += 1
        XU = self.xu[kc % 2]
        xt = lambda nm: ("coreXU", nm, kc % 2)
        bC, bD, bE = PS[4], PS[5], PS[6]
        tC, tD, tE = ("bank", 4), ("bank", 5), ("bank", 6)
        hsl = [slice(h * hd, (h + 1) * hd) for h in range(NH)]
        csl = [slice(g * W + h * CH, g * W + (h + 1) * CH) for h in range(NH)]
        xr, ur = bC[0:CH, 0:128], bC[0:CH, 128:256]
        orr = bD[0:CH, 0:128]
        sr = bE[:, 0:hd]
        mm = self._mm
        A = lambda nm: w[nm][0]
        Tk = lambda nm: w[nm][1]
        if self.delta:
            for h in range(NH):
                l, lt = self._L(w, "aTs", h)
                mm(xr[:, hsl[h]], tC, l, lt, S, Stok, True, False)
                mm(xr[:, hsl[h]], tC, T["Aak"][:, csl[h]], tk("Aak"), A("V")[:, hsl[h]], Tk("V"), False, True)
            P.op("act", "activation", out=XU["X"], in_=xr, func=AF.Copy, reads=[], writes=[xt("X"), tC])
            yield
            for h in range(NH):
                mm(ur[:, hsl[h]], tC, T["Xa"][:, csl[h]], tk("Xa"), XU["X"][:, hsl[h]], xt("X"))
            P.op("act", "activation", out=XU["U"], in_=ur, func=AF.Copy, reads=[], writes=[xt("U"), tC])
            yield
        for h in range(NH):
            if self.delta:
                mm(sr[hsl[h], :], tE, A("Bst")[:, hsl[h]], Tk("Bst"), XU["U"][:, hsl[h]], xt("U"), True, False)
            mm(sr[hsl[h], :], tE, A("Kst")[:, hsl[h]], Tk("Kst"), A("V")[:, hsl[h]], Tk("V"), not self.delta, True)
        for h in range(NH):
            l, lt = self._L(w, "qTs", h)
            mm(orr[:, hsl[h]], tD, l, lt, S, Stok, True, False)
            if self.delta:
                mm(orr[:, hsl[h]], tD, T["Aqb"][:, csl[h]], tk("Aqb"), XU["U"][:, hsl[h]], xt("U"), False, False)
            mm(orr[:, hsl[h]], tD, T["Aqk"][:, csl[h]], tk("Aqk"), A("V")[:, hsl[h]], Tk("V"), False, True)
        P.op("dve", "scalar_tensor_tensor", out=S, in0=S, scalar=A("cs"), in1=sr, op0=ALU.mult, op1=ALU.add,
             reads=[Tk("cs")], writes=[Stok, tE])
        P.op("act", "activation", out=O, in_=orr, func=AF.Copy, reads=[], writes=[Otok, tD])
        yield


def interleave(a, b):
    live = [g for g in (a, b) if g is not None]
    while live:
        for g in list(live):
            try:
                next(g)
            except StopIteration:
                live.remove(g)


RW_BLOCKS = [(0, 256, 1)] + [(256 + i * 256, 256, 0) for i in range(32)]


def _dir_chunks(d, blocks=None):
    blocks = A_BLOCKS if blocks is None else blocks
    order = [blocks[0]] + (blocks[:0:-1] if d == 1 else blocks[1:])
    res = []
    for (t0, n, s) in order:
        offs = list(range(0, n, CH))
        if d == 1:
            offs = offs[::-1]
        res.append((t0, n, s, offs))
    return res


CA = {}
_o = 0
for _nm, _n in (("RELF", 64), ("RELB", 64), ("MASKF", 64), ("MASKB", 64), ("MSTRF", 64), ("MSTRB", 64), ("POS1F", 512), ("POS1B", 512),
                ("KDF", 512), ("KDB", 512), ("RSTF", 512), ("RSTB", 512), ("NEGINF", 64), ("NEGINB", 64), ("NEGEXF", 64),
                ("NEGEXB", 64), ("IDC", 512), ("BLK", 128), ("MSTRF4", 512), ("MSTRB4", 512), ("MASKF4", 512), ("MASKB4", 512)):
    CA[_nm] = (_o, _n)
    _o += _n
CA_COLS = _o


def make_cA():
    c = np.zeros((128, CA_COLS), np.float32)
    s = np.arange(64)[:, None]
    t = np.arange(64)[None, :]
    def put(nm, a):
        o, n = CA[nm]
        c[:a.shape[0], o:o + n] = a
    put("RELF", np.where(s <= t, t - s, 0)); put("RELB", np.where(s >= t, s - t, 0))
    put("MASKF", (s <= t) * 1.0); put("MASKB", (s >= t) * 1.0)
    put("MSTRF", (s < t) * 1.0); put("MSTRB", (s > t) * 1.0)
    tt = np.arange(512)[None, :] % 64
    put("POS1F", np.broadcast_to(tt + 1.0, (128, 512))); put("POS1B", np.broadcast_to(64.0 - tt, (128, 512)))
    put("KDF", np.broadcast_to(63.0 - tt, (128, 512))); put("KDB", np.broadcast_to(tt * 1.0, (128, 512)))
    put("RSTF", np.broadcast_to((tt != 0) * 1.0, (128, 512))); put("RSTB", np.broadcast_to((tt != 63) * 1.0, (128, 512)))
    NEG = -30000.0
    put("NEGINF", np.where(s <= t, 0.0, NEG)); put("NEGINB", np.where(s >= t, 0.0, NEG))
    put("NEGEXF", np.where(s < t, 0.0, NEG)); put("NEGEXB", np.where(s > t, 0.0, NEG))
    put("IDC", np.concatenate([np.eye(64)] * 8, 1))
    put("MSTRF4", np.concatenate([(s < t) * 1.0] * 8, 1)); put("MSTRB4", np.concatenate([(s > t) * 1.0] * 8, 1))
    put("MASKF4", np.concatenate([(s <= t) * 1.0] * 8, 1)); put("MASKB4", np.concatenate([(s >= t) * 1.0] * 8, 1))
    blk = np.zeros((128, 128)); blk[:64, :64] = 1; blk[64:, 64:] = 1
    put("BLK", blk)
    return c


def make_rope():
    tpos = np.arange(SEQ)
    row = (tpos // 64).astype(np.float32)
    col = (tpos % 64).astype(np.float32)
    inv = (10000.0 ** (-np.arange(32, dtype=np.float32) / 32)).astype(np.float32)
    ang = np.concatenate([row[:, None] * inv, col[:, None] * inv], -1)
    cos, sin = np.cos(ang).astype(np.float32), np.sin(ang).astype(np.float32)
    CC = np.ones((128, TTOT), np.float32)
    SS = np.zeros((128, TTOT), np.float32)
    CC[:64, CTX:] = cos.T; CC[64:, CTX:] = cos.T
    SS[:64, CTX:] = -sin.T; SS[64:, CTX:] = sin.T
    return CC, SS


def _ld_const(P, io, cAs, nm, rows=128):
    return cAs[nm][0:rows, :]


def _load_consts(P, io, names):
    tot = sum(CA[nm][1] for nm in names)
    tile = P.carve([128, tot], F32)
    res = {}
    items = []
    o = 0
    for i, nm in enumerate(names):
        so, n = CA[nm]
        res[nm] = tile[:, o:o + n]
        items.append(("sp" if i % 2 == 0 else "pool", tile[:, o:o + n], io["cA"][:, so:so + n]))
        o += n
    P.dma_group(items, writes=["cAs"])
    return res


def _ret(P, cst, PS, io, PT, ysT, OB):
    m0 = P.mark()
    cAs = _load_consts(P, io, ("RELF", "RELB", "MASKF", "MASKB", "POS1F", "POS1B", "KDF", "KDB"))
    lg = P.carve([128, 2], F32)
    onec = P.carve([128, 1], F32)
    P.op("pool", "memset", ap=onec, constant=1.0, writes=["onec"])
    P.dma("sp", lg, io["ret_de"].partition_broadcast(128), writes=["lg"])
    P.op("act", "activation", out=lg, in_=lg, func=AF.Exp, scale=-float(np.log(2.0)), reads=["lg"], writes=["lg"])
    P.op("act", "activation", out=lg, in_=lg, func=AF.Ln, scale=-1.0, bias=onec, reads=["lg", "onec"], writes=["lg"])
    MinT = [P.carve([CH, CH], F32) for _ in range(2)]
    POSQ = [P.carve([128, 512], F32) for _ in range(2)]
    KDEC = [P.carve([128, 512], F32) for _ in range(2)]
    csc = P.carve([128, 2], F32)
    P.op("act", "activation", out=csc, in_=lg, func=AF.Exp, scale=float(CH), reads=["lg"], writes=["csc"])
    for d in range(2):
        sfx = "FB"[d]
        P.op("act", "activation", out=MinT[d], in_=_ld_const(P, io, cAs, "REL" + sfx, 64), func=AF.Exp, scale=lg[0:CH, d:d + 1],
             reads=["cAs", "lg"], writes=[("MinT", d)])
        P.op("dve", "scalar_tensor_tensor", out=MinT[d], in0=MinT[d], scalar=128.0 ** -0.5, in1=_ld_const(P, io, cAs, "MASK" + sfx, 64),
             op0=ALU.mult, op1=ALU.mult, reads=[("MinT", d), "cAs"], writes=[("MinT", d)])
        P.op("act", "activation", out=POSQ[d], in_=_ld_const(P, io, cAs, "POS1" + sfx), func=AF.Exp, scale=lg[:, d:d + 1],
             reads=["cAs", "lg"], writes=[("POSQ", d)])
        P.op("act", "activation", out=KDEC[d], in_=_ld_const(P, io, cAs, "KD" + sfx), func=AF.Exp, scale=lg[:, d:d + 1],
             reads=["cAs", "lg"], writes=[("KDEC", d)])
        P.op("dve", "tensor_scalar", out=KDEC[d], in0=KDEC[d], scalar1=128.0 ** -0.5, scalar2=None, op0=ALU.mult,
             reads=[("KDEC", d)], writes=[("KDEC", d)])
    core = Core(P, cst, PS, 1, False, None)
    MinT8 = [P.carve([CH, 8, CH], F32) for _ in range(2)]
    for d in range(2):
        for r_ in range(8):
            P.op("pool" if r_ % 2 else "dve", "tensor_copy", out=MinT8[d][:, r_, :], in_=MinT[d], reads=[("MinT", d)], writes=[("MinT8", d, r_)])
    S = P.carve([128, 128], F32)
    ld = {nm: [P.carve([128, 512], F32) for _ in range(2)] for nm in ("q", "qs", "k", "ks", "v", "cc", "ss", "g")}
    qr = [P.carve([128, 512], F32) for _ in range(2)]
    kr = [P.carve([128, 512], F32) for _ in range(2)]
    qsd = [P.carve([128, 512], F32) for _ in range(2)]
    kdc = [P.carve([128, 512], F32) for _ in range(2)]
    outb = [P.carve([128, 512], F32) for _ in range(2)]
    Vt = [[P.carve([CH, 128], F32) for _ in range(8)] for _ in range(2)]
    Kst = [[P.carve([CH, 128], F32) for _ in range(8)] for _ in range(2)]
    Ot = [P.carve([CH, 128], F32) for _ in range(4)]
    Ob = [P.carve([CH, 128], F32) for _ in range(4)]
    ssq = [P.carve([CH, 2], F32) for _ in range(4)]
    junk = P.carve([CH, 128], F32)
    epsC = cst["epsb"]
    bi_ = 0
    cctr = [0]
    kbc = [0]

    def recur_batch(kb, batch, d, b2, t0, n, lastbatch):
        for gi, (w, c0, ci_) in enumerate(batch):
            c4 = cctr[0] % 4
            cctr[0] += 1
            cs_ = slice(c0, c0 + CH)
            yield from core.recur_gen(kb, gi, w, S, "S", Ot[c4], ("Ot", c4))
            tg = t0 + c0
            if d == 1:
                P.dma("sp", OB[tg:tg + CH, :], Ot[c4], reads=[("Ot", c4)], writes=[("OB", tg)])
            else:
                P.dma("sp", Ob[c4], OB[tg:tg + CH, :], reads=[("OB", tg)], writes=[("Ob", c4)])
                P.op("dve", "tensor_tensor", out=Ot[c4], in0=Ot[c4], in1=Ob[c4], op=ALU.add, reads=[("Ot", c4), ("Ob", c4)],
                     writes=[("Ot", c4)])
                P.op("act", "activation", out=junk, in_=Ot[c4], func=AF.Square, accum_out=ssq[c4][:, 0:1], reads=[("Ot", c4)],
                     writes=["junk", ("ssq", c4)])
                P.op("act", "activation", out=ssq[c4][:, 1:2], in_=ssq[c4][:, 0:1], func=AF.Ln, scale=1.0 / 128, bias=epsC[0:CH, :],
                     reads=[("ssq", c4), "epsb"], writes=[("ssq", c4)])
                P.op("act", "activation", out=ssq[c4][:, 1:2], in_=ssq[c4][:, 1:2], func=AF.Exp, scale=-0.5, reads=[("ssq", c4)],
                     writes=[("ssq", c4)])
                yield
                P.op("dve", "tensor_scalar", out=Ot[c4], in0=Ot[c4], scalar1=ssq[c4][:, 1:2], scalar2=None, op0=ALU.mult,
                     reads=[("Ot", c4), ("ssq", c4)], writes=[("Ot", c4)])
                tp3 = PS[1][:, 0:CH]
                P.op("pe", "transpose", out=tp3, in_=Ot[c4], identity=cst["ident"][0:CH, 0:CH], reads=[("Ot", c4), "ident"],
                     writes=[("bank", 1)])
                P.op("dve", "tensor_tensor", out=outb[b2][:, cs_], in0=tp3, in1=ld["g"][b2][:, cs_], op=ALU.mult,
                     reads=[("ld", "g", b2)], writes=[("outb", b2, c0), ("bank", 1)])
            yield
        if d == 0 and lastbatch:
            P.dma("pool", ysT[0:128, t0:t0 + n], outb[b2][:, 0:n], reads=[("outb", b2, c0_) for c0_ in range(0, n, CH)],
                  writes=[("ysT", 0, t0)])
        yield

    for d in (1, 0):
        P.op("pool", "memset", ap=S, constant=0.0, reads=[], writes=["S"])
        pending = None
        for (t0, n, s, offs) in _dir_chunks(d):
            b2 = bi_ % 2
            bi_ += 1
            names = ["q", "qs", "k", "ks", "v", "cc", "ss"] + (["g"] if d == 0 else [])
            for i, nm in enumerate(names):
                src = {"q": PT["ret_q"], "qs": PT["ret_qs"], "k": PT["ret_k"], "ks": PT["ret_ks"], "v": PT["ret_v"],
                       "g": PT["ret_g"], "cc": io["ropeC"], "ss": io["ropeS"]}[nm]
                P.dma("sp" if i % 2 == 0 else "pool", ld[nm][b2][:, 0:n], src[:, t0:t0 + n], writes=[("ld", nm, b2)])
            for (dst, a_, b_, nm) in ((qr[b2], "q", "qs", "qr"), (kr[b2], "k", "ks", "kr")):
                P.op("dve", "tensor_tensor", out=dst[:, 0:n], in0=ld[a_][b2][:, 0:n], in1=ld["cc"][b2][:, 0:n], op=ALU.mult,
                     reads=[("ld", a_, b2), ("ld", "cc", b2)], writes=[(nm, b2)])
                P.op("pool", "tensor_tensor", out=ld[b_][b2][:, 0:n], in0=ld[b_][b2][:, 0:n], in1=ld["ss"][b2][:, 0:n], op=ALU.mult,
                     reads=[("ld", b_, b2), ("ld", "ss", b2)], writes=[("ld", b_, b2)])
                P.op("dve", "tensor_tensor", out=dst[:, 0:n], in0=dst[:, 0:n], in1=ld[b_][b2][:, 0:n], op=ALU.add,
                     reads=[(nm, b2), ("ld", b_, b2)], writes=[(nm, b2)])
            P.op("pool", "tensor_tensor", out=qsd[b2][:, 0:n], in0=qr[b2][:, 0:n], in1=POSQ[d][:, 0:n], op=ALU.mult,
                 reads=[("qr", b2), ("POSQ", d)], writes=[("qsd", b2)])
            P.op("pool", "tensor_tensor", out=kdc[b2][:, 0:n], in0=kr[b2][:, 0:n], in1=KDEC[d][:, 0:n], op=ALU.mult,
                 reads=[("kr", b2), ("KDEC", d)], writes=[("kdc", b2)])
            if d == 0:
                P.op("act", "activation", out=ld["g"][b2][:, 0:n], in_=ld["g"][b2][:, 0:n], func=AF.Silu, reads=[("ld", "g", b2)],
                     writes=[("ld", "g", b2)])
            wsb = []
            for ci_, c0 in enumerate(offs):
                cs_ = slice(c0, c0 + CH)
                tp = PS[7][0:CH, 0:128]
                P.op("pe", "transpose", out=tp, in_=ld["v"][b2][:, cs_], identity=cst["ident"][:], reads=[("ld", "v", b2), "ident"],
                     writes=[("bank", 7)])
                P.op("act", "activation", out=Vt[b2][ci_], in_=tp, func=AF.Copy, reads=[], writes=[("Vt", b2, ci_), ("bank", 7)])
                tp2 = PS[0][0:CH, 0:128]
                P.op("pe", "transpose", out=tp2, in_=kdc[b2][:, cs_], identity=cst["ident"][:], reads=[("kdc", b2), "ident"],
                     writes=[("bank", 0)])
                P.op("dve", "tensor_copy", out=Kst[b2][ci_], in_=tp2, reads=[], writes=[("Kst", b2, ci_), ("bank", 0)])
                w = {"kT": (kr[b2][:, cs_], ("kr", b2)), "qT": (qr[b2][:, cs_], ("qr", b2)), "qTs": (qsd[b2][:, cs_], ("qsd", b2)),
                     "Kst": (Kst[b2][ci_], ("Kst", b2, ci_)), "V": (Vt[b2][ci_], ("Vt", b2, ci_)), "cs": (csc[:, d:d + 1], "csc")}
                wsb.append((w, c0, ci_))
            G = core.G
            for st in range(0, len(wsb), G):
                batch = wsb[st:st + G]
                kb = kbc[0]
                kbc[0] += 1
                m8 = MinT8[d].rearrange("p a b -> p (a b)")[:, 0:len(batch) * CH]
                pg = core.prep_gen(kb, [x[0] for x in batch], None, [], m8, [("MinT8", d, r_) for r_ in range(8)])
                interleave(pg, pending)
                pending = recur_batch(kb, batch, d, b2, t0, n, st + G >= len(wsb))
        interleave(None, pending)
    P.release(m0)


def _gdn(P, cst, PS, io, PT, ysT, OB):
    m0 = P.mark()
    cAs = _load_consts(P, io, ("RSTF", "RSTB", "NEGINF", "NEGINB", "NEGEXF", "NEGEXB", "IDC"))
    cw = P.carve([128, 3, 4], F32)
    sc = P.carve([1, 4], F32)
    nga = P.carve([1, 2], F32)
    ng = P.carve([128, 1], F32)
    one1 = P.carve([1, 128], F32)
    P.op("pool", "memset", ap=one1, constant=1.0, writes=["one1"])
    P.dma("sp", cw, io["gdn_cw"].rearrange("m j c -> c m j"), writes=["cw"], allow_slow_non_contiguous=True)
    P.dma("sp", sc, io["gdn_sc"], writes=["sc"])
    P.dma("sp", ng, io["gdn_ng"].rearrange("o c -> c o"), writes=["ng"], allow_slow_non_contiguous=True)
    P.op("act", "activation", out=nga, in_=sc[:, 0:2], func=AF.Exp, reads=["sc"], writes=["nga"])
    P.op("dve", "tensor_scalar", out=nga, in0=nga, scalar1=-1.0, scalar2=None, op0=ALU.mult, reads=["nga"], writes=["nga"])
    core = Core(P, cst, PS, 1, True, _ld_const(P, io, cAs, "IDC", 64))
    S = P.carve([128, 128], F32)
    xh = {nm: [P.carve([128, 516], F32) for _ in range(2)] for nm in "qkv"}
    cv = {nm: [P.carve([128, 512], F32) for _ in range(2)] for nm in "qkv"}
    zt = [P.carve([128, 512], F32) for _ in range(2)]
    sqt = P.carve([128, 512], F32)
    rs = P.carve([128, 512], F32)
    rows = {nm: [P.carve([1, 512], F32)] * 2 for nm in ("a", "b", "g", "gc", "gcx", "ngc", "r", "ein", "eex", "cb", "dend")}
    bt = {nm: [P.carve([128, 512], F32) for _ in range(2)] for nm in ("bT", "kT", "aTs", "qTs", "KsT", "BsT", "EinS")}
    outb = [P.carve([128, 512], F32) for _ in range(2)]
    Mx = [P.carve([CH, 512], F32) for _ in range(2)]
    Mi = [P.carve([CH, 512], F32) for _ in range(2)]
    Vt = [[P.carve([CH, 128], F32) for _ in range(8)] for _ in range(2)]
    Kst = [[P.carve([CH, 128], F32) for _ in range(8)] for _ in range(2)]
    Bst = [[P.carve([CH, 128], F32) for _ in range(8)] for _ in range(2)]
    Ot = [P.carve([CH, 128], F32) for _ in range(4)]
    Ob = [P.carve([CH, 128], F32) for _ in range(4)]
    ssq = [P.carve([CH, 2], F32) for _ in range(4)]
    junk = P.carve([CH, 128], F32)
    epsC = cst["epsb"]
    b0, b1, b7 = ("bank", 0), ("bank", 1), ("bank", 7)
    bi_ = 0
    cctr = [0]
    kbc = [0]

    def recur_batch(kb, batch, d, b2, t0, n, lastbatch):
        for gi, (w, c0, ci_) in enumerate(batch):
            c4 = cctr[0] % 4
            cctr[0] += 1
            cs_ = slice(c0, c0 + CH)
            yield from core.recur_gen(kb, gi, w, S, "S", Ot[c4], ("Ot", c4))
            tg = t0 + c0
            if d == 1:
                P.dma("sp", OB[tg:tg + CH, :], Ot[c4], reads=[("Ot", c4)], writes=[("OB", tg)])
            else:
                P.dma("sp", Ob[c4], OB[tg:tg + CH, :], reads=[("OB", tg)], writes=[("Ob", c4)])
                P.op("dve", "tensor_tensor", out=Ot[c4], in0=Ot[c4], in1=Ob[c4], op=ALU.add, reads=[("Ot", c4), ("Ob", c4)],
                     writes=[("Ot", c4)])
                P.op("act", "activation", out=junk, in_=Ot[c4], func=AF.Square, accum_out=ssq[c4][:, 0:1], reads=[("Ot", c4)],
                     writes=["junk", ("ssq", c4)])
                P.op("act", "activation", out=ssq[c4][:, 1:2], in_=ssq[c4][:, 0:1], func=AF.Ln, scale=1.0 / 128, bias=epsC[0:CH, :],
                     reads=[("ssq", c4), "epsb"], writes=[("ssq", c4)])
                P.op("act", "activation", out=ssq[c4][:, 1:2], in_=ssq[c4][:, 1:2], func=AF.Exp, scale=-0.5, reads=[("ssq", c4)],
                     writes=[("ssq", c4)])
                yield
                P.op("dve", "tensor_scalar", out=Ot[c4], in0=Ot[c4], scalar1=ssq[c4][:, 1:2], scalar2=None, op0=ALU.mult,
                     reads=[("Ot", c4), ("ssq", c4)], writes=[("Ot", c4)])
                tp3 = PS[7][:, 384:384 + CH]
                P.op("pe", "transpose", out=tp3, in_=Ot[c4], identity=cst["ident"][0:CH, 0:CH], reads=[("Ot", c4), "ident"],
                     writes=[b7])
                P.op("dve", "scalar_tensor_tensor", out=outb[b2][:, cs_], in0=tp3, scalar=ng[:, 0:1], in1=zt[b2][:, cs_],
                     op0=ALU.mult, op1=ALU.mult, reads=[("zt", b2), "ng"], writes=[("outb", b2, c0), b7])
            yield
        if d == 0 and lastbatch:
            P.dma("pool", ysT[256:384, t0:t0 + n], outb[b2][:, 0:n], reads=[("outb", b2, c0_) for c0_ in range(0, n, CH)],
                  writes=[("ysT", 2, t0)])
        yield

    for d in (1, 0):
        sfx = "FB"[d]
        osfx = "BF"[d]
        P.op("pool", "memset", ap=S, constant=0.0, reads=[], writes=["S"])
        pending = None
        for (t0, n, s, offs) in _dir_chunks(d):
            b2 = bi_ % 2
            bi_ += 1
            seg0, seg1 = (0, 256) if s else (256, TTOT)
            lo, hi = max(t0 - 2, seg0), min(t0 + n + 1, seg1)
            for mi_, nm in enumerate("qkv"):
                x_ = xh[nm][b2]
                if lo > t0 - 2 or hi < t0 + n + 1:
                    P.op("pool", "memset", ap=x_[:, 0:n + 3], constant=0.0, writes=[("xh", nm, b2)])
                P.dma("sp" if mi_ != 1 else "pool", x_[:, lo - (t0 - 2):hi - (t0 - 2)], PT["gdn_" + nm][:, lo:hi], writes=[("xh", nm, b2)])
                c_ = cv[nm][b2]
                eng = "dve" if mi_ != 2 else "pool"
                P.op("dve", "tensor_scalar", out=c_[:, 0:n], in0=x_[:, 0:n], scalar1=cw[:, mi_, 0:1], scalar2=None, op0=ALU.mult,
                     reads=[("xh", nm, b2), "cw"], writes=[("cv", nm, b2)])
                for j in range(1, 4):
                    P.op("dve", "scalar_tensor_tensor", out=c_[:, 0:n], in0=x_[:, j:j + n], scalar=cw[:, mi_, j:j + 1], in1=c_[:, 0:n],
                         op0=ALU.mult, op1=ALU.add, reads=[("xh", nm, b2), "cw", ("cv", nm, b2)], writes=[("cv", nm, b2)])
                P.op("act", "activation", out=c_[:, 0:n], in_=c_[:, 0:n], func=AF.Silu, reads=[("cv", nm, b2)], writes=[("cv", nm, b2)])
            for nm, scl in (("q", 128.0 ** -0.5), ("k", 1.0)):
                c_ = cv[nm][b2]
                P.op("act", "activation", out=sqt[:, 0:n], in_=c_[:, 0:n], func=AF.Square, reads=[("cv", nm, b2)], writes=["sqt"])
                P.op("pe", "matmul", out=PS[1][:, 0:n], lhsT=cst["ones_f"][:], rhs=sqt[:, 0:n], start=True, stop=True,
                     reads=["sqt", "ones_f"], writes=[b1])
                P.op("act", "activation", out=rs[:, 0:n], in_=PS[1][:, 0:n], func=AF.Ln, bias=epsC[:], reads=["epsb"], writes=["rs", b1])
                P.op("act", "activation", out=rs[:, 0:n], in_=rs[:, 0:n], func=AF.Exp, scale=-0.5, reads=["rs"], writes=["rs"])
                P.op("dve", "scalar_tensor_tensor", out=c_[:, 0:n], in0=c_[:, 0:n], scalar=scl, in1=rs[:, 0:n], op0=ALU.mult, op1=ALU.mult,
                     reads=[("cv", nm, b2), "rs"], writes=[("cv", nm, b2)])
            R = {nm: rows[nm][0] for nm in rows}
            rt = lambda nm: ("row", nm, 0)
            P.dma("pool", R["a"][:, 0:n], PT["gdn_af" if d == 0 else "gdn_abk"][:, t0:t0 + n], writes=[rt("a")])
            P.dma("pool", R["b"][:, 0:n], PT["gdn_bf" if d == 0 else "gdn_bb"][:, t0:t0 + n], writes=[rt("b")])
            P.op("act", "activation", out=R["g"][:, 0:n], in_=R["a"][:, 0:n], func=AF.Exp, bias=sc[:, 2 + d:3 + d], reads=[rt("a"), "sc"],
                 writes=[rt("g")])
            P.op("act", "activation", out=R["g"][:, 0:n], in_=R["g"][:, 0:n], func=AF.Ln, bias=one1[:, 0:1], reads=[rt("g"), "one1"],
                 writes=[rt("g")])
            P.op("dve", "tensor_scalar", out=R["g"][:, 0:n], in0=R["g"][:, 0:n], scalar1=nga[:, d:d + 1], scalar2=None, op0=ALU.mult,
                 reads=[rt("g"), "nga"], writes=[rt("g")])
            P.op("act", "activation", out=R["b"][:, 0:n], in_=R["b"][:, 0:n], func=AF.Sigmoid, reads=[rt("b")], writes=[rt("b")])
            rstm = _ld_const(P, io, cAs, "RST" + sfx, 1)
            rsto = _ld_const(P, io, cAs, "RST" + osfx, 1)
            rv = (lambda ap: ap[:, 0:n][:, ::-1]) if d == 1 else (lambda ap: ap[:, 0:n])
            rvo = (lambda ap: ap[:, 0:n][:, ::-1]) if d == 0 else (lambda ap: ap[:, 0:n])
            P.op("dve", "tensor_tensor_scan", out=rv(R["gc"]), data0=rv(rstm), data1=rv(R["g"]), initial=0.0, op0=ALU.mult, op1=ALU.add,
                 reads=[rt("g"), "cAs"], writes=[rt("gc")])
            P.op("dve", "tensor_tensor_scan", out=rvo(R["r"]), data0=rvo(rsto), data1=rvo(R["g"]), initial=0.0, op0=ALU.mult, op1=ALU.add,
                 reads=[rt("g"), "cAs"], writes=[rt("r")])
            P.op("dve", "tensor_tensor", out=R["gcx"][:, 0:n], in0=R["gc"][:, 0:n], in1=R["g"][:, 0:n], op=ALU.subtract,
                 reads=[rt("gc"), rt("g")], writes=[rt("gcx")])
            P.op("dve", "tensor_scalar", out=R["ngc"][:, 0:n], in0=R["gc"][:, 0:n], scalar1=-1.0, scalar2=None, op0=ALU.mult,
                 reads=[rt("gc")], writes=[rt("ngc")])
            P.op("dve", "tensor_tensor", out=R["r"][:, 0:n], in0=R["r"][:, 0:n], in1=R["g"][:, 0:n], op=ALU.subtract,
                 reads=[rt("r"), rt("g")], writes=[rt("r")])
            P.op("act", "activation", out=R["dend"][:, 0:n], in_=R["r"][:, 0:n], func=AF.Exp, reads=[rt("r")], writes=[rt("dend")])
            P.op("act", "activation", out=R["ein"][:, 0:n], in_=R["gc"][:, 0:n], func=AF.Exp, reads=[rt("gc")], writes=[rt("ein")])
            P.op("act", "activation", out=R["eex"][:, 0:n], in_=R["gcx"][:, 0:n], func=AF.Exp, reads=[rt("gcx")], writes=[rt("eex")])
            P.op("act", "activation", out=R["cb"][:, 0:n], in_=R["g"][:, 0:n], func=AF.Exp, reads=[rt("g")], writes=[rt("cb")])
            P.op("dve", "scalar_tensor_tensor", out=R["cb"][:, 0:n], in0=R["cb"][:, 0:n], scalar=-1.0, in1=R["b"][:, 0:n], op0=ALU.mult,
                 op1=ALU.mult, reads=[rt("cb"), rt("b")], writes=[rt("cb")])
            B = {nm: bt[nm][b2] for nm in bt}
            btk = lambda nm: ("bt", nm, b2)
            kn, qn = cv["k"][b2], cv["q"][b2]

            def bcast_mul(row, rtoks, dst, dtok, src, stok, bank, bk):
                P.op("pe", "matmul", out=bank[:, 0:n], lhsT=one1[0:1, :], rhs=row[:, 0:n], start=True, stop=True,
                     reads=rtoks + ["one1"], writes=[bk])
                if src is None:
                    P.op("act", "activation", out=dst[:, 0:n], in_=bank[:, 0:n], func=AF.Copy, reads=[], writes=[dtok, bk])
                else:
                    P.op("dve", "tensor_tensor", out=dst[:, 0:n], in0=src[:, 0:n], in1=bank[:, 0:n], op=ALU.mult, reads=[stok],
                         writes=[dtok, bk])
            bcast_mul(R["ein"], [rt("ein")], B["EinS"], btk("EinS"), None, None, PS[1], b1)
            P.op("pool", "tensor_tensor", out=B["qTs"][:, 0:n], in0=qn[:, 0:n], in1=B["EinS"][:, 0:n], op=ALU.mult,
                 reads=[("cv", "q", b2), btk("EinS")], writes=[btk("qTs")])
            bcast_mul(R["eex"], [rt("eex")], B["aTs"], btk("aTs"), kn, ("cv", "k", b2), PS[1], b1)
            bcast_mul(R["cb"], [rt("cb")], B["bT"], btk("bT"), kn, ("cv", "k", b2), PS[1], b1)
            bcast_mul(R["b"], [rt("b")], B["kT"], btk("kT"), kn, ("cv", "k", b2), PS[1], b1)
            bcast_mul(R["dend"], [rt("dend")], B["KsT"], btk("KsT"), B["kT"], btk("kT"), PS[1], b1)
            bcast_mul(R["dend"], [rt("dend")], B["BsT"], btk("BsT"), B["bT"], btk("bT"), PS[1], b1)
            if d == 0:
                P.dma("pool", zt[b2][:, 0:n], PT["gdn_z"][:, t0:t0 + n], writes=[("zt", b2)])
                P.op("act", "activation", out=zt[b2][:, 0:n], in_=zt[b2][:, 0:n], func=AF.Silu, reads=[("zt", b2)], writes=[("zt", b2)])
            wsb = []
            for ci_, c0 in enumerate(offs):
                cs_ = slice(c0, c0 + CH)
                ms_ = slice(ci_ * CH, (ci_ + 1) * CH)
                for (row, negnm, Mt, mnm, col) in ((R["gcx"], "NEGEX" + sfx, Mx[b2], "Mx", 0), (R["gc"], "NEGIN" + sfx, Mi[b2], "Mi", 64)):
                    reg = PS[0][0:CH, col:col + CH]
                    P.op("pe", "matmul", out=reg, lhsT=one1[0:1, 0:CH], rhs=row[:, cs_], start=True, stop=False,
                         reads=["one1", rt("gcx"), rt("gc")], writes=[b0])
                    P.op("pe", "matmul", out=reg, lhsT=R["ngc"][:, cs_], rhs=one1[0:1, 0:CH], start=False, stop=False,
                         reads=["one1", rt("ngc")], writes=[b0])
                    P.op("pe", "matmul", out=reg, lhsT=cst["ident"][0:CH, 0:CH], rhs=_ld_const(P, io, cAs, negnm, 64), start=False,
                         stop=True, reads=["ident", "cAs"], writes=[b0])
                    P.op("act", "activation", out=Mt[:, ms_], in_=reg, func=AF.Exp, reads=[], writes=[(mnm, b2, ci_), b0])
                for (src, stok, dst, dnm, col, eng) in ((cv["v"][b2], ("cv", "v", b2), Vt[b2][ci_], "Vt", 0, "act"),
                                                        (B["KsT"], btk("KsT"), Kst[b2][ci_], "Kst", 128, "dve"),
                                                        (B["BsT"], btk("BsT"), Bst[b2][ci_], "Bst", 256, "act")):
                    tp = PS[7][0:CH, col:col + 128]
                    P.op("pe", "transpose", out=tp, in_=src[:, cs_], identity=cst["ident"][:], reads=[stok, "ident"], writes=[b7])
                    if eng == "act":
                        P.op("act", "activation", out=dst, in_=tp, func=AF.Copy, reads=[], writes=[(dnm, b2, ci_), b7])
                    else:
                        P.op("dve", "tensor_copy", out=dst, in_=tp, reads=[], writes=[(dnm, b2, ci_), b7])
                last = c0 + CH - 1 if d == 0 else c0
                w = {"aT": (kn[:, cs_], ("cv", "k", b2)), "qT": (qn[:, cs_], ("cv", "q", b2)), "bT": (B["bT"][:, cs_], btk("bT")),
                     "kT": (B["kT"][:, cs_], btk("kT")), "aTs": (B["aTs"][:, cs_], btk("aTs")), "qTs": (B["qTs"][:, cs_], btk("qTs")),
                     "Bst": (Bst[b2][ci_], ("Bst", b2, ci_)), "Kst": (Kst[b2][ci_], ("Kst", b2, ci_)), "V": (Vt[b2][ci_], ("Vt", b2, ci_)),
                     "cs": (B["EinS"][:, last:last + 1], btk("EinS"))}
                wsb.append((w, c0, ci_))
            G = core.G
            for st in range(0, len(wsb), G):
                batch = wsb[st:st + G]
                kb = kbc[0]
                kbc[0] += 1
                gw = slice(st * CH, (st + len(batch)) * CH)
                pg = core.prep_gen(kb, [x[0] for x in batch], Mx[b2][:, gw], [("Mx", b2, x[2]) for x in batch], Mi[b2][:, gw],
                                   [("Mi", b2, x[2]) for x in batch])
                interleave(pg, pending)
                pending = recur_batch(kb, batch, d, b2, t0, n, st + G >= len(wsb))
        interleave(None, pending)
    P.release(m0)


NEG_EM05 = -float(np.exp(-0.5))
RWKV_LN_EPS = 64e-5


def _rwkv(P, cst, PS, io, PT, ysT, OB):
    m0 = P.mark()
    cAs = _load_consts(P, io, ("RSTF", "RSTB", "IDC", "BLK", "MSTRF4", "MSTRB4", "MASKF4", "MASKB4"))
    pc = P.carve([128, 16], F32)
    pc64 = P.carve([64, 4], F32)
    P.dma("sp", pc[:, 0:13], io["rw_pc"].rearrange("j c -> c j"), writes=["pc"], allow_slow_non_contiguous=True)
    P.dma("sp", pc64, io["rw_pc64"].rearrange("j c -> c j"), writes=["pc64"], allow_slow_non_contiguous=True)
    om = P.carve([128, 4], F32); hm = P.carve([128, 4], F32); om64 = P.carve([64, 4], F32); hm64 = P.carve([64, 4], F32)
    omka = P.carve([128, 1], F32)
    epsl = P.carve([128, 1], F32)
    P.op("pool", "memset", ap=epsl, constant=RWKV_LN_EPS, writes=["epsl"])
    for (o_, h_, src, tk_) in ((om, hm, pc[:, 0:4], "pc"), (om64, hm64, pc64, "pc64")):
        P.op("dve", "tensor_scalar", out=o_, in0=src, scalar1=-1.0, scalar2=1.0, op0=ALU.mult, op1=ALU.add, reads=[tk_], writes=["omhm"])
        P.op("dve", "tensor_scalar", out=h_, in0=src, scalar1=0.5, scalar2=None, op0=ALU.mult, reads=[tk_], writes=["omhm2"])
    P.op("dve", "tensor_scalar", out=omka, in0=pc[:, 9:10], scalar1=-1.0, scalar2=1.0, op0=ALU.mult, op1=ALU.add, reads=["pc"],
         writes=["omka"])
    w2 = P.carve([64, 2, 128], F32); a2 = P.carve([64, 2, 128], F32); g2 = P.carve([128, 128], F32)
    P.dma("sp", w2, io["rw_w2"].rearrange("d r c -> r d c"), writes=["w2"])
    P.dma("sp", a2, io["rw_a2"].rearrange("d r c -> r d c"), writes=["a2"])
    P.dma("sp", g2, io["rw_g2"], writes=["g2"])
    BLK = _ld_const(P, io, cAs, "BLK")
    core = Core(P, cst, PS, 2, True, _ld_const(P, io, cAs, "IDC", 64))
    BN = 256
    S = P.carve([128, 64], F32)
    raw = {nm: [P.carve([128, BN + 4], F32)] * 2 for nm in ("r", "k", "v", "gd", "wd", "ad", "adb")}
    X = {nm: P.carve([128, BN], F32) for nm in ("k", "gd", "wd", "ad", "adb", "s")}
    Xr = [P.carve([128, BN], F32) for _ in range(2)]
    Xv = [P.carve([128, BN], F32) for _ in range(2)]
    logw = P.carve([128, BN], F32); asig = P.carve([128, BN], F32); kk = P.carve([128, BN], F32); kd = P.carve([128, BN], F32)
    t1 = P.carve([128, BN], F32); t2 = P.carve([128, BN], F32)
    Einv = P.carve([128, BN], F32); Eex = P.carve([128, BN], F32)
    DB = {nm: [P.carve([128, BN], F32) for _ in range(2)] for nm in ("E", "qT", "aT", "KsT", "BsT", "qm0", "qm1", "am0", "am1", "bm0",
                                                                      "bm1", "km0", "km1")}
    SB1 = {nm: P.carve([128, BN], F32) for nm in ("kT", "bT")}
    gtb = [P.carve([128, BN], F32) for _ in range(2)]
    bonusb = [P.carve([128, BN], F32) for _ in range(2)]
    YT = P.carve([128, BN], F32)
    outb = [P.carve([128, BN], F32) for _ in range(2)]
    Vt = [[P.carve([CH, 128], F32) for _ in range(4)] for _ in range(2)]
    Kst = [[P.carve([CH, 128], F32) for _ in range(4)] for _ in range(2)]
    Bst = [[P.carve([CH, 128], F32) for _ in range(4)] for _ in range(2)]
    Ot = [P.carve([CH, 128], F32) for _ in range(4)]
    Ob = [P.carve([CH, 128], F32) for _ in range(4)]
    b0, b1, b7 = ("bank", 0), ("bank", 1), ("bank", 7)
    bi_ = 0
    cctr = [0]
    kbc = [0]

    def recur_batch(kb, batch, d, b2, t0, n, offs):
        gt, bonus = gtb[b2], bonusb[b2]
        for gi, (w, c0, ci_) in enumerate(batch):
            c4 = cctr[0] % 4
            cctr[0] += 1
            cs_ = slice(c0, c0 + CH)
            yield from core.recur_gen(kb, gi, w, S, "S", Ot[c4], ("Ot", c4))
            tg = t0 + c0
            if d == 1:
                P.dma("sp", OB[tg:tg + CH, :], Ot[c4], reads=[("Ot", c4)], writes=[("OB", tg)])
            else:
                P.dma("sp", Ob[c4], OB[tg:tg + CH, :], reads=[("OB", tg)], writes=[("Ob", c4)])
                P.op("dve", "tensor_tensor", out=Ot[c4], in0=Ot[c4], in1=Ob[c4], op=ALU.add, reads=[("Ot", c4), ("Ob", c4)],
                     writes=[("Ot", c4)])
                tp3 = PS[7][:, 384:384 + CH]
                P.op("pe", "transpose", out=tp3, in_=Ot[c4], identity=cst["ident"][0:CH, 0:CH], reads=[("Ot", c4), "ident"],
                     writes=[b7])
                P.op("act", "activation", out=YT[:, cs_], in_=tp3, func=AF.Copy, reads=[], writes=[("YT", c0), b7])
            yield
        if d == 0:
            ytoks = [("YT", c0) for c0 in offs]
            o_ = outb[b2]
            P.op("pe", "matmul", out=PS[0][:, 0:n], lhsT=BLK, rhs=YT[:, 0:n], start=True, stop=True, reads=["cAs"] + ytoks, writes=[b0])
            P.op("dve", "scalar_tensor_tensor", out=t1[:, 0:n], in0=PS[0][:, 0:n], scalar=-1.0 / 64, in1=YT[:, 0:n], op0=ALU.mult,
                 op1=ALU.add, reads=ytoks, writes=["t1", b0])
            P.op("act", "activation", out=t2[:, 0:n], in_=t1[:, 0:n], func=AF.Square, reads=["t1"], writes=["t2"])
            yield
            P.op("pe", "matmul", out=PS[1][:, 0:n], lhsT=BLK, rhs=t2[:, 0:n], start=True, stop=True, reads=["cAs", "t2"], writes=[b1])
            P.op("act", "activation", out=t2[:, 0:n], in_=PS[1][:, 0:n], func=AF.Ln, scale=1.0 / 64, bias=epsl[:], reads=["epsl"],
                 writes=["t2", b1])
            P.op("act", "activation", out=t2[:, 0:n], in_=t2[:, 0:n], func=AF.Exp, scale=-0.5, reads=["t2"], writes=["t2"])
            yield
            P.op("dve", "tensor_tensor", out=t1[:, 0:n], in0=t1[:, 0:n], in1=t2[:, 0:n], op=ALU.mult, reads=["t1", "t2"], writes=["t1"])
            P.op("dve", "tensor_scalar", out=t1[:, 0:n], in0=t1[:, 0:n], scalar1=pc[:, 11:12], scalar2=pc[:, 12:13], op0=ALU.mult,
                 op1=ALU.add, reads=["t1", "pc"], writes=["t1"])
            P.op("pool", "tensor_tensor", out=t1[:, 0:n], in0=t1[:, 0:n], in1=bonus[:, 0:n], op=ALU.add, reads=["t1", ("bonus", b2)],
                 writes=["t1"])
            P.op("dve", "tensor_tensor", out=o_[:, 0:n], in0=t1[:, 0:n], in1=gt[:, 0:n], op=ALU.mult, reads=["t1", ("gt", b2)],
                 writes=[("outb", b2)])
            P.dma("pool", ysT[384:512, t0:t0 + n], o_[:, 0:n], reads=[("outb", b2)], writes=[("ysT", 3, t0)])
        yield

    for d in (1, 0):
        sfx = "FB"[d]
        P.op("pool", "memset", ap=S, constant=0.0, reads=[], writes=["S"])
        pending = None
        for (t0, n, s, offs) in _dir_chunks(d, RW_BLOCKS):
            b2 = bi_ % 2
            bi_ += 1
            gt, bonus = gtb[b2], bonusb[b2]
            seg0, seg1 = (0, 256) if s else (256, TTOT)
            lo, hi = max(t0 - 1, seg0), min(t0 + n + 1, seg1)
            srcs = [("r", PT["rw_r"], 128, 0), ("k", PT["rw_k"], 128, 1), ("v", PT["rw_v"], 128, 2), ("gd", PT["rw_gd"], 128, 3),
                    ("wd", PT["rw_wdf" if d == 0 else "rw_wdb"], 64, d), ("ad", PT["rw_adf" if d == 0 else "rw_adb"], 64, 2 + d)]
            if d == 0:
                srcs.append(("adb", PT["rw_adb"], 64, 3))
            else:
                srcs = [x for x in srcs if x[0] != "gd"]
            for i, (nm, src, rows_, mc) in enumerate(srcs):
                x_ = raw[nm][b2]
                if lo > t0 - 1 or hi < t0 + n + 1:
                    P.op("pool", "memset", ap=x_[0:rows_, 0:n + 2], constant=0.0, writes=[("raw", nm, 0)])
                P.dma("sp" if i % 2 == 0 else "pool", x_[0:rows_, lo - (t0 - 1):hi - (t0 - 1)], src[:, lo:hi], writes=[("raw", nm, 0)])
                dst = Xr[b2] if nm == "r" else Xv[b2] if nm == "v" else X[nm]
                dtok = ("Xr", b2) if nm == "r" else ("Xv", b2) if nm == "v" else ("X", nm)
                omc, hmc = (om[:, mc:mc + 1], hm[:, mc:mc + 1]) if rows_ == 128 else (om64[:, mc:mc + 1], hm64[:, mc:mc + 1])
                P.op("pool", "tensor_tensor", out=X["s"][0:rows_, 0:n], in0=x_[0:rows_, 0:n], in1=x_[0:rows_, 2:n + 2], op=ALU.add,
                     reads=[("raw", nm, 0)], writes=[("X", "s")])
                P.op("dve", "tensor_scalar", out=dst[0:rows_, 0:n], in0=x_[0:rows_, 1:n + 1], scalar1=omc, scalar2=None, op0=ALU.mult,
                     reads=[("raw", nm, 0), "omhm"], writes=[dtok])
                P.op("dve", "scalar_tensor_tensor", out=dst[0:rows_, 0:n], in0=X["s"][0:rows_, 0:n], scalar=hmc, in1=dst[0:rows_, 0:n],
                     op0=ALU.mult, op1=ALU.add, reads=[("X", "s"), "omhm2", dtok], writes=[dtok])
            xr_, xv_ = Xr[b2], Xv[b2]
            P.op("act", "activation", out=X["wd"][0:64, 0:n], in_=X["wd"][0:64, 0:n], func=AF.Tanh, reads=[("X", "wd")], writes=[("X", "wd")])
            P.op("pe", "matmul", out=PS[0][:, 0:n], lhsT=w2[:, d, :], rhs=X["wd"][0:64, 0:n], start=True, stop=True, reads=["w2", ("X", "wd")],
                 writes=[b0])
            P.op("act", "activation", out=logw[:, 0:n], in_=PS[0][:, 0:n], func=AF.Sigmoid, bias=pc[:, 4 + d:5 + d], reads=["pc"],
                 writes=["logw", b0])
            P.op("dve", "tensor_scalar", out=logw[:, 0:n], in0=logw[:, 0:n], scalar1=NEG_EM05, scalar2=None, op0=ALU.mult, reads=["logw"],
                 writes=["logw"])
            P.op("pe", "matmul", out=PS[1][:, 0:n], lhsT=a2[:, d, :], rhs=X["ad"][0:64, 0:n], start=True, stop=True, reads=["a2", ("X", "ad")],
                 writes=[b1])
            P.op("act", "activation", out=asig[:, 0:n], in_=PS[1][:, 0:n], func=AF.Sigmoid, bias=pc[:, 6 + d:7 + d], reads=["pc"],
                 writes=["asig", b1])
            P.op("dve", "tensor_scalar", out=kk[:, 0:n], in0=X["k"][:, 0:n], scalar1=pc[:, 8:9], scalar2=None, op0=ALU.mult,
                 reads=[("X", "k"), "pc"], writes=["kk"])
            P.op("act", "activation", out=t1[:, 0:n], in_=kk[:, 0:n], func=AF.Square, reads=["kk"], writes=["t1"])
            P.op("pe", "matmul", out=PS[0][:, 0:n], lhsT=BLK, rhs=t1[:, 0:n], start=True, stop=True, reads=["cAs", "t1"], writes=[b0])
            P.op("act", "activation", out=t1[:, 0:n], in_=PS[0][:, 0:n], func=AF.Ln, bias=cst["epsb"][:], reads=["epsb"], writes=["t1", b0])
            P.op("act", "activation", out=t1[:, 0:n], in_=t1[:, 0:n], func=AF.Exp, scale=-0.5, reads=["t1"], writes=["t1"])
            P.op("dve", "tensor_tensor", out=kk[:, 0:n], in0=kk[:, 0:n], in1=t1[:, 0:n], op=ALU.mult, reads=["kk", "t1"], writes=["kk"])
            P.op("dve", "tensor_scalar", out=kd[:, 0:n], in0=asig[:, 0:n], scalar1=pc[:, 9:10], scalar2=omka[:, 0:1], op0=ALU.mult,
                 op1=ALU.add, reads=["asig", "pc", "omka"], writes=["kd"])
            P.op("dve", "tensor_tensor", out=kd[:, 0:n], in0=kd[:, 0:n], in1=X["k"][:, 0:n], op=ALU.mult, reads=["kd", ("X", "k")],
                 writes=["kd"])
            if d == 0:
                P.op("act", "activation", out=X["gd"][:, 0:n], in_=X["gd"][:, 0:n], func=AF.Sigmoid, reads=[("X", "gd")], writes=[("X", "gd")])
                P.op("pe", "matmul", out=PS[1][:, 0:n], lhsT=g2, rhs=X["gd"][:, 0:n], start=True, stop=True, reads=["g2", ("X", "gd")],
                     writes=[b1])
                P.op("act", "activation", out=gt[:, 0:n], in_=PS[1][:, 0:n], func=AF.Copy, reads=[], writes=[("gt", b2), b1])
                P.op("pe", "matmul", out=PS[0][:, 0:n], lhsT=a2[:, 1, :], rhs=X["adb"][0:64, 0:n], start=True, stop=True,
                     reads=["a2", ("X", "adb")], writes=[b0])
                P.op("act", "activation", out=t2[:, 0:n], in_=PS[0][:, 0:n], func=AF.Sigmoid, bias=pc[:, 7:8], reads=["pc"], writes=["t2", b0])
                P.op("dve", "tensor_scalar", out=t2[:, 0:n], in0=t2[:, 0:n], scalar1=pc[:, 9:10], scalar2=omka[:, 0:1], op0=ALU.mult,
                     op1=ALU.add, reads=["t2", "pc", "omka"], writes=["t2"])
                P.op("dve", "tensor_tensor", out=t2[:, 0:n], in0=t2[:, 0:n], in1=X["k"][:, 0:n], op=ALU.mult, reads=["t2", ("X", "k")],
                     writes=["t2"])
                P.op("dve", "tensor_tensor", out=t2[:, 0:n], in0=t2[:, 0:n], in1=kd[:, 0:n], op=ALU.add, reads=["t2", "kd"], writes=["t2"])
                P.op("dve", "scalar_tensor_tensor", out=t2[:, 0:n], in0=t2[:, 0:n], scalar=pc[:, 10:11], in1=xr_[:, 0:n], op0=ALU.mult,
                     op1=ALU.mult, reads=["t2", "pc", ("Xr", b2)], writes=["t2"])
                P.op("pe", "matmul", out=PS[1][:, 0:n], lhsT=BLK, rhs=t2[:, 0:n], start=True, stop=True, reads=["cAs", "t2"], writes=[b1])
                P.op("dve", "tensor_tensor", out=bonus[:, 0:n], in0=PS[1][:, 0:n], in1=xv_[:, 0:n], op=ALU.mult, reads=[("Xv", b2)],
                     writes=[("bonus", b2), b1])
            E = DB["E"][b2]
            rst = _ld_const(P, io, cAs, "RST" + sfx)
            rv = (lambda ap: ap[:, 0:n][:, ::-1]) if d == 1 else (lambda ap: ap[:, 0:n])
            P.op("dve", "tensor_tensor_scan", out=rv(E), data0=rv(rst), data1=rv(logw), initial=0.0, op0=ALU.mult, op1=ALU.add,
                 reads=["logw", "cAs"], writes=[("E", b2)])
            P.op("act", "activation", out=Einv[:, 0:n], in_=E[:, 0:n], func=AF.Exp, scale=-1.0, reads=[("E", b2)], writes=["Einv"])
            P.op("dve", "tensor_tensor", out=Eex[:, 0:n], in0=E[:, 0:n], in1=logw[:, 0:n], op=ALU.subtract, reads=[("E", b2), "logw"],
                 writes=["Eex"])
            P.op("act", "activation", out=Eex[:, 0:n], in_=Eex[:, 0:n], func=AF.Exp, reads=["Eex"], writes=["Eex"])
            P.op("act", "activation", out=E[:, 0:n], in_=E[:, 0:n], func=AF.Exp, reads=[("E", b2)], writes=[("E", b2)])
            T = {nm: DB[nm][b2] for nm in DB}
            T.update(SB1)
            dt = lambda nm: (nm, b2) if nm in DB else (nm, 0)
            P.op("pool", "tensor_tensor", out=T["qT"][:, 0:n], in0=xr_[:, 0:n], in1=E[:, 0:n], op=ALU.mult, reads=[("Xr", b2), ("E", b2)],
                 writes=[dt("qT")])
            P.op("pool", "tensor_tensor", out=T["aT"][:, 0:n], in0=kk[:, 0:n], in1=Eex[:, 0:n], op=ALU.mult, reads=["kk", "Eex"],
                 writes=[dt("aT")])
            P.op("dve", "tensor_tensor", out=T["kT"][:, 0:n], in0=kd[:, 0:n], in1=Einv[:, 0:n], op=ALU.mult, reads=["kd", "Einv"],
                 writes=[dt("kT")])
            P.op("pool", "tensor_tensor", out=t1[:, 0:n], in0=kk[:, 0:n], in1=asig[:, 0:n], op=ALU.mult, reads=["kk", "asig"], writes=["t1"])
            P.op("dve", "scalar_tensor_tensor", out=T["bT"][:, 0:n], in0=t1[:, 0:n], scalar=-1.0, in1=Einv[:, 0:n], op0=ALU.mult,
                 op1=ALU.mult, reads=["t1", "Einv"], writes=[dt("bT")])
            for (src_, pre, eng) in (("qT", "qm", "pool"), ("aT", "am", "dve"), ("bT", "bm", "pool"), ("kT", "km", "dve")):
                for hh in range(2):
                    P.op(eng, "tensor_scalar", out=T[pre + str(hh)][:, 0:n], in0=T[src_][:, 0:n], scalar1=BLK[:, hh * 64:hh * 64 + 1],
                         scalar2=None, op0=ALU.mult, reads=[dt(src_), "cAs"], writes=[dt(pre + str(hh))])
            for c0 in offs:
                last = c0 + CH - 1 if d == 0 else c0
                cs_ = slice(c0, c0 + CH)
                P.op("dve", "tensor_scalar", out=T["KsT"][:, cs_], in0=T["kT"][:, cs_], scalar1=E[:, last:last + 1], scalar2=None,
                     op0=ALU.mult, reads=[dt("kT"), ("E", b2)], writes=[("KsT", b2, c0)])
                P.op("pool", "tensor_scalar", out=T["BsT"][:, cs_], in0=T["bT"][:, cs_], scalar1=E[:, last:last + 1], scalar2=None,
                     op0=ALU.mult, reads=[dt("bT"), ("E", b2)], writes=[("BsT", b2, c0)])
            wsb = []
            for ci_, c0 in enumerate(offs):
                cs_ = slice(c0, c0 + CH)
                last = c0 + CH - 1 if d == 0 else c0
                for (src, stok, dst, dnm, col, eng) in ((xv_, ("Xv", b2), Vt[b2][ci_], "Vt", 0, "act"),
                                                        (T["KsT"], ("KsT", b2, c0), Kst[b2][ci_], "Kst", 128, "dve"),
                                                        (T["BsT"], ("BsT", b2, c0), Bst[b2][ci_], "Bst", 256, "act")):
                    tp = PS[7][0:CH, col:col + 128]
                    P.op("pe", "transpose", out=tp, in_=src[:, cs_], identity=cst["ident"][:], reads=[stok, "ident"], writes=[b7])
                    if eng == "act":
                        P.op("act", "activation", out=dst, in_=tp, func=AF.Copy, reads=[], writes=[(dnm, b2, ci_), b7])
                    else:
                        P.op("dve", "tensor_copy", out=dst, in_=tp, reads=[], writes=[(dnm, b2, ci_), b7])
                hm_ = lambda pre: [(T[pre + str(hh)][:, cs_], dt(pre + str(hh))) for hh in range(2)]
                w = {"aT": (T["aT"][:, cs_], dt("aT")), "qT": (T["qT"][:, cs_], dt("qT")), "bTm": hm_("bm"), "kTm": hm_("km"),
                     "aTsm": hm_("am"), "qTsm": hm_("qm"),
                     "Bst": (Bst[b2][ci_], ("Bst", b2, ci_)), "Kst": (Kst[b2][ci_], ("Kst", b2, ci_)), "V": (Vt[b2][ci_], ("Vt", b2, ci_)),
                     "cs": (E[:, last:last + 1], ("E", b2))}
                wsb.append((w, c0, ci_))
            kb = kbc[0]
            kbc[0] += 1
            GW = len(wsb) * 128
            pg = core.prep_gen(kb, [x[0] for x in wsb], _ld_const(P, io, cAs, "MSTR" + sfx + "4", 64)[:, 0:GW], "cAs",
                               _ld_const(P, io, cAs, "MASK" + sfx + "4", 64)[:, 0:GW], "cAs")
            interleave(pg, pending)
            pending = recur_batch(kb, wsb, d, b2, t0, n, offs)
        interleave(None, pending)
    P.release(m0)


def A_gather(outs):
    ys = np.zeros((2, TTOT, 4, 512), np.float32)
    for core in range(8):
        b, h = core // 4, core % 4
        o = outs[core].reshape(4, 128, TTOT)
        ys[b, :, :, h * 128:(h + 1) * 128] = np.transpose(o, (2, 0, 1))
    ys = ys.reshape(2, TTOT, 2048)
    return ys[:, CTX:], ys[:, :CTX]


def kernel(**inputs):
    inp = {k: np.asarray(v) for k, v in inputs.items()}
    h_lat, h_ctx = inp["x"].astype(np.float32), inp["ctx"].astype(np.float32)
    cores = list(range(8))
    for l in range(2):
        last = l == 1
        ncA, _ = build_A()
        resA = run_bass_kernel_spmd(ncA, A_inmaps(l, h_lat, h_ctx, inp), core_ids=cores)
        ysl, ysc = A_gather([r["ysT"] for r in resA.results])
        ncB, _ = build_B(last)
        resB = run_bass_kernel_spmd(ncB, B_inmaps(l, last, h_lat, h_ctx, ysl, ysc, inp), core_ids=cores)
        h_lat, h_ctx = B_gather(last, [r["out"] for r in resB.results])
    return np.ascontiguousarray(h_lat, dtype=np.float32)
```

```python
import contextlib
import numpy as np
import concourse.bass as bass
import concourse.mybir as mybir
from concourse.bass_utils import run_bass_kernel_spmd

F32 = mybir.dt.float32
BF16 = mybir.dt.bfloat16
AF = mybir.ActivationFunctionType
ALU = mybir.AluOpType
AX = mybir.AxisListType

ENG_NAMES = ("pe", "act", "dve", "pool", "sp")
EPOCH = 16000
N_DMA_SEMS = 12

D = 1024
NB = 2
SEQ = 8192
CTX = 256
TTOT = SEQ + CTX
DFF = 2816
N_IN = 11152
GATE_OFF = N_IN - 4096
EPS = 1e-6


class Prog:
    def __init__(self, nc, same_engine_sync=True):
        self.nc = nc
        self.es = contextlib.ExitStack()
        self.ops = {e: [] for e in ENG_NAMES}
        self.cnt = {e: 0 for e in ENG_NAMES}
        self.sem = {}
        self.known = {e: {} for e in ENG_NAMES}
        self.semobj = {}
        self.ep = {}
        self.finals = []
        for e in ENG_NAMES:
            if e != "sp":
                self._new_epoch(e)
        self.dma_sems = {}
        self.dma_k = {}
        for q in ("sp", "act", "pool"):
            self.dma_sems[q] = []
            for i in range(N_DMA_SEMS):
                nm = f"d_{q}_{i}"
                self.semobj[nm] = self.es.enter_context(nc.semaphore(nm))
                self.dma_sems[q].append(nm)
            self.dma_k[q] = 0
        self.lastw = {}
        self.readers = {}
        self.env = {}
        self.same_engine_sync = same_engine_sync
        self.n_wait = 0
        self.n_ins = 0

    def _new_epoch(self, e):
        if e in self.sem:
            self.finals.append((self.sem[e], self.cnt[e]))
        k = self.ep.get(e, -1) + 1
        self.ep[e] = k
        nm = f"s_{e}_{k}"
        self.semobj[nm] = self.es.enter_context(self.nc.semaphore(nm))
        self.sem[e] = nm
        self.cnt[e] = 0

    def sb(self, name, shape, dtype=F32):
        return self.es.enter_context(self.nc.sbuf_tensor("sb_" + name, list(shape), dtype))

    def ps(self, name, shape, dtype=F32):
        return self.es.enter_context(self.nc.psum_tensor("ps_" + name, list(shape), dtype))

    def _deps(self, eng, reads, writes):
        need = {}
        for t in reads:
            for ev in self.lastw.get(t, ()):
                need[ev[0]] = max(need.get(ev[0], 0), ev[1])
        for t in writes:
            for ev in self.lastw.get(t, ()):
                need[ev[0]] = max(need.get(ev[0], 0), ev[1])
            for ev in self.readers.get(t, ()):
                need[ev[0]] = max(need.get(ev[0], 0), ev[1])
        kn = self.known[eng]
        for s, v in need.items():
            if kn.get(s, 0) >= v:
                continue
            if s.startswith("s_" + eng + "_") and (eng == "pe" or not self.same_engine_sync):
                continue
            self.ops[eng].append(("wait", self.semobj[s], v))
            self.n_wait += 1
            kn[s] = v

    def _commit(self, evs, reads, writes):
        for t in writes:
            self.lastw[t] = list(evs)
            self.readers[t] = []
        for t in reads:
            if t in writes:
                continue
            self.readers.setdefault(t, []).extend(evs)

    def op(self, eng, meth, reads=(), writes=(), **kw):
        if self.cnt[eng] >= EPOCH:
            self._new_epoch(eng)
        self._deps(eng, reads, writes)
        self.cnt[eng] += 1
        s = self.sem[eng]
        self.ops[eng].append(("ins", meth, kw, self.semobj[s], 1))
        self._commit([(s, self.cnt[eng])], reads, writes)
        self.n_ins += 1

    def dma(self, q, out, in_, reads=(), writes=(), **kw):
        self.dma_group([(q, out, in_)], reads, writes, **kw)

    def dma_group(self, items, reads=(), writes=(), **kw):
        for q in dict.fromkeys(it[0] for it in items):
            self._deps(q, reads, writes)
        evs = []
        for (q, out, in_) in items:
            k = self.dma_k[q]
            self.dma_k[q] += 1
            s = self.dma_sems[q][k % N_DMA_SEMS]
            target = 16 * (k // N_DMA_SEMS + 1)
            if target > 16 and self.known[q].get(s, 0) < target - 16:
                self.ops[q].append(("wait", self.semobj[s], target - 16))
                self.known[q][s] = target - 16
            self.ops[q].append(("ins", "dma_start", dict(kw, out=out, in_=in_), self.semobj[s], 16))
            evs.append((s, target))
            self.n_ins += 1
        self._commit(evs, reads, writes)

    def coll(self, kind, in_ap, out_ap, groups, reads=(), writes=()):
        q = "pool"
        self._deps(q, reads, writes)
        k = self.dma_k[q]
        self.dma_k[q] += 1
        s = self.dma_sems[q][k % N_DMA_SEMS]
        target = 16 * (k // N_DMA_SEMS + 1)
        if target > 16 and self.known[q].get(s, 0) < target - 16:
            self.ops[q].append(("wait", self.semobj[s], target - 16))
            self.known[q][s] = target - 16
        self.ops[q].append(("ins", "collective_compute", dict(kind=kind, op=ALU.bypass, replica_groups=groups, ins=[in_ap],
                                                               outs=[out_ap]), self.semobj[s], 16))
        self._commit([(s, target)], reads, writes)
        self.n_ins += 1

    def raw(self, eng, fn):
        self.ops[eng].append(("raw", fn))

    def finish_wait(self, eng, tokens):
        self._deps(eng, tokens, ())

    def barrier(self):
        evs = [(self.sem[x], self.cnt[x]) for x in ENG_NAMES if x != "sp" and self.cnt[x] > 0] + list(self.finals)
        for q in self.dma_sems:
            k = self.dma_k[q]
            for i, s in enumerate(self.dma_sems[q]):
                n = (k - i + N_DMA_SEMS - 1) // N_DMA_SEMS if k > i else 0
                if n > 0:
                    evs.append((s, 16 * n))
        for e in ENG_NAMES:
            kn = self.known[e]
            for (s, v) in evs:
                if kn.get(s, 0) >= v:
                    continue
                if s.startswith("s_" + e + "_"):
                    continue
                self.ops[e].append(("wait", self.semobj[s], v))
                kn[s] = v
        self.lastw.clear()
        self.readers.clear()

    def arena_init(self, nwords):
        self.arena = self.sb("arena", [128, nwords], F32)
        self.aoff = 0
        self.anw = nwords

    def carve(self, shape, dtype=F32):
        n = 1
        for d in shape[1:]:
            n *= d
        nw = n if dtype == F32 else (n + 1) // 2
        assert self.aoff + nw <= self.anw, ("arena overflow", self.aoff, nw, self.anw)
        v = self.arena[0:shape[0], self.aoff:self.aoff + nw]
        self.aoff += nw
        if dtype != F32:
            v = v.bitcast(dtype)[:, 0:n]
        if len(shape) == 3:
            v = v.rearrange("p (a b) -> p a b", a=shape[1])
        elif len(shape) == 4:
            v = v.rearrange("p (a b c) -> p a b c", a=shape[1], b=shape[2])
        return v

    def mark(self):
        return self.aoff

    def release(self, m):
        self.barrier()
        self.aoff = m

    def emit(self):
        nc = self.nc
        with nc.Block() as block:
            def mk(ename):
                lst = self.ops[ename]

                def body(e):
                    for it in lst:
                        if it[0] == "wait":
                            e.wait_ge(it[1], it[2])
                        elif it[0] == "raw":
                            it[1](e, self.env)
                        else:
                            kw = {k: (v(self.env) if callable(v) else v) for k, v in it[2].items()}
                            getattr(e, it[1])(**kw).then_inc(it[3], it[4])
                return body
            block.tensor(mk("pe"))
            block.scalar(mk("act"))
            block.vector(mk("dve"))
            block.gpsimd(mk("pool"))
            block.sync(mk("sp"))
        self.es.close()


class RR:
    def __init__(self, items):
        self.items = list(items)
        self.i = 0

    def next(self):
        x = self.items[self.i % len(self.items)]
        self.i += 1
        return x


def _common_consts(P):
    c = {}
    c["ones_bf"] = P.sb("ones_bf", [128, 128], BF16)
    P.op("pool", "memset", ap=c["ones_bf"][:], constant=1.0, writes=["ones_bf"])
    c["ones_f"] = P.sb("ones_f", [128, 128], F32)
    P.op("pool", "memset", ap=c["ones_f"][:], constant=1.0, writes=["ones_f"])
    c["ident"] = P.sb("ident", [128, 128], F32)
    P.op("pool", "memset", ap=c["ident"][:], constant=1.0, writes=["ident"])
    P.op("pool", "affine_select", out=c["ident"][:], in_=c["ident"][:], pattern=[[-1, 128]],
         compare_op=ALU.is_equal, fill=0.0, base=0, channel_multiplier=1, reads=["ident"], writes=["ident"])
    c["epsb"] = P.sb("epsb", [128, 1], F32)
    P.op("pool", "memset", ap=c["epsb"][:], constant=EPS, writes=["epsb"])
    return c


def _mod_vectors(P, cst, cvec, mod_w, mod_b, nchunks, wst, psum, modsb):
    craw = P.sb("craw", [128, 8, 2], F32)
    csil = P.sb("csil", [128, 8, 2], F32)
    mb = P.sb("modb", [128, 48], F32)
    P.dma_group([("sp", craw[:, :, s], cvec[s, :].rearrange("(k p) -> p k", p=128)) for s in range(2)], writes=["craw"],
                allow_slow_non_contiguous=True)
    P.dma("sp", mb[:, 0:nchunks], mod_b[0, 0:nchunks * 128].rearrange("(j p) -> p j", p=128), writes=["modb"],
          allow_slow_non_contiguous=True)
    P.op("act", "activation", out=csil[:], in_=craw[:], func=AF.Silu, reads=["craw"], writes=["csil"])
    ng = nchunks // 4
    for g in range(ng):
        st = wst[g % 2]
        tok = ("wst", g % 2)
        v = st[:, 0:4096].rearrange("p (k c) -> p k c", k=8)
        P.dma_group([("sp" if kc % 2 == 0 else "pool", v[:, kc, :], mod_w[kc * 128:(kc + 1) * 128, g * 512:(g + 1) * 512])
                     for kc in range(8)], writes=[tok])
        for jj in range(4):
            j = g * 4 + jj
            for kc in range(8):
                P.op("pe", "matmul", out=psum[:, j, :], lhsT=v[:, kc, jj * 128:(jj + 1) * 128], rhs=csil[:, kc, :],
                     start=(kc == 0), stop=(kc == 7), reads=[tok, "csil"], writes=["modps"])
    for s in range(2):
        P.op("dve", "tensor_tensor", out=modsb[:, 0:nchunks, s], in0=psum[:, 0:nchunks, s], in1=mb[:, 0:nchunks],
             op=ALU.add, reads=["modps", "modb"], writes=["modsb"])


def _rms_stats(P, cst, hT, nk, t0, tn, htok, sq, ssps, rstd, tagsfx):
    for kc in range(nk):
        P.op("act", "activation", out=sq[:, kc, 0:tn], in_=hT[:, kc, t0:t0 + tn], func=AF.Square,
             reads=[htok(kc)], writes=[("sq", kc)])
    for kc in range(nk):
        P.op("pe", "matmul", out=ssps[:, 0:tn], lhsT=cst["ones_bf"][:], rhs=sq[:, kc, 0:tn], start=(kc == 0),
             stop=(kc == nk - 1), reads=[("sq", kc), "ones_bf"], writes=["ssps"])
    P.op("act", "activation", out=rstd[:, 0:tn], in_=ssps[:, 0:tn], func=AF.Ln, scale=1.0 / (nk * 128),
         bias=cst["epsb"][:], reads=["ssps", "epsb"], writes=["rstd"])
    P.op("act", "activation", out=rstd[:, 0:tn], in_=rstd[:, 0:tn], func=AF.Exp, scale=-0.5,
         reads=["rstd"], writes=["rstd"])


def build_B(last: bool):
    NL = 2048
    NC_ = 0 if last else 64
    NT = NL + NC_
    nc = bass.Bass("TRN2", target_bir_lowering=False)
    hT_in = nc.dram_tensor("hT", [D, NT], F32, kind="ExternalInput").ap()
    ysT_in = nc.dram_tensor("ysT", [2048, NT], F32, kind="ExternalInput").ap()
    cvec = nc.dram_tensor("cvec", [2, D], F32, kind="ExternalInput").ap()
    mod_w = nc.dram_tensor("mod_w", [D, 6 * D], F32, kind="ExternalInput").ap()
    mod_b = nc.dram_tensor("mod_b", [1, 6 * D], F32, kind="ExternalInput").ap()
    n1g = nc.dram_tensor("n1g", [1, D], F32, kind="ExternalInput").ap()
    n2g = nc.dram_tensor("n2g", [1, D], F32, kind="ExternalInput").ap()
    fng = nc.dram_tensor("fng", [1, D], F32, kind="ExternalInput").ap()
    wg = nc.dram_tensor("wg", [D, 4096], F32, kind="ExternalInput").ap()
    gate_b = nc.dram_tensor("gate_b", [1, 4096], F32, kind="ExternalInput").ap()
    wbr = nc.dram_tensor("wbr", [4, 512, D], F32, kind="ExternalInput").ap()
    wo = nc.dram_tensor("wo", [D, D], F32, kind="ExternalInput").ap()
    w1 = nc.dram_tensor("w1", [D, DFF], F32, kind="ExternalInput").ap()
    w3 = nc.dram_tensor("w3", [D, DFF], F32, kind="ExternalInput").ap()
    w2 = nc.dram_tensor("w2", [DFF, D], F32, kind="ExternalInput").ap()
    if last:
        out = nc.dram_tensor("out", [NL, D], F32, kind="ExternalOutput").ap()
    else:
        out = nc.dram_tensor("out", [D, NT], F32, kind="ExternalOutput").ap()

    P = Prog(nc)
    cst = _common_consts(P)
    HMAX = 1088
    hT = P.sb("hT", [128, 8, HMAX], F32)
    xu = P.sb("xu", [128, 8, HMAX], BF16)
    A = P.sb("A", [128, 22, HMAX], BF16)
    mg = P.sb("mg", [128, 8, HMAX], BF16)
    wst = [P.sb(f"wst{i}", [128, 4096], F32) for i in range(2)]
    wbf = [P.sb(f"wbf{i}", [128, 4096], BF16) for i in range(2)]
    yst = [P.sb(f"yst{i}", [128, 512], F32) for i in range(2)]
    sq = P.sb("sq", [128, 8, 512], BF16)
    rstd = P.sb("rstd", [128, 512], F32)
    tmp = [P.sb(f"tmp{i}", [128, 512], F32) for i in range(3)]
    modsb = P.sb("modsb", [128, 48, 2], F32)
    gvec = P.sb("gvec", [128, 8, 3], F32)
    gbv = P.sb("gbv", [128, 32], F32)
    GS = P.sb("GS", [128, 8, 2, 4], F32)
    psA = [P.ps(f"psA{i}", [128, 512], F32) for i in range(2)]
    psB = [P.ps(f"psB{i}", [128, 512], F32) for i in range(2)]
    psS = P.ps("psS", [128, 512], F32)
    psM = P.ps("psM", [128, 48, 2], F32)
    psT = [P.ps(f"psT{i}", [128, 512], F32) for i in range(2)]

    P.dma_group([("sp", gvec[:, :, i], g_[0, :].rearrange("(k p) -> p k", p=128)) for i, g_ in enumerate((n1g, n2g, fng))],
                writes=["gvec"], allow_slow_non_contiguous=True)
    P.dma("sp", gbv[:], gate_b[0, :].rearrange("(j p) -> p j", p=128), writes=["gbv"], allow_slow_non_contiguous=True)
    _mod_vectors(P, cst, cvec, mod_w, mod_b, 48, wst, psM, modsb)
    for s in range(2):
        for (gi, sc_c, sh_c, col) in ((0, 1, 0, 0), (1, 4, 3, 2)):
            P.op("dve", "scalar_tensor_tensor", out=GS[:, :, s, col], in0=modsb[:, sc_c * 8:(sc_c + 1) * 8, s], scalar=1.0,
                 in1=gvec[:, :, gi], op0=ALU.add, op1=ALU.mult, reads=["modsb", "gvec"], writes=["GS"])
            P.op("dve", "tensor_copy", out=GS[:, :, s, col + 1], in_=modsb[:, sh_c * 8:(sh_c + 1) * 8, s],
                 reads=["modsb"], writes=["GS"])

    castq = RR(["dve", "pool"])
    dq = RR(["sp", "pool"])

    def cast(out, in_, reads, writes):
        e = castq.next()
        P.op(e, "tensor_copy", out=out, in_=in_, reads=reads, writes=writes)

    wctr = [0]

    def load_w(pieces, n):
        i = wctr[0] % 2
        wctr[0] += 1
        P.dma_group([(dq.next(), vf(wst[i]), src) for (vf, src) in pieces], writes=[("wst", i)])
        cast(wbf[i][:, 0:n], wst[i][:, 0:n], [("wst", i)], [("wbf", i)])
        return wbf[i], ("wbf", i)

    halves = [(0, 1024, 0), (1024, 1024, NC_)]
    for (l0, nl, ncx) in halves:
        ntok = nl + ncx
        blocks = [(o, 512, 0) for o in range(0, nl, 512)] + ([(nl, ncx, 1)] if ncx else [])
        def gcol(o):
            return l0 + o if o < nl else NL + (o - nl)
        for kc in range(8):
            for (o, n, s) in blocks:
                P.dma(dq.next(), hT[:, kc, o:o + n], hT_in[kc * 128:(kc + 1) * 128, gcol(o):gcol(o) + n],
                      writes=[("h", kc, o)])
        yi = 0
        for c16 in range(16):
            for (o, n, s) in blocks:
                st = yst[yi % 2]
                P.dma(dq.next(), st[:, 0:n], ysT_in[c16 * 128:(c16 + 1) * 128, gcol(o):gcol(o) + n], writes=[("yst", yi % 2)])
                cast(A[:, c16, o:o + n], st[:, 0:n], [("yst", yi % 2)], [("A", c16, o)])
                yi += 1
        for (o, n, s) in blocks:
            _rms_stats(P, cst, hT, 8, o, n, lambda kc: ("h", kc, o), sq, psS, rstd, "")
            for kc in range(8):
                t = tmp[kc % 2]
                P.op("dve", "tensor_tensor", out=t[:, 0:n], in0=hT[:, kc, o:o + n], in1=rstd[:, 0:n], op=ALU.mult,
                     reads=[("h", kc, o), "rstd"], writes=[("tmp", kc % 2)])
                P.op("pool", "tensor_scalar", out=xu[:, kc, o:o + n], in0=t[:, 0:n], scalar1=GS[:, kc, s, 0:1],
                     scalar2=GS[:, kc, s, 1:2], op0=ALU.mult, op1=ALU.add, reads=[("tmp", kc % 2), "GS"],
                     writes=[("xu", kc, o)])
        for j in range(8):
            gsrc = wg.rearrange("(kc p) (k j c) -> p k kc j c", p=128, k=4, j=8)
            pieces = [((lambda t, k=k: t[:, k * 1024:(k + 1) * 1024].rearrange("p (kc c) -> p kc c", kc=8)),
                       gsrc[:, k, :, j, :]) for k in range(4)]
            wgt, wgtok = load_w(pieces, 4096)
            wgv = wgt[:, 0:4096].rearrange("p (k kc c) -> p k kc c", k=4, kc=8)
            bsrc = wbr.rearrange("k (kc p) (j c) -> p k kc j c", p=128, j=8)
            pieces = [((lambda t, k=k: t[:, k * 512:(k + 1) * 512].rearrange("p (kc c) -> p kc c", kc=4)),
                       bsrc[:, k, :, j, :]) for k in range(4)]
            wbt, wbtok = load_w(pieces, 2048)
            wbv = wbt[:, 0:2048].rearrange("p (k kc c) -> p k kc c", k=4, kc=4)
            for (o, n, s) in blocks:
                for k in range(4):
                    pa = psA[k % 2]
                    pb = psB[k % 2]
                    for kc in range(8):
                        P.op("pe", "matmul", out=pa[:, 0:n], lhsT=wgv[:, k, kc, :], rhs=xu[:, kc, o:o + n], start=(kc == 0),
                             stop=(kc == 7), reads=[wgtok, ("xu", kc, o)], writes=[("psA", k % 2)])
                    for kc in range(4):
                        P.op("pe", "matmul", out=pb[:, 0:n], lhsT=wbv[:, k, kc, :], rhs=A[:, k * 4 + kc, o:o + n],
                             start=(kc == 0), stop=(kc == 3), reads=[wbtok, ("A", k * 4 + kc, o)], writes=[("psB", k % 2)])
                    sg = tmp[k % 2]
                    P.op("act", "activation", out=sg[:, 0:n], in_=pa[:, 0:n], func=AF.Sigmoid,
                         bias=gbv[:, k * 8 + j:k * 8 + j + 1], reads=[("psA", k % 2), "gbv"], writes=[("tmp", k % 2)])
                    if k == 0:
                        P.op("dve", "tensor_tensor", out=tmp[2][:, 0:n], in0=sg[:, 0:n], in1=pb[:, 0:n], op=ALU.mult,
                             reads=[("tmp", 0), ("psB", 0)], writes=[("tmp", 2)])
                    else:
                        P.op("dve", "tensor_tensor", out=sg[:, 0:n], in0=sg[:, 0:n], in1=pb[:, 0:n], op=ALU.mult,
                             reads=[("tmp", k % 2), ("psB", k % 2)], writes=[("tmp", k % 2)])
                        if k < 3:
                            P.op("pool", "tensor_tensor", out=tmp[2][:, 0:n], in0=tmp[2][:, 0:n], in1=sg[:, 0:n], op=ALU.add,
                                 reads=[("tmp", 2), ("tmp", k % 2)], writes=[("tmp", 2)])
                        else:
                            P.op("pool", "tensor_tensor", out=mg[:, j, o:o + n], in0=tmp[2][:, 0:n], in1=sg[:, 0:n], op=ALU.add,
                                 reads=[("tmp", 2), ("tmp", k % 2)], writes=[("mg", j, o)])
        for j in range(8):
            osrc = wo.rearrange("(kc p) (j c) -> p kc j c", p=128, j=8)
            wt, wtok = load_w([((lambda t: t[:, 0:1024].rearrange("p (kc c) -> p kc c", kc=8)), osrc[:, :, j, :])], 1024)
            wv = wt[:, 0:1024].rearrange("p (kc c) -> p kc c", kc=8)
            for bi, (o, n, s) in enumerate(blocks):
                pa = psA[bi % 2]
                for kc in range(8):
                    P.op("pe", "matmul", out=pa[:, 0:n], lhsT=wv[:, kc, :], rhs=mg[:, kc, o:o + n], start=(kc == 0),
                         stop=(kc == 7), reads=[wtok, ("mg", kc, o)], writes=[("psA", bi % 2)])
                P.op("dve", "scalar_tensor_tensor", out=hT[:, j, o:o + n], in0=pa[:, 0:n], scalar=modsb[:, 16 + j, s:s + 1],
                     in1=hT[:, j, o:o + n], op0=ALU.mult, op1=ALU.add, reads=[("psA", bi % 2), "modsb", ("h", j, o)],
                     writes=[("h", j, o)])
        for (o, n, s) in blocks:
            _rms_stats(P, cst, hT, 8, o, n, lambda kc: ("h", kc, o), sq, psS, rstd, "")
            for kc in range(8):
                t = tmp[kc % 2]
                P.op("dve", "tensor_tensor", out=t[:, 0:n], in0=hT[:, kc, o:o + n], in1=rstd[:, 0:n], op=ALU.mult,
                     reads=[("h", kc, o), "rstd"], writes=[("tmp", kc % 2)])
                P.op("pool", "tensor_scalar", out=xu[:, kc, o:o + n], in0=t[:, 0:n], scalar1=GS[:, kc, s, 2:3],
                     scalar2=GS[:, kc, s, 3:4], op0=ALU.mult, op1=ALU.add, reads=[("tmp", kc % 2), "GS"],
                     writes=[("xu", kc, o)])
        for c2 in range(22):
            s1 = w1.rearrange("(kc p) (j c) -> p kc j c", p=128, c=128)
            s3 = w3.rearrange("(kc p) (j c) -> p kc j c", p=128, c=128)
            wt, wtok = load_w([((lambda t: t[:, 0:1024].rearrange("p (kc c) -> p kc c", kc=8)), s1[:, :, c2, :]),
                               ((lambda t: t[:, 1024:2048].rearrange("p (kc c) -> p kc c", kc=8)), s3[:, :, c2, :])], 2048)
            wv = wt[:, 0:2048].rearrange("p (m kc c) -> p m kc c", m=2, kc=8)
            for bi, (o, n, s) in enumerate(blocks):
                pa = psA[bi % 2]
                pb = psB[bi % 2]
                for kc in range(8):
                    P.op("pe", "matmul", out=pa[:, 0:n], lhsT=wv[:, 0, kc, :], rhs=xu[:, kc, o:o + n], start=(kc == 0),
                         stop=(kc == 7), reads=[wtok, ("xu", kc, o)], writes=[("psA", bi % 2)])
                for kc in range(8):
                    P.op("pe", "matmul", out=pb[:, 0:n], lhsT=wv[:, 1, kc, :], rhs=xu[:, kc, o:o + n], start=(kc == 0),
                         stop=(kc == 7), reads=[wtok, ("xu", kc, o)], writes=[("psB", bi % 2)])
                t = tmp[bi % 2]
                P.op("act", "activation", out=t[:, 0:n], in_=pa[:, 0:n], func=AF.Silu, reads=[("psA", bi % 2)],
                     writes=[("tmp", bi % 2)])
                P.op("dve", "tensor_tensor", out=A[:, c2, o:o + n], in0=t[:, 0:n], in1=pb[:, 0:n], op=ALU.mult,
                     reads=[("tmp", bi % 2), ("psB", bi % 2)], writes=[("A", c2, o)])
        for j in range(8):
            s2 = w2.rearrange("(kc p) (j c) -> p kc j c", p=128, j=8)
            wt, wtok = load_w([((lambda t: t[:, 0:2816].rearrange("p (kc c) -> p kc c", kc=22)), s2[:, :, j, :])], 2816)
            wv = wt[:, 0:2816].rearrange("p (kc c) -> p kc c", kc=22)
            for bi, (o, n, s) in enumerate(blocks):
                pa = psA[bi % 2]
                for kc in range(22):
                    P.op("pe", "matmul", out=pa[:, 0:n], lhsT=wv[:, kc, :], rhs=A[:, kc, o:o + n], start=(kc == 0),
                         stop=(kc == 21), reads=[wtok, ("A", kc, o)], writes=[("psA", bi % 2)])
                P.op("dve", "scalar_tensor_tensor", out=hT[:, j, o:o + n], in0=pa[:, 0:n], scalar=modsb[:, 40 + j, s:s + 1],
                     in1=hT[:, j, o:o + n], op0=ALU.mult, op1=ALU.add, reads=[("psA", bi % 2), "modsb", ("h", j, o)],
                     writes=[("h", j, o)])
        if not last:
            for kc in range(8):
                for (o, n, s) in blocks:
                    P.dma(dq.next(), out[kc * 128:(kc + 1) * 128, gcol(o):gcol(o) + n], hT[:, kc, o:o + n],
                          reads=[("h", kc, o)], writes=[("out", kc, gcol(o))])
        else:
            ti = 0
            for (o, n, s) in blocks:
                _rms_stats(P, cst, hT, 8, o, n, lambda kc: ("h", kc, o), sq, psS, rstd, "")
                for kc in range(8):
                    P.op("dve", "scalar_tensor_tensor", out=hT[:, kc, o:o + n], in0=hT[:, kc, o:o + n], scalar=gvec[:, kc, 2:3],
                         in1=rstd[:, 0:n], op0=ALU.mult, op1=ALU.mult, reads=[("h", kc, o), "rstd", "gvec"], writes=[("h", kc, o)])
                for tt in range(n // 128):
                    for q4 in range(2):
                        pt = psT[ti % 2]
                        for kk in range(4):
                            kc = q4 * 4 + kk
                            P.op("pe", "transpose", out=pt[:, kk * 128:(kk + 1) * 128], in_=hT[:, kc, o + tt * 128:o + (tt + 1) * 128],
                                 identity=cst["ident"][:], reads=[("h", kc, o), "ident"], writes=[("psT", ti % 2)])
                        ot = tmp[ti % 2]
                        P.op("act" if ti % 2 else "dve", "activation" if ti % 2 else "tensor_copy", out=ot[:], in_=pt[:],
                             reads=[("psT", ti % 2)], writes=[("tmp", ti % 2)], **({"func": AF.Copy} if ti % 2 else {}))
                        r0 = l0 + o + tt * 128
                        P.dma(dq.next(), out[r0:r0 + 128, q4 * 512:(q4 + 1) * 512], ot[:], reads=[("tmp", ti % 2)],
                              writes=[("out", r0, q4)])
                        ti += 1
    outs = [k for k in P.lastw if isinstance(k, tuple) and k[0] == "out"]
    P.finish_wait("sp", outs)
    P.emit()
    return nc, P


def _c(a):
    return np.ascontiguousarray(a, dtype=np.float32)


def B_inmaps(l, last, h_lat, h_ctx, ysl, ysc, inp):
    maps = []
    shared = {
        "mod_w": _c(inp["mod_w"][l]), "mod_b": _c(inp["mod_b"][l][None]), "n1g": _c(inp["norm1_g"][l][None]),
        "n2g": _c(inp["norm2_g"][l][None]), "fng": _c(inp["final_norm_g"][None]),
        "wg": _c(inp["in_w"][l][:, GATE_OFF:]), "gate_b": _c(inp["gate_b"][l][None]), "wbr": _c(inp["branch_w"][l]),
        "wo": _c(inp["out_w"][l]), "w1": _c(inp["ffn_w1"][l]), "w3": _c(inp["ffn_w3"][l]), "w2": _c(inp["ffn_w2"][l]),
    }
    for core in range(8):
        b, j = core // 4, core % 4
        hl = h_lat[b, j * 2048:(j + 1) * 2048]
        yl = ysl[b, j * 2048:(j + 1) * 2048]
        if not last:
            hl = np.concatenate([hl, h_ctx[b, j * 64:(j + 1) * 64]], 0)
            yl = np.concatenate([yl, ysc[b, j * 64:(j + 1) * 64]], 0)
        m = dict(shared)
        m["hT"] = _c(hl.T)
        m["ysT"] = _c(yl.T)
        m["cvec"] = _c(np.stack([inp["c"][b], inp["c_ctx"]], 0))
        maps.append(m)
    return maps


def B_gather(last, outs):
    if last:
        return np.stack([np.concatenate(outs[0:4], 0), np.concatenate(outs[4:8], 0)], 0), None
    hl = np.stack([np.concatenate([o[:, :2048].T for o in outs[b * 4:(b + 1) * 4]], 0) for b in range(2)], 0)
    hc = np.stack([np.concatenate([o[:, 2048:].T for o in outs[b * 4:(b + 1) * 4]], 0) for b in range(2)], 0)
    return hl, hc


A_SLOTS = ["ret_q", "ret_qs", "ret_k", "ret_ks", "ret_v", "ret_g", "lru_x", "lru_y", "gdn_q", "gdn_k", "gdn_v", "gdn_z",
           "gdn_ab", "rw_r", "rw_k", "rw_v", "rw_gd", "rw_wd", "rw_ad"]
A_OUTS = [(n, n, 0, 128) for n in A_SLOTS if n not in ("gdn_ab", "rw_wd", "rw_ad")] + [
    ("gdn_af", "gdn_ab", 0, 1), ("gdn_abk", "gdn_ab", 1, 1), ("gdn_bf", "gdn_ab", 2, 1), ("gdn_bb", "gdn_ab", 3, 1),
    ("rw_wdf", "rw_wd", 0, 64), ("rw_wdb", "rw_wd", 64, 64), ("rw_adf", "rw_ad", 0, 64), ("rw_adb", "rw_ad", 64, 64)]
NSLOT = len(A_SLOTS)
A_BLOCKS = [(0, 256, 1)] + [(256 + i * 512, 512, 0) for i in range(16)]
GELU_C = 1.5957691216057308


def _a1_inproj(P, cst, PS, io, PT, outs_enabled):
    hT_in, wA, cvec, mod_w, mod_b, n1g = io["hT"], io["wA"], io["cvec"], io["mod_w"], io["mod_b"], io["n1g"]
    m0 = P.mark()
    wAb = P.carve([128, 8, NSLOT * 128], BF16)
    modsb = P.carve([128, 16, 2], F32)
    GS = P.carve([128, 8, 2, 2], F32)
    gv = P.carve([128, 8], F32)
    m1 = P.mark()
    wst = [P.carve([128, 4096], F32) for _ in range(2)]
    P.dma("sp", gv, n1g[0, :].rearrange("(k p) -> p k", p=128), writes=["gv"], allow_slow_non_contiguous=True)
    _mod_vectors(P, cst, cvec, mod_w, mod_b, 16, wst, PS[7][:, 0:96].rearrange("p (j s) -> p j s", s=2), modsb)
    for s in range(2):
        P.op("dve", "scalar_tensor_tensor", out=GS[:, :, s, 0], in0=modsb[:, 8:16, s], scalar=1.0, in1=gv, op0=ALU.add,
             op1=ALU.mult, reads=["modsb", "gv"], writes=["GS"])
        P.op("dve", "tensor_copy", out=GS[:, :, s, 1], in_=modsb[:, 0:8, s], reads=["modsb"], writes=["GS"])
    for kc in range(8):
        st = wst[kc % 2]
        P.dma("sp" if kc % 2 == 0 else "pool", st[:, 0:NSLOT * 128], wA[kc * 128:(kc + 1) * 128, :], writes=[("wst", kc % 2)])
        P.op("dve" if kc % 2 == 0 else "pool", "tensor_copy", out=wAb[:, kc, :], in_=st[:, 0:NSLOT * 128],
             reads=[("wst", kc % 2)], writes=[("wAb", kc)])
    P.release(m1)
    hblk = [P.carve([128, 8, 512], F32) for _ in range(2)]
    xu = [P.carve([128, 8, 512], BF16) for _ in range(2)]
    sq = P.carve([128, 8, 512], BF16)
    rstd = P.carve([128, 512], F32)
    tmp = [P.carve([128, 512], F32) for _ in range(2)]
    stg = [P.carve([128, 512], F32) for _ in range(4)]
    oi = 0
    for bi, (t0, n, s) in enumerate(A_BLOCKS):
        hb_, xb = hblk[bi % 2], xu[bi % 2]
        for kc in range(8):
            P.dma("sp" if kc % 2 == 0 else "pool", hb_[:, kc, 0:n], hT_in[kc * 128:(kc + 1) * 128, t0:t0 + n],
                  writes=[("hb", bi % 2, kc)])
        _rms_stats(P, cst, hb_, 8, 0, n, lambda kc: ("hb", bi % 2, kc), sq, PS[6], rstd, "")
        for kc in range(8):
            t = tmp[kc % 2]
            P.op("dve", "tensor_tensor", out=t[:, 0:n], in0=hb_[:, kc, 0:n], in1=rstd[:, 0:n], op=ALU.mult,
                 reads=[("hb", bi % 2, kc), "rstd"], writes=[("tmp", kc % 2)])
            P.op("pool", "tensor_scalar", out=xb[:, kc, 0:n], in0=t[:, 0:n], scalar1=GS[:, kc, s, 0:1], scalar2=GS[:, kc, s, 1:2],
                 op0=ALU.mult, op1=ALU.add, reads=[("tmp", kc % 2), "GS"], writes=[("xu", bi % 2, kc)])
        for (name, slot, c0, M) in A_OUTS:
            if name not in outs_enabled:
                continue
            col = A_SLOTS.index(slot) * 128 + c0
            ps = PS[oi % 4]
            for kc in range(8):
                P.op("pe", "matmul", out=ps[0:M, 0:n], lhsT=wAb[:, kc, col:col + M], rhs=xb[:, kc, 0:n], start=(kc == 0),
                     stop=(kc == 7), reads=[("wAb", kc), ("xu", bi % 2, kc)], writes=[("PS", oi % 4)])
            sg = stg[oi % 4]
            if oi % 2 == 0:
                P.op("act", "activation", out=sg[0:M, 0:n], in_=ps[0:M, 0:n], func=AF.Copy, reads=[("PS", oi % 4)],
                     writes=[("stg", oi % 4)])
            else:
                P.op("dve", "tensor_copy", out=sg[0:M, 0:n], in_=ps[0:M, 0:n], reads=[("PS", oi % 4)], writes=[("stg", oi % 4)])
            P.dma("sp" if oi % 2 == 0 else "pool", PT[name][:, t0:t0 + n], sg[0:M, 0:n], reads=[("stg", oi % 4)],
                  writes=[("PT", name, bi)])
            oi += 1
    ptw = {k: list(v) for k, v in P.lastw.items() if isinstance(k, tuple) and k[0] == "PT"}
    P.release(m0)


def _lru(P, cst, PS, io, PT, ysT):
    m0 = P.mark()
    cw = P.carve([128, 4], F32)
    cb = P.carve([128, 1], F32)
    gw = P.carve([128, 4, 128], F32)
    gb = P.carve([128, 4], F32)
    lam = P.carve([128, 2], F32)
    L8 = P.carve([128, 2], F32)
    L16 = P.carve([128, 2], F32)
    onec = P.carve([128, 1], F32)
    hbk = P.carve([128, TTOT], F32)
    P.op("pool", "memset", ap=onec, constant=1.0, writes=["onec"])
    P.dma("sp", cw, io["lru_cw"].rearrange("j c -> c j"), writes=["cw"], allow_slow_non_contiguous=True)
    P.dma("sp", cb, io["lru_cb"].rearrange("o c -> c o"), writes=["cb"], allow_slow_non_contiguous=True)
    P.dma("sp", gw, io["lru_gw"].rearrange("g c z -> c g z"), writes=["gw"])
    P.dma("sp", gb, io["lru_gb"].rearrange("g z -> z g"), writes=["gb"], allow_slow_non_contiguous=True)
    P.dma("sp", lam, io["lru_lam"].rearrange("d z -> z d"), writes=["lam"], allow_slow_non_contiguous=True)
    P.op("act", "activation", out=L8, in_=lam, func=AF.Exp, scale=-1.0, reads=["lam"], writes=["L8"])
    P.op("act", "activation", out=L8, in_=L8, func=AF.Ln, bias=onec, reads=["L8", "onec"], writes=["L8"])
    P.op("dve", "tensor_scalar", out=L16, in0=L8, scalar1=-16.0, scalar2=None, op0=ALU.mult, reads=["L8"], writes=["L16"])
    P.op("dve", "tensor_scalar", out=L8, in0=L8, scalar1=-8.0, scalar2=None, op0=ALU.mult, reads=["L8"], writes=["L8"])
    xh = [P.carve([128, 516], F32) for _ in range(2)]
    xc = [P.carve([128, 512], F32) for _ in range(2)]
    yb = [P.carve([128, 512], F32) for _ in range(2)]
    gr = P.carve([128, 512], F32)
    gi = P.carve([128, 512], F32)
    av = P.carve([128, 512], F32)
    bx = P.carve([128, 512], F32)
    hf = [P.carve([128, 512], F32) for _ in range(2)]
    ot = [P.carve([128, 512], F32) for _ in range(2)]
    it = 0
    for d in (1, 0):
        order = [A_BLOCKS[0]] + (A_BLOCKS[:0:-1] if d == 1 else A_BLOCKS[1:])
        prev = None
        for bi, (t0, n, s) in enumerate(order):
            k2 = it % 2
            it += 1
            x_, c_ = xh[k2], xc[k2]
            seg0, seg1 = (0, 256) if s else (256, TTOT)
            lo, hi = max(t0 - 2, seg0), min(t0 + n + 1, seg1)
            if lo > t0 - 2 or hi < t0 + n + 1:
                P.op("pool", "memset", ap=x_[:, 0:n + 3], constant=0.0, writes=[("xh", k2)])
            P.dma("sp", x_[:, lo - (t0 - 2):hi - (t0 - 2)], PT["lru_x"][:, lo:hi], writes=[("xh", k2)])
            P.op("dve", "tensor_scalar", out=c_[:, 0:n], in0=x_[:, 0:n], scalar1=cw[:, 0:1], scalar2=cb[:, 0:1], op0=ALU.mult,
                 op1=ALU.add, reads=[("xh", k2), "cw", "cb"], writes=[("xc", k2)])
            for j in range(1, 4):
                P.op("dve", "scalar_tensor_tensor", out=c_[:, 0:n], in0=x_[:, j:j + n], scalar=cw[:, j:j + 1], in1=c_[:, 0:n],
                     op0=ALU.mult, op1=ALU.add, reads=[("xh", k2), "cw", ("xc", k2)], writes=[("xc", k2)])
            for g, gt in ((0, gr), (1, gi)):
                ps = PS[g]
                P.op("pe", "matmul", out=ps[:, 0:n], lhsT=gw[:, d * 2 + g, :], rhs=c_[:, 0:n], start=True, stop=True,
                     reads=["gw", ("xc", k2)], writes=[("PS", g)])
                P.op("act", "activation", out=gt[:, 0:n], in_=ps[:, 0:n], func=AF.Sigmoid, bias=gb[:, d * 2 + g:d * 2 + g + 1],
                     reads=[("PS", g), "gb"], writes=[("g", g)])
            P.op("act", "activation", out=av[:, 0:n], in_=gr[:, 0:n], func=AF.Exp, scale=L8[:, d:d + 1], reads=[("g", 0), "L8"],
                 writes=["av"])
            P.op("act", "activation", out=gr[:, 0:n], in_=gr[:, 0:n], func=AF.Exp, scale=L16[:, d:d + 1], reads=[("g", 0), "L16"],
                 writes=[("g", 0)])
            P.op("dve", "tensor_scalar", out=gr[:, 0:n], in0=gr[:, 0:n], scalar1=-1.0, scalar2=1.0, op0=ALU.mult, op1=ALU.add,
                 reads=[("g", 0)], writes=[("g", 0)])
            P.op("act", "activation", out=gr[:, 0:n], in_=gr[:, 0:n], func=AF.Sqrt, reads=[("g", 0)], writes=[("g", 0)])
            P.op("pool", "tensor_tensor", out=gi[:, 0:n], in0=gi[:, 0:n], in1=c_[:, 0:n], op=ALU.mult, reads=[("g", 1), ("xc", k2)],
                 writes=[("g", 1)])
            P.op("dve", "tensor_tensor", out=bx[:, 0:n], in0=gi[:, 0:n], in1=gr[:, 0:n], op=ALU.mult, reads=[("g", 1), ("g", 0)],
                 writes=["bx"])
            if d == 1:
                init = 0.0 if prev is None else hbk[:, prev:prev + 1]
                P.op("dve", "tensor_tensor_scan", out=hbk[:, t0:t0 + n][:, ::-1], data0=av[:, 0:n][:, ::-1], data1=bx[:, 0:n][:, ::-1],
                     initial=init, op0=ALU.mult, op1=ALU.add, reads=["av", "bx", "hbk_c"], writes=[("hbk", t0), "hbk_c"])
                prev = t0
            else:
                h_ = hf[k2]
                init = 0.0 if prev is None else hf[1 - k2][:, prev - 1:prev]
                P.op("dve", "tensor_tensor_scan", out=h_[:, 0:n], data0=av[:, 0:n], data1=bx[:, 0:n], initial=init, op0=ALU.mult,
                     op1=ALU.add, reads=["av", "bx", ("hf", 1 - k2)], writes=[("hf", k2)])
                prev = n
                y_ = yb[k2]
                P.dma("pool", y_[:, 0:n], PT["lru_y"][:, t0:t0 + n], writes=[("yb", k2)])
                o_ = ot[k2]
                P.op("pool", "tensor_tensor", out=o_[:, 0:n], in0=y_[:, 0:n], in1=y_[:, 0:n], op=ALU.mult, reads=[("yb", k2)],
                     writes=[("ot", k2)])
                P.op("pool", "tensor_scalar", out=o_[:, 0:n], in0=o_[:, 0:n], scalar1=0.044715, scalar2=1.0, op0=ALU.mult,
                     op1=ALU.add, reads=[("ot", k2)], writes=[("ot", k2)])
                P.op("pool", "tensor_tensor", out=o_[:, 0:n], in0=o_[:, 0:n], in1=y_[:, 0:n], op=ALU.mult, reads=[("ot", k2), ("yb", k2)],
                     writes=[("ot", k2)])
                P.op("act", "activation", out=o_[:, 0:n], in_=o_[:, 0:n], func=AF.Sigmoid, scale=GELU_C, reads=[("ot", k2)],
                     writes=[("ot", k2)])
                P.op("pool", "tensor_tensor", out=o_[:, 0:n], in0=o_[:, 0:n], in1=y_[:, 0:n], op=ALU.mult, reads=[("ot", k2), ("yb", k2)],
                     writes=[("ot", k2)])
                P.op("dve", "tensor_tensor", out=y_[:, 0:n], in0=h_[:, 0:n], in1=hbk[:, t0:t0 + n], op=ALU.add,
                     reads=[("hf", k2), ("hbk", t0)], writes=[("yb", k2)])
                P.op("dve", "tensor_tensor", out=o_[:, 0:n], in0=o_[:, 0:n], in1=y_[:, 0:n], op=ALU.mult, reads=[("ot", k2), ("yb", k2)],
                     writes=[("ot", k2)])
                P.dma("sp", ysT[128:256, t0:t0 + n], o_[:, 0:n], reads=[("ot", k2)], writes=[("ysT", 1, t0)])
    P.release(m0)


def build_A(enabled=("inproj", "lru", "ret", "gdn", "rwkv")):
    nc = bass.Bass("TRN2", target_bir_lowering=False)
    io = {}

    def inp(name, shape):
        io[name] = nc.dram_tensor(name, list(shape), F32, kind="ExternalInput").ap()
    inp("hT", [D, TTOT]); inp("wA", [D, NSLOT * 128]); inp("cvec", [2, D]); inp("mod_w", [D, 6 * D]); inp("mod_b", [1, 6 * D])
    inp("n1g", [1, D])
    inp("lru_cw", [4, 128]); inp("lru_cb", [1, 128]); inp("lru_gw", [4, 128, 128]); inp("lru_gb", [4, 128]); inp("lru_lam", [2, 128])
    inp("cA", [128, CA_COLS]); inp("ropeC", [128, TTOT]); inp("ropeS", [128, TTOT]); inp("ret_de", [1, 2])
    inp("gdn_cw", [3, 4, 128]); inp("gdn_sc", [1, 4]); inp("gdn_ng", [1, 128])
    inp("rw_pc", [13, 128]); inp("rw_pc64", [4, 64]); inp("rw_w2", [2, 64, 128]); inp("rw_a2", [2, 64, 128]); inp("rw_g2", [128, 128])
    OB = {m: nc.dram_tensor("ob_" + m, [TTOT, 128], F32, kind="Internal").ap() for m in ("ret", "gdn", "rwkv")}
    ysT = nc.dram_tensor("ysT", [512, TTOT], F32, kind="ExternalOutput").ap()
    ptkind = "ExternalOutput" if "debug_pt" in enabled else "Internal"
    PT = {n: nc.dram_tensor("pt_" + n, [M, TTOT], F32, kind=ptkind).ap() for (n, _, _, M) in A_OUTS}
    P = Prog(nc)
    cst = _common_consts(P)
    PS = [P.ps(f"bank{i}", [128, 512], F32) for i in range(8)]
    P.arena_init(48000)
    need = set()
    if "lru" in enabled:
        need |= {"lru_x", "lru_y"}
    if "ret" in enabled:
        need |= {"ret_q", "ret_qs", "ret_k", "ret_ks", "ret_v", "ret_g"}
    if "gdn" in enabled:
        need |= {"gdn_q", "gdn_k", "gdn_v", "gdn_z", "gdn_af", "gdn_abk", "gdn_bf", "gdn_bb"}
    if "rwkv" in enabled:
        need |= {"rw_r", "rw_k", "rw_v", "rw_gd", "rw_wdf", "rw_wdb", "rw_adf", "rw_adb"}
    if "debug_pt" in enabled:
        need = set(n for (n, _, _, _) in A_OUTS)
    _a1_inproj(P, cst, PS, io, PT, need)
    if "lru" in enabled:
        _lru(P, cst, PS, io, PT, ysT)
    if "ret" in enabled:
        _ret(P, cst, PS, io, PT, ysT, OB["ret"])
    if "gdn" in enabled:
        _gdn(P, cst, PS, io, PT, ysT, OB["gdn"])
    if "rwkv" in enabled:
        _rwkv(P, cst, PS, io, PT, ysT, OB["rwkv"])
    P.barrier()
    P.emit()
    return nc, P


def A_weight_cols(h):
    def rng(a, n=128):
        return list(range(a, a + n))
    cols = {}
    cols["ret_q"] = rng(0 + h * 128)
    cols["ret_qs"] = rng(h * 128 + 64, 64) + rng(h * 128, 64)
    cols["ret_k"] = rng(512 + h * 128)
    cols["ret_ks"] = rng(512 + h * 128 + 64, 64) + rng(512 + h * 128, 64)
    cols["ret_v"] = rng(1024 + h * 128)
    cols["ret_g"] = rng(1536 + h * 128)
    cols["lru_x"] = rng(2048 + h * 128)
    cols["lru_y"] = rng(2560 + h * 128)
    cols["gdn_q"] = rng(3072 + h * 128)
    cols["gdn_k"] = rng(3584 + h * 128)
    cols["gdn_v"] = rng(4096 + h * 128)
    cols["gdn_z"] = rng(4608 + h * 128)
    cols["gdn_ab"] = [5120 + h, 5120 + 4 + h, 5128 + h, 5128 + 4 + h] + [-1] * 124
    R0 = 5136
    cols["rw_r"] = rng(R0 + h * 128)
    cols["rw_k"] = rng(R0 + 512 + h * 128)
    cols["rw_v"] = rng(R0 + 1024 + h * 128)
    cols["rw_gd"] = rng(R0 + 1536)
    cols["rw_wd"] = rng(R0 + 1664)
    cols["rw_ad"] = rng(R0 + 1792)
    idx = []
    for s in A_SLOTS:
        idx += cols[s]
    return np.array(idx)


def A_inmaps(l, h_lat, h_ctx, inp):
    maps = []
    in_w = np.concatenate([inp["in_w"][l], np.zeros((D, 1), np.float32)], 1)
    for core in range(8):
        b, h = core // 4, core % 4
        m = {}
        m["hT"] = _c(np.concatenate([h_ctx[b], h_lat[b]], 0).T)
        m["wA"] = _c(in_w[:, A_weight_cols(h)])
        m["cvec"] = _c(np.stack([inp["c"][b], inp["c_ctx"]], 0))
        m["mod_w"] = _c(inp["mod_w"][l]); m["mod_b"] = _c(inp["mod_b"][l][None]); m["n1g"] = _c(inp["norm1_g"][l][None])
        sl = slice(h * 128, (h + 1) * 128)
        m["lru_cw"] = _c(inp["lru_conv_w"][l][:, sl]); m["lru_cb"] = _c(inp["lru_conv_b"][l][None, sl])
        m["lru_gw"] = _c(inp["lru_gate_w"][l][:, :, h].reshape(4, 128, 128))
        m["lru_gb"] = _c(inp["lru_gate_b"][l][:, :, sl].reshape(4, 128)); m["lru_lam"] = _c(inp["lru_lambda"][l][:, sl])
        m["cA"] = make_cA(); m["ropeC"], m["ropeS"] = make_rope()
        m["ret_de"] = _c(inp["ret_decay_exp"][l][:, h][None])
        gcw = inp["gdn_conv_w"][l]
        m["gdn_cw"] = _c(np.stack([gcw[:, i * 512 + h * 128:i * 512 + (h + 1) * 128] for i in range(3)], 0))
        m["gdn_sc"] = _c(np.concatenate([inp["gdn_a_log"][l][:, h], inp["gdn_dt_bias"][l][:, h]])[None])
        m["gdn_ng"] = _c(inp["gdn_norm_g"][l][None])
        mu = inp["rwkv_mu"][l]
        m["rw_pc"] = _c(np.stack([mu[0:512][sl], mu[512:1024][sl], mu[1024:1536][sl], mu[1536:1664], inp["rwkv_w0"][l][0][sl],
                                  inp["rwkv_w0"][l][1][sl], inp["rwkv_a0"][l][0][sl], inp["rwkv_a0"][l][1][sl], inp["rwkv_k_k"][l][sl],
                                  inp["rwkv_k_a"][l][sl], inp["rwkv_r_k"][l].reshape(512)[sl], inp["rwkv_ln_g"][l][sl],
                                  inp["rwkv_ln_b"][l][sl]], 0))
        m["rw_pc64"] = _c(np.stack([mu[1664:1728], mu[1728:1792], mu[1792:1856], mu[1856:1920]], 0))
        m["rw_w2"] = _c(inp["rwkv_w2"][l][:, :, sl]); m["rw_a2"] = _c(inp["rwkv_a2"][l][:, :, sl]); m["rw_g2"] = _c(inp["rwkv_g2"][l][:, sl])
        maps.append(m)
    return maps


CH = 64
CORE_FP32R = False


class Core:
    def __init__(self, P, cst, PS, NH, has_delta, identC):
        self.P, self.cst, self.PS, self.NH, self.hd, self.delta = P, cst, PS, NH, 128 // NH, has_delta
        self.identC = identC
        self.W = NH * CH
        self.G = 512 // self.W
        self.sets = []
        for i in range(2):
            d = {}
            for nm in (("Pm", "Nm", "Pm2", "Nm2", "Xa", "Aak", "Aqb", "Aqk") if has_delta else ("Aqk",)):
                d[nm] = P.carve([CH, 512], F32)
            self.sets.append(d)
        self.xu = [{nm: P.carve([CH, 128], F32) for nm in ("X", "U")} for _ in range(2)]
        self.kb = 0
        self.kc = 0

    def _mm(self, out, otok, lhsT, ltok, rhs, rtok, start=True, stop=True, fast=True):
        if CORE_FP32R and fast:
            lhsT = lhsT.bitcast(mybir.dt.float32r)
            rhs = rhs.bitcast(mybir.dt.float32r)
        self.P.op("pe", "matmul", out=out, lhsT=lhsT, rhs=rhs, start=start, stop=stop, reads=[ltok, rtok], writes=[otok])

    def _L(self, w, nm, h):
        return (w[nm][0], w[nm][1]) if self.NH == 1 else w[nm + "m"][h]

    def prep_gen(self, kb, ws, MexT, MexTok, MinT, MinTok):
        P, PS, NH, W = self.P, self.PS, self.NH, self.W
        T = self.sets[kb % 2]
        tk = lambda nm: ("core", nm, kb % 2)
        G = len(ws)
        GW = G * W
        MexTok = list(MexTok) if isinstance(MexTok, list) else [MexTok]
        MinTok = list(MinTok) if isinstance(MinTok, list) else [MinTok]
        bA, bB = PS[2], PS[3]
        tA, tB = ("bank", 2), ("bank", 3)
        rA, rB = bA[0:CH, 0:GW], bB[0:CH, 0:GW]
        cols = [(g, h, slice(g * W + h * CH, g * W + (h + 1) * CH)) for g in range(G) for h in range(NH)]
        mm = self._mm
        if self.delta:
            for (g, h, c) in cols:
                l, lt = self._L(ws[g], "bT", h)
                mm(rA[:, c], tA, l, lt, ws[g]["aT"][0], ws[g]["aT"][1])
            P.op("dve", "tensor_tensor", out=T["Pm"][:, 0:GW], in0=rA, in1=MexT, op=ALU.mult, reads=MexTok, writes=[tk("Pm"), tA])
            yield
            for (g, h, c) in cols:
                P.op("pe", "transpose", out=rB[:, c], in_=T["Pm"][:, c], identity=self.cst["ident"][0:CH, 0:CH],
                     reads=[tk("Pm"), "ident"], writes=[tB])
            P.op("act", "activation", out=T["Nm"][:, 0:GW], in_=rB, func=AF.Copy, reads=[], writes=[tk("Nm"), tB])
            P.op("pool", "tensor_tensor", out=T["Xa"][:, 0:GW], in0=T["Pm"][:, 0:GW], in1=self.identC[:, 0:GW], op=ALU.add,
                 reads=[tk("Pm"), "cAs"], writes=[tk("Xa")])
            yield
            Pm, Nm, Pm2, Nm2 = "Pm", "Nm", "Pm2", "Nm2"
            for lvl in range(1, 6):
                if lvl < 5:
                    for (g, h, c) in cols:
                        mm(rA[:, c], tA, T[Nm][:, c], tk(Nm), T[Pm][:, c], tk(Pm))
                    P.op("act", "activation", out=T[Pm2][:, 0:GW], in_=rA, func=AF.Copy, reads=[], writes=[tk(Pm2), tA])
                for (g, h, c) in cols:
                    mm(rB[:, c], tB, T[Pm][:, c], tk(Pm), T[Nm][:, c], tk(Nm))
                P.op("dve", "tensor_copy", out=T[Nm2][:, 0:GW], in_=rB, reads=[], writes=[tk(Nm2), tB])
                yield
                for (g, h, c) in cols:
                    mm(rA[:, c], tA, T[Nm2][:, c], tk(Nm2), T["Xa"][:, c], tk("Xa"))
                P.op("dve", "tensor_tensor", out=T["Xa"][:, 0:GW], in0=rA, in1=T["Xa"][:, 0:GW], op=ALU.add, reads=[],
                     writes=[tk("Xa"), tA])
                yield
                Pm, Pm2 = Pm2, Pm
                Nm, Nm2 = Nm2, Nm
            for (g, h, c) in cols:
                l, lt = self._L(ws[g], "kT", h)
                mm(rB[:, c], tB, l, lt, ws[g]["aT"][0], ws[g]["aT"][1])
            P.op("dve", "tensor_tensor", out=T["Aak"][:, 0:GW], in0=rB, in1=MexT, op=ALU.mult, reads=MexTok, writes=[tk("Aak"), tB])
            yield
            for (g, h, c) in cols:
                l, lt = self._L(ws[g], "bT", h)
                mm(rA[:, c], tA, l, lt, ws[g]["qT"][0], ws[g]["qT"][1])
            P.op("dve", "tensor_tensor", out=T["Aqb"][:, 0:GW], in0=rA, in1=MinT, op=ALU.mult, reads=MinTok, writes=[tk("Aqb"), tA])
            yield
        for (g, h, c) in cols:
            l, lt = self._L(ws[g], "kT", h)
            mm(rB[:, c], tB, l, lt, ws[g]["qT"][0], ws[g]["qT"][1])
        P.op("dve", "tensor_tensor", out=T["Aqk"][:, 0:GW], in0=rB, in1=MinT, op=ALU.mult, reads=MinTok, writes=[tk("Aqk"), tB])
        yield

    def recur_gen(self, kb, g, w, S, Stok, O, Otok):
        P, PS, NH, hd, W = self.P, self.PS, self.NH, self.hd, self.W
        T = self.sets[kb % 2]
        tk = lambda nm: ("core", nm, kb % 2)
        kc = self.kc
        self.kc += 1
        XU = self.xu[kc % 2]
        xt = lambda nm: ("coreXU", nm, kc % 2)
        bC, bD, bE = PS[4], PS[5], PS[6]
        tC, tD, tE = ("bank", 4), ("bank", 5), ("bank", 6)
        hsl = [slice(h * hd, (h + 1) * hd) for h in range(NH)]
        csl = [slice(g * W + h * CH, g * W + (h + 1) * CH) for h in range(NH)]
        xr, ur = bC[0:CH, 0:128], bC[0:CH, 128:256]
        orr = bD[0:CH, 0:128]
        sr = bE[:, 0:hd]
        mm = self._mm
        A = lambda nm: w[nm][0]
        Tk = lambda nm: w[nm][1]
        if self.delta:
            for h in range(NH):
                l, lt = self._L(w, "aTs", h)
                mm(xr[:, hsl[h]], tC, l, lt, S, Stok, True, False)
                mm(xr[:, hsl[h]], tC, T["Aak"][:, csl[h]], tk("Aak"), A("V")[:, hsl[h]], Tk("V"), False, True)
            P.op("act", "activation", out=XU["X"], in_=xr, func=AF.Copy, reads=[], writes=[xt("X"), tC])
            yield
            for h in range(NH):
                mm(ur[:, hsl[h]], tC, T["Xa"][:, csl[h]], tk("Xa"), XU["X"][:, hsl[h]], xt("X"))
            P.op("act", "activation", out=XU["U"], in_=ur, func=AF.Copy, reads=[], writes=[xt("U"), tC])
            yield
        for h in range(NH):
            if self.delta:
                mm(sr[hsl[h], :], tE, A("Bst")[:, hsl[h]], Tk("Bst"), XU["U"][:, hsl[h]], xt("U"), True, False, fast=(NH == 1))
            mm(sr[hsl[h], :], tE, A("Kst")[:, hsl[h]], Tk("Kst"), A("V")[:, hsl[h]], Tk("V"), not self.delta, True, fast=(NH == 1))
        for h in range(NH):
            l, lt = self._L(w, "qTs", h)
            mm(orr[:, hsl[h]], tD, l, lt, S, Stok, True, False)
            if self.delta:
                mm(orr[:, hsl[h]], tD, T["Aqb"][:, csl[h]], tk("Aqb"), XU["U"][:, hsl[h]], xt("U"), False, False)
            mm(orr[:, hsl[h]], tD, T["Aqk"][:, csl[h]], tk("Aqk"), A("V")[:, hsl[h]], Tk("V"), False, True)
        P.op("dve", "scalar_tensor_tensor", out=S, in0=S, scalar=A("cs"), in1=sr, op0=ALU.mult, op1=ALU.add,
             reads=[Tk("cs")], writes=[Stok, tE])
        P.op("act", "activation", out=O, in_=orr, func=AF.Copy, reads=[], writes=[Otok, tD])
        yield


def interleave(a, b):
    live = [g for g in (a, b) if g is not None]
    while live:
        for g in list(live):
            try:
                next(g)
            except StopIteration:
                live.remove(g)


RW_BLOCKS = [(0, 256, 1)] + [(256 + i * 256, 256, 0) for i in range(32)]


def _dir_chunks(d, blocks=None):
    blocks = A_BLOCKS if blocks is None else blocks
    order = [blocks[0]] + (blocks[:0:-1] if d == 1 else blocks[1:])
    res = []
    for (t0, n, s) in order:
        offs = list(range(0, n, CH))
        if d == 1:
            offs = offs[::-1]
        res.append((t0, n, s, offs))
    return res


CA = {}
_o = 0
for _nm, _n in (("RELF", 64), ("RELB", 64), ("MASKF", 64), ("MASKB", 64), ("MSTRF", 64), ("MSTRB", 64), ("POS1F", 512), ("POS1B", 512),
                ("KDF", 512), ("KDB", 512), ("RSTF", 512), ("RSTB", 512), ("NEGINF", 64), ("NEGINB", 64), ("NEGEXF", 64),
                ("NEGEXB", 64), ("IDC", 512), ("BLK", 128), ("MSTRF4", 512), ("MSTRB4", 512), ("MASKF4", 512), ("MASKB4", 512)):
    CA[_nm] = (_o, _n)
    _o += _n
CA_COLS = _o


def make_cA():
    c = np.zeros((128, CA_COLS), np.float32)
    s = np.arange(64)[:, None]
    t = np.arange(64)[None, :]
    def put(nm, a):
        o, n = CA[nm]
        c[:a.shape[0], o:o + n] = a
    put("RELF", np.where(s <= t, t - s, 0)); put("RELB", np.where(s >= t, s - t, 0))
    put("MASKF", (s <= t) * 1.0); put("MASKB", (s >= t) * 1.0)
    put("MSTRF", (s < t) * 1.0); put("MSTRB", (s > t) * 1.0)
    tt = np.arange(512)[None, :] % 64
    put("POS1F", np.broadcast_to(tt + 1.0, (128, 512))); put("POS1B", np.broadcast_to(64.0 - tt, (128, 512)))
    put("KDF", np.broadcast_to(63.0 - tt, (128, 512))); put("KDB", np.broadcast_to(tt * 1.0, (128, 512)))
    put("RSTF", np.broadcast_to((tt != 0) * 1.0, (128, 512))); put("RSTB", np.broadcast_to((tt != 63) * 1.0, (128, 512)))
    NEG = -30000.0
    put("NEGINF", np.where(s <= t, 0.0, NEG)); put("NEGINB", np.where(s >= t, 0.0, NEG))
    put("NEGEXF", np.where(s < t, 0.0, NEG)); put("NEGEXB", np.where(s > t, 0.0, NEG))
    put("IDC", np.concatenate([np.eye(64)] * 8, 1))
    put("MSTRF4", np.concatenate([(s < t) * 1.0] * 8, 1)); put("MSTRB4", np.concatenate([(s > t) * 1.0] * 8, 1))
    put("MASKF4", np.concatenate([(s <= t) * 1.0] * 8, 1)); put("MASKB4", np.concatenate([(s >= t) * 1.0] * 8, 1))
    blk = np.zeros((128, 128)); blk[:64, :64] = 1; blk[64:, 64:] = 1
    put("BLK", blk)
    return c


def make_rope():
    tpos = np.arange(SEQ)
    row = (tpos // 64).astype(np.float32)
    col = (tpos % 64).astype(np.float32)
    inv = (10000.0 ** (-np.arange(32, dtype=np.float32) / 32)).astype(np.float32)
    ang = np.concatenate([row[:, None] * inv, col[:, None] * inv], -1)
    cos, sin = np.cos(ang).astype(np.float32), np.sin(ang).astype(np.float32)
    CC = np.ones((128, TTOT), np.float32)
    SS = np.zeros((128, TTOT), np.float32)
    CC[:64, CTX:] = cos.T; CC[64:, CTX:] = cos.T
    SS[:64, CTX:] = -sin.T; SS[64:, CTX:] = sin.T
    return CC, SS


def _ld_const(P, io, cAs, nm, rows=128):
    return cAs[nm][0:rows, :]


def _load_consts(P, io, names):
    tot = sum(CA[nm][1] for nm in names)
    tile = P.carve([128, tot], F32)
    res = {}
    items = []
    o = 0
    for i, nm in enumerate(names):
        so, n = CA[nm]
        res[nm] = tile[:, o:o + n]
        items.append(("sp" if i % 2 == 0 else "pool", tile[:, o:o + n], io["cA"][:, so:so + n]))
        o += n
    P.dma_group(items, writes=["cAs"])
    return res


def _ret(P, cst, PS, io, PT, ysT, OB):
    m0 = P.mark()
    cAs = _load_consts(P, io, ("RELF", "RELB", "MASKF", "MASKB", "POS1F", "POS1B", "KDF", "KDB"))
    lg = P.carve([128, 2], F32)
    onec = P.carve([128, 1], F32)
    P.op("pool", "memset", ap=onec, constant=1.0, writes=["onec"])
    P.dma("sp", lg, io["ret_de"].partition_broadcast(128), writes=["lg"])
    P.op("act", "activation", out=lg, in_=lg, func=AF.Exp, scale=-float(np.log(2.0)), reads=["lg"], writes=["lg"])
    P.op("act", "activation", out=lg, in_=lg, func=AF.Ln, scale=-1.0, bias=onec, reads=["lg", "onec"], writes=["lg"])
    MinT = [P.carve([CH, CH], F32) for _ in range(2)]
    POSQ = [P.carve([128, 512], F32) for _ in range(2)]
    KDEC = [P.carve([128, 512], F32) for _ in range(2)]
    csc = P.carve([128, 2], F32)
    P.op("act", "activation", out=csc, in_=lg, func=AF.Exp, scale=float(CH), reads=["lg"], writes=["csc"])
    for d in range(2):
        sfx = "FB"[d]
        P.op("act", "activation", out=MinT[d], in_=_ld_const(P, io, cAs, "REL" + sfx, 64), func=AF.Exp, scale=lg[0:CH, d:d + 1],
             reads=["cAs", "lg"], writes=[("MinT", d)])
        P.op("dve", "scalar_tensor_tensor", out=MinT[d], in0=MinT[d], scalar=128.0 ** -0.5, in1=_ld_const(P, io, cAs, "MASK" + sfx, 64),
             op0=ALU.mult, op1=ALU.mult, reads=[("MinT", d), "cAs"], writes=[("MinT", d)])
        P.op("act", "activation", out=POSQ[d], in_=_ld_const(P, io, cAs, "POS1" + sfx), func=AF.Exp, scale=lg[:, d:d + 1],
             reads=["cAs", "lg"], writes=[("POSQ", d)])
        P.op("act", "activation", out=KDEC[d], in_=_ld_const(P, io, cAs, "KD" + sfx), func=AF.Exp, scale=lg[:, d:d + 1],
             reads=["cAs", "lg"], writes=[("KDEC", d)])
        P.op("dve", "tensor_scalar", out=KDEC[d], in0=KDEC[d], scalar1=128.0 ** -0.5, scalar2=None, op0=ALU.mult,
             reads=[("KDEC", d)], writes=[("KDEC", d)])
    core = Core(P, cst, PS, 1, False, None)
    MinT8 = [P.carve([CH, 8, CH], F32) for _ in range(2)]
    for d in range(2):
        for r_ in range(8):
            P.op("pool" if r_ % 2 else "dve", "tensor_copy", out=MinT8[d][:, r_, :], in_=MinT[d], reads=[("MinT", d)], writes=[("MinT8", d, r_)])
    S = P.carve([128, 128], F32)
    ld = {nm: [P.carve([128, 512], F32) for _ in range(2)] for nm in ("q", "qs", "k", "ks", "v", "cc", "ss", "g")}
    qr = [P.carve([128, 512], F32) for _ in range(2)]
    kr = [P.carve([128, 512], F32) for _ in range(2)]
    qsd = [P.carve([128, 512], F32) for _ in range(2)]
    kdc = [P.carve([128, 512], F32) for _ in range(2)]
    outb = [P.carve([128, 512], F32) for _ in range(2)]
    Vt = [[P.carve([CH, 128], F32) for _ in range(8)] for _ in range(2)]
    Kst = [[P.carve([CH, 128], F32) for _ in range(8)] for _ in range(2)]
    Ot = [P.carve([CH, 128], F32) for _ in range(4)]
    Ob = [P.carve([CH, 128], F32) for _ in range(4)]
    ssq = [P.carve([CH, 2], F32) for _ in range(4)]
    junk = P.carve([CH, 128], F32)
    epsC = cst["epsb"]
    bi_ = 0
    cctr = [0]
    kbc = [0]

    def recur_batch(kb, batch, d, b2, t0, n, lastbatch):
        for gi, (w, c0, ci_) in enumerate(batch):
            c4 = cctr[0] % 4
            cctr[0] += 1
            cs_ = slice(c0, c0 + CH)
            yield from core.recur_gen(kb, gi, w, S, "S", Ot[c4], ("Ot", c4))
            tg = t0 + c0
            if d == 1:
                P.dma("sp", OB[tg:tg + CH, :], Ot[c4], reads=[("Ot", c4)], writes=[("OB", tg)])
            else:
                P.dma("sp", Ob[c4], OB[tg:tg + CH, :], reads=[("OB", tg)], writes=[("Ob", c4)])
                P.op("dve", "tensor_tensor", out=Ot[c4], in0=Ot[c4], in1=Ob[c4], op=ALU.add, reads=[("Ot", c4), ("Ob", c4)],
                     writes=[("Ot", c4)])
                P.op("act", "activation", out=junk, in_=Ot[c4], func=AF.Square, accum_out=ssq[c4][:, 0:1], reads=[("Ot", c4)],
                     writes=["junk", ("ssq", c4)])
                P.op("act", "activation", out=ssq[c4][:, 1:2], in_=ssq[c4][:, 0:1], func=AF.Ln, scale=1.0 / 128, bias=epsC[0:CH, :],
                     reads=[("ssq", c4), "epsb"], writes=[("ssq", c4)])
                P.op("act", "activation", out=ssq[c4][:, 1:2], in_=ssq[c4][:, 1:2], func=AF.Exp, scale=-0.5, reads=[("ssq", c4)],
                     writes=[("ssq", c4)])
                yield
                P.op("dve", "tensor_scalar", out=Ot[c4], in0=Ot[c4], scalar1=ssq[c4][:, 1:2], scalar2=None, op0=ALU.mult,
                     reads=[("Ot", c4), ("ssq", c4)], writes=[("Ot", c4)])
                tp3 = PS[1][:, 0:CH]
                P.op("pe", "transpose", out=tp3, in_=Ot[c4], identity=cst["ident"][0:CH, 0:CH], reads=[("Ot", c4), "ident"],
                     writes=[("bank", 1)])
                P.op("dve", "tensor_tensor", out=outb[b2][:, cs_], in0=tp3, in1=ld["g"][b2][:, cs_], op=ALU.mult,
                     reads=[("ld", "g", b2)], writes=[("outb", b2, c0), ("bank", 1)])
            yield
        if d == 0 and lastbatch:
            P.dma("pool", ysT[0:128, t0:t0 + n], outb[b2][:, 0:n], reads=[("outb", b2, c0_) for c0_ in range(0, n, CH)],
                  writes=[("ysT", 0, t0)])
        yield

    for d in (1, 0):
        P.op("pool", "memset", ap=S, constant=0.0, reads=[], writes=["S"])
        pending = None
        for (t0, n, s, offs) in _dir_chunks(d):
            b2 = bi_ % 2
            bi_ += 1
            names = ["q", "qs", "k", "ks", "v", "cc", "ss"] + (["g"] if d == 0 else [])
            for i, nm in enumerate(names):
                src = {"q": PT["ret_q"], "qs": PT["ret_qs"], "k": PT["ret_k"], "ks": PT["ret_ks"], "v": PT["ret_v"],
                       "g": PT["ret_g"], "cc": io["ropeC"], "ss": io["ropeS"]}[nm]
                P.dma("sp" if i % 2 == 0 else "pool", ld[nm][b2][:, 0:n], src[:, t0:t0 + n], writes=[("ld", nm, b2)])
            for (dst, a_, b_, nm) in ((qr[b2], "q", "qs", "qr"), (kr[b2], "k", "ks", "kr")):
                P.op("dve", "tensor_tensor", out=dst[:, 0:n], in0=ld[a_][b2][:, 0:n], in1=ld["cc"][b2][:, 0:n], op=ALU.mult,
                     reads=[("ld", a_, b2), ("ld", "cc", b2)], writes=[(nm, b2)])
                P.op("pool", "tensor_tensor", out=ld[b_][b2][:, 0:n], in0=ld[b_][b2][:, 0:n], in1=ld["ss"][b2][:, 0:n], op=ALU.mult,
                     reads=[("ld", b_, b2), ("ld", "ss", b2)], writes=[("ld", b_, b2)])
                P.op("dve", "tensor_tensor", out=dst[:, 0:n], in0=dst[:, 0:n], in1=ld[b_][b2][:, 0:n], op=ALU.add,
                     reads=[(nm, b2), ("ld", b_, b2)], writes=[(nm, b2)])
            P.op("pool", "tensor_tensor", out=qsd[b2][:, 0:n], in0=qr[b2][:, 0:n], in1=POSQ[d][:, 0:n], op=ALU.mult,
                 reads=[("qr", b2), ("POSQ", d)], writes=[("qsd", b2)])
            P.op("pool", "tensor_tensor", out=kdc[b2][:, 0:n], in0=kr[b2][:, 0:n], in1=KDEC[d][:, 0:n], op=ALU.mult,
                 reads=[("kr", b2), ("KDEC", d)], writes=[("kdc", b2)])
            if d == 0:
                P.op("act", "activation", out=ld["g"][b2][:, 0:n], in_=ld["g"][b2][:, 0:n], func=AF.Silu, reads=[("ld", "g", b2)],
                     writes=[("ld", "g", b2)])
            wsb = []
            for ci_, c0 in enumerate(offs):
                cs_ = slice(c0, c0 + CH)
                tp = PS[7][0:CH, 0:128]
                P.op("pe", "transpose", out=tp, in_=ld["v"][b2][:, cs_], identity=cst["ident"][:], reads=[("ld", "v", b2), "ident"],
                     writes=[("bank", 7)])
                P.op("act", "activation", out=Vt[b2][ci_], in_=tp, func=AF.Copy, reads=[], writes=[("Vt", b2, ci_), ("bank", 7)])
                tp2 = PS[0][0:CH, 0:128]
                P.op("pe", "transpose", out=tp2, in_=kdc[b2][:, cs_], identity=cst["ident"][:], reads=[("kdc", b2), "ident"],
                     writes=[("bank", 0)])
                P.op("dve", "tensor_copy", out=Kst[b2][ci_], in_=tp2, reads=[], writes=[("Kst", b2, ci_), ("bank", 0)])
                w = {"kT": (kr[b2][:, cs_], ("kr", b2)), "qT": (qr[b2][:, cs_], ("qr", b2)), "qTs": (qsd[b2][:, cs_], ("qsd", b2)),
                     "Kst": (Kst[b2][ci_], ("Kst", b2, ci_)), "V": (Vt[b2][ci_], ("Vt", b2, ci_)), "cs": (csc[:, d:d + 1], "csc")}
                wsb.append((w, c0, ci_))
            G = core.G
            for st in range(0, len(wsb), G):
                batch = wsb[st:st + G]
                kb = kbc[0]
                kbc[0] += 1
                m8 = MinT8[d].rearrange("p a b -> p (a b)")[:, 0:len(batch) * CH]
                pg = core.prep_gen(kb, [x[0] for x in batch], None, [], m8, [("MinT8", d, r_) for r_ in range(8)])
                interleave(pg, pending)
                pending = recur_batch(kb, batch, d, b2, t0, n, st + G >= len(wsb))
        interleave(None, pending)
    P.release(m0)


def _gdn(P, cst, PS, io, PT, ysT, OB):
    m0 = P.mark()
    cAs = _load_consts(P, io, ("RSTF", "RSTB", "NEGINF", "NEGINB", "NEGEXF", "NEGEXB", "IDC"))
    cw = P.carve([128, 3, 4], F32)
    sc = P.carve([1, 4], F32)
    nga = P.carve([1, 2], F32)
    ng = P.carve([128, 1], F32)
    one1 = P.carve([1, 128], F32)
    P.op("pool", "memset", ap=one1, constant=1.0, writes=["one1"])
    P.dma("sp", cw, io["gdn_cw"].rearrange("m j c -> c m j"), writes=["cw"], allow_slow_non_contiguous=True)
    P.dma("sp", sc, io["gdn_sc"], writes=["sc"])
    P.dma("sp", ng, io["gdn_ng"].rearrange("o c -> c o"), writes=["ng"], allow_slow_non_contiguous=True)
    P.op("act", "activation", out=nga, in_=sc[:, 0:2], func=AF.Exp, reads=["sc"], writes=["nga"])
    P.op("dve", "tensor_scalar", out=nga, in0=nga, scalar1=-1.0, scalar2=None, op0=ALU.mult, reads=["nga"], writes=["nga"])
    core = Core(P, cst, PS, 1, True, _ld_const(P, io, cAs, "IDC", 64))
    S = P.carve([128, 128], F32)
    xh = {nm: [P.carve([128, 516], F32) for _ in range(2)] for nm in "qkv"}
    cv = {nm: [P.carve([128, 512], F32) for _ in range(2)] for nm in "qkv"}
    zt = [P.carve([128, 512], F32) for _ in range(2)]
    sqt = P.carve([128, 512], F32)
    rs = P.carve([128, 512], F32)
    rows = {nm: [P.carve([1, 512], F32)] * 2 for nm in ("a", "b", "g", "gc", "gcx", "ngc", "r", "ein", "eex", "cb", "dend")}
    bt = {nm: [P.carve([128, 512], F32) for _ in range(2)] for nm in ("bT", "kT", "aTs", "qTs", "KsT", "BsT", "EinS")}
    outb = [P.carve([128, 512], F32) for _ in range(2)]
    Mx = [P.carve([CH, 512], F32) for _ in range(2)]
    Mi = [P.carve([CH, 512], F32) for _ in range(2)]
    Vt = [[P.carve([CH, 128], F32) for _ in range(8)] for _ in range(2)]
    Kst = [[P.carve([CH, 128], F32) for _ in range(8)] for _ in range(2)]
    Bst = [[P.carve([CH, 128], F32) for _ in range(8)] for _ in range(2)]
    Ot = [P.carve([CH, 128], F32) for _ in range(4)]
    Ob = [P.carve([CH, 128], F32) for _ in range(4)]
    ssq = [P.carve([CH, 2], F32) for _ in range(4)]
    junk = P.carve([CH, 128], F32)
    epsC = cst["epsb"]
    b0, b1, b7 = ("bank", 0), ("bank", 1), ("bank", 7)
    bi_ = 0
    cctr = [0]
    kbc = [0]

    def recur_batch(kb, batch, d, b2, t0, n, lastbatch):
        for gi, (w, c0, ci_) in enumerate(batch):
            c4 = cctr[0] % 4
            cctr[0] += 1
            cs_ = slice(c0, c0 + CH)
            yield from core.recur_gen(kb, gi, w, S, "S", Ot[c4], ("Ot", c4))
            tg = t0 + c0
            if d == 1:
                P.dma("sp", OB[tg:tg + CH, :], Ot[c4], reads=[("Ot", c4)], writes=[("OB", tg)])
            else:
                P.dma("sp", Ob[c4], OB[tg:tg + CH, :], reads=[("OB", tg)], writes=[("Ob", c4)])
                P.op("dve", "tensor_tensor", out=Ot[c4], in0=Ot[c4], in1=Ob[c4], op=ALU.add, reads=[("Ot", c4), ("Ob", c4)],
                     writes=[("Ot", c4)])
                P.op("act", "activation", out=junk, in_=Ot[c4], func=AF.Square, accum_out=ssq[c4][:, 0:1], reads=[("Ot", c4)],
                     writes=["junk", ("ssq", c4)])
                P.op("act", "activation", out=ssq[c4][:, 1:2], in_=ssq[c4][:, 0:1], func=AF.Ln, scale=1.0 / 128, bias=epsC[0:CH, :],
                     reads=[("ssq", c4), "epsb"], writes=[("ssq", c4)])
                P.op("act", "activation", out=ssq[c4][:, 1:2], in_=ssq[c4][:, 1:2], func=AF.Exp, scale=-0.5, reads=[("ssq", c4)],
                     writes=[("ssq", c4)])
                yield
                P.op("dve", "tensor_scalar", out=Ot[c4], in0=Ot[c4], scalar1=ssq[c4][:, 1:2], scalar2=None, op0=ALU.mult,
                     reads=[("Ot", c4), ("ssq", c4)], writes=[("Ot", c4)])
                tp3 = PS[7][:, 384:384 + CH]
                P.op("pe", "transpose", out=tp3, in_=Ot[c4], identity=cst["ident"][0:CH, 0:CH], reads=[("Ot", c4), "ident"],
                     writes=[b7])
                P.op("dve", "scalar_tensor_tensor", out=outb[b2][:, cs_], in0=tp3, scalar=ng[:, 0:1], in1=zt[b2][:, cs_],
                     op0=ALU.mult, op1=ALU.mult, reads=[("zt", b2), "ng"], writes=[("outb", b2, c0), b7])
            yield
        if d == 0 and lastbatch:
            P.dma("pool", ysT[256:384, t0:t0 + n], outb[b2][:, 0:n], reads=[("outb", b2, c0_) for c0_ in range(0, n, CH)],
                  writes=[("ysT", 2, t0)])
        yield

    for d in (1, 0):
        sfx = "FB"[d]
        osfx = "BF"[d]
        P.op("pool", "memset", ap=S, constant=0.0, reads=[], writes=["S"])
        pending = None
        for (t0, n, s, offs) in _dir_chunks(d):
            b2 = bi_ % 2
            bi_ += 1
            seg0, seg1 = (0, 256) if s else (256, TTOT)
            lo, hi = max(t0 - 2, seg0), min(t0 + n + 1, seg1)
            for mi_, nm in enumerate("qkv"):
                x_ = xh[nm][b2]
                if lo > t0 - 2 or hi < t0 + n + 1:
                    P.op("pool", "memset", ap=x_[:, 0:n + 3], constant=0.0, writes=[("xh", nm, b2)])
                P.dma("sp" if mi_ != 1 else "pool", x_[:, lo - (t0 - 2):hi - (t0 - 2)], PT["gdn_" + nm][:, lo:hi], writes=[("xh", nm, b2)])
                c_ = cv[nm][b2]
                eng = "dve" if mi_ != 2 else "pool"
                P.op("dve", "tensor_scalar", out=c_[:, 0:n], in0=x_[:, 0:n], scalar1=cw[:, mi_, 0:1], scalar2=None, op0=ALU.mult,
                     reads=[("xh", nm, b2), "cw"], writes=[("cv", nm, b2)])
                for j in range(1, 4):
                    P.op("dve", "scalar_tensor_tensor", out=c_[:, 0:n], in0=x_[:, j:j + n], scalar=cw[:, mi_, j:j + 1], in1=c_[:, 0:n],
                         op0=ALU.mult, op1=ALU.add, reads=[("xh", nm, b2), "cw", ("cv", nm, b2)], writes=[("cv", nm, b2)])
                P.op("act", "activation", out=c_[:, 0:n], in_=c_[:, 0:n], func=AF.Silu, reads=[("cv", nm, b2)], writes=[("cv", nm, b2)])
            for nm, scl in (("q", 128.0 ** -0.5), ("k", 1.0)):
                c_ = cv[nm][b2]
                P.op("act", "activation", out=sqt[:, 0:n], in_=c_[:, 0:n], func=AF.Square, reads=[("cv", nm, b2)], writes=["sqt"])
                P.op("pe", "matmul", out=PS[1][:, 0:n], lhsT=cst["ones_f"][:], rhs=sqt[:, 0:n], start=True, stop=True,
                     reads=["sqt", "ones_f"], writes=[b1])
                P.op("act", "activation", out=rs[:, 0:n], in_=PS[1][:, 0:n], func=AF.Ln, bias=epsC[:], reads=["epsb"], writes=["rs", b1])
                P.op("act", "activation", out=rs[:, 0:n], in_=rs[:, 0:n], func=AF.Exp, scale=-0.5, reads=["rs"], writes=["rs"])
                P.op("dve", "scalar_tensor_tensor", out=c_[:, 0:n], in0=c_[:, 0:n], scalar=scl, in1=rs[:, 0:n], op0=ALU.mult, op1=ALU.mult,
                     reads=[("cv", nm, b2), "rs"], writes=[("cv", nm, b2)])
            R = {nm: rows[nm][0] for nm in rows}
            rt = lambda nm: ("row", nm, 0)
            P.dma("pool", R["a"][:, 0:n], PT["gdn_af" if d == 0 else "gdn_abk"][:, t0:t0 + n], writes=[rt("a")])
            P.dma("pool", R["b"][:, 0:n], PT["gdn_bf" if d == 0 else "gdn_bb"][:, t0:t0 + n], writes=[rt("b")])
            P.op("act", "activation", out=R["g"][:, 0:n], in_=R["a"][:, 0:n], func=AF.Exp, bias=sc[:, 2 + d:3 + d], reads=[rt("a"), "sc"],
                 writes=[rt("g")])
            P.op("act", "activation", out=R["g"][:, 0:n], in_=R["g"][:, 0:n], func=AF.Ln, bias=one1[:, 0:1], reads=[rt("g"), "one1"],
                 writes=[rt("g")])
            P.op("dve", "tensor_scalar", out=R["g"][:, 0:n], in0=R["g"][:, 0:n], scalar1=nga[:, d:d + 1], scalar2=None, op0=ALU.mult,
                 reads=[rt("g"), "nga"], writes=[rt("g")])
            P.op("act", "activation", out=R["b"][:, 0:n], in_=R["b"][:, 0:n], func=AF.Sigmoid, reads=[rt("b")], writes=[rt("b")])
            rstm = _ld_const(P, io, cAs, "RST" + sfx, 1)
            rsto = _ld_const(P, io, cAs, "RST" + osfx, 1)
            rv = (lambda ap: ap[:, 0:n][:, ::-1]) if d == 1 else (lambda ap: ap[:, 0:n])
            rvo = (lambda ap: ap[:, 0:n][:, ::-1]) if d == 0 else (lambda ap: ap[:, 0:n])
            P.op("dve", "tensor_tensor_scan", out=rv(R["gc"]), data0=rv(rstm), data1=rv(R["g"]), initial=0.0, op0=ALU.mult, op1=ALU.add,
                 reads=[rt("g"), "cAs"], writes=[rt("gc")])
            P.op("dve", "tensor_tensor_scan", out=rvo(R["r"]), data0=rvo(rsto), data1=rvo(R["g"]), initial=0.0, op0=ALU.mult, op1=ALU.add,
                 reads=[rt("g"), "cAs"], writes=[rt("r")])
            P.op("dve", "tensor_tensor", out=R["gcx"][:, 0:n], in0=R["gc"][:, 0:n], in1=R["g"][:, 0:n], op=ALU.subtract,
                 reads=[rt("gc"), rt("g")], writes=[rt("gcx")])
            P.op("dve", "tensor_scalar", out=R["ngc"][:, 0:n], in0=R["gc"][:, 0:n], scalar1=-1.0, scalar2=None, op0=ALU.mult,
                 reads=[rt("gc")], writes=[rt("ngc")])
            P.op("dve", "tensor_tensor", out=R["r"][:, 0:n], in0=R["r"][:, 0:n], in1=R["g"][:, 0:n], op=ALU.subtract,
                 reads=[rt("r"), rt("g")], writes=[rt("r")])
            P.op("act", "activation", out=R["dend"][:, 0:n], in_=R["r"][:, 0:n], func=AF.Exp, reads=[rt("r")], writes=[rt("dend")])
            P.op("act", "activation", out=R["ein"][:, 0:n], in_=R["gc"][:, 0:n], func=AF.Exp, reads=[rt("gc")], writes=[rt("ein")])
            P.op("act", "activation", out=R["eex"][:, 0:n], in_=R["gcx"][:, 0:n], func=AF.Exp, reads=[rt("gcx")], writes=[rt("eex")])
            P.op("act", "activation", out=R["cb"][:, 0:n], in_=R["g"][:, 0:n], func=AF.Exp, reads=[rt("g")], writes=[rt("cb")])
            P.op("dve", "scalar_tensor_tensor", out=R["cb"][:, 0:n], in0=R["cb"][:, 0:n], scalar=-1.0, in1=R["b"][:, 0:n], op0=ALU.mult,
                 op1=ALU.mult, reads=[rt("cb"), rt("b")], writes=[rt("cb")])
            B = {nm: bt[nm][b2] for nm in bt}
            btk = lambda nm: ("bt", nm, b2)
            kn, qn = cv["k"][b2], cv["q"][b2]

            def bcast_mul(row, rtoks, dst, dtok, src, stok, bank, bk):
                P.op("pe", "matmul", out=bank[:, 0:n], lhsT=one1[0:1, :], rhs=row[:, 0:n], start=True, stop=True,
                     reads=rtoks + ["one1"], writes=[bk])
                if src is None:
                    P.op("act", "activation", out=dst[:, 0:n], in_=bank[:, 0:n], func=AF.Copy, reads=[], writes=[dtok, bk])
                else:
                    P.op("dve", "tensor_tensor", out=dst[:, 0:n], in0=src[:, 0:n], in1=bank[:, 0:n], op=ALU.mult, reads=[stok],
                         writes=[dtok, bk])
            bcast_mul(R["ein"], [rt("ein")], B["EinS"], btk("EinS"), None, None, PS[1], b1)
            P.op("pool", "tensor_tensor", out=B["qTs"][:, 0:n], in0=qn[:, 0:n], in1=B["EinS"][:, 0:n], op=ALU.mult,
                 reads=[("cv", "q", b2), btk("EinS")], writes=[btk("qTs")])
            bcast_mul(R["eex"], [rt("eex")], B["aTs"], btk("aTs"), kn, ("cv", "k", b2), PS[1], b1)
            bcast_mul(R["cb"], [rt("cb")], B["bT"], btk("bT"), kn, ("cv", "k", b2), PS[1], b1)
            bcast_mul(R["b"], [rt("b")], B["kT"], btk("kT"), kn, ("cv", "k", b2), PS[1], b1)
            bcast_mul(R["dend"], [rt("dend")], B["KsT"], btk("KsT"), B["kT"], btk("kT"), PS[1], b1)
            bcast_mul(R["dend"], [rt("dend")], B["BsT"], btk("BsT"), B["bT"], btk("bT"), PS[1], b1)
            if d == 0:
                P.dma("pool", zt[b2][:, 0:n], PT["gdn_z"][:, t0:t0 + n], writes=[("zt", b2)])
                P.op("act", "activation", out=zt[b2][:, 0:n], in_=zt[b2][:, 0:n], func=AF.Silu, reads=[("zt", b2)], writes=[("zt", b2)])
            wsb = []
            for ci_, c0 in enumerate(offs):
                cs_ = slice(c0, c0 + CH)
                ms_ = slice(ci_ * CH, (ci_ + 1) * CH)
                for (row, negnm, Mt, mnm, col) in ((R["gcx"], "NEGEX" + sfx, Mx[b2], "Mx", 0), (R["gc"], "NEGIN" + sfx, Mi[b2], "Mi", 64)):
                    reg = PS[0][0:CH, col:col + CH]
                    P.op("pe", "matmul", out=reg, lhsT=one1[0:1, 0:CH], rhs=row[:, cs_], start=True, stop=False,
                         reads=["one1", rt("gcx"), rt("gc")], writes=[b0])
                    P.op("pe", "matmul", out=reg, lhsT=R["ngc"][:, cs_], rhs=one1[0:1, 0:CH], start=False, stop=False,
                         reads=["one1", rt("ngc")], writes=[b0])
                    P.op("pe", "matmul", out=reg, lhsT=cst["ident"][0:CH, 0:CH], rhs=_ld_const(P, io, cAs, negnm, 64), start=False,
                         stop=True, reads=["ident", "cAs"], writes=[b0])
                    P.op("act", "activation", out=Mt[:, ms_], in_=reg, func=AF.Exp, reads=[], writes=[(mnm, b2, ci_), b0])
                for (src, stok, dst, dnm, col, eng) in ((cv["v"][b2], ("cv", "v", b2), Vt[b2][ci_], "Vt", 0, "act"),
                                                        (B["KsT"], btk("KsT"), Kst[b2][ci_], "Kst", 128, "dve"),
                                                        (B["BsT"], btk("BsT"), Bst[b2][ci_], "Bst", 256, "act")):
                    tp = PS[7][0:CH, col:col + 128]
                    P.op("pe", "transpose", out=tp, in_=src[:, cs_], identity=cst["ident"][:], reads=[stok, "ident"], writes=[b7])
                    if eng == "act":
                        P.op("act", "activation", out=dst, in_=tp, func=AF.Copy, reads=[], writes=[(dnm, b2, ci_), b7])
                    else:
                        P.op("dve", "tensor_copy", out=dst, in_=tp, reads=[], writes=[(dnm, b2, ci_), b7])
                last = c0 + CH - 1 if d == 0 else c0
                w = {"aT": (kn[:, cs_], ("cv", "k", b2)), "qT": (qn[:, cs_], ("cv", "q", b2)), "bT": (B["bT"][:, cs_], btk("bT")),
                     "kT": (B["kT"][:, cs_], btk("kT")), "aTs": (B["aTs"][:, cs_], btk("aTs")), "qTs": (B["qTs"][:, cs_], btk("qTs")),
                     "Bst": (Bst[b2][ci_], ("Bst", b2, ci_)), "Kst": (Kst[b2][ci_], ("Kst", b2, ci_)), "V": (Vt[b2][ci_], ("Vt", b2, ci_)),
                     "cs": (B["EinS"][:, last:last + 1], btk("EinS"))}
                wsb.append((w, c0, ci_))
            G = core.G
            for st in range(0, len(wsb), G):
                batch = wsb[st:st + G]
                kb = kbc[0]
                kbc[0] += 1
                gw = slice(st * CH, (st + len(batch)) * CH)
                pg = core.prep_gen(kb, [x[0] for x in batch], Mx[b2][:, gw], [("Mx", b2, x[2]) for x in batch], Mi[b2][:, gw],
                                   [("Mi", b2, x[2]) for x in batch])
                interleave(pg, pending)
                pending = recur_batch(kb, batch, d, b2, t0, n, st + G >= len(wsb))
        interleave(None, pending)
    P.release(m0)


NEG_EM05 = -float(np.exp(-0.5))
RWKV_LN_EPS = 64e-5


def _rwkv(P, cst, PS, io, PT, ysT, OB):
    m0 = P.mark()
    cAs = _load_consts(P, io, ("RSTF", "RSTB", "IDC", "BLK", "MSTRF4", "MSTRB4", "MASKF4", "MASKB4"))
    pc = P.carve([128, 16], F32)
    pc64 = P.carve([64, 4], F32)
    P.dma("sp", pc[:, 0:13], io["rw_pc"].rearrange("j c -> c j"), writes=["pc"], allow_slow_non_contiguous=True)
    P.dma("sp", pc64, io["rw_pc64"].rearrange("j c -> c j"), writes=["pc64"], allow_slow_non_contiguous=True)
    om = P.carve([128, 4], F32); hm = P.carve([128, 4], F32); om64 = P.carve([64, 4], F32); hm64 = P.carve([64, 4], F32)
    omka = P.carve([128, 1], F32)
    epsl = P.carve([128, 1], F32)
    P.op("pool", "memset", ap=epsl, constant=RWKV_LN_EPS, writes=["epsl"])
    for (o_, h_, src, tk_) in ((om, hm, pc[:, 0:4], "pc"), (om64, hm64, pc64, "pc64")):
        P.op("dve", "tensor_scalar", out=o_, in0=src, scalar1=-1.0, scalar2=1.0, op0=ALU.mult, op1=ALU.add, reads=[tk_], writes=["omhm"])
        P.op("dve", "tensor_scalar", out=h_, in0=src, scalar1=0.5, scalar2=None, op0=ALU.mult, reads=[tk_], writes=["omhm2"])
    P.op("dve", "tensor_scalar", out=omka, in0=pc[:, 9:10], scalar1=-1.0, scalar2=1.0, op0=ALU.mult, op1=ALU.add, reads=["pc"],
         writes=["omka"])
    w2 = P.carve([64, 2, 128], F32); a2 = P.carve([64, 2, 128], F32); g2 = P.carve([128, 128], F32)
    P.dma("sp", w2, io["rw_w2"].rearrange("d r c -> r d c"), writes=["w2"])
    P.dma("sp", a2, io["rw_a2"].rearrange("d r c -> r d c"), writes=["a2"])
    P.dma("sp", g2, io["rw_g2"], writes=["g2"])
    BLK = _ld_const(P, io, cAs, "BLK")
    core = Core(P, cst, PS, 2, True, _ld_const(P, io, cAs, "IDC", 64))
    BN = 256
    S = P.carve([128, 64], F32)
    raw = {nm: [P.carve([128, BN + 4], F32)] * 2 for nm in ("r", "k", "v", "gd", "wd", "ad", "adb")}
    X = {nm: P.carve([128, BN], F32) for nm in ("k", "gd", "wd", "ad", "adb", "s")}
    Xr = [P.carve([128, BN], F32) for _ in range(2)]
    Xv = [P.carve([128, BN], F32) for _ in range(2)]
    logw = P.carve([128, BN], F32); asig = P.carve([128, BN], F32); kk = P.carve([128, BN], F32); kd = P.carve([128, BN], F32)
    t1 = P.carve([128, BN], F32); t2 = P.carve([128, BN], F32)
    Einv = P.carve([128, BN], F32); Eex = P.carve([128, BN], F32)
    DB = {nm: [P.carve([128, BN], F32) for _ in range(2)] for nm in ("E", "qT", "aT", "KsT", "BsT", "qm0", "qm1", "am0", "am1", "bm0",
                                                                      "bm1", "km0", "km1")}
    SB1 = {nm: P.carve([128, BN], F32) for nm in ("kT", "bT")}
    gtb = [P.carve([128, BN], F32) for _ in range(2)]
    bonusb = [P.carve([128, BN], F32) for _ in range(2)]
    YT = P.carve([128, BN], F32)
    outb = [P.carve([128, BN], F32) for _ in range(2)]
    Vt = [[P.carve([CH, 128], F32) for _ in range(4)] for _ in range(2)]
    Kst = [[P.carve([CH, 128], F32) for _ in range(4)] for _ in range(2)]
    Bst = [[P.carve([CH, 128], F32) for _ in range(4)] for _ in range(2)]
    Ot = [P.carve([CH, 128], F32) for _ in range(4)]
    Ob = [P.carve([CH, 128], F32) for _ in range(4)]
    b0, b1, b7 = ("bank", 0), ("bank", 1), ("bank", 7)
    bi_ = 0
    cctr = [0]
    kbc = [0]

    def recur_batch(kb, batch, d, b2, t0, n, offs):
        gt, bonus = gtb[b2], bonusb[b2]
        for gi, (w, c0, ci_) in enumerate(batch):
            c4 = cctr[0] % 4
            cctr[0] += 1
            cs_ = slice(c0, c0 + CH)
            yield from core.recur_gen(kb, gi, w, S, "S", Ot[c4], ("Ot", c4))
            tg = t0 + c0
            if d == 1:
                P.dma("sp", OB[tg:tg + CH, :], Ot[c4], reads=[("Ot", c4)], writes=[("OB", tg)])
            else:
                P.dma("sp", Ob[c4], OB[tg:tg + CH, :], reads=[("OB", tg)], writes=[("Ob", c4)])
                P.op("dve", "tensor_tensor", out=Ot[c4], in0=Ot[c4], in1=Ob[c4], op=ALU.add, reads=[("Ot", c4), ("Ob", c4)],
                     writes=[("Ot", c4)])
                tp3 = PS[7][:, 384:384 + CH]
                P.op("pe", "transpose", out=tp3, in_=Ot[c4], identity=cst["ident"][0:CH, 0:CH], reads=[("Ot", c4), "ident"],
                     writes=[b7])
                P.op("act", "activation", out=YT[:, cs_], in_=tp3, func=AF.Copy, reads=[], writes=[("YT", c0), b7])
            yield
        if d == 0:
            ytoks = [("YT", c0) for c0 in offs]
            o_ = outb[b2]
            P.op("pe", "matmul", out=PS[0][:, 0:n], lhsT=BLK, rhs=YT[:, 0:n], start=True, stop=True, reads=["cAs"] + ytoks, writes=[b0])
            P.op("dve", "scalar_tensor_tensor", out=t1[:, 0:n], in0=PS[0][:, 0:n], scalar=-1.0 / 64, in1=YT[:, 0:n], op0=ALU.mult,
                 op1=ALU.add, reads=ytoks, writes=["t1", b0])
            P.op("act", "activation", out=t2[:, 0:n], in_=t1[:, 0:n], func=AF.Square, reads=["t1"], writes=["t2"])
            yield
            P.op("pe", "matmul", out=PS[1][:, 0:n], lhsT=BLK, rhs=t2[:, 0:n], start=True, stop=True, reads=["cAs", "t2"], writes=[b1])
            P.op("act", "activation", out=t2[:, 0:n], in_=PS[1][:, 0:n], func=AF.Ln, scale=1.0 / 64, bias=epsl[:], reads=["epsl"],
                 writes=["t2", b1])
            P.op("act", "activation", out=t2[:, 0:n], in_=t2[:, 0:n], func=AF.Exp, scale=-0.5, reads=["t2"], writes=["t2"])
            yield
            P.op("dve", "tensor_tensor", out=t1[:, 0:n], in0=t1[:, 0:n], in1=t2[:, 0:n], op=ALU.mult, reads=["t1", "t2"], writes=["t1"])
            P.op("dve", "tensor_scalar", out=t1[:, 0:n], in0=t1[:, 0:n], scalar1=pc[:, 11:12], scalar2=pc[:, 12:13], op0=ALU.mult,
                 op1=ALU.add, reads=["t1", "pc"], writes=["t1"])
            P.op("pool", "tensor_tensor", out=t1[:, 0:n], in0=t1[:, 0:n], in1=bonus[:, 0:n], op=ALU.add, reads=["t1", ("bonus", b2)],
                 writes=["t1"])
            P.op("dve", "tensor_tensor", out=o_[:, 0:n], in0=t1[:, 0:n], in1=gt[:, 0:n], op=ALU.mult, reads=["t1", ("gt", b2)],
                 writes=[("outb", b2)])
            P.dma("pool", ysT[384:512, t0:t0 + n], o_[:, 0:n], reads=[("outb", b2)], writes=[("ysT", 3, t0)])
        yield

    for d in (1, 0):
        sfx = "FB"[d]
        P.op("pool", "memset", ap=S, constant=0.0, reads=[], writes=["S"])
        pending = None
        for (t0, n, s, offs) in _dir_chunks(d, RW_BLOCKS):
            b2 = bi_ % 2
            bi_ += 1
            gt, bonus = gtb[b2], bonusb[b2]
            seg0, seg1 = (0, 256) if s else (256, TTOT)
            lo, hi = max(t0 - 1, seg0), min(t0 + n + 1, seg1)
            srcs = [("r", PT["rw_r"], 128, 0), ("k", PT["rw_k"], 128, 1), ("v", PT["rw_v"], 128, 2), ("gd", PT["rw_gd"], 128, 3),
                    ("wd", PT["rw_wdf" if d == 0 else "rw_wdb"], 64, d), ("ad", PT["rw_adf" if d == 0 else "rw_adb"], 64, 2 + d)]
            if d == 0:
                srcs.append(("adb", PT["rw_adb"], 64, 3))
            else:
                srcs = [x for x in srcs if x[0] != "gd"]
            for i, (nm, src, rows_, mc) in enumerate(srcs):
                x_ = raw[nm][b2]
                if lo > t0 - 1 or hi < t0 + n + 1:
                    P.op("pool", "memset", ap=x_[0:rows_, 0:n + 2], constant=0.0, writes=[("raw", nm, 0)])
                P.dma("sp" if i % 2 == 0 else "pool", x_[0:rows_, lo - (t0 - 1):hi - (t0 - 1)], src[:, lo:hi], writes=[("raw", nm, 0)])
                dst = Xr[b2] if nm == "r" else Xv[b2] if nm == "v" else X[nm]
                dtok = ("Xr", b2) if nm == "r" else ("Xv", b2) if nm == "v" else ("X", nm)
                omc, hmc = (om[:, mc:mc + 1], hm[:, mc:mc + 1]) if rows_ == 128 else (om64[:, mc:mc + 1], hm64[:, mc:mc + 1])
                P.op("pool", "tensor_tensor", out=X["s"][0:rows_, 0:n], in0=x_[0:rows_, 0:n], in1=x_[0:rows_, 2:n + 2], op=ALU.add,
                     reads=[("raw", nm, 0)], writes=[("X", "s")])
                P.op("dve", "tensor_scalar", out=dst[0:rows_, 0:n], in0=x_[0:rows_, 1:n + 1], scalar1=omc, scalar2=None, op0=ALU.mult,
                     reads=[("raw", nm, 0), "omhm"], writes=[dtok])
                P.op("dve", "scalar_tensor_tensor", out=dst[0:rows_, 0:n], in0=X["s"][0:rows_, 0:n], scalar=hmc, in1=dst[0:rows_, 0:n],
                     op0=ALU.mult, op1=ALU.add, reads=[("X", "s"), "omhm2", dtok], writes=[dtok])
            xr_, xv_ = Xr[b2], Xv[b2]
            P.op("act", "activation", out=X["wd"][0:64, 0:n], in_=X["wd"][0:64, 0:n], func=AF.Tanh, reads=[("X", "wd")], writes=[("X", "wd")])
            P.op("pe", "matmul", out=PS[0][:, 0:n], lhsT=w2[:, d, :], rhs=X["wd"][0:64, 0:n], start=True, stop=True, reads=["w2", ("X", "wd")],
                 writes=[b0])
            P.op("act", "activation", out=logw[:, 0:n], in_=PS[0][:, 0:n], func=AF.Sigmoid, bias=pc[:, 4 + d:5 + d], reads=["pc"],
                 writes=["logw", b0])
            P.op("dve", "tensor_scalar", out=logw[:, 0:n], in0=logw[:, 0:n], scalar1=NEG_EM05, scalar2=None, op0=ALU.mult, reads=["logw"],
                 writes=["logw"])
            P.op("pe", "matmul", out=PS[1][:, 0:n], lhsT=a2[:, d, :], rhs=X["ad"][0:64, 0:n], start=True, stop=True, reads=["a2", ("X", "ad")],
                 writes=[b1])
            P.op("act", "activation", out=asig[:, 0:n], in_=PS[1][:, 0:n], func=AF.Sigmoid, bias=pc[:, 6 + d:7 + d], reads=["pc"],
                 writes=["asig", b1])
            P.op("dve", "tensor_scalar", out=kk[:, 0:n], in0=X["k"][:, 0:n], scalar1=pc[:, 8:9], scalar2=None, op0=ALU.mult,
                 reads=[("X", "k"), "pc"], writes=["kk"])
            P.op("act", "activation", out=t1[:, 0:n], in_=kk[:, 0:n], func=AF.Square, reads=["kk"], writes=["t1"])
            P.op("pe", "matmul", out=PS[0][:, 0:n], lhsT=BLK, rhs=t1[:, 0:n], start=True, stop=True, reads=["cAs", "t1"], writes=[b0])
            P.op("act", "activation", out=t1[:, 0:n], in_=PS[0][:, 0:n], func=AF.Ln, bias=cst["epsb"][:], reads=["epsb"], writes=["t1", b0])
            P.op("act", "activation", out=t1[:, 0:n], in_=t1[:, 0:n], func=AF.Exp, scale=-0.5, reads=["t1"], writes=["t1"])
            P.op("dve", "tensor_tensor", out=kk[:, 0:n], in0=kk[:, 0:n], in1=t1[:, 0:n], op=ALU.mult, reads=["kk", "t1"], writes=["kk"])
            P.op("dve", "tensor_scalar", out=kd[:, 0:n], in0=asig[:, 0:n], scalar1=pc[:, 9:10], scalar2=omka[:, 0:1], op0=ALU.mult,
                 op1=ALU.add, reads=["asig", "pc", "omka"], writes=["kd"])
            P.op("dve", "tensor_tensor", out=kd[:, 0:n], in0=kd[:, 0:n], in1=X["k"][:, 0:n], op=ALU.mult, reads=["kd", ("X", "k")],
                 writes=["kd"])
            if d == 0:
                P.op("act", "activation", out=X["gd"][:, 0:n], in_=X["gd"][:, 0:n], func=AF.Sigmoid, reads=[("X", "gd")], writes=[("X", "gd")])
                P.op("pe", "matmul", out=PS[1][:, 0:n], lhsT=g2, rhs=X["gd"][:, 0:n], start=True, stop=True, reads=["g2", ("X", "gd")],
                     writes=[b1])
                P.op("act", "activation", out=gt[:, 0:n], in_=PS[1][:, 0:n], func=AF.Copy, reads=[], writes=[("gt", b2), b1])
                P.op("pe", "matmul", out=PS[0][:, 0:n], lhsT=a2[:, 1, :], rhs=X["adb"][0:64, 0:n], start=True, stop=True,
                     reads=["a2", ("X", "adb")], writes=[b0])
                P.op("act", "activation", out=t2[:, 0:n], in_=PS[0][:, 0:n], func=AF.Sigmoid, bias=pc[:, 7:8], reads=["pc"], writes=["t2", b0])
                P.op("dve", "tensor_scalar", out=t2[:, 0:n], in0=t2[:, 0:n], scalar1=pc[:, 9:10], scalar2=omka[:, 0:1], op0=ALU.mult,
                     op1=ALU.add, reads=["t2", "pc", "omka"], writes=["t2"])
                P.op("dve", "tensor_tensor", out=t2[:, 0:n], in0=t2[:, 0:n], in1=X["k"][:, 0:n], op=ALU.mult, reads=["t2", ("X", "k")],
                     writes=["t2"])
                P.op("dve", "tensor_tensor", out=t2[:, 0:n], in0=t2[:, 0:n], in1=kd[:, 0:n], op=ALU.add, reads=["t2", "kd"], writes=["t2"])
                P.op("dve", "scalar_tensor_tensor", out=t2[:, 0:n], in0=t2[:, 0:n], scalar=pc[:, 10:11], in1=xr_[:, 0:n], op0=ALU.mult,
                     op1=ALU.mult, reads=["t2", "pc", ("Xr", b2)], writes=["t2"])
                P.op("pe", "matmul", out=PS[1][:, 0:n], lhsT=BLK, rhs=t2[:, 0:n], start=True, stop=True, reads=["cAs", "t2"], writes=[b1])
                P.op("dve", "tensor_tensor", out=bonus[:, 0:n], in0=PS[1][:, 0:n], in1=xv_[:, 0:n], op=ALU.mult, reads=[("Xv", b2)],
                     writes=[("bonus", b2), b1])
            E = DB["E"][b2]
            rst = _ld_const(P, io, cAs, "RST" + sfx)
            rv = (lambda ap: ap[:, 0:n][:, ::-1]) if d == 1 else (lambda ap: ap[:, 0:n])
            P.op("dve", "tensor_tensor_scan", out=rv(E), data0=rv(rst), data1=rv(logw), initial=0.0, op0=ALU.mult, op1=ALU.add,
                 reads=["logw", "cAs"], writes=[("E", b2)])
            P.op("act", "activation", out=Einv[:, 0:n], in_=E[:, 0:n], func=AF.Exp, scale=-1.0, reads=[("E", b2)], writes=["Einv"])
            P.op("dve", "tensor_tensor", out=Eex[:, 0:n], in0=E[:, 0:n], in1=logw[:, 0:n], op=ALU.subtract, reads=[("E", b2), "logw"],
                 writes=["Eex"])
            P.op("act", "activation", out=Eex[:, 0:n], in_=Eex[:, 0:n], func=AF.Exp, reads=["Eex"], writes=["Eex"])
            P.op("act", "activation", out=E[:, 0:n], in_=E[:, 0:n], func=AF.Exp, reads=[("E", b2)], writes=[("E", b2)])
            T = {nm: DB[nm][b2] for nm in DB}
            T.update(SB1)
            dt = lambda nm: (nm, b2) if nm in DB else (nm, 0)
            P.op("pool", "tensor_tensor", out=T["qT"][:, 0:n], in0=xr_[:, 0:n], in1=E[:, 0:n], op=ALU.mult, reads=[("Xr", b2), ("E", b2)],
                 writes=[dt("qT")])
            P.op("pool", "tensor_tensor", out=T["aT"][:, 0:n], in0=kk[:, 0:n], in1=Eex[:, 0:n], op=ALU.mult, reads=["kk", "Eex"],
                 writes=[dt("aT")])
            P.op("dve", "tensor_tensor", out=T["kT"][:, 0:n], in0=kd[:, 0:n], in1=Einv[:, 0:n], op=ALU.mult, reads=["kd", "Einv"],
                 writes=[dt("kT")])
            P.op("pool", "tensor_tensor", out=t1[:, 0:n], in0=kk[:, 0:n], in1=asig[:, 0:n], op=ALU.mult, reads=["kk", "asig"], writes=["t1"])
            P.op("dve", "scalar_tensor_tensor", out=T["bT"][:, 0:n], in0=t1[:, 0:n], scalar=-1.0, in1=Einv[:, 0:n], op0=ALU.mult,
                 op1=ALU.mult, reads=["t1", "Einv"], writes=[dt("bT")])
            for (src_, pre, eng) in (("qT", "qm", "pool"), ("aT", "am", "dve"), ("bT", "bm", "pool"), ("kT", "km", "dve")):
                for hh in range(2):
                    P.op(eng, "tensor_scalar", out=T[pre + str(hh)][:, 0:n], in0=T[src_][:, 0:n], scalar1=BLK[:, hh * 64:hh * 64 + 1],
                         scalar2=None, op0=ALU.mult, reads=[dt(src_), "cAs"], writes=[dt(pre + str(hh))])
            for c0 in offs:
                last = c0 + CH - 1 if d == 0 else c0
                cs_ = slice(c0, c0 + CH)
                P.op("dve", "tensor_scalar", out=T["KsT"][:, cs_], in0=T["kT"][:, cs_], scalar1=E[:, last:last + 1], scalar2=None,
                     op0=ALU.mult, reads=[dt("kT"), ("E", b2)], writes=[("KsT", b2, c0)])
                P.op("pool", "tensor_scalar", out=T["BsT"][:, cs_], in0=T["bT"][:, cs_], scalar1=E[:, last:last + 1], scalar2=None,
                     op0=ALU.mult, reads=[dt("bT"), ("E", b2)], writes=[("BsT", b2, c0)])
            wsb = []
            for ci_, c0 in enumerate(offs):
                cs_ = slice(c0, c0 + CH)
                last = c0 + CH - 1 if d == 0 else c0
                for (src, stok, dst, dnm, col, eng) in ((xv_, ("Xv", b2), Vt[b2][ci_], "Vt", 0, "act"),
                                                        (T["KsT"], ("KsT", b2, c0), Kst[b2][ci_], "Kst", 128, "dve"),
                                                        (T["BsT"], ("BsT", b2, c0), Bst[b2][ci_], "Bst", 256, "act")):
                    tp = PS[7][0:CH, col:col + 128]
                    P.op("pe", "transpose", out=tp, in_=src[:, cs_], identity=cst["ident"][:], reads=[stok, "ident"], writes=[b7])
                    if eng == "act":
                        P.op("act", "activation", out=dst, in_=tp, func=AF.Copy, reads=[], writes=[(dnm, b2, ci_), b7])
                    else:
                        P.op("dve", "tensor_copy", out=dst, in_=tp, reads=[], writes=[(dnm, b2, ci_), b7])
                hm_ = lambda pre: [(T[pre + str(hh)][:, cs_], dt(pre + str(hh))) for hh in range(2)]
                w = {"aT": (T["aT"][:, cs_], dt("aT")), "qT": (T["qT"][:, cs_], dt("qT")), "bTm": hm_("bm"), "kTm": hm_("km"),
                     "aTsm": hm_("am"), "qTsm": hm_("qm"),
                     "Bst": (Bst[b2][ci_], ("Bst", b2, ci_)), "Kst": (Kst[b2][ci_], ("Kst", b2, ci_)), "V": (Vt[b2][ci_], ("Vt", b2, ci_)),
                     "cs": (E[:, last:last + 1], ("E", b2))}
                wsb.append((w, c0, ci_))
            kb = kbc[0]
            kbc[0] += 1
            GW = len(wsb) * 128
            pg = core.prep_gen(kb, [x[0] for x in wsb], _ld_const(P, io, cAs, "MSTR" + sfx + "4", 64)[:, 0:GW], "cAs",
                               _ld_const(P, io, cAs, "MASK" + sfx + "4", 64)[:, 0:GW], "cAs")
            interleave(pg, pending)
            pending = recur_batch(kb, wsb, d, b2, t0, n, offs)
        interleave(None, pending)
    P.release(m0)


def A_gather(outs):
    ys = np.zeros((2, TTOT, 4, 512), np.float32)
    for core in range(8):
        b, h = core // 4, core % 4
        o = outs[core].reshape(4, 128, TTOT)
        ys[b, :, :, h * 128:(h + 1) * 128] = np.transpose(o, (2, 0, 1))
    ys = ys.reshape(2, TTOT, 2048)
    return ys[:, CTX:], ys[:, :CTX]


def kernel(**inputs):
    inp = {k: np.asarray(v) for k, v in inputs.items()}
    h_lat, h_ctx = inp["x"].astype(np.float32), inp["ctx"].astype(np.float32)
    cores = list(range(8))
    for l in range(2):
        last = l == 1
        ncA, _ = build_A()
        resA = run_bass_kernel_spmd(ncA, A_inmaps(l, h_lat, h_ctx, inp), core_ids=cores)
        ysl, ysc = A_gather([r["ysT"] for r in resA.results])
        ncB, _ = build_B(last)
        resB = run_bass_kernel_spmd(ncB, B_inmaps(l, last, h_lat, h_ctx, ysl, ysc, inp), core_ids=cores)
        h_lat, h_ctx = B_gather(last, [r["out"] for r in resB.results])
    return np.ascontiguousarray(h_lat, dtype=np.float32)
```

```python
import contextlib
import numpy as np
import concourse.bass as bass
import concourse.mybir as mybir
from concourse.bass_utils import run_bass_kernel_spmd

F32 = mybir.dt.float32
BF16 = mybir.dt.bfloat16
AF = mybir.ActivationFunctionType
ALU = mybir.AluOpType
AX = mybir.AxisListType

ENG_NAMES = ("pe", "act", "dve", "pool", "sp")
EPOCH = 16000
N_DMA_SEMS = 12

D = 1024
NB = 2
SEQ = 8192
CTX = 256
TTOT = SEQ + CTX
DFF = 2816
N_IN = 11152
GATE_OFF = N_IN - 4096
EPS = 1e-6


class Prog:
    def __init__(self, nc, same_engine_sync=True):
        self.nc = nc
        self.es = contextlib.ExitStack()
        self.ops = {e: [] for e in ENG_NAMES}
        self.cnt = {e: 0 for e in ENG_NAMES}
        self.sem = {}
        self.known = {e: {} for e in ENG_NAMES}
        self.semobj = {}
        self.ep = {}
        self.finals = []
        for e in ENG_NAMES:
            if e != "sp":
                self._new_epoch(e)
        self.dma_sems = {}
        self.dma_k = {}
        for q in ("sp", "act", "pool"):
            self.dma_sems[q] = []
            for i in range(N_DMA_SEMS):
                nm = f"d_{q}_{i}"
                self.semobj[nm] = self.es.enter_context(nc.semaphore(nm))
                self.dma_sems[q].append(nm)
            self.dma_k[q] = 0
        self.lastw = {}
        self.readers = {}
        self.env = {}
        self.same_engine_sync = same_engine_sync
        self.n_wait = 0
        self.n_ins = 0

    def _new_epoch(self, e):
        if e in self.sem:
            self.finals.append((self.sem[e], self.cnt[e]))
        k = self.ep.get(e, -1) + 1
        self.ep[e] = k
        nm = f"s_{e}_{k}"
        self.semobj[nm] = self.es.enter_context(self.nc.semaphore(nm))
        self.sem[e] = nm
        self.cnt[e] = 0

    def sb(self, name, shape, dtype=F32):
        return self.es.enter_context(self.nc.sbuf_tensor("sb_" + name, list(shape), dtype))

    def ps(self, name, shape, dtype=F32):
        return self.es.enter_context(self.nc.psum_tensor("ps_" + name, list(shape), dtype))

    def _deps(self, eng, reads, writes):
        need = {}
        for t in reads:
            for ev in self.lastw.get(t, ()):
                need[ev[0]] = max(need.get(ev[0], 0), ev[1])
        for t in writes:
            for ev in self.lastw.get(t, ()):
                need[ev[0]] = max(need.get(ev[0], 0), ev[1])
            for ev in self.readers.get(t, ()):
                need[ev[0]] = max(need.get(ev[0], 0), ev[1])
        kn = self.known[eng]
        for s, v in need.items():
            if kn.get(s, 0) >= v:
                continue
            if s.startswith("s_" + eng + "_") and (eng == "pe" or not self.same_engine_sync):
                continue
            self.ops[eng].append(("wait", self.semobj[s], v))
            self.n_wait += 1
            kn[s] = v

    def _commit(self, evs, reads, writes):
        for t in writes:
            self.lastw[t] = list(evs)
            self.readers[t] = []
        for t in reads:
            if t in writes:
                continue
            self.readers.setdefault(t, []).extend(evs)

    def op(self, eng, meth, reads=(), writes=(), **kw):
        if self.cnt[eng] >= EPOCH:
            self._new_epoch(eng)
        self._deps(eng, reads, writes)
        self.cnt[eng] += 1
        s = self.sem[eng]
        self.ops[eng].append(("ins", meth, kw, self.semobj[s], 1))
        self._commit([(s, self.cnt[eng])], reads, writes)
        self.n_ins += 1

    def dma(self, q, out, in_, reads=(), writes=(), **kw):
        self.dma_group([(q, out, in_)], reads, writes, **kw)

    def dma_group(self, items, reads=(), writes=(), **kw):
        for q in dict.fromkeys(it[0] for it in items):
            self._deps(q, reads, writes)
        evs = []
        for (q, out, in_) in items:
            k = self.dma_k[q]
            self.dma_k[q] += 1
            s = self.dma_sems[q][k % N_DMA_SEMS]
            target = 16 * (k // N_DMA_SEMS + 1)
            if target > 16 and self.known[q].get(s, 0) < target - 16:
                self.ops[q].append(("wait", self.semobj[s], target - 16))
                self.known[q][s] = target - 16
            self.ops[q].append(("ins", "dma_start", dict(kw, out=out, in_=in_), self.semobj[s], 16))
            evs.append((s, target))
            self.n_ins += 1
        self._commit(evs, reads, writes)

    def coll(self, kind, in_ap, out_ap, groups, reads=(), writes=()):
        q = "pool"
        self._deps(q, reads, writes)
        k = self.dma_k[q]
        self.dma_k[q] += 1
        s = self.dma_sems[q][k % N_DMA_SEMS]
        target = 16 * (k // N_DMA_SEMS + 1)
        if target > 16 and self.known[q].get(s, 0) < target - 16:
            self.ops[q].append(("wait", self.semobj[s], target - 16))
            self.known[q][s] = target - 16
        self.ops[q].append(("ins", "collective_compute", dict(kind=kind, op=ALU.bypass, replica_groups=groups, ins=[in_ap],
                                                               outs=[out_ap]), self.semobj[s], 16))
        self._commit([(s, target)], reads, writes)
        self.n_ins += 1

    def raw(self, eng, fn):
        self.ops[eng].append(("raw", fn))

    def finish_wait(self, eng, tokens):
        self._deps(eng, tokens, ())

    def barrier(self):
        evs = [(self.sem[x], self.cnt[x]) for x in ENG_NAMES if x != "sp" and self.cnt[x] > 0] + list(self.finals)
        for q in self.dma_sems:
            k = self.dma_k[q]
            for i, s in enumerate(self.dma_sems[q]):
                n = (k - i + N_DMA_SEMS - 1) // N_DMA_SEMS if k > i else 0
                if n > 0:
                    evs.append((s, 16 * n))
        for e in ENG_NAMES:
            kn = self.known[e]
            for (s, v) in evs:
                if kn.get(s, 0) >= v:
                    continue
                if s.startswith("s_" + e + "_"):
                    continue
                self.ops[e].append(("wait", self.semobj[s], v))
                kn[s] = v
        self.lastw.clear()
        self.readers.clear()

    def arena_init(self, nwords):
        self.arena = self.sb("arena", [128, nwords], F32)
        self.aoff = 0
        self.anw = nwords

    def carve(self, shape, dtype=F32):
        n = 1
        for d in shape[1:]:
            n *= d
        nw = n if dtype == F32 else (n + 1) // 2
        assert self.aoff + nw <= self.anw, ("arena overflow", self.aoff, nw, self.anw)
        v = self.arena[0:shape[0], self.aoff:self.aoff + nw]
        self.aoff += nw
        if dtype != F32:
            v = v.bitcast(dtype)[:, 0:n]
        if len(shape) == 3:
            v = v.rearrange("p (a b) -> p a b", a=shape[1])
        elif len(shape) == 4:
            v = v.rearrange("p (a b c) -> p a b c", a=shape[1], b=shape[2])
        return v

    def mark(self):
        return self.aoff

    def release(self, m):
        self.barrier()
        self.aoff = m

    def emit(self):
        nc = self.nc
        with nc.Block() as block:
            def mk(ename):
                lst = self.ops[ename]

                def body(e):
                    for it in lst:
                        if it[0] == "wait":
                            e.wait_ge(it[1], it[2])
                        elif it[0] == "raw":
                            it[1](e, self.env)
                        else:
                            kw = {k: (v(self.env) if callable(v) else v) for k, v in it[2].items()}
                            getattr(e, it[1])(**kw).then_inc(it[3], it[4])
                return body
            block.tensor(mk("pe"))
            block.scalar(mk("act"))
            block.vector(mk("dve"))
            block.gpsimd(mk("pool"))
            block.sync(mk("sp"))
        self.es.close()


class RR:
    def __init__(self, items):
        self.items = list(items)
        self.i = 0

    def next(self):
        x = self.items[self.i % len(self.items)]
        self.i += 1
        return x


def _common_consts(P):
    c = {}
    c["ones_bf"] = P.sb("ones_bf", [128, 128], BF16)
    P.op("pool", "memset", ap=c["ones_bf"][:], constant=1.0, writes=["ones_bf"])
    c["ones_f"] = P.sb("ones_f", [128, 128], F32)
    P.op("pool", "memset", ap=c["ones_f"][:], constant=1.0, writes=["ones_f"])
    c["ident"] = P.sb("ident", [128, 128], F32)
    P.op("pool", "memset", ap=c["ident"][:], constant=1.0, writes=["ident"])
    P.op("pool", "affine_select", out=c["ident"][:], in_=c["ident"][:], pattern=[[-1, 128]],
         compare_op=ALU.is_equal, fill=0.0, base=0, channel_multiplier=1, reads=["ident"], writes=["ident"])
    c["epsb"] = P.sb("epsb", [128, 1], F32)
    P.op("pool", "memset", ap=c["epsb"][:], constant=EPS, writes=["epsb"])
    return c


def _mod_vectors(P, cst, cvec, mod_w, mod_b, nchunks, wst, psum, modsb):
    craw = P.sb("craw", [128, 8, 2], F32)
    csil = P.sb("csil", [128, 8, 2], F32)
    mb = P.sb("modb", [128, 48], F32)
    P.dma_group([("sp", craw[:, :, s], cvec[s, :].rearrange("(k p) -> p k", p=128)) for s in range(2)], writes=["craw"],
                allow_slow_non_contiguous=True)
    P.dma("sp", mb[:, 0:nchunks], mod_b[0, 0:nchunks * 128].rearrange("(j p) -> p j", p=128), writes=["modb"],
          allow_slow_non_contiguous=True)
    P.op("act", "activation", out=csil[:], in_=craw[:], func=AF.Silu, reads=["craw"], writes=["csil"])
    ng = nchunks // 4
    for g in range(ng):
        st = wst[g % 2]
        tok = ("wst", g % 2)
        v = st[:, 0:4096].rearrange("p (k c) -> p k c", k=8)
        P.dma_group([("sp" if kc % 2 == 0 else "pool", v[:, kc, :], mod_w[kc * 128:(kc + 1) * 128, g * 512:(g + 1) * 512])
                     for kc in range(8)], writes=[tok])
        for jj in range(4):
            j = g * 4 + jj
            for kc in range(8):
                P.op("pe", "matmul", out=psum[:, j, :], lhsT=v[:, kc, jj * 128:(jj + 1) * 128], rhs=csil[:, kc, :],
                     start=(kc == 0), stop=(kc == 7), reads=[tok, "csil"], writes=["modps"])
    for s in range(2):
        P.op("dve", "tensor_tensor", out=modsb[:, 0:nchunks, s], in0=psum[:, 0:nchunks, s], in1=mb[:, 0:nchunks],
             op=ALU.add, reads=["modps", "modb"], writes=["modsb"])


def _rms_stats(P, cst, hT, nk, t0, tn, htok, sq, ssps, rstd, tagsfx):
    for kc in range(nk):
        P.op("act", "activation", out=sq[:, kc, 0:tn], in_=hT[:, kc, t0:t0 + tn], func=AF.Square,
             reads=[htok(kc)], writes=[("sq", kc)])
    for kc in range(nk):
        P.op("pe", "matmul", out=ssps[:, 0:tn], lhsT=cst["ones_bf"][:], rhs=sq[:, kc, 0:tn], start=(kc == 0),
             stop=(kc == nk - 1), reads=[("sq", kc), "ones_bf"], writes=["ssps"])
    P.op("act", "activation", out=rstd[:, 0:tn], in_=ssps[:, 0:tn], func=AF.Ln, scale=1.0 / (nk * 128),
         bias=cst["epsb"][:], reads=["ssps", "epsb"], writes=["rstd"])
    P.op("act", "activation", out=rstd[:, 0:tn], in_=rstd[:, 0:tn], func=AF.Exp, scale=-0.5,
         reads=["rstd"], writes=["rstd"])


def build_B(last: bool):
    NL = 2048
    NC_ = 0 if last else 64
    NT = NL + NC_
    nc = bass.Bass("TRN2", target_bir_lowering=False)
    hT_in = nc.dram_tensor("hT", [D, NT], F32, kind="ExternalInput").ap()
    ysT_in = nc.dram_tensor("ysT", [2048, NT], F32, kind="ExternalInput").ap()
    cvec = nc.dram_tensor("cvec", [2, D], F32, kind="ExternalInput").ap()
    mod_w = nc.dram_tensor("mod_w", [D, 6 * D], F32, kind="ExternalInput").ap()
    mod_b = nc.dram_tensor("mod_b", [1, 6 * D], F32, kind="ExternalInput").ap()
    n1g = nc.dram_tensor("n1g", [1, D], F32, kind="ExternalInput").ap()
    n2g = nc.dram_tensor("n2g", [1, D], F32, kind="ExternalInput").ap()
    fng = nc.dram_tensor("fng", [1, D], F32, kind="ExternalInput").ap()
    wg = nc.dram_tensor("wg", [D, 4096], F32, kind="ExternalInput").ap()
    gate_b = nc.dram_tensor("gate_b", [1, 4096], F32, kind="ExternalInput").ap()
    wbr = nc.dram_tensor("wbr", [4, 512, D], F32, kind="ExternalInput").ap()
    wo = nc.dram_tensor("wo", [D, D], F32, kind="ExternalInput").ap()
    w1 = nc.dram_tensor("w1", [D, DFF], F32, kind="ExternalInput").ap()
    w3 = nc.dram_tensor("w3", [D, DFF], F32, kind="ExternalInput").ap()
    w2 = nc.dram_tensor("w2", [DFF, D], F32, kind="ExternalInput").ap()
    if last:
        out = nc.dram_tensor("out", [NL, D], F32, kind="ExternalOutput").ap()
    else:
        out = nc.dram_tensor("out", [D, NT], F32, kind="ExternalOutput").ap()

    P = Prog(nc)
    cst = _common_consts(P)
    HMAX = 1088
    hT = P.sb("hT", [128, 8, HMAX], F32)
    xu = P.sb("xu", [128, 8, HMAX], BF16)
    A = P.sb("A", [128, 22, HMAX], BF16)
    mg = P.sb("mg", [128, 8, HMAX], BF16)
    wst = [P.sb(f"wst{i}", [128, 4096], F32) for i in range(2)]
    wbf = [P.sb(f"wbf{i}", [128, 4096], BF16) for i in range(2)]
    yst = [P.sb(f"yst{i}", [128, 512], F32) for i in range(2)]
    sq = P.sb("sq", [128, 8, 512], BF16)
    rstd = P.sb("rstd", [128, 512], F32)
    tmp = [P.sb(f"tmp{i}", [128, 512], F32) for i in range(3)]
    modsb = P.sb("modsb", [128, 48, 2], F32)
    gvec = P.sb("gvec", [128, 8, 3], F32)
    gbv = P.sb("gbv", [128, 32], F32)
    GS = P.sb("GS", [128, 8, 2, 4], F32)
    psA = [P.ps(f"psA{i}", [128, 512], F32) for i in range(2)]
    psB = [P.ps(f"psB{i}", [128, 512], F32) for i in range(2)]
    psS = P.ps("psS", [128, 512], F32)
    psM = P.ps("psM", [128, 48, 2], F32)
    psT = [P.ps(f"psT{i}", [128, 512], F32) for i in range(2)]

    P.dma_group([("sp", gvec[:, :, i], g_[0, :].rearrange("(k p) -> p k", p=128)) for i, g_ in enumerate((n1g, n2g, fng))],
                writes=["gvec"], allow_slow_non_contiguous=True)
    P.dma("sp", gbv[:], gate_b[0, :].rearrange("(j p) -> p j", p=128), writes=["gbv"], allow_slow_non_contiguous=True)
    _mod_vectors(P, cst, cvec, mod_w, mod_b, 48, wst, psM, modsb)
    for s in range(2):
        for (gi, sc_c, sh_c, col) in ((0, 1, 0, 0), (1, 4, 3, 2)):
            P.op("dve", "scalar_tensor_tensor", out=GS[:, :, s, col], in0=modsb[:, sc_c * 8:(sc_c + 1) * 8, s], scalar=1.0,
                 in1=gvec[:, :, gi], op0=ALU.add, op1=ALU.mult, reads=["modsb", "gvec"], writes=["GS"])
            P.op("dve", "tensor_copy", out=GS[:, :, s, col + 1], in_=modsb[:, sh_c * 8:(sh_c + 1) * 8, s],
                 reads=["modsb"], writes=["GS"])

    castq = RR(["dve", "pool"])
    dq = RR(["sp", "pool"])

    def cast(out, in_, reads, writes):
        e = castq.next()
        P.op(e, "tensor_copy", out=out, in_=in_, reads=reads, writes=writes)

    wctr = [0]

    def load_w(pieces, n):
        i = wctr[0] % 2
        wctr[0] += 1
        P.dma_group([(dq.next(), vf(wst[i]), src) for (vf, src) in pieces], writes=[("wst", i)])
        cast(wbf[i][:, 0:n], wst[i][:, 0:n], [("wst", i)], [("wbf", i)])
        return wbf[i], ("wbf", i)

    halves = [(0, 1024, 0), (1024, 1024, NC_)]
    for (l0, nl, ncx) in halves:
        ntok = nl + ncx
        blocks = [(o, 512, 0) for o in range(0, nl, 512)] + ([(nl, ncx, 1)] if ncx else [])
        def gcol(o):
            return l0 + o if o < nl else NL + (o - nl)
        for kc in range(8):
            for (o, n, s) in blocks:
                P.dma(dq.next(), hT[:, kc, o:o + n], hT_in[kc * 128:(kc + 1) * 128, gcol(o):gcol(o) + n],
                      writes=[("h", kc, o)])
        yi = 0
        for c16 in range(16):
            for (o, n, s) in blocks:
                st = yst[yi % 2]
                P.dma(dq.next(), st[:, 0:n], ysT_in[c16 * 128:(c16 + 1) * 128, gcol(o):gcol(o) + n], writes=[("yst", yi % 2)])
                cast(A[:, c16, o:o + n], st[:, 0:n], [("yst", yi % 2)], [("A", c16, o)])
                yi += 1
        for (o, n, s) in blocks:
            _rms_stats(P, cst, hT, 8, o, n, lambda kc: ("h", kc, o), sq, psS, rstd, "")
            for kc in range(8):
                t = tmp[kc % 2]
                P.op("dve", "tensor_tensor", out=t[:, 0:n], in0=hT[:, kc, o:o + n], in1=rstd[:, 0:n], op=ALU.mult,
                     reads=[("h", kc, o), "rstd"], writes=[("tmp", kc % 2)])
                P.op("pool", "tensor_scalar", out=xu[:, kc, o:o + n], in0=t[:, 0:n], scalar1=GS[:, kc, s, 0:1],
                     scalar2=GS[:, kc, s, 1:2], op0=ALU.mult, op1=ALU.add, reads=[("tmp", kc % 2), "GS"],
                     writes=[("xu", kc, o)])
        for j in range(8):
            gsrc = wg.rearrange("(kc p) (k j c) -> p k kc j c", p=128, k=4, j=8)
            pieces = [((lambda t, k=k: t[:, k * 1024:(k + 1) * 1024].rearrange("p (kc c) -> p kc c", kc=8)),
                       gsrc[:, k, :, j, :]) for k in range(4)]
            wgt, wgtok = load_w(pieces, 4096)
            wgv = wgt[:, 0:4096].rearrange("p (k kc c) -> p k kc c", k=4, kc=8)
            bsrc = wbr.rearrange("k (kc p) (j c) -> p k kc j c", p=128, j=8)
            pieces = [((lambda t, k=k: t[:, k * 512:(k + 1) * 512].rearrange("p (kc c) -> p kc c", kc=4)),
                       bsrc[:, k, :, j, :]) for k in range(4)]
            wbt, wbtok = load_w(pieces, 2048)
            wbv = wbt[:, 0:2048].rearrange("p (k kc c) -> p k kc c", k=4, kc=4)
            for (o, n, s) in blocks:
                for k in range(4):
                    pa = psA[k % 2]
                    pb = psB[k % 2]
                    for kc in range(8):
                        P.op("pe", "matmul", out=pa[:, 0:n], lhsT=wgv[:, k, kc, :], rhs=xu[:, kc, o:o + n], start=(kc == 0),
                             stop=(kc == 7), reads=[wgtok, ("xu", kc, o)], writes=[("psA", k % 2)])
                    for kc in range(4):
                        P.op("pe", "matmul", out=pb[:, 0:n], lhsT=wbv[:, k, kc, :], rhs=A[:, k * 4 + kc, o:o + n],
                             start=(kc == 0), stop=(kc == 3), reads=[wbtok, ("A", k * 4 + kc, o)], writes=[("psB", k % 2)])
                    sg = tmp[k % 2]
                    P.op("act", "activation", out=sg[:, 0:n], in_=pa[:, 0:n], func=AF.Sigmoid,
                         bias=gbv[:, k * 8 + j:k * 8 + j + 1], reads=[("psA", k % 2), "gbv"], writes=[("tmp", k % 2)])
                    if k == 0:
                        P.op("dve", "tensor_tensor", out=tmp[2][:, 0:n], in0=sg[:, 0:n], in1=pb[:, 0:n], op=ALU.mult,
                             reads=[("tmp", 0), ("psB", 0)], writes=[("tmp", 2)])
                    else:
                        P.op("dve", "tensor_tensor", out=sg[:, 0:n], in0=sg[:, 0:n], in1=pb[:, 0:n], op=ALU.mult,
                             reads=[("tmp", k % 2), ("psB", k % 2)], writes=[("tmp", k % 2)])
                        if k < 3:
                            P.op("pool", "tensor_tensor", out=tmp[2][:, 0:n], in0=tmp[2][:, 0:n], in1=sg[:, 0:n], op=ALU.add,
                                 reads=[("tmp", 2), ("tmp", k % 2)], writes=[("tmp", 2)])
                        else:
                            P.op("pool", "tensor_tensor", out=mg[:, j, o:o + n], in0=tmp[2][:, 0:n], in1=sg[:, 0:n], op=ALU.add,
                                 reads=[("tmp", 2), ("tmp", k % 2)], writes=[("mg", j, o)])
        for j in range(8):
            osrc = wo.rearrange("(kc p) (j c) -> p kc j c", p=128, j=8)
            wt, wtok = load_w([((lambda t: t[:, 0:1024].rearrange("p (kc c) -> p kc c", kc=8)), osrc[:, :, j, :])], 1024)
            wv = wt[:, 0:1024].rearrange("p (kc c) -> p kc c", kc=8)
            for bi, (o, n, s) in enumerate(blocks):
                pa = psA[bi % 2]
                for kc in range(8):
                    P.op("pe", "matmul", out=pa[:, 0:n], lhsT=wv[:, kc, :], rhs=mg[:, kc, o:o + n], start=(kc == 0),
                         stop=(kc == 7), reads=[wtok, ("mg", kc, o)], writes=[("psA", bi % 2)])
                P.op("dve", "scalar_tensor_tensor", out=hT[:, j, o:o + n], in0=pa[:, 0:n], scalar=modsb[:, 16 + j, s:s + 1],
                     in1=hT[:, j, o:o + n], op0=ALU.mult, op1=ALU.add, reads=[("psA", bi % 2), "modsb", ("h", j, o)],
                     writes=[("h", j, o)])
        for (o, n, s) in blocks:
            _rms_stats(P, cst, hT, 8, o, n, lambda kc: ("h", kc, o), sq, psS, rstd, "")
            for kc in range(8):
                t = tmp[kc % 2]
                P.op("dve", "tensor_tensor", out=t[:, 0:n], in0=hT[:, kc, o:o + n], in1=rstd[:, 0:n], op=ALU.mult,
                     reads=[("h", kc, o), "rstd"], writes=[("tmp", kc % 2)])
                P.op("pool", "tensor_scalar", out=xu[:, kc, o:o + n], in0=t[:, 0:n], scalar1=GS[:, kc, s, 2:3],
                     scalar2=GS[:, kc, s, 3:4], op0=ALU.mult, op1=ALU.add, reads=[("tmp", kc % 2), "GS"],
                     writes=[("xu", kc, o)])
        for c2 in range(22):
            s1 = w1.rearrange("(kc p) (j c) -> p kc j c", p=128, c=128)
            s3 = w3.rearrange("(kc p) (j c) -> p kc j c", p=128, c=128)
            wt, wtok = load_w([((lambda t: t[:, 0:1024].rearrange("p (kc c) -> p kc c", kc=8)), s1[:, :, c2, :]),
                               ((lambda t: t[:, 1024:2048].rearrange("p (kc c) -> p kc c", kc=8)), s3[:, :, c2, :])], 2048)
            wv = wt[:, 0:2048].rearrange("p (m kc c) -> p m kc c", m=2, kc=8)
            for bi, (o, n, s) in enumerate(blocks):
                pa = psA[bi % 2]
                pb = psB[bi % 2]
                for kc in range(8):
                    P.op("pe", "matmul", out=pa[:, 0:n], lhsT=wv[:, 0, kc, :], rhs=xu[:, kc, o:o + n], start=(kc == 0),
                         stop=(kc == 7), reads=[wtok, ("xu", kc, o)], writes=[("psA", bi % 2)])
                for kc in range(8):
                    P.op("pe", "matmul", out=pb[:, 0:n], lhsT=wv[:, 1, kc, :], rhs=xu[:, kc, o:o + n], start=(kc == 0),
                         stop=(kc == 7), reads=[wtok, ("xu", kc, o)], writes=[("psB", bi % 2)])
                t = tmp[bi % 2]
                P.op("act", "activation", out=t[:, 0:n], in_=pa[:, 0:n], func=AF.Silu, reads=[("psA", bi % 2)],
                     writes=[("tmp", bi % 2)])
                P.op("dve", "tensor_tensor", out=A[:, c2, o:o + n], in0=t[:, 0:n], in1=pb[:, 0:n], op=ALU.mult,
                     reads=[("tmp", bi % 2), ("psB", bi % 2)], writes=[("A", c2, o)])
        for j in range(8):
            s2 = w2.rearrange("(kc p) (j c) -> p kc j c", p=128, j=8)
            wt, wtok = load_w([((lambda t: t[:, 0:2816].rearrange("p (kc c) -> p kc c", kc=22)), s2[:, :, j, :])], 2816)
            wv = wt[:, 0:2816].rearrange("p (kc c) -> p kc c", kc=22)
            for bi, (o, n, s) in enumerate(blocks):
                pa = psA[bi % 2]
                for kc in range(22):
                    P.op("pe", "matmul", out=pa[:, 0:n], lhsT=wv[:, kc, :], rhs=A[:, kc, o:o + n], start=(kc == 0),
                         stop=(kc == 21), reads=[wtok, ("A", kc, o)], writes=[("psA", bi % 2)])
                P.op("dve", "scalar_tensor_tensor", out=hT[:, j, o:o + n], in0=pa[:, 0:n], scalar=modsb[:, 40 + j, s:s + 1],
                     in1=hT[:, j, o:o + n], op0=ALU.mult, op1=ALU.add, reads=[("psA", bi % 2), "modsb", ("h", j, o)],
                     writes=[("h", j, o)])
        if not last:
            for kc in range(8):
                for (o, n, s) in blocks:
                    P.dma(dq.next(), out[kc * 128:(kc + 1) * 128, gcol(o):gcol(o) + n], hT[:, kc, o:o + n],
                          reads=[("h", kc, o)], writes=[("out", kc, gcol(o))])
        else:
            ti = 0
            for (o, n, s) in blocks:
                _rms_stats(P, cst, hT, 8, o, n, lambda kc: ("h", kc, o), sq, psS, rstd, "")
                for kc in range(8):
                    P.op("dve", "scalar_tensor_tensor", out=hT[:, kc, o:o + n], in0=hT[:, kc, o:o + n], scalar=gvec[:, kc, 2:3],
                         in1=rstd[:, 0:n], op0=ALU.mult, op1=ALU.mult, reads=[("h", kc, o), "rstd", "gvec"], writes=[("h", kc, o)])
                for tt in range(n // 128):
                    for q4 in range(2):
                        pt = psT[ti % 2]
                        for kk in range(4):
                            kc = q4 * 4 + kk
                            P.op("pe", "transpose", out=pt[:, kk * 128:(kk + 1) * 128], in_=hT[:, kc, o + tt * 128:o + (tt + 1) * 128],
                                 identity=cst["ident"][:], reads=[("h", kc, o), "ident"], writes=[("psT", ti % 2)])
                        ot = tmp[ti % 2]
                        P.op("act" if ti % 2 else "dve", "activation" if ti % 2 else "tensor_copy", out=ot[:], in_=pt[:],
                             reads=[("psT", ti % 2)], writes=[("tmp", ti % 2)], **({"func": AF.Copy} if ti % 2 else {}))
                        r0 = l0 + o + tt * 128
                        P.dma(dq.next(), out[r0:r0 + 128, q4 * 512:(q4 + 1) * 512], ot[:], reads=[("tmp", ti % 2)],
                              writes=[("out", r0, q4)])
                        ti += 1
    outs = [k for k in P.lastw if isinstance(k, tuple) and k[0] == "out"]
    P.finish_wait("sp", outs)
    P.emit()
    return nc, P


def _c(a):
    return np.ascontiguousarray(a, dtype=np.float32)


def B_inmaps(l, last, h_lat, h_ctx, ysl, ysc, inp):
    maps = []
    shared = {
        "mod_w": _c(inp["mod_w"][l]), "mod_b": _c(inp["mod_b"][l][None]), "n1g": _c(inp["norm1_g"][l][None]),
        "n2g": _c(inp["norm2_g"][l][None]), "fng": _c(inp["final_norm_g"][None]),
        "wg": _c(inp["in_w"][l][:, GATE_OFF:]), "gate_b": _c(inp["gate_b"][l][None]), "wbr": _c(inp["branch_w"][l]),
        "wo": _c(inp["out_w"][l]), "w1": _c(inp["ffn_w1"][l]), "w3": _c(inp["ffn_w3"][l]), "w2": _c(inp["ffn_w2"][l]),
    }
    for core in range(8):
        b, j = core // 4, core % 4
        hl = h_lat[b, j * 2048:(j + 1) * 2048]
        yl = ysl[b, j * 2048:(j + 1) * 2048]
        if not last:
            hl = np.concatenate([hl, h_ctx[b, j * 64:(j + 1) * 64]], 0)
            yl = np.concatenate([yl, ysc[b, j * 64:(j + 1) * 64]], 0)
        m = dict(shared)
        m["hT"] = _c(hl.T)
        m["ysT"] = _c(yl.T)
        m["cvec"] = _c(np.stack([inp["c"][b], inp["c_ctx"]], 0))
        maps.append(m)
    return maps


def B_gather(last, outs):
    if last:
        return np.stack([np.concatenate(outs[0:4], 0), np.concatenate(outs[4:8], 0)], 0), None
    hl = np.stack([np.concatenate([o[:, :2048].T for o in outs[b * 4:(b + 1) * 4]], 0) for b in range(2)], 0)
    hc = np.stack([np.concatenate([o[:, 2048:].T for o in outs[b * 4:(b + 1) * 4]], 0) for b in range(2)], 0)
    return hl, hc


A_SLOTS = ["ret_q", "ret_qs", "ret_k", "ret_ks", "ret_v", "ret_g", "lru_x", "lru_y", "gdn_q", "gdn_k", "gdn_v", "gdn_z",
           "gdn_ab", "rw_r", "rw_k", "rw_v", "rw_gd", "rw_wd", "rw_ad"]
A_OUTS = [(n, n, 0, 128) for n in A_SLOTS if n not in ("gdn_ab", "rw_wd", "rw_ad")] + [
    ("gdn_af", "gdn_ab", 0, 1), ("gdn_abk", "gdn_ab", 1, 1), ("gdn_bf", "gdn_ab", 2, 1), ("gdn_bb", "gdn_ab", 3, 1),
    ("rw_wdf", "rw_wd", 0, 64), ("rw_wdb", "rw_wd", 64, 64), ("rw_adf", "rw_ad", 0, 64), ("rw_adb", "rw_ad", 64, 64)]
NSLOT = len(A_SLOTS)
A_BLOCKS = [(0, 256, 1)] + [(256 + i * 512, 512, 0) for i in range(16)]
GELU_C = 1.5957691216057308


def _a1_inproj(P, cst, PS, io, PT, outs_enabled):
    hT_in, wA, cvec, mod_w, mod_b, n1g = io["hT"], io["wA"], io["cvec"], io["mod_w"], io["mod_b"], io["n1g"]
    m0 = P.mark()
    wAb = P.carve([128, 8, NSLOT * 128], BF16)
    modsb = P.carve([128, 16, 2], F32)
    GS = P.carve([128, 8, 2, 2], F32)
    gv = P.carve([128, 8], F32)
    m1 = P.mark()
    wst = [P.carve([128, 4096], F32) for _ in range(2)]
    P.dma("sp", gv, n1g[0, :].rearrange("(k p) -> p k", p=128), writes=["gv"], allow_slow_non_contiguous=True)
    _mod_vectors(P, cst, cvec, mod_w, mod_b, 16, wst, PS[7][:, 0:96].rearrange("p (j s) -> p j s", s=2), modsb)
    for s in range(2):
        P.op("dve", "scalar_tensor_tensor", out=GS[:, :, s, 0], in0=modsb[:, 8:16, s], scalar=1.0, in1=gv, op0=ALU.add,
             op1=ALU.mult, reads=["modsb", "gv"], writes=["GS"])
        P.op("dve", "tensor_copy", out=GS[:, :, s, 1], in_=modsb[:, 0:8, s], reads=["modsb"], writes=["GS"])
    for kc in range(8):
        st = wst[kc % 2]
        P.dma("sp" if kc % 2 == 0 else "pool", st[:, 0:NSLOT * 128], wA[kc * 128:(kc + 1) * 128, :], writes=[("wst", kc % 2)])
        P.op("dve" if kc % 2 == 0 else "pool", "tensor_copy", out=wAb[:, kc, :], in_=st[:, 0:NSLOT * 128],
             reads=[("wst", kc % 2)], writes=[("wAb", kc)])
    P.release(m1)
    hblk = [P.carve([128, 8, 512], F32) for _ in range(2)]
    xu = [P.carve([128, 8, 512], BF16) for _ in range(2)]
    sq = P.carve([128, 8, 512], BF16)
    rstd = P.carve([128, 512], F32)
    tmp = [P.carve([128, 512], F32) for _ in range(2)]
    stg = [P.carve([128, 512], F32) for _ in range(4)]
    oi = 0
    for bi, (t0, n, s) in enumerate(A_BLOCKS):
        hb_, xb = hblk[bi % 2], xu[bi % 2]
        for kc in range(8):
            P.dma("sp" if kc % 2 == 0 else "pool", hb_[:, kc, 0:n], hT_in[kc * 128:(kc + 1) * 128, t0:t0 + n],
                  writes=[("hb", bi % 2, kc)])
        _rms_stats(P, cst, hb_, 8, 0, n, lambda kc: ("hb", bi % 2, kc), sq, PS[6], rstd, "")
        for kc in range(8):
            t = tmp[kc % 2]
            P.op("dve", "tensor_tensor", out=t[:, 0:n], in0=hb_[:, kc, 0:n], in1=rstd[:, 0:n], op=ALU.mult,
                 reads=[("hb", bi % 2, kc), "rstd"], writes=[("tmp", kc % 2)])
            P.op("pool", "tensor_scalar", out=xb[:, kc, 0:n], in0=t[:, 0:n], scalar1=GS[:, kc, s, 0:1], scalar2=GS[:, kc, s, 1:2],
                 op0=ALU.mult, op1=ALU.add, reads=[("tmp", kc % 2), "GS"], writes=[("xu", bi % 2, kc)])
        for (name, slot, c0, M) in A_OUTS:
            if name not in outs_enabled:
                continue
            col = A_SLOTS.index(slot) * 128 + c0
            ps = PS[oi % 4]
            for kc in range(8):
                P.op("pe", "matmul", out=ps[0:M, 0:n], lhsT=wAb[:, kc, col:col + M], rhs=xb[:, kc, 0:n], start=(kc == 0),
                     stop=(kc == 7), reads=[("wAb", kc), ("xu", bi % 2, kc)], writes=[("PS", oi % 4)])
            sg = stg[oi % 4]
            if oi % 2 == 0:
                P.op("act", "activation", out=sg[0:M, 0:n], in_=ps[0:M, 0:n], func=AF.Copy, reads=[("PS", oi % 4)],
                     writes=[("stg", oi % 4)])
            else:
                P.op("dve", "tensor_copy", out=sg[0:M, 0:n], in_=ps[0:M, 0:n], reads=[("PS", oi % 4)], writes=[("stg", oi % 4)])
            P.dma("sp" if oi % 2 == 0 else "pool", PT[name][:, t0:t0 + n], sg[0:M, 0:n], reads=[("stg", oi % 4)],
                  writes=[("PT", name, bi)])
            oi += 1
    ptw = {k: list(v) for k, v in P.lastw.items() if isinstance(k, tuple) and k[0] == "PT"}
    P.release(m0)


def _lru(P, cst, PS, io, PT, ysT):
    m0 = P.mark()
    cw = P.carve([128, 4], F32)
    cb = P.carve([128, 1], F32)
    gw = P.carve([128, 4, 128], F32)
    gb = P.carve([128, 4], F32)
    lam = P.carve([128, 2], F32)
    L8 = P.carve([128, 2], F32)
    L16 = P.carve([128, 2], F32)
    onec = P.carve([128, 1], F32)
    hbk = P.carve([128, TTOT], F32)
    P.op("pool", "memset", ap=onec, constant=1.0, writes=["onec"])
    P.dma("sp", cw, io["lru_cw"].rearrange("j c -> c j"), writes=["cw"], allow_slow_non_contiguous=True)
    P.dma("sp", cb, io["lru_cb"].rearrange("o c -> c o"), writes=["cb"], allow_slow_non_contiguous=True)
    P.dma("sp", gw, io["lru_gw"].rearrange("g c z -> c g z"), writes=["gw"])
    P.dma("sp", gb, io["lru_gb"].rearrange("g z -> z g"), writes=["gb"], allow_slow_non_contiguous=True)
    P.dma("sp", lam, io["lru_lam"].rearrange("d z -> z d"), writes=["lam"], allow_slow_non_contiguous=True)
    P.op("act", "activation", out=L8, in_=lam, func=AF.Exp, scale=-1.0, reads=["lam"], writes=["L8"])
    P.op("act", "activation", out=L8, in_=L8, func=AF.Ln, bias=onec, reads=["L8", "onec"], writes=["L8"])
    P.op("dve", "tensor_scalar", out=L16, in0=L8, scalar1=-16.0, scalar2=None, op0=ALU.mult, reads=["L8"], writes=["L16"])
    P.op("dve", "tensor_scalar", out=L8, in0=L8, scalar1=-8.0, scalar2=None, op0=ALU.mult, reads=["L8"], writes=["L8"])
    xh = [P.carve([128, 516], F32) for _ in range(2)]
    xc = [P.carve([128, 512], F32) for _ in range(2)]
    yb = [P.carve([128, 512], F32) for _ in range(2)]
    gr = P.carve([128, 512], F32)
    gi = P.carve([128, 512], F32)
    av = P.carve([128, 512], F32)
    bx = P.carve([128, 512], F32)
    hf = [P.carve([128, 512], F32) for _ in range(2)]
    ot = [P.carve([128, 512], F32) for _ in range(2)]
    it = 0
    for d in (1, 0):
        order = [A_BLOCKS[0]] + (A_BLOCKS[:0:-1] if d == 1 else A_BLOCKS[1:])
        prev = None
        for bi, (t0, n, s) in enumerate(order):
            k2 = it % 2
            it += 1
            x_, c_ = xh[k2], xc[k2]
            seg0, seg1 = (0, 256) if s else (256, TTOT)
            lo, hi = max(t0 - 2, seg0), min(t0 + n + 1, seg1)
            if lo > t0 - 2 or hi < t0 + n + 1:
                P.op("pool", "memset", ap=x_[:, 0:n + 3], constant=0.0, writes=[("xh", k2)])
            P.dma("sp", x_[:, lo - (t0 - 2):hi - (t0 - 2)], PT["lru_x"][:, lo:hi], writes=[("xh", k2)])
            P.op("dve", "tensor_scalar", out=c_[:, 0:n], in0=x_[:, 0:n], scalar1=cw[:, 0:1], scalar2=cb[:, 0:1], op0=ALU.mult,
                 op1=ALU.add, reads=[("xh", k2), "cw", "cb"], writes=[("xc", k2)])
            for j in range(1, 4):
                P.op("dve", "scalar_tensor_tensor", out=c_[:, 0:n], in0=x_[:, j:j + n], scalar=cw[:, j:j + 1], in1=c_[:, 0:n],
                     op0=ALU.mult, op1=ALU.add, reads=[("xh", k2), "cw", ("xc", k2)], writes=[("xc", k2)])
            for g, gt in ((0, gr), (1, gi)):
                ps = PS[g]
                P.op("pe", "matmul", out=ps[:, 0:n], lhsT=gw[:, d * 2 + g, :], rhs=c_[:, 0:n], start=True, stop=True,
                     reads=["gw", ("xc", k2)], writes=[("PS", g)])
                P.op("act", "activation", out=gt[:, 0:n], in_=ps[:, 0:n], func=AF.Sigmoid, bias=gb[:, d * 2 + g:d * 2 + g + 1],
                     reads=[("PS", g), "gb"], writes=[("g", g)])
            P.op("act", "activation", out=av[:, 0:n], in_=gr[:, 0:n], func=AF.Exp, scale=L8[:, d:d + 1], reads=[("g", 0), "L8"],
                 writes=["av"])
            P.op("act", "activation", out=gr[:, 0:n], in_=gr[:, 0:n], func=AF.Exp, scale=L16[:, d:d + 1], reads=[("g", 0), "L16"],
                 writes=[("g", 0)])
            P.op("dve", "tensor_scalar", out=gr[:, 0:n], in0=gr[:, 0:n], scalar1=-1.0, scalar2=1.0, op0=ALU.mult, op1=ALU.add,
                 reads=[("g", 0)], writes=[("g", 0)])
            P.op("act", "activation", out=gr[:, 0:n], in_=gr[:, 0:n], func=AF.Sqrt, reads=[("g", 0)], writes=[("g", 0)])
            P.op("pool", "tensor_tensor", out=gi[:, 0:n], in0=gi[:, 0:n], in1=c_[:, 0:n], op=ALU.mult, reads=[("g", 1), ("xc", k2)],
                 writes=[("g", 1)])
            P.op("dve", "tensor_tensor", out=bx[:, 0:n], in0=gi[:, 0:n], in1=gr[:, 0:n], op=ALU.mult, reads=[("g", 1), ("g", 0)],
                 writes=["bx"])
            if d == 1:
                init = 0.0 if prev is None else hbk[:, prev:prev + 1]
                P.op("dve", "tensor_tensor_scan", out=hbk[:, t0:t0 + n][:, ::-1], data0=av[:, 0:n][:, ::-1], data1=bx[:, 0:n][:, ::-1],
                     initial=init, op0=ALU.mult, op1=ALU.add, reads=["av", "bx", "hbk_c"], writes=[("hbk", t0), "hbk_c"])
                prev = t0
            else:
                h_ = hf[k2]
                init = 0.0 if prev is None else hf[1 - k2][:, prev - 1:prev]
                P.op("dve", "tensor_tensor_scan", out=h_[:, 0:n], data0=av[:, 0:n], data1=bx[:, 0:n], initial=init, op0=ALU.mult,
                     op1=ALU.add, reads=["av", "bx", ("hf", 1 - k2)], writes=[("hf", k2)])
                prev = n
                y_ = yb[k2]
                P.dma("pool", y_[:, 0:n], PT["lru_y"][:, t0:t0 + n], writes=[("yb", k2)])
                o_ = ot[k2]
                P.op("pool", "tensor_tensor", out=o_[:, 0:n], in0=y_[:, 0:n], in1=y_[:, 0:n], op=ALU.mult, reads=[("yb", k2)],
                     writes=[("ot", k2)])
                P.op("pool", "tensor_scalar", out=o_[:, 0:n], in0=o_[:, 0:n], scalar1=0.044715, scalar2=1.0, op0=ALU.mult,
                     op1=ALU.add, reads=[("ot", k2)], writes=[("ot", k2)])
                P.op("pool", "tensor_tensor", out=o_[:, 0:n], in0=o_[:, 0:n], in1=y_[:, 0:n], op=ALU.mult, reads=[("ot", k2), ("yb", k2)],
                     writes=[("ot", k2)])
                P.op("act", "activation", out=o_[:, 0:n], in_=o_[:, 0:n], func=AF.Sigmoid, scale=GELU_C, reads=[("ot", k2)],
                     writes=[("ot", k2)])
                P.op("pool", "tensor_tensor", out=o_[:, 0:n], in0=o_[:, 0:n], in1=y_[:, 0:n], op=ALU.mult, reads=[("ot", k2), ("yb", k2)],
                     writes=[("ot", k2)])
                P.op("dve", "tensor_tensor", out=y_[:, 0:n], in0=h_[:, 0:n], in1=hbk[:, t0:t0 + n], op=ALU.add,
                     reads=[("hf", k2), ("hbk", t0)], writes=[("yb", k2)])
                P.op("dve", "tensor_tensor", out=o_[:, 0:n], in0=o_[:, 0:n], in1=y_[:, 0:n], op=ALU.mult, reads=[("ot", k2), ("yb", k2)],
                     writes=[("ot", k2)])
                P.dma("sp", ysT[128:256, t0:t0 + n], o_[:, 0:n], reads=[("ot", k2)], writes=[("ysT", 1, t0)])
    P.release(m0)


def build_A(enabled=("inproj", "lru", "ret", "gdn", "rwkv")):
    nc = bass.Bass("TRN2", target_bir_lowering=False)
    io = {}

    def inp(name, shape):
        io[name] = nc.dram_tensor(name, list(shape), F32, kind="ExternalInput").ap()
    inp("hT", [D, TTOT]); inp("wA", [D, NSLOT * 128]); inp("cvec", [2, D]); inp("mod_w", [D, 6 * D]); inp("mod_b", [1, 6 * D])
    inp("n1g", [1, D])
    inp("lru_cw", [4, 128]); inp("lru_cb", [1, 128]); inp("lru_gw", [4, 128, 128]); inp("lru_gb", [4, 128]); inp("lru_lam", [2, 128])
    inp("cA", [128, CA_COLS]); inp("ropeC", [128, TTOT]); inp("ropeS", [128, TTOT]); inp("ret_de", [1, 2])
    inp("gdn_cw", [3, 4, 128]); inp("gdn_sc", [1, 4]); inp("gdn_ng", [1, 128])
    inp("rw_pc", [13, 128]); inp("rw_pc64", [4, 64]); inp("rw_w2", [2, 64, 128]); inp("rw_a2", [2, 64, 128]); inp("rw_g2", [128, 128])
    OB = {m: nc.dram_tensor("ob_" + m, [TTOT, 128], F32, kind="Internal").ap() for m in ("ret", "gdn", "rwkv")}
    ysT = nc.dram_tensor("ysT", [512, TTOT], F32, kind="ExternalOutput").ap()
    ptkind = "ExternalOutput" if "debug_pt" in enabled else "Internal"
    PT = {n: nc.dram_tensor("pt_" + n, [M, TTOT], F32, kind=ptkind).ap() for (n, _, _, M) in A_OUTS}
    P = Prog(nc)
    cst = _common_consts(P)
    PS = [P.ps(f"bank{i}", [128, 512], F32) for i in range(8)]
    P.arena_init(44000)
    need = set()
    if "lru" in enabled:
        need |= {"lru_x", "lru_y"}
    if "ret" in enabled:
        need |= {"ret_q", "ret_qs", "ret_k", "ret_ks", "ret_v", "ret_g"}
    if "gdn" in enabled:
        need |= {"gdn_q", "gdn_k", "gdn_v", "gdn_z", "gdn_af", "gdn_abk", "gdn_bf", "gdn_bb"}
    if "rwkv" in enabled:
        need |= {"rw_r", "rw_k", "rw_v", "rw_gd", "rw_wdf", "rw_wdb", "rw_adf", "rw_adb"}
    if "debug_pt" in enabled:
        need = set(n for (n, _, _, _) in A_OUTS)
    _a1_inproj(P, cst, PS, io, PT, need)
    if "lru" in enabled:
        _lru(P, cst, PS, io, PT, ysT)
    if "ret" in enabled:
        _ret(P, cst, PS, io, PT, ysT, OB["ret"])
    if "gdn" in enabled:
        _gdn(P, cst, PS, io, PT, ysT, OB["gdn"])
    if "rwkv" in enabled:
        _rwkv(P, cst, PS, io, PT, ysT, OB["rwkv"])
    P.barrier()
    P.emit()
    return nc, P


def A_weight_cols(h):
    def rng(a, n=128):
        return list(range(a, a + n))
    cols = {}
    cols["ret_q"] = rng(0 + h * 128)
    cols["ret_qs"] = rng(h * 128 + 64, 64) + rng(h * 128, 64)
    cols["ret_k"] = rng(512 + h * 128)
    cols["ret_ks"] = rng(512 + h * 128 + 64, 64) + rng(512 + h * 128, 64)
    cols["ret_v"] = rng(1024 + h * 128)
    cols["ret_g"] = rng(1536 + h * 128)
    cols["lru_x"] = rng(2048 + h * 128)
    cols["lru_y"] = rng(2560 + h * 128)
    cols["gdn_q"] = rng(3072 + h * 128)
    cols["gdn_k"] = rng(3584 + h * 128)
    cols["gdn_v"] = rng(4096 + h * 128)
    cols["gdn_z"] = rng(4608 + h * 128)
    cols["gdn_ab"] = [5120 + h, 5120 + 4 + h, 5128 + h, 5128 + 4 + h] + [-1] * 124
    R0 = 5136
    cols["rw_r"] = rng(R0 + h * 128)
    cols["rw_k"] = rng(R0 + 512 + h * 128)
    cols["rw_v"] = rng(R0 + 1024 + h * 128)
    cols["rw_gd"] = rng(R0 + 1536)
    cols["rw_wd"] = rng(R0 + 1664)
    cols["rw_ad"] = rng(R0 + 1792)
    idx = []
    for s in A_SLOTS:
        idx += cols[s]
    return np.array(idx)


def A_inmaps(l, h_lat, h_ctx, inp):
    maps = []
    in_w = np.concatenate([inp["in_w"][l], np.zeros((D, 1), np.float32)], 1)
    for core in range(8):
        b, h = core // 4, core % 4
        m = {}
        m["hT"] = _c(np.concatenate([h_ctx[b], h_lat[b]], 0).T)
        m["wA"] = _c(in_w[:, A_weight_cols(h)])
        m["cvec"] = _c(np.stack([inp["c"][b], inp["c_ctx"]], 0))
        m["mod_w"] = _c(inp["mod_w"][l]); m["mod_b"] = _c(inp["mod_b"][l][None]); m["n1g"] = _c(inp["norm1_g"][l][None])
        sl = slice(h * 128, (h + 1) * 128)
        m["lru_cw"] = _c(inp["lru_conv_w"][l][:, sl]); m["lru_cb"] = _c(inp["lru_conv_b"][l][None, sl])
        m["lru_gw"] = _c(inp["lru_gate_w"][l][:, :, h].reshape(4, 128, 128))
        m["lru_gb"] = _c(inp["lru_gate_b"][l][:, :, sl].reshape(4, 128)); m["lru_lam"] = _c(inp["lru_lambda"][l][:, sl])
        m["cA"] = make_cA(); m["ropeC"], m["ropeS"] = make_rope()
        m["ret_de"] = _c(inp["ret_decay_exp"][l][:, h][None])
        gcw = inp["gdn_conv_w"][l]
        m["gdn_cw"] = _c(np.stack([gcw[:, i * 512 + h * 128:i * 512 + (h + 1) * 128] for i in range(3)], 0))
        m["gdn_sc"] = _c(np.concatenate([inp["gdn_a_log"][l][:, h], inp["gdn_dt_bias"][l][:, h]])[None])
        m["gdn_ng"] = _c(inp["gdn_norm_g"][l][None])
        mu = inp["rwkv_mu"][l]
        m["rw_pc"] = _c(np.stack([mu[0:512][sl], mu[512:1024][sl], mu[1024:1536][sl], mu[1536:1664], inp["rwkv_w0"][l][0][sl],
                                  inp["rwkv_w0"][l][1][sl], inp["rwkv_a0"][l][0][sl], inp["rwkv_a0"][l][1][sl], inp["rwkv_k_k"][l][sl],
                                  inp["rwkv_k_a"][l][sl], inp["rwkv_r_k"][l].reshape(512)[sl], inp["rwkv_ln_g"][l][sl],
                                  inp["rwkv_ln_b"][l][sl]], 0))
        m["rw_pc64"] = _c(np.stack([mu[1664:1728], mu[1728:1792], mu[1792:1856], mu[1856:1920]], 0))
        m["rw_w2"] = _c(inp["rwkv_w2"][l][:, :, sl]); m["rw_a2"] = _c(inp["rwkv_a2"][l][:, :, sl]); m["rw_g2"] = _c(inp["rwkv_g2"][l][:, sl])
        maps.append(m)
    return maps


CH = 64
CORE_FP32R = False


class Core:
    def __init__(self, P, cst, PS, NH, has_delta, identC):
        self.P, self.cst, self.PS, self.NH, self.hd, self.delta = P, cst, PS, NH, 128 // NH, has_delta
        self.identC = identC
        self.W = NH * CH
        self.G = 512 // self.W
        self.sets = []
        for i in range(2):
            d = {}
            for nm in (("Pm", "Nm", "Pm2", "Nm2", "Xa", "Aak", "Aqb", "Aqk") if has_delta else ("Aqk",)):
                d[nm] = P.carve([CH, 512], F32)
            self.sets.append(d)
        self.xu = [{nm: P.carve([CH, 128], F32) for nm in ("X", "U")} for _ in range(2)]
        self.kb = 0
        self.kc = 0

    def _mm(self, out, otok, lhsT, ltok, rhs, rtok, start=True, stop=True, fast=True):
        if CORE_FP32R and fast:
            lhsT = lhsT.bitcast(mybir.dt.float32r)
            rhs = rhs.bitcast(mybir.dt.float32r)
        self.P.op("pe", "matmul", out=out, lhsT=lhsT, rhs=rhs, start=start, stop=stop, reads=[ltok, rtok], writes=[otok])

    def _L(self, w, nm, h):
        return (w[nm][0], w[nm][1]) if self.NH == 1 else w[nm + "m"][h]

    def prep_gen(self, kb, ws, MexT, MexTok, MinT, MinTok):
        P, PS, NH, W = self.P, self.PS, self.NH, self.W
        T = self.sets[kb % 2]
        tk = lambda nm: ("core", nm, kb % 2)
        G = len(ws)
        GW = G * W
        MexTok = list(MexTok) if isinstance(MexTok, list) else [MexTok]
        MinTok = list(MinTok) if isinstance(MinTok, list) else [MinTok]
        bA, bB = PS[2], PS[3]
        tA, tB = ("bank", 2), ("bank", 3)
        rA, rB = bA[0:CH, 0:GW], bB[0:CH, 0:GW]
        cols = [(g, h, slice(g * W + h * CH, g * W + (h + 1) * CH)) for g in range(G) for h in range(NH)]
        mm = self._mm
        if self.delta:
            for (g, h, c) in cols:
                l, lt = self._L(ws[g], "bT", h)
                mm(rA[:, c], tA, l, lt, ws[g]["aT"][0], ws[g]["aT"][1])
            P.op("dve", "tensor_tensor", out=T["Pm"][:, 0:GW], in0=rA, in1=MexT, op=ALU.mult, reads=MexTok, writes=[tk("Pm"), tA])
            yield
            for (g, h, c) in cols:
                P.op("pe", "transpose", out=rB[:, c], in_=T["Pm"][:, c], identity=self.cst["ident"][0:CH, 0:CH],
                     reads=[tk("Pm"), "ident"], writes=[tB])
            P.op("act", "activation", out=T["Nm"][:, 0:GW], in_=rB, func=AF.Copy, reads=[], writes=[tk("Nm"), tB])
            P.op("pool", "tensor_tensor", out=T["Xa"][:, 0:GW], in0=T["Pm"][:, 0:GW], in1=self.identC[:, 0:GW], op=ALU.add,
                 reads=[tk("Pm"), "cAs"], writes=[tk("Xa")])
            yield
            Pm, Nm, Pm2, Nm2 = "Pm", "Nm", "Pm2", "Nm2"
            for lvl in range(1, 6):
                if lvl < 5:
                    for (g, h, c) in cols:
                        mm(rA[:, c], tA, T[Nm][:, c], tk(Nm), T[Pm][:, c], tk(Pm))
                    P.op("act", "activation", out=T[Pm2][:, 0:GW], in_=rA, func=AF.Copy, reads=[], writes=[tk(Pm2), tA])
                for (g, h, c) in cols:
                    mm(rB[:, c], tB, T[Pm][:, c], tk(Pm), T[Nm][:, c], tk(Nm))
                P.op("dve", "tensor_copy", out=T[Nm2][:, 0:GW], in_=rB, reads=[], writes=[tk(Nm2), tB])
                yield
                for (g, h, c) in cols:
                    mm(rA[:, c], tA, T[Nm2][:, c], tk(Nm2), T["Xa"][:, c], tk("Xa"))
                P.op("dve", "tensor_tensor", out=T["Xa"][:, 0:GW], in0=rA, in1=T["Xa"][:, 0:GW], op=ALU.add, reads=[],
                     writes=[tk("Xa"), tA])
                yield
                Pm, Pm2 = Pm2, Pm
                Nm, Nm2 = Nm2, Nm
            for (g, h, c) in cols:
                l, lt = self._L(ws[g], "kT", h)
                mm(rB[:, c], tB, l, lt, ws[g]["aT"][0], ws[g]["aT"][1])
            P.op("dve", "tensor_tensor", out=T["Aak"][:, 0:GW], in0=rB, in1=MexT, op=ALU.mult, reads=MexTok, writes=[tk("Aak"), tB])
            yield
            for (g, h, c) in cols:
                l, lt = self._L(ws[g], "bT", h)
                mm(rA[:, c], tA, l, lt, ws[g]["qT"][0], ws[g]["qT"][1])
            P.op("dve", "tensor_tensor", out=T["Aqb"][:, 0:GW], in0=rA, in1=MinT, op=ALU.mult, reads=MinTok, writes=[tk("Aqb"), tA])
            yield
        for (g, h, c) in cols:
            l, lt = self._L(ws[g], "kT", h)
            mm(rB[:, c], tB, l, lt, ws[g]["qT"][0], ws[g]["qT"][1])
        P.op("dve", "tensor_tensor", out=T["Aqk"][:, 0:GW], in0=rB, in1=MinT, op=ALU.mult, reads=MinTok, writes=[tk("Aqk"), tB])
        yield

    def recur_gen(self, kb, g, w, S, Stok, O, Otok):
        P, PS, NH, hd, W = self.P, self.PS, self.NH, self.hd, self.W
        T = self.sets[kb % 2]
        tk = lambda nm: ("core", nm, kb % 2)
        kc = self.kc
        self.kc += 1
        XU = self.xu[kc % 2]
        xt = lambda nm: ("coreXU", nm, kc % 2)
        bC, bD, bE = PS[4], PS[5], PS[6]
        tC, tD, tE = ("bank", 4), ("bank", 5), ("bank", 6)
        hsl = [slice(h * hd, (h + 1) * hd) for h in range(NH)]
        csl = [slice(g * W + h * CH, g * W + (h + 1) * CH) for h in range(NH)]
        xr, ur = bC[0:CH, 0:128], bC[0:CH, 128:256]
        orr = bD[0:CH, 0:128]
        sr = bE[:, 0:hd]
        mm = self._mm
        A = lambda nm: w[nm][0]
        Tk = lambda nm: w[nm][1]
        if self.delta:
            for h in range(NH):
                l, lt = self._L(w, "aTs", h)
                mm(xr[:, hsl[h]], tC, l, lt, S, Stok, True, False)
                mm(xr[:, hsl[h]], tC, T["Aak"][:, csl[h]], tk("Aak"), A("V")[:, hsl[h]], Tk("V"), False, True)
            P.op("act", "activation", out=XU["X"], in_=xr, func=AF.Copy, reads=[], writes=[xt("X"), tC])
            yield
            for h in range(NH):
                mm(ur[:, hsl[h]], tC, T["Xa"][:, csl[h]], tk("Xa"), XU["X"][:, hsl[h]], xt("X"))
            P.op("act", "activation", out=XU["U"], in_=ur, func=AF.Copy, reads=[], writes=[xt("U"), tC])
            yield
        for h in range(NH):
            if self.delta:
                mm(sr[hsl[h], :], tE, A("Bst")[:, hsl[h]], Tk("Bst"), XU["U"][:, hsl[h]], xt("U"), True, False, fast=(NH == 1))
            mm(sr[hsl[h], :], tE, A("Kst")[:, hsl[h]], Tk("Kst"), A("V")[:, hsl[h]], Tk("V"), not self.delta, True, fast=(NH == 1))
        for h in range(NH):
            l, lt = self._L(w, "qTs", h)
            mm(orr[:, hsl[h]], tD, l, lt, S, Stok, True, False)
            if self.delta:
                mm(orr[:, hsl[h]], tD, T["Aqb"][:, csl[h]], tk("Aqb"), XU["U"][:, hsl[h]], xt("U"), False, False)
            mm(orr[:, hsl[h]], tD, T["Aqk"][:, csl[h]], tk("Aqk"), A("V")[:, hsl[h]], Tk("V"), False, True)
        P.op("dve", "scalar_tensor_tensor", out=S, in0=S, scalar=A("cs"), in1=sr, op0=ALU.mult, op1=ALU.add,
             reads=[Tk("cs")], writes=[Stok, tE])
        P.op("act", "activation", out=O, in_=orr, func=AF.Copy, reads=[], writes=[Otok, tD])
        yield


def interleave(a, b):
    live = [g for g in (a, b) if g is not None]
    while live:
        for g in list(live):
            try:
                next(g)
            except StopIteration:
                live.remove(g)


RW_BLOCKS = [(0, 256, 1)] + [(256 + i * 256, 256, 0) for i in range(32)]


def _dir_chunks(d, blocks=None):
    blocks = A_BLOCKS if blocks is None else blocks
    order = [blocks[0]] + (blocks[:0:-1] if d == 1 else blocks[1:])
    res = []
    for (t0, n, s) in order:
        offs = list(range(0, n, CH))
        if d == 1:
            offs = offs[::-1]
        res.append((t0, n, s, offs))
    return res


CA = {}
_o = 0
for _nm, _n in (("RELF", 64), ("RELB", 64), ("MASKF", 64), ("MASKB", 64), ("MSTRF", 64), ("MSTRB", 64), ("POS1F", 512), ("POS1B", 512),
                ("KDF", 512), ("KDB", 512), ("RSTF", 512), ("RSTB", 512), ("NEGINF", 64), ("NEGINB", 64), ("NEGEXF", 64),
                ("NEGEXB", 64), ("IDC", 512), ("BLK", 128), ("MSTRF4", 512), ("MSTRB4", 512), ("MASKF4", 512), ("MASKB4", 512)):
    CA[_nm] = (_o, _n)
    _o += _n
CA_COLS = _o


def make_cA():
    c = np.zeros((128, CA_COLS), np.float32)
    s = np.arange(64)[:, None]
    t = np.arange(64)[None, :]
    def put(nm, a):
        o, n = CA[nm]
        c[:a.shape[0], o:o + n] = a
    put("RELF", np.where(s <= t, t - s, 0)); put("RELB", np.where(s >= t, s - t, 0))
    put("MASKF", (s <= t) * 1.0); put("MASKB", (s >= t) * 1.0)
    put("MSTRF", (s < t) * 1.0); put("MSTRB", (s > t) * 1.0)
    tt = np.arange(512)[None, :] % 64
    put("POS1F", np.broadcast_to(tt + 1.0, (128, 512))); put("POS1B", np.broadcast_to(64.0 - tt, (128, 512)))
    put("KDF", np.broadcast_to(63.0 - tt, (128, 512))); put("KDB", np.broadcast_to(tt * 1.0, (128, 512)))
    put("RSTF", np.broadcast_to((tt != 0) * 1.0, (128, 512))); put("RSTB", np.broadcast_to((tt != 63) * 1.0, (128, 512)))
    NEG = -30000.0
    put("NEGINF", np.where(s <= t, 0.0, NEG)); put("NEGINB", np.where(s >= t, 0.0, NEG))
    put("NEGEXF", np.where(s < t, 0.0, NEG)); put("NEGEXB", np.where(s > t, 0.0, NEG))
    put("IDC", np.concatenate([np.eye(64)] * 8, 1))
    put("MSTRF4", np.concatenate([(s < t) * 1.0] * 8, 1)); put("MSTRB4", np.concatenate([(s > t) * 1.0] * 8, 1))
    put("MASKF4", np.concatenate([(s <= t) * 1.0] * 8, 1)); put("MASKB4", np.concatenate([(s >= t) * 1.0] * 8, 1))
    blk = np.zeros((128, 128)); blk[:64, :64] = 1; blk[64:, 64:] = 1
    put("BLK", blk)
    return c


def make_rope():
    tpos = np.arange(SEQ)
    row = (tpos // 64).astype(np.float32)
    col = (tpos % 64).astype(np.float32)
    inv = (10000.0 ** (-np.arange(32, dtype=np.float32) / 32)).astype(np.float32)
    ang = np.concatenate([row[:, None] * inv, col[:, None] * inv], -1)
    cos, sin = np.cos(ang).astype(np.float32), np.sin(ang).astype(np.float32)
    CC = np.ones((128, TTOT), np.float32)
    SS = np.zeros((128, TTOT), np.float32)
    CC[:64, CTX:] = cos.T; CC[64:, CTX:] = cos.T
    SS[:64, CTX:] = -sin.T; SS[64:, CTX:] = sin.T
    return CC, SS


def _ld_const(P, io, cAs, nm, rows=128):
    return cAs[nm][0:rows, :]


def _load_consts(P, io, names):
    tot = sum(CA[nm][1] for nm in names)
    tile = P.carve([128, tot], F32)
    res = {}
    items = []
    o = 0
    for i, nm in enumerate(names):
        so, n = CA[nm]
        res[nm] = tile[:, o:o + n]
        items.append(("sp" if i % 2 == 0 else "pool", tile[:, o:o + n], io["cA"][:, so:so + n]))
        o += n
    P.dma_group(items, writes=["cAs"])
    return res


def _ret(P, cst, PS, io, PT, ysT, OB):
    m0 = P.mark()
    cAs = _load_consts(P, io, ("RELF", "RELB", "MASKF", "MASKB", "POS1F", "POS1B", "KDF", "KDB"))
    lg = P.carve([128, 2], F32)
    onec = P.carve([128, 1], F32)
    P.op("pool", "memset", ap=onec, constant=1.0, writes=["onec"])
    P.dma("sp", lg, io["ret_de"].partition_broadcast(128), writes=["lg"])
    P.op("act", "activation", out=lg, in_=lg, func=AF.Exp, scale=-float(np.log(2.0)), reads=["lg"], writes=["lg"])
    P.op("act", "activation", out=lg, in_=lg, func=AF.Ln, scale=-1.0, bias=onec, reads=["lg", "onec"], writes=["lg"])
    MinT = [P.carve([CH, CH], F32) for _ in range(2)]
    POSQ = [P.carve([128, 512], F32) for _ in range(2)]
    KDEC = [P.carve([128, 512], F32) for _ in range(2)]
    csc = P.carve([128, 2], F32)
    P.op("act", "activation", out=csc, in_=lg, func=AF.Exp, scale=float(CH), reads=["lg"], writes=["csc"])
    for d in range(2):
        sfx = "FB"[d]
        P.op("act", "activation", out=MinT[d], in_=_ld_const(P, io, cAs, "REL" + sfx, 64), func=AF.Exp, scale=lg[0:CH, d:d + 1],
             reads=["cAs", "lg"], writes=[("MinT", d)])
        P.op("dve", "scalar_tensor_tensor", out=MinT[d], in0=MinT[d], scalar=128.0 ** -0.5, in1=_ld_const(P, io, cAs, "MASK" + sfx, 64),
             op0=ALU.mult, op1=ALU.mult, reads=[("MinT", d), "cAs"], writes=[("MinT", d)])
        P.op("act", "activation", out=POSQ[d], in_=_ld_const(P, io, cAs, "POS1" + sfx), func=AF.Exp, scale=lg[:, d:d + 1],
             reads=["cAs", "lg"], writes=[("POSQ", d)])
        P.op("act", "activation", out=KDEC[d], in_=_ld_const(P, io, cAs, "KD" + sfx), func=AF.Exp, scale=lg[:, d:d + 1],
             reads=["cAs", "lg"], writes=[("KDEC", d)])
        P.op("dve", "tensor_scalar", out=KDEC[d], in0=KDEC[d], scalar1=128.0 ** -0.5, scalar2=None, op0=ALU.mult,
             reads=[("KDEC", d)], writes=[("KDEC", d)])
    core = Core(P, cst, PS, 1, False, None)
    MinT8 = [P.carve([CH, 8, CH], F32) for _ in range(2)]
    for d in range(2):
        for r_ in range(8):
            P.op("pool" if r_ % 2 else "dve", "tensor_copy", out=MinT8[d][:, r_, :], in_=MinT[d], reads=[("MinT", d)], writes=[("MinT8", d, r_)])
    S = P.carve([128, 128], F32)
    ld = {nm: [P.carve([128, 512], F32) for _ in range(2)] for nm in ("q", "qs", "k", "ks", "v", "cc", "ss", "g")}
    qr = [P.carve([128, 512], F32) for _ in range(2)]
    kr = [P.carve([128, 512], F32) for _ in range(2)]
    qsd = [P.carve([128, 512], F32) for _ in range(2)]
    kdc = [P.carve([128, 512], F32) for _ in range(2)]
    outb = [P.carve([128, 512], F32) for _ in range(2)]
    Vt = [[P.carve([CH, 128], F32) for _ in range(8)] for _ in range(2)]
    Kst = [[P.carve([CH, 128], F32) for _ in range(8)] for _ in range(2)]
    Ot = [P.carve([CH, 128], F32) for _ in range(4)]
    Ob = [P.carve([CH, 128], F32) for _ in range(4)]
    ssq = [P.carve([CH, 2], F32) for _ in range(4)]
    junk = P.carve([CH, 128], F32)
    epsC = cst["epsb"]
    bi_ = 0
    cctr = [0]
    kbc = [0]

    def recur_batch(kb, batch, d, b2, t0, n, lastbatch):
        for gi, (w, c0, ci_) in enumerate(batch):
            c4 = cctr[0] % 4
            cctr[0] += 1
            cs_ = slice(c0, c0 + CH)
            yield from core.recur_gen(kb, gi, w, S, "S", Ot[c4], ("Ot", c4))
            tg = t0 + c0
            if d == 1:
                P.dma("sp", OB[tg:tg + CH, :], Ot[c4], reads=[("Ot", c4)], writes=[("OB", tg)])
            else:
                P.dma("sp", Ob[c4], OB[tg:tg + CH, :], reads=[("OB", tg)], writes=[("Ob", c4)])
                P.op("dve", "tensor_tensor", out=Ot[c4], in0=Ot[c4], in1=Ob[c4], op=ALU.add, reads=[("Ot", c4), ("Ob", c4)],
                     writes=[("Ot", c4)])
                P.op("act", "activation", out=junk, in_=Ot[c4], func=AF.Square, accum_out=ssq[c4][:, 0:1], reads=[("Ot", c4)],
                     writes=["junk", ("ssq", c4)])
                P.op("act", "activation", out=ssq[c4][:, 1:2], in_=ssq[c4][:, 0:1], func=AF.Ln, scale=1.0 / 128, bias=epsC[0:CH, :],
                     reads=[("ssq", c4), "epsb"], writes=[("ssq", c4)])
                P.op("act", "activation", out=ssq[c4][:, 1:2], in_=ssq[c4][:, 1:2], func=AF.Exp, scale=-0.5, reads=[("ssq", c4)],
                     writes=[("ssq", c4)])
                yield
                P.op("dve", "tensor_scalar", out=Ot[c4], in0=Ot[c4], scalar1=ssq[c4][:, 1:2], scalar2=None, op0=ALU.mult,
                     reads=[("Ot", c4), ("ssq", c4)], writes=[("Ot", c4)])
                tp3 = PS[1][:, 0:CH]
                P.op("pe", "transpose", out=tp3, in_=Ot[c4], identity=cst["ident"][0:CH, 0:CH], reads=[("Ot", c4), "ident"],
                     writes=[("bank", 1)])
                P.op("dve", "tensor_tensor", out=outb[b2][:, cs_], in0=tp3, in1=ld["g"][b2][:, cs_], op=ALU.mult,
                     reads=[("ld", "g", b2)], writes=[("outb", b2, c0), ("bank", 1)])
            yield
        if d == 0 and lastbatch:
            P.dma("pool", ysT[0:128, t0:t0 + n], outb[b2][:, 0:n], reads=[("outb", b2, c0_) for c0_ in range(0, n, CH)],
                  writes=[("ysT", 0, t0)])
        yield

    for d in (1, 0):
        P.op("pool", "memset", ap=S, constant=0.0, reads=[], writes=["S"])
        pending = None
        for (t0, n, s, offs) in _dir_chunks(d):
            b2 = bi_ % 2
            bi_ += 1
            names = ["q", "qs", "k", "ks", "v", "cc", "ss"] + (["g"] if d == 0 else [])
            for i, nm in enumerate(names):
                src = {"q": PT["ret_q"], "qs": PT["ret_qs"], "k": PT["ret_k"], "ks": PT["ret_ks"], "v": PT["ret_v"],
                       "g": PT["ret_g"], "cc": io["ropeC"], "ss": io["ropeS"]}[nm]
                P.dma("sp" if i % 2 == 0 else "pool", ld[nm][b2][:, 0:n], src[:, t0:t0 + n], writes=[("ld", nm, b2)])
            for (dst, a_, b_, nm) in ((qr[b2], "q", "qs", "qr"), (kr[b2], "k", "ks", "kr")):
                P.op("dve", "tensor_tensor", out=dst[:, 0:n], in0=ld[a_][b2][:, 0:n], in1=ld["cc"][b2][:, 0:n], op=ALU.mult,
                     reads=[("ld", a_, b2), ("ld", "cc", b2)], writes=[(nm, b2)])
                P.op("pool", "tensor_tensor", out=ld[b_][b2][:, 0:n], in0=ld[b_][b2][:, 0:n], in1=ld["ss"][b2][:, 0:n], op=ALU.mult,
                     reads=[("ld", b_, b2), ("ld", "ss", b2)], writes=[("ld", b_, b2)])
                P.op("dve", "tensor_tensor", out=dst[:, 0:n], in0=dst[:, 0:n], in1=ld[b_][b2][:, 0:n], op=ALU.add,
                     reads=[(nm, b2), ("ld", b_, b2)], writes=[(nm, b2)])
            P.op("pool", "tensor_tensor", out=qsd[b2][:, 0:n], in0=qr[b2][:, 0:n], in1=POSQ[d][:, 0:n], op=ALU.mult,
                 reads=[("qr", b2), ("POSQ", d)], writes=[("qsd", b2)])
            P.op("pool", "tensor_tensor", out=kdc[b2][:, 0:n], in0=kr[b2][:, 0:n], in1=KDEC[d][:, 0:n], op=ALU.mult,
                 reads=[("kr", b2), ("KDEC", d)], writes=[("kdc", b2)])
            if d == 0:
                P.op("act", "activation", out=ld["g"][b2][:, 0:n], in_=ld["g"][b2][:, 0:n], func=AF.Silu, reads=[("ld", "g", b2)],
                     writes=[("ld", "g", b2)])
            wsb = []
            for ci_, c0 in enumerate(offs):
                cs_ = slice(c0, c0 + CH)
                tp = PS[7][0:CH, 0:128]
                P.op("pe", "transpose", out=tp, in_=ld["v"][b2][:, cs_], identity=cst["ident"][:], reads=[("ld", "v", b2), "ident"],
                     writes=[("bank", 7)])
                P.op("act", "activation", out=Vt[b2][ci_], in_=tp, func=AF.Copy, reads=[], writes=[("Vt", b2, ci_), ("bank", 7)])
                tp2 = PS[0][0:CH, 0:128]
                P.op("pe", "transpose", out=tp2, in_=kdc[b2][:, cs_], identity=cst["ident"][:], reads=[("kdc", b2), "ident"],
                     writes=[("bank", 0)])
                P.op("dve", "tensor_copy", out=Kst[b2][ci_], in_=tp2, reads=[], writes=[("Kst", b2, ci_), ("bank", 0)])
                w = {"kT": (kr[b2][:, cs_], ("kr", b2)), "qT": (qr[b2][:, cs_], ("qr", b2)), "qTs": (qsd[b2][:, cs_], ("qsd", b2)),
                     "Kst": (Kst[b2][ci_], ("Kst", b2, ci_)), "V": (Vt[b2][ci_], ("Vt", b2, ci_)), "cs": (csc[:, d:d + 1], "csc")}
                wsb.append((w, c0, ci_))
            G = core.G
            for st in range(0, len(wsb), G):
                batch = wsb[st:st + G]
                kb = kbc[0]
                kbc[0] += 1
                m8 = MinT8[d].rearrange("p a b -> p (a b)")[:, 0:len(batch) * CH]
                pg = core.prep_gen(kb, [x[0] for x in batch], None, [], m8, [("MinT8", d, r_) for r_ in range(8)])
                interleave(pg, pending)
                pending = recur_batch(kb, batch, d, b2, t0, n, st + G >= len(wsb))
        interleave(None, pending)
    P.release(m0)


def _gdn(P, cst, PS, io, PT, ysT, OB):
    m0 = P.mark()
    cAs = _load_consts(P, io, ("RSTF", "RSTB", "NEGINF", "NEGINB", "NEGEXF", "NEGEXB", "IDC"))
    cw = P.carve([128, 3, 4], F32)
    sc = P.carve([1, 4], F32)
    nga = P.carve([1, 2], F32)
    ng = P.carve([128, 1], F32)
    one1 = P.carve([1, 128], F32)
    P.op("pool", "memset", ap=one1, constant=1.0, writes=["one1"])
    P.dma("sp", cw, io["gdn_cw"].rearrange("m j c -> c m j"), writes=["cw"], allow_slow_non_contiguous=True)
    P.dma("sp", sc, io["gdn_sc"], writes=["sc"])
    P.dma("sp", ng, io["gdn_ng"].rearrange("o c -> c o"), writes=["ng"], allow_slow_non_contiguous=True)
    P.op("act", "activation", out=nga, in_=sc[:, 0:2], func=AF.Exp, reads=["sc"], writes=["nga"])
    P.op("dve", "tensor_scalar", out=nga, in0=nga, scalar1=-1.0, scalar2=None, op0=ALU.mult, reads=["nga"], writes=["nga"])
    core = Core(P, cst, PS, 1, True, _ld_const(P, io, cAs, "IDC", 64))
    S = P.carve([128, 128], F32)
    xh = {nm: [P.carve([128, 516], F32) for _ in range(2)] for nm in "qkv"}
    cv = {nm: [P.carve([128, 512], F32) for _ in range(2)] for nm in "qkv"}
    zt = [P.carve([128, 512], F32) for _ in range(2)]
    sqt = P.carve([128, 512], F32)
    rs = P.carve([128, 512], F32)
    rows = {nm: [P.carve([1, 512], F32)] * 2 for nm in ("a", "b", "g", "gc", "gcx", "ngc", "r", "ein", "eex", "cb", "dend")}
    bt = {nm: [P.carve([128, 512], F32) for _ in range(2)] for nm in ("bT", "kT", "aTs", "qTs", "KsT", "BsT", "EinS")}
    outb = [P.carve([128, 512], F32) for _ in range(2)]
    Mx = [P.carve([CH, 512], F32) for _ in range(2)]
    Mi = [P.carve([CH, 512], F32) for _ in range(2)]
    Vt = [[P.carve([CH, 128], F32) for _ in range(8)] for _ in range(2)]
    Kst = [[P.carve([CH, 128], F32) for _ in range(8)] for _ in range(2)]
    Bst = [[P.carve([CH, 128], F32) for _ in range(8)] for _ in range(2)]
    Ot = [P.carve([CH, 128], F32) for _ in range(4)]
    Ob = [P.carve([CH, 128], F32) for _ in range(4)]
    ssq = [P.carve([CH, 2], F32) for _ in range(4)]
    junk = P.carve([CH, 128], F32)
    epsC = cst["epsb"]
    b0, b1, b7 = ("bank", 0), ("bank", 1), ("bank", 7)
    bi_ = 0
    cctr = [0]
    kbc = [0]

    def recur_batch(kb, batch, d, b2, t0, n, lastbatch):
        for gi, (w, c0, ci_) in enumerate(batch):
            c4 = cctr[0] % 4
            cctr[0] += 1
            cs_ = slice(c0, c0 + CH)
            yield from core.recur_gen(kb, gi, w, S, "S", Ot[c4], ("Ot", c4))
            tg = t0 + c0
            if d == 1:
                P.dma("sp", OB[tg:tg + CH, :], Ot[c4], reads=[("Ot", c4)], writes=[("OB", tg)])
            else:
                P.dma("sp", Ob[c4], OB[tg:tg + CH, :], reads=[("OB", tg)], writes=[("Ob", c4)])
                P.op("dve", "tensor_tensor", out=Ot[c4], in0=Ot[c4], in1=Ob[c4], op=ALU.add, reads=[("Ot", c4), ("Ob", c4)],
                     writes=[("Ot", c4)])
                P.op("act", "activation", out=junk, in_=Ot[c4], func=AF.Square, accum_out=ssq[c4][:, 0:1], reads=[("Ot", c4)],
                     writes=["junk", ("ssq", c4)])
                P.op("act", "activation", out=ssq[c4][:, 1:2], in_=ssq[c4][:, 0:1], func=AF.Ln, scale=1.0 / 128, bias=epsC[0:CH, :],
                     reads=[("ssq", c4), "epsb"], writes=[("ssq", c4)])
                P.op("act", "activation", out=ssq[c4][:, 1:2], in_=ssq[c4][:, 1:2], func=AF.Exp, scale=-0.5, reads=[("ssq", c4)],
                     writes=[("ssq", c4)])
                yield
                P.op("dve", "tensor_scalar", out=Ot[c4], in0=Ot[c4], scalar1=ssq[c4][:, 1:2], scalar2=None, op0=ALU.mult,
                     reads=[("Ot", c4), ("ssq", c4)], writes=[("Ot", c4)])
                tp3 = PS[7][:, 384:384 + CH]
                P.op("pe", "transpose", out=tp3, in_=Ot[c4], identity=cst["ident"][0:CH, 0:CH], reads=[("Ot", c4), "ident"],
                     writes=[b7])
                P.op("dve", "scalar_tensor_tensor", out=outb[b2][:, cs_], in0=tp3, scalar=ng[:, 0:1], in1=zt[b2][:, cs_],
                     op0=ALU.mult, op1=ALU.mult, reads=[("zt", b2), "ng"], writes=[("outb", b2, c0), b7])
            yield
        if d == 0 and lastbatch:
            P.dma("pool", ysT[256:384, t0:t0 + n], outb[b2][:, 0:n], reads=[("outb", b2, c0_) for c0_ in range(0, n, CH)],
                  writes=[("ysT", 2, t0)])
        yield

    for d in (1, 0):
        sfx = "FB"[d]
        osfx = "BF"[d]
        P.op("pool", "memset", ap=S, constant=0.0, reads=[], writes=["S"])
        pending = None
        for (t0, n, s, offs) in _dir_chunks(d):
            b2 = bi_ % 2
            bi_ += 1
            seg0, seg1 = (0, 256) if s else (256, TTOT)
            lo, hi = max(t0 - 2, seg0), min(t0 + n + 1, seg1)
            for mi_, nm in enumerate("qkv"):
                x_ = xh[nm][b2]
                if lo > t0 - 2 or hi < t0 + n + 1:
                    P.op("pool", "memset", ap=x_[:, 0:n + 3], constant=0.0, writes=[("xh", nm, b2)])
                P.dma("sp" if mi_ != 1 else "pool", x_[:, lo - (t0 - 2):hi - (t0 - 2)], PT["gdn_" + nm][:, lo:hi], writes=[("xh", nm, b2)])
                c_ = cv[nm][b2]
                eng = "dve" if mi_ != 2 else "pool"
                P.op("dve", "tensor_scalar", out=c_[:, 0:n], in0=x_[:, 0:n], scalar1=cw[:, mi_, 0:1], scalar2=None, op0=ALU.mult,
                     reads=[("xh", nm, b2), "cw"], writes=[("cv", nm, b2)])
                for j in range(1, 4):
                    P.op("dve", "scalar_tensor_tensor", out=c_[:, 0:n], in0=x_[:, j:j + n], scalar=cw[:, mi_, j:j + 1], in1=c_[:, 0:n],
                         op0=ALU.mult, op1=ALU.add, reads=[("xh", nm, b2), "cw", ("cv", nm, b2)], writes=[("cv", nm, b2)])
                P.op("act", "activation", out=c_[:, 0:n], in_=c_[:, 0:n], func=AF.Silu, reads=[("cv", nm, b2)], writes=[("cv", nm, b2)])
            for nm, scl in (("q", 128.0 ** -0.5), ("k", 1.0)):
                c_ = cv[nm][b2]
                P.op("act", "activation", out=sqt[:, 0:n], in_=c_[:, 0:n], func=AF.Square, reads=[("cv", nm, b2)], writes=["sqt"])
                P.op("pe", "matmul", out=PS[1][:, 0:n], lhsT=cst["ones_f"][:], rhs=sqt[:, 0:n], start=True, stop=True,
                     reads=["sqt", "ones_f"], writes=[b1])
                P.op("act", "activation", out=rs[:, 0:n], in_=PS[1][:, 0:n], func=AF.Ln, bias=epsC[:], reads=["epsb"], writes=["rs", b1])
                P.op("act", "activation", out=rs[:, 0:n], in_=rs[:, 0:n], func=AF.Exp, scale=-0.5, reads=["rs"], writes=["rs"])
                P.op("dve", "scalar_tensor_tensor", out=c_[:, 0:n], in0=c_[:, 0:n], scalar=scl, in1=rs[:, 0:n], op0=ALU.mult, op1=ALU.mult,
                     reads=[("cv", nm, b2), "rs"], writes=[("cv", nm, b2)])
            R = {nm: rows[nm][0] for nm in rows}
            rt = lambda nm: ("row", nm, 0)
            P.dma("pool", R["a"][:, 0:n], PT["gdn_af" if d == 0 else "gdn_abk"][:, t0:t0 + n], writes=[rt("a")])
            P.dma("pool", R["b"][:, 0:n], PT["gdn_bf" if d == 0 else "gdn_bb"][:, t0:t0 + n], writes=[rt("b")])
            P.op("act", "activation", out=R["g"][:, 0:n], in_=R["a"][:, 0:n], func=AF.Exp, bias=sc[:, 2 + d:3 + d], reads=[rt("a"), "sc"],
                 writes=[rt("g")])
            P.op("act", "activation", out=R["g"][:, 0:n], in_=R["g"][:, 0:n], func=AF.Ln, bias=one1[:, 0:1], reads=[rt("g"), "one1"],
                 writes=[rt("g")])
            P.op("dve", "tensor_scalar", out=R["g"][:, 0:n], in0=R["g"][:, 0:n], scalar1=nga[:, d:d + 1], scalar2=None, op0=ALU.mult,
                 reads=[rt("g"), "nga"], writes=[rt("g")])
            P.op("act", "activation", out=R["b"][:, 0:n], in_=R["b"][:, 0:n], func=AF.Sigmoid, reads=[rt("b")], writes=[rt("b")])
            rstm = _ld_const(P, io, cAs, "RST" + sfx, 1)
            rsto = _ld_const(P, io, cAs, "RST" + osfx, 1)
            rv = (lambda ap: ap[:, 0:n][:, ::-1]) if d == 1 else (lambda ap: ap[:, 0:n])
            rvo = (lambda ap: ap[:, 0:n][:, ::-1]) if d == 0 else (lambda ap: ap[:, 0:n])
            P.op("dve", "tensor_tensor_scan", out=rv(R["gc"]), data0=rv(rstm), data1=rv(R["g"]), initial=0.0, op0=ALU.mult, op1=ALU.add,
                 reads=[rt("g"), "cAs"], writes=[rt("gc")])
            P.op("dve", "tensor_tensor_scan", out=rvo(R["r"]), data0=rvo(rsto), data1=rvo(R["g"]), initial=0.0, op0=ALU.mult, op1=ALU.add,
                 reads=[rt("g"), "cAs"], writes=[rt("r")])
            P.op("dve", "tensor_tensor", out=R["gcx"][:, 0:n], in0=R["gc"][:, 0:n], in1=R["g"][:, 0:n], op=ALU.subtract,
                 reads=[rt("gc"), rt("g")], writes=[rt("gcx")])
            P.op("dve", "tensor_scalar", out=R["ngc"][:, 0:n], in0=R["gc"][:, 0:n], scalar1=-1.0, scalar2=None, op0=ALU.mult,
                 reads=[rt("gc")], writes=[rt("ngc")])
            P.op("dve", "tensor_tensor", out=R["r"][:, 0:n], in0=R["r"][:, 0:n], in1=R["g"][:, 0:n], op=ALU.subtract,
                 reads=[rt("r"), rt("g")], writes=[rt("r")])
            P.op("act", "activation", out=R["dend"][:, 0:n], in_=R["r"][:, 0:n], func=AF.Exp, reads=[rt("r")], writes=[rt("dend")])
            P.op("act", "activation", out=R["ein"][:, 0:n], in_=R["gc"][:, 0:n], func=AF.Exp, reads=[rt("gc")], writes=[rt("ein")])
            P.op("act", "activation", out=R["eex"][:, 0:n], in_=R["gcx"][:, 0:n], func=AF.Exp, reads=[rt("gcx")], writes=[rt("eex")])
            P.op("act", "activation", out=R["cb"][:, 0:n], in_=R["g"][:, 0:n], func=AF.Exp, reads=[rt("g")], writes=[rt("cb")])
            P.op("dve", "scalar_tensor_tensor", out=R["cb"][:, 0:n], in0=R["cb"][:, 0:n], scalar=-1.0, in1=R["b"][:, 0:n], op0=ALU.mult,
                 op1=ALU.mult, reads=[rt("cb"), rt("b")], writes=[rt("cb")])
            B = {nm: bt[nm][b2] for nm in bt}
            btk = lambda nm: ("bt", nm, b2)
            kn, qn = cv["k"][b2], cv["q"][b2]

            def bcast_mul(row, rtoks, dst, dtok, src, stok, bank, bk):
                P.op("pe", "matmul", out=bank[:, 0:n], lhsT=one1[0:1, :], rhs=row[:, 0:n], start=True, stop=True,
                     reads=rtoks + ["one1"], writes=[bk])
                if src is None:
                    P.op("act", "activation", out=dst[:, 0:n], in_=bank[:, 0:n], func=AF.Copy, reads=[], writes=[dtok, bk])
                else:
                    P.op("dve", "tensor_tensor", out=dst[:, 0:n], in0=src[:, 0:n], in1=bank[:, 0:n], op=ALU.mult, reads=[stok],
                         writes=[dtok, bk])
            bcast_mul(R["ein"], [rt("ein")], B["EinS"], btk("EinS"), None, None, PS[1], b1)
            P.op("pool", "tensor_tensor", out=B["qTs"][:, 0:n], in0=qn[:, 0:n], in1=B["EinS"][:, 0:n], op=ALU.mult,
                 reads=[("cv", "q", b2), btk("EinS")], writes=[btk("qTs")])
            bcast_mul(R["eex"], [rt("eex")], B["aTs"], btk("aTs"), kn, ("cv", "k", b2), PS[1], b1)
            bcast_mul(R["cb"], [rt("cb")], B["bT"], btk("bT"), kn, ("cv", "k", b2), PS[1], b1)
            bcast_mul(R["b"], [rt("b")], B["kT"], btk("kT"), kn, ("cv", "k", b2), PS[1], b1)
            bcast_mul(R["dend"], [rt("dend")], B["KsT"], btk("KsT"), B["kT"], btk("kT"), PS[1], b1)
            bcast_mul(R["dend"], [rt("dend")], B["BsT"], btk("BsT"), B["bT"], btk("bT"), PS[1], b1)
            if d == 0:
                P.dma("pool", zt[b2][:, 0:n], PT["gdn_z"][:, t0:t0 + n], writes=[("zt", b2)])
                P.op("act", "activation", out=zt[b2][:, 0:n], in_=zt[b2][:, 0:n], func=AF.Silu, reads=[("zt", b2)], writes=[("zt", b2)])
            wsb = []
            for ci_, c0 in enumerate(offs):
                cs_ = slice(c0, c0 + CH)
                ms_ = slice(ci_ * CH, (ci_ + 1) * CH)
                for (row, negnm, Mt, mnm, col) in ((R["gcx"], "NEGEX" + sfx, Mx[b2], "Mx", 0), (R["gc"], "NEGIN" + sfx, Mi[b2], "Mi", 64)):
                    reg = PS[0][0:CH, col:col + CH]
                    P.op("pe", "matmul", out=reg, lhsT=one1[0:1, 0:CH], rhs=row[:, cs_], start=True, stop=False,
                         reads=["one1", rt("gcx"), rt("gc")], writes=[b0])
                    P.op("pe", "matmul", out=reg, lhsT=R["ngc"][:, cs_], rhs=one1[0:1, 0:CH], start=False, stop=False,
                         reads=["one1", rt("ngc")], writes=[b0])
                    P.op("pe", "matmul", out=reg, lhsT=cst["ident"][0:CH, 0:CH], rhs=_ld_const(P, io, cAs, negnm, 64), start=False,
                         stop=True, reads=["ident", "cAs"], writes=[b0])
                    P.op("act", "activation", out=Mt[:, ms_], in_=reg, func=AF.Exp, reads=[], writes=[(mnm, b2, ci_), b0])
                for (src, stok, dst, dnm, col, eng) in ((cv["v"][b2], ("cv", "v", b2), Vt[b2][ci_], "Vt", 0, "act"),
                                                        (B["KsT"], btk("KsT"), Kst[b2][ci_], "Kst", 128, "dve"),
                                                        (B["BsT"], btk("BsT"), Bst[b2][ci_], "Bst", 256, "act")):
                    tp = PS[7][0:CH, col:col + 128]
                    P.op("pe", "transpose", out=tp, in_=src[:, cs_], identity=cst["ident"][:], reads=[stok, "ident"], writes=[b7])
                    if eng == "act":
                        P.op("act", "activation", out=dst, in_=tp, func=AF.Copy, reads=[], writes=[(dnm, b2, ci_), b7])
                    else:
                        P.op("dve", "tensor_copy", out=dst, in_=tp, reads=[], writes=[(dnm, b2, ci_), b7])
                last = c0 + CH - 1 if d == 0 else c0
                w = {"aT": (kn[:, cs_], ("cv", "k", b2)), "qT": (qn[:, cs_], ("cv", "q", b2)), "bT": (B["bT"][:, cs_], btk("bT")),
                     "kT": (B["kT"][:, cs_], btk("kT")), "aTs": (B["aTs"][:, cs_], btk("aTs")), "qTs": (B["qTs"][:, cs_], btk("qTs")),
                     "Bst": (Bst[b2][ci_], ("Bst", b2, ci_)), "Kst": (Kst[b2][ci_], ("Kst", b2, ci_)), "V": (Vt[b2][ci_], ("Vt", b2, ci_)),
                     "cs": (B["EinS"][:, last:last + 1], btk("EinS"))}
                wsb.append((w, c0, ci_))
            G = core.G
            for st in range(0, len(wsb), G):
                batch = wsb[st:st + G]
                kb = kbc[0]
                kbc[0] += 1
                gw = slice(st * CH, (st + len(batch)) * CH)
                pg = core.prep_gen(kb, [x[0] for x in batch], Mx[b2][:, gw], [("Mx", b2, x[2]) for x in batch], Mi[b2][:, gw],
                                   [("Mi", b2, x[2]) for x in batch])
                interleave(pg, pending)
                pending = recur_batch(kb, batch, d, b2, t0, n, st + G >= len(wsb))
        interleave(None, pending)
    P.release(m0)


NEG_EM05 = -float(np.exp(-0.5))
RWKV_LN_EPS = 64e-5


def _rwkv(P, cst, PS, io, PT, ysT, OB):
    m0 = P.mark()
    cAs = _load_consts(P, io, ("RSTF", "RSTB", "IDC", "BLK", "MSTRF4", "MSTRB4", "MASKF4", "MASKB4"))
    pc = P.carve([128, 16], F32)
    pc64 = P.carve([64, 4], F32)
    P.dma("sp", pc[:, 0:13], io["rw_pc"].rearrange("j c -> c j"), writes=["pc"], allow_slow_non_contiguous=True)
    P.dma("sp", pc64, io["rw_pc64"].rearrange("j c -> c j"), writes=["pc64"], allow_slow_non_contiguous=True)
    om = P.carve([128, 4], F32); hm = P.carve([128, 4], F32); om64 = P.carve([64, 4], F32); hm64 = P.carve([64, 4], F32)
    omka = P.carve([128, 1], F32)
    epsl = P.carve([128, 1], F32)
    P.op("pool", "memset", ap=epsl, constant=RWKV_LN_EPS, writes=["epsl"])
    for (o_, h_, src, tk_) in ((om, hm, pc[:, 0:4], "pc"), (om64, hm64, pc64, "pc64")):
        P.op("dve", "tensor_scalar", out=o_, in0=src, scalar1=-1.0, scalar2=1.0, op0=ALU.mult, op1=ALU.add, reads=[tk_], writes=["omhm"])
        P.op("dve", "tensor_scalar", out=h_, in0=src, scalar1=0.5, scalar2=None, op0=ALU.mult, reads=[tk_], writes=["omhm2"])
    P.op("dve", "tensor_scalar", out=omka, in0=pc[:, 9:10], scalar1=-1.0, scalar2=1.0, op0=ALU.mult, op1=ALU.add, reads=["pc"],
         writes=["omka"])
    w2 = P.carve([64, 2, 128], F32); a2 = P.carve([64, 2, 128], F32); g2 = P.carve([128, 128], F32)
    P.dma("sp", w2, io["rw_w2"].rearrange("d r c -> r d c"), writes=["w2"])
    P.dma("sp", a2, io["rw_a2"].rearrange("d r c -> r d c"), writes=["a2"])
    P.dma("sp", g2, io["rw_g2"], writes=["g2"])
    BLK = _ld_const(P, io, cAs, "BLK")
    core = Core(P, cst, PS, 2, True, _ld_const(P, io, cAs, "IDC", 64))
    BN = 256
    S = P.carve([128, 64], F32)
    raw = {nm: [P.carve([128, BN + 4], F32)] * 2 for nm in ("r", "k", "v", "gd", "wd", "ad", "adb")}
    X = {nm: P.carve([128, BN], F32) for nm in ("k", "gd", "wd", "ad", "adb", "s")}
    Xr = [P.carve([128, BN], F32) for _ in range(2)]
    Xv = [P.carve([128, BN], F32) for _ in range(2)]
    logw = P.carve([128, BN], F32); asig = P.carve([128, BN], F32); kk = P.carve([128, BN], F32); kd = P.carve([128, BN], F32)
    t1 = P.carve([128, BN], F32); t2 = P.carve([128, BN], F32)
    Einv = P.carve([128, BN], F32); Eex = P.carve([128, BN], F32)
    DB = {nm: [P.carve([128, BN], F32) for _ in range(2)] for nm in ("E", "qT", "aT", "KsT", "BsT", "qm0", "qm1", "am0", "am1", "bm0",
                                                                      "bm1", "km0", "km1")}
    SB1 = {nm: P.carve([128, BN], F32) for nm in ("kT", "bT")}
    gtb = [P.carve([128, BN], F32) for _ in range(2)]
    bonusb = [P.carve([128, BN], F32) for _ in range(2)]
    YT = P.carve([128, BN], F32)
    outb = [P.carve([128, BN], F32) for _ in range(2)]
    Vt = [[P.carve([CH, 128], F32) for _ in range(4)] for _ in range(2)]
    Kst = [[P.carve([CH, 128], F32) for _ in range(4)] for _ in range(2)]
    Bst = [[P.carve([CH, 128], F32) for _ in range(4)] for _ in range(2)]
    Ot = [P.carve([CH, 128], F32) for _ in range(4)]
    Ob = [P.carve([CH, 128], F32) for _ in range(4)]
    b0, b1, b7 = ("bank", 0), ("bank", 1), ("bank", 7)
    bi_ = 0
    cctr = [0]
    kbc = [0]

    def recur_batch(kb, batch, d, b2, t0, n, offs):
        gt, bonus = gtb[b2], bonusb[b2]
        for gi, (w, c0, ci_) in enumerate(batch):
            c4 = cctr[0] % 4
            cctr[0] += 1
            cs_ = slice(c0, c0 + CH)
            yield from core.recur_gen(kb, gi, w, S, "S", Ot[c4], ("Ot", c4))
            tg = t0 + c0
            if d == 1:
                P.dma("sp", OB[tg:tg + CH, :], Ot[c4], reads=[("Ot", c4)], writes=[("OB", tg)])
            else:
                P.dma("sp", Ob[c4], OB[tg:tg + CH, :], reads=[("OB", tg)], writes=[("Ob", c4)])
                P.op("dve", "tensor_tensor", out=Ot[c4], in0=Ot[c4], in1=Ob[c4], op=ALU.add, reads=[("Ot", c4), ("Ob", c4)],
                     writes=[("Ot", c4)])
                tp3 = PS[7][:, 384:384 + CH]
                P.op("pe", "transpose", out=tp3, in_=Ot[c4], identity=cst["ident"][0:CH, 0:CH], reads=[("Ot", c4), "ident"],
                     writes=[b7])
                P.op("act", "activation", out=YT[:, cs_], in_=tp3, func=AF.Copy, reads=[], writes=[("YT", c0), b7])
            yield
        if d == 0:
            ytoks = [("YT", c0) for c0 in offs]
            o_ = outb[b2]
            P.op("pe", "matmul", out=PS[0][:, 0:n], lhsT=BLK, rhs=YT[:, 0:n], start=True, stop=True, reads=["cAs"] + ytoks, writes=[b0])
            P.op("dve", "scalar_tensor_tensor", out=t1[:, 0:n], in0=PS[0][:, 0:n], scalar=-1.0 / 64, in1=YT[:, 0:n], op0=ALU.mult,
                 op1=ALU.add, reads=ytoks, writes=["t1", b0])
            P.op("act", "activation", out=t2[:, 0:n], in_=t1[:, 0:n], func=AF.Square, reads=["t1"], writes=["t2"])
            yield
            P.op("pe", "matmul", out=PS[1][:, 0:n], lhsT=BLK, rhs=t2[:, 0:n], start=True, stop=True, reads=["cAs", "t2"], writes=[b1])
            P.op("act", "activation", out=t2[:, 0:n], in_=PS[1][:, 0:n], func=AF.Ln, scale=1.0 / 64, bias=epsl[:], reads=["epsl"],
                 writes=["t2", b1])
            P.op("act", "activation", out=t2[:, 0:n], in_=t2[:, 0:n], func=AF.Exp, scale=-0.5, reads=["t2"], writes=["t2"])
            yield
            P.op("dve", "tensor_tensor", out=t1[:, 0:n], in0=t1[:, 0:n], in1=t2[:, 0:n], op=ALU.mult, reads=["t1", "t2"], writes=["t1"])
            P.op("dve", "tensor_scalar", out=t1[:, 0:n], in0=t1[:, 0:n], scalar1=pc[:, 11:12], scalar2=pc[:, 12:13], op0=ALU.mult,
                 op1=ALU.add, reads=["t1", "pc"], writes=["t1"])
            P.op("pool", "tensor_tensor", out=t1[:, 0:n], in0=t1[:, 0:n], in1=bonus[:, 0:n], op=ALU.add, reads=["t1", ("bonus", b2)],
                 writes=["t1"])
            P.op("dve", "tensor_tensor", out=o_[:, 0:n], in0=t1[:, 0:n], in1=gt[:, 0:n], op=ALU.mult, reads=["t1", ("gt", b2)],
                 writes=[("outb", b2)])
            P.dma("pool", ysT[384:512, t0:t0 + n], o_[:, 0:n], reads=[("outb", b2)], writes=[("ysT", 3, t0)])
        yield

    for d in (1, 0):
        sfx = "FB"[d]
        P.op("pool", "memset", ap=S, constant=0.0, reads=[], writes=["S"])
        pending = None
        for (t0, n, s, offs) in _dir_chunks(d, RW_BLOCKS):
            b2 = bi_ % 2
            bi_ += 1
            gt, bonus = gtb[b2], bonusb[b2]
            seg0, seg1 = (0, 256) if s else (256, TTOT)
            lo, hi = max(t0 - 1, seg0), min(t0 + n + 1, seg1)
            srcs = [("r", PT["rw_r"], 128, 0), ("k", PT["rw_k"], 128, 1), ("v", PT["rw_v"], 128, 2), ("gd", PT["rw_gd"], 128, 3),
                    ("wd", PT["rw_wdf" if d == 0 else "rw_wdb"], 64, d), ("ad", PT["rw_adf" if d == 0 else "rw_adb"], 64, 2 + d)]
            if d == 0:
                srcs.append(("adb", PT["rw_adb"], 64, 3))
            else:
                srcs = [x for x in srcs if x[0] != "gd"]
            for i, (nm, src, rows_, mc) in enumerate(srcs):
                x_ = raw[nm][b2]
                if lo > t0 - 1 or hi < t0 + n + 1:
                    P.op("pool", "memset", ap=x_[0:rows_, 0:n + 2], constant=0.0, writes=[("raw", nm, 0)])
                P.dma("sp" if i % 2 == 0 else "pool", x_[0:rows_, lo - (t0 - 1):hi - (t0 - 1)], src[:, lo:hi], writes=[("raw", nm, 0)])
                dst = Xr[b2] if nm == "r" else Xv[b2] if nm == "v" else X[nm]
                dtok = ("Xr", b2) if nm == "r" else ("Xv", b2) if nm == "v" else ("X", nm)
                omc, hmc = (om[:, mc:mc + 1], hm[:, mc:mc + 1]) if rows_ == 128 else (om64[:, mc:mc + 1], hm64[:, mc:mc + 1])
                P.op("pool", "tensor_tensor", out=X["s"][0:rows_, 0:n], in0=x_[0:rows_, 0:n], in1=x_[0:rows_, 2:n + 2], op=ALU.add,
                     reads=[("raw", nm, 0)], writes=[("X", "s")])
                P.op("dve", "tensor_scalar", out=dst[0:rows_, 0:n], in0=x_[0:rows_, 1:n + 1], scalar1=omc, scalar2=None, op0=ALU.mult,
                     reads=[("raw", nm, 0), "omhm"], writes=[dtok])
                P.op("dve", "scalar_tensor_tensor", out=dst[0:rows_, 0:n], in0=X["s"][0:rows_, 0:n], scalar=hmc, in1=dst[0:rows_, 0:n],
                     op0=ALU.mult, op1=ALU.add, reads=[("X", "s"), "omhm2", dtok], writes=[dtok])
            xr_, xv_ = Xr[b2], Xv[b2]
            P.op("act", "activation", out=X["wd"][0:64, 0:n], in_=X["wd"][0:64, 0:n], func=AF.Tanh, reads=[("X", "wd")], writes=[("X", "wd")])
            P.op("pe", "matmul", out=PS[0][:, 0:n], lhsT=w2[:, d, :], rhs=X["wd"][0:64, 0:n], start=True, stop=True, reads=["w2", ("X", "wd")],
                 writes=[b0])
            P.op("act", "activation", out=logw[:, 0:n], in_=PS[0][:, 0:n], func=AF.Sigmoid, bias=pc[:, 4 + d:5 + d], reads=["pc"],
                 writes=["logw", b0])
            P.op("dve", "tensor_scalar", out=logw[:, 0:n], in0=logw[:, 0:n], scalar1=NEG_EM05, scalar2=None, op0=ALU.mult, reads=["logw"],
                 writes=["logw"])
            P.op("pe", "matmul", out=PS[1][:, 0:n], lhsT=a2[:, d, :], rhs=X["ad"][0:64, 0:n], start=True, stop=True, reads=["a2", ("X", "ad")],
                 writes=[b1])
            P.op("act", "activation", out=asig[:, 0:n], in_=PS[1][:, 0:n], func=AF.Sigmoid, bias=pc[:, 6 + d:7 + d], reads=["pc"],
                 writes=["asig", b1])
            P.op("dve", "tensor_scalar", out=kk[:, 0:n], in0=X["k"][:, 0:n], scalar1=pc[:, 8:9], scalar2=None, op0=ALU.mult,
                 reads=[("X", "k"), "pc"], writes=["kk"])
            P.op("act", "activation", out=t1[:, 0:n], in_=kk[:, 0:n], func=AF.Square, reads=["kk"], writes=["t1"])
            P.op("pe", "matmul", out=PS[0][:, 0:n], lhsT=BLK, rhs=t1[:, 0:n], start=True, stop=True, reads=["cAs", "t1"], writes=[b0])
            P.op("act", "activation", out=t1[:, 0:n], in_=PS[0][:, 0:n], func=AF.Ln, bias=cst["epsb"][:], reads=["epsb"], writes=["t1", b0])
            P.op("act", "activation", out=t1[:, 0:n], in_=t1[:, 0:n], func=AF.Exp, scale=-0.5, reads=["t1"], writes=["t1"])
            P.op("dve", "tensor_tensor", out=kk[:, 0:n], in0=kk[:, 0:n], in1=t1[:, 0:n], op=ALU.mult, reads=["kk", "t1"], writes=["kk"])
            P.op("dve", "tensor_scalar", out=kd[:, 0:n], in0=asig[:, 0:n], scalar1=pc[:, 9:10], scalar2=omka[:, 0:1], op0=ALU.mult,
                 op1=ALU.add, reads=["asig", "pc", "omka"], writes=["kd"])
            P.op("dve", "tensor_tensor", out=kd[:, 0:n], in0=kd[:, 0:n], in1=X["k"][:, 0:n], op=ALU.mult, reads=["kd", ("X", "k")],
                 writes=["kd"])
            if d == 0:
                P.op("act", "activation", out=X["gd"][:, 0:n], in_=X["gd"][:, 0:n], func=AF.Sigmoid, reads=[("X", "gd")], writes=[("X", "gd")])
                P.op("pe", "matmul", out=PS[1][:, 0:n], lhsT=g2, rhs=X["gd"][:, 0:n], start=True, stop=True, reads=["g2", ("X", "gd")],
                     writes=[b1])
                P.op("act", "activation", out=gt[:, 0:n], in_=PS[1][:, 0:n], func=AF.Copy, reads=[], writes=[("gt", b2), b1])
                P.op("pe", "matmul", out=PS[0][:, 0:n], lhsT=a2[:, 1, :], rhs=X["adb"][0:64, 0:n], start=True, stop=True,
                     reads=["a2", ("X", "adb")], writes=[b0])
                P.op("act", "activation", out=t2[:, 0:n], in_=PS[0][:, 0:n], func=AF.Sigmoid, bias=pc[:, 7:8], reads=["pc"], writes=["t2", b0])
                P.op("dve", "tensor_scalar", out=t2[:, 0:n], in0=t2[:, 0:n], scalar1=pc[:, 9:10], scalar2=omka[:, 0:1], op0=ALU.mult,
                     op1=ALU.add, reads=["t2", "pc", "omka"], writes=["t2"])
                P.op("dve", "tensor_tensor", out=t2[:, 0:n], in0=t2[:, 0:n], in1=X["k"][:, 0:n], op=ALU.mult, reads=["t2", ("X", "k")],
                     writes=["t2"])
                P.op("dve", "tensor_tensor", out=t2[:, 0:n], in0=t2[:, 0:n], in1=kd[:, 0:n], op=ALU.add, reads=["t2", "kd"], writes=["t2"])
                P.op("dve", "scalar_tensor_tensor", out=t2[:, 0:n], in0=t2[:, 0:n], scalar=pc[:, 10:11], in1=xr_[:, 0:n], op0=ALU.mult,
                     op1=ALU.mult, reads=["t2", "pc", ("Xr", b2)], writes=["t2"])
                P.op("pe", "matmul", out=PS[1][:, 0:n], lhsT=BLK, rhs=t2[:, 0:n], start=True, stop=True, reads=["cAs", "t2"], writes=[b1])
                P.op("dve", "tensor_tensor", out=bonus[:, 0:n], in0=PS[1][:, 0:n], in1=xv_[:, 0:n], op=ALU.mult, reads=[("Xv", b2)],
                     writes=[("bonus", b2), b1])
            E = DB["E"][b2]
            rst = _ld_const(P, io, cAs, "RST" + sfx)
            rv = (lambda ap: ap[:, 0:n][:, ::-1]) if d == 1 else (lambda ap: ap[:, 0:n])
            P.op("dve", "tensor_tensor_scan", out=rv(E), data0=rv(rst), data1=rv(logw), initial=0.0, op0=ALU.mult, op1=ALU.add,
                 reads=["logw", "cAs"], writes=[("E", b2)])
            P.op("act", "activation", out=Einv[:, 0:n], in_=E[:, 0:n], func=AF.Exp, scale=-1.0, reads=[("E", b2)], writes=["Einv"])
            P.op("dve", "tensor_tensor", out=Eex[:, 0:n], in0=E[:, 0:n], in1=logw[:, 0:n], op=ALU.subtract, reads=[("E", b2), "logw"],
                 writes=["Eex"])
            P.op("act", "activation", out=Eex[:, 0:n], in_=Eex[:, 0:n], func=AF.Exp, reads=["Eex"], writes=["Eex"])
            P.op("act", "activation", out=E[:, 0:n], in_=E[:, 0:n], func=AF.Exp, reads=[("E", b2)], writes=[("E", b2)])
            T = {nm: DB[nm][b2] for nm in DB}
            T.update(SB1)
            dt = lambda nm: (nm, b2) if nm in DB else (nm, 0)
            P.op("pool", "tensor_tensor", out=T["qT"][:, 0:n], in0=xr_[:, 0:n], in1=E[:, 0:n], op=ALU.mult, reads=[("Xr", b2), ("E", b2)],
                 writes=[dt("qT")])
            P.op("pool", "tensor_tensor", out=T["aT"][:, 0:n], in0=kk[:, 0:n], in1=Eex[:, 0:n], op=ALU.mult, reads=["kk", "Eex"],
                 writes=[dt("aT")])
            P.op("dve", "tensor_tensor", out=T["kT"][:, 0:n], in0=kd[:, 0:n], in1=Einv[:, 0:n], op=ALU.mult, reads=["kd", "Einv"],
                 writes=[dt("kT")])
            P.op("pool", "tensor_tensor", out=t1[:, 0:n], in0=kk[:, 0:n], in1=asig[:, 0:n], op=ALU.mult, reads=["kk", "asig"], writes=["t1"])
            P.op("dve", "scalar_tensor_tensor", out=T["bT"][:, 0:n], in0=t1[:, 0:n], scalar=-1.0, in1=Einv[:, 0:n], op0=ALU.mult,
                 op1=ALU.mult, reads=["t1", "Einv"], writes=[dt("bT")])
            for (src_, pre, eng) in (("qT", "qm", "pool"), ("aT", "am", "dve"), ("bT", "bm", "pool"), ("kT", "km", "dve")):
                for hh in range(2):
                    P.op(eng, "tensor_scalar", out=T[pre + str(hh)][:, 0:n], in0=T[src_][:, 0:n], scalar1=BLK[:, hh * 64:hh * 64 + 1],
                         scalar2=None, op0=ALU.mult, reads=[dt(src_), "cAs"], writes=[dt(pre + str(hh))])
            for c0 in offs:
                last = c0 + CH - 1 if d == 0 else c0
                cs_ = slice(c0, c0 + CH)
                P.op("dve", "tensor_scalar", out=T["KsT"][:, cs_], in0=T["kT"][:, cs_], scalar1=E[:, last:last + 1], scalar2=None,
                     op0=ALU.mult, reads=[dt("kT"), ("E", b2)], writes=[("KsT", b2, c0)])
                P.op("pool", "tensor_scalar", out=T["BsT"][:, cs_], in0=T["bT"][:, cs_], scalar1=E[:, last:last + 1], scalar2=None,
                     op0=ALU.mult, reads=[dt("bT"), ("E", b2)], writes=[("BsT", b2, c0)])
            wsb = []
            for ci_, c0 in enumerate(offs):
                cs_ = slice(c0, c0 + CH)
                last = c0 + CH - 1 if d == 0 else c0
                for (src, stok, dst, dnm, col, eng) in ((xv_, ("Xv", b2), Vt[b2][ci_], "Vt", 0, "act"),
                                                        (T["KsT"], ("KsT", b2, c0), Kst[b2][ci_], "Kst", 128, "dve"),
                                                        (T["BsT"], ("BsT", b2, c0), Bst[b2][ci_], "Bst", 256, "act")):
                    tp = PS[7][0:CH, col:col + 128]
                    P.op("pe", "transpose", out=tp, in_=src[:, cs_], identity=cst["ident"][:], reads=[stok, "ident"], writes=[b7])
                    if eng == "act":
                        P.op("act", "activation", out=dst, in_=tp, func=AF.Copy, reads=[], writes=[(dnm, b2, ci_), b7])
                    else:
                        P.op("dve", "tensor_copy", out=dst, in_=tp, reads=[], writes=[(dnm, b2, ci_), b7])
                hm_ = lambda pre: [(T[pre + str(hh)][:, cs_], dt(pre + str(hh))) for hh in range(2)]
                w = {"aT": (T["aT"][:, cs_], dt("aT")), "qT": (T["qT"][:, cs_], dt("qT")), "bTm": hm_("bm"), "kTm": hm_("km"),
                     "aTsm": hm_("am"), "qTsm": hm_("qm"),
                     "Bst": (Bst[b2][ci_], ("Bst", b2, ci_)), "Kst": (Kst[b2][ci_], ("Kst", b2, ci_)), "V": (Vt[b2][ci_], ("Vt", b2, ci_)),
                     "cs": (E[:, last:last + 1], ("E", b2))}
                wsb.append((w, c0, ci_))
            kb = kbc[0]
            kbc[0] += 1
            GW = len(wsb) * 128
            pg = core.prep_gen(kb, [x[0] for x in wsb], _ld_const(P, io, cAs, "MSTR" + sfx + "4", 64)[:, 0:GW], "cAs",
                               _ld_const(P, io, cAs, "MASK" + sfx + "4", 64)[:, 0:GW], "cAs")
            interleave(pg, pending)
            pending = recur_batch(kb, wsb, d, b2, t0, n, offs)
        interleave(None, pending)
    P.release(m0)


def A_gather(outs):
    ys = np.zeros((2, TTOT, 4, 512), np.float32)
    for core in range(8):
        b, h = core // 4, core % 4
        o = outs[core].reshape(4, 128, TTOT)
        ys[b, :, :, h * 128:(h + 1) * 128] = np.transpose(o, (2, 0, 1))
    ys = ys.reshape(2, TTOT, 2048)
    return ys[:, CTX:], ys[:, :CTX]


def kernel(**inputs):
    inp = {k: np.asarray(v) for k, v in inputs.items()}
    h_lat, h_ctx = inp["x"].astype(np.float32), inp["ctx"].astype(np.float32)
    cores = list(range(8))
    for l in range(2):
        last = l == 1
        ncA, _ = build_A()
        resA = run_bass_kernel_spmd(ncA, A_inmaps(l, h_lat, h_ctx, inp), core_ids=cores)
        ysl, ysc = A_gather([r["ysT"] for r in resA.results])
        ncB, _ = build_B(last)
        resB = run_bass_kernel_spmd(ncB, B_inmaps(l, last, h_lat, h_ctx, ysl, ysc, inp), core_ids=cores)
        h_lat, h_ctx = B_gather(last, [r["out"] for r in resB.results])
    return np.ascontiguousarray(h_lat, dtype=np.float32)
```
